# Optimizing a Trainium2 kernel written in Bass

```python
import math
import jax, jax.numpy as jnp
from jax import lax
import numpy as np

D_MODEL = 1024
BATCH = 16
SEQ = 4096
DEPTH = 1

D_MIX = D_MODEL
D_MLSTM = D_MIX // 2
D_SSM = D_MIX - D_MLSTM
MLSTM_HEADS = 4
MLSTM_DV = D_MLSTM // MLSTM_HEADS
MLSTM_DQK = MLSTM_DV // 2
MLSTM_CHUNK = 128
QK_CONV = 5
SSM_GROUP = 16
SSM_GROUPS = D_SSM // SSM_GROUP
SSM_STATE = 64
D_FF = 2816
N_SUB = 3
DEEP_ALPHA = (2.0 * DEPTH) ** 0.25
DEEP_BETA = (8.0 * DEPTH) ** -0.25
LN_EPS = 1e-5
F_BIAS_LO = 3.0
F_BIAS_HI = 6.0
LOG_STEP_MIN = math.log(1e-3)
LOG_STEP_MAX = math.log(1e-1)

Q_COLS = MLSTM_HEADS * MLSTM_DQK
K_COLS = MLSTM_HEADS * MLSTM_DQK
V_COLS = D_MLSTM
O_COLS = D_MLSTM
IG_COLS = 2 * MLSTM_HEADS
FG_COLS = 2 * MLSTM_HEADS
U_COLS = D_SSM
SPLIT_POINTS = (Q_COLS, Q_COLS + K_COLS, Q_COLS + K_COLS + V_COLS,
                Q_COLS + K_COLS + V_COLS + O_COLS,
                Q_COLS + K_COLS + V_COLS + O_COLS + IG_COLS,
                Q_COLS + K_COLS + V_COLS + O_COLS + IG_COLS + FG_COLS)
FG_OFFSET = Q_COLS + K_COLS + V_COLS + O_COLS + IG_COLS
IN_COLS = Q_COLS + K_COLS + V_COLS + O_COLS + IG_COLS + FG_COLS + U_COLS

kernel_name = "hymba_mlstm_s5_macaron_deepnorm_adaln"


def _layer_norm(x, g, b):
    xf = x.astype(jnp.float32)
    mu = jnp.mean(xf, axis=-1, keepdims=True)
    var = jnp.mean(jnp.square(xf - mu), axis=-1, keepdims=True)
    return ((xf - mu) * lax.rsqrt(var + LN_EPS) * g + b).astype(x.dtype)


def _rms_norm(x, g):
    xf = x.astype(jnp.float32)
    return xf * lax.rsqrt(jnp.mean(jnp.square(xf), axis=-1, keepdims=True) + LN_EPS) * g


def _modulate(x, shift, scale):
    return x * (1.0 + scale[:, None, :]) + shift[:, None, :]


def _post_norm(x, f_out, gate, g, b):
    return _layer_norm(DEEP_ALPHA * x + (1.0 + gate[:, None, :]) * f_out, g, b)


def _swiglu(h, w_up, w_down):
    gate, up = jnp.split(h @ w_up, 2, axis=-1)
    return (jax.nn.silu(gate) * up) @ w_down


def _centred_dwconv(x, w, b):
    ch = x.shape[-1]
    y = lax.conv_general_dilated(
        x, w[:, None, :].astype(x.dtype), window_strides=(1,),
        padding=[(QK_CONV // 2, QK_CONV // 2)],
        dimension_numbers=('NWC', 'WIO', 'NWC'), feature_group_count=ch)
    return y + b


def _mlstm_chunkwise(q, k, v, i_pre, f_pre):
    bsz, nh, s_len, dqk = q.shape
    dv = v.shape[-1]
    nc = s_len // MLSTM_CHUNK

    def chunks(t):
        t = t.reshape((bsz, nh, nc, MLSTM_CHUNK) + t.shape[3:])
        return jnp.moveaxis(t, 2, 0)

    xs = (chunks(q), chunks(k), chunks(v), chunks(i_pre), chunks(jax.nn.log_sigmoid(f_pre)))
    lower = jnp.tril(jnp.ones((MLSTM_CHUNK, MLSTM_CHUNK), dtype=bool))

    def step(carry, inp):
        c_mem, n_mem, m_mem = carry
        q_c, k_c, v_c, i_c, lf_c = inp
        b_cum = jnp.cumsum(lf_c, axis=-1)
        d_log = b_cum[..., :, None] - b_cum[..., None, :] + i_c[..., None, :]
        d_log = jnp.where(lower, d_log, -jnp.inf)
        inter = b_cum + m_mem[..., None]
        m_t = jnp.maximum(inter, jnp.max(d_log, axis=-1))
        w_intra = jnp.exp(d_log - m_t[..., None])
        w_inter = jnp.exp(inter - m_t)
        scores = jnp.einsum('bhtd,bhsd->bhts', q_c, k_c) * w_intra
        num = (jnp.einsum('bhts,bhsv->bhtv', scores, v_c)
               + w_inter[..., None] * jnp.einsum('bhtd,bhdv->bhtv', q_c, c_mem))
        den = jnp.sum(scores, axis=-1) + w_inter * jnp.einsum('bhtd,bhd->bht', q_c, n_mem)
        h = num / jnp.maximum(jnp.abs(den), jnp.exp(-m_t))[..., None]
        b_end = b_cum[..., -1]
        g_log = b_end[..., None] - b_cum + i_c
        m_new = jnp.maximum(b_end + m_mem, jnp.max(g_log, axis=-1))
        w_state = jnp.exp(g_log - m_new[..., None])
        decay = jnp.exp(b_end + m_mem - m_new)
        c_new = decay[..., None, None] * c_mem + jnp.einsum('bhs,bhsd,bhsv->bhdv', w_state, k_c, v_c)
        n_new = decay[..., None] * n_mem + jnp.einsum('bhs,bhsd->bhd', w_state, k_c)
        return (c_new, n_new, m_new), h

    init = (jnp.zeros((bsz, nh, dqk, dv), jnp.float32),
            jnp.zeros((bsz, nh, dqk), jnp.float32),
            jnp.zeros((bsz, nh), jnp.float32))
    _, h_chunks = lax.scan(step, init, xs)
    return jnp.moveaxis(h_chunks, 0, 2).reshape(bsz, nh, s_len, dv)


def _complex_linear_combine(left, right):
    a1r, a1i, b1r, b1i = left
    a2r, a2i, b2r, b2i = right
    return (a2r * a1r - a2i * a1i, a2r * a1i + a2i * a1r,
            a2r * b1r - a2i * b1i + b2r, a2r * b1i + a2i * b1r + b2i)


def _s5_bidir(u, lam_re, lam_im, log_step, b_re, b_im, c_re, c_im, d_skip):
    bsz, s_len, _ = u.shape
    uf = u.astype(jnp.float32)
    ug = uf.reshape(bsz, s_len, SSM_GROUPS, SSM_GROUP)
    bu_re = jnp.einsum('bsgh,gph->bsgp', ug, b_re)
    bu_im = jnp.einsum('bsgh,gph->bsgp', ug, b_im)
    y = d_skip * uf
    for direction, reverse in ((0, False), (1, True)):
        lr, li = lam_re[direction], lam_im[direction]
        step = jnp.exp(log_step[direction])[:, None]
        mag = jnp.exp(lr * step)
        ab_re, ab_im = mag * jnp.cos(li * step), mag * jnp.sin(li * step)
        nr, ni = ab_re - 1.0, ab_im
        inv = 1.0 / (lr * lr + li * li)
        cf_re = (nr * lr + ni * li) * inv
        cf_im = (ni * lr - nr * li) * inv
        x_re = cf_re * bu_re - cf_im * bu_im
        x_im = cf_re * bu_im + cf_im * bu_re
        a_re = jnp.broadcast_to(ab_re, (1, s_len) + ab_re.shape)
        a_im = jnp.broadcast_to(ab_im, (1, s_len) + ab_im.shape)
        _, _, h_re, h_im = lax.associative_scan(
            _complex_linear_combine, (a_re, a_im, x_re, x_im), axis=1, reverse=reverse)
        y_dir = (jnp.einsum('bsgp,ghp->bsgh', h_re, c_re[direction])
                 - jnp.einsum('bsgp,ghp->bsgh', h_im, c_im[direction]))
        y = y + y_dir.reshape(bsz, s_len, D_SSM)
    return y


def _token_mixer(u, w_in, b_in, qk_conv_w, qk_conv_b, mlstm_norm_g, ssm_lam_re, ssm_lam_im,
                 ssm_log_step, ssm_b_re, ssm_b_im, ssm_c_re, ssm_c_im, ssm_d, ssm_glu_w,
                 ssm_glu_b, ssm_norm_g, w_out):
    bsz, s_len, _ = u.shape
    proj = u @ w_in + b_in
    q, k, v, o, ig, fg, us = jnp.split(proj, SPLIT_POINTS, axis=-1)
    qk = jax.nn.silu(_centred_dwconv(jnp.concatenate([q, k], axis=-1), qk_conv_w, qk_conv_b))
    q, k = jnp.split(qk, 2, axis=-1)

    def heads(t):
        return t.reshape(bsz, s_len, MLSTM_HEADS, -1).transpose(0, 2, 1, 3).astype(jnp.float32)

    def gates(t):
        return t.astype(jnp.float32).reshape(bsz, s_len, 2, MLSTM_HEADS).transpose(2, 0, 3, 1)

    qh, kh, vh = heads(q), heads(k) * (MLSTM_DQK ** -0.5), heads(v)
    igs, fgs = gates(ig), gates(fg)
    flip = lambda t: jnp.flip(t, axis=2)
    cell = (_mlstm_chunkwise(qh, kh, vh, igs[0], fgs[0])
            + flip(_mlstm_chunkwise(flip(qh), flip(kh), flip(vh), flip(igs[1]), flip(fgs[1]))))
    h = jax.nn.sigmoid(heads(o)) * cell
    mu = jnp.mean(h, axis=-1, keepdims=True)
    var = jnp.mean(jnp.square(h - mu), axis=-1, keepdims=True)
    h = (h - mu) * lax.rsqrt(var + LN_EPS) * mlstm_norm_g.reshape(MLSTM_HEADS, 1, MLSTM_DV)
    h_mlstm = h.transpose(0, 2, 1, 3).reshape(bsz, s_len, D_MLSTM)

    y = _s5_bidir(us, ssm_lam_re, ssm_lam_im, ssm_log_step, ssm_b_re, ssm_b_im,
                  ssm_c_re, ssm_c_im, ssm_d)
    y = jax.nn.gelu(y)
    y = y * jax.nn.sigmoid(y @ ssm_glu_w + ssm_glu_b)
    y = _rms_norm(y, ssm_norm_g)
    mixed = jnp.concatenate([h_mlstm, y], axis=-1).astype(u.dtype)
    return mixed @ w_out


def setup_inputs(seed: int = 0) -> dict:
    key = jax.random.key(seed)
    ks = jax.random.split(key, 32)
    L = DEPTH
    nrm = lambda k, shape, s: jax.random.normal(k, shape, jnp.float32) * s
    gain = lambda k, shape: 1.0 + nrm(k, shape, 0.01)
    b_in = nrm(ks[9], (L, IN_COLS), 0.02)
    f_bias = jnp.tile(jnp.linspace(F_BIAS_LO, F_BIAS_HI, MLSTM_HEADS), 2)
    b_in = b_in.at[:, FG_OFFSET:FG_OFFSET + FG_COLS].add(f_bias)
    lam_im0 = math.pi * jnp.arange(SSM_STATE, dtype=jnp.float32)
    return {
        "x": nrm(ks[0], (BATCH, SEQ, D_MODEL), 1.0),
        "c": nrm(ks[1], (BATCH, D_MODEL), 1.0),
        "w_ada": nrm(ks[2], (L, D_MODEL, N_SUB * 3 * D_MODEL), 0.5 * D_MODEL ** -0.5),
        "b_ada": nrm(ks[3], (L, N_SUB * 3 * D_MODEL), 0.02),
        "ffn1_w_up": nrm(ks[4], (L, D_MODEL, 2 * D_FF), D_MODEL ** -0.5),
        "ffn1_w_down": nrm(ks[5], (L, D_FF, D_MODEL), DEEP_BETA * D_FF ** -0.5),
        "ln1_g": gain(ks[6], (L, D_MODEL)),
        "ln1_b": nrm(ks[7], (L, D_MODEL), 0.01),
        "w_in": nrm(ks[8], (L, D_MODEL, IN_COLS), D_MODEL ** -0.5),
        "b_in": b_in,
        "qk_conv_w": nrm(ks[10], (L, QK_CONV, Q_COLS + K_COLS), QK_CONV ** -0.5),
        "qk_conv_b": nrm(ks[11], (L, Q_COLS + K_COLS), 0.01),
        "mlstm_norm_g": gain(ks[12], (L, D_MLSTM)),
        "ssm_lam_re": -0.5 + nrm(ks[13], (L, 2, SSM_GROUPS, SSM_STATE), 0.01),
        "ssm_lam_im": lam_im0 + nrm(ks[14], (L, 2, SSM_GROUPS, SSM_STATE), 0.01),
        "ssm_log_step": jax.random.uniform(ks[15], (L, 2, SSM_GROUPS), jnp.float32,
                                           LOG_STEP_MIN, LOG_STEP_MAX),
        "ssm_b_re": nrm(ks[16], (L, SSM_GROUPS, SSM_STATE, SSM_GROUP), (2 * SSM_GROUP) ** -0.5),
        "ssm_b_im": nrm(ks[17], (L, SSM_GROUPS, SSM_STATE, SSM_GROUP), (2 * SSM_GROUP) ** -0.5),
        "ssm_c_re": nrm(ks[18], (L, 2, SSM_GROUPS, SSM_GROUP, SSM_STATE), 0.5),
        "ssm_c_im": nrm(ks[19], (L, 2, SSM_GROUPS, SSM_GROUP, SSM_STATE), 0.5),
        "ssm_d": nrm(ks[20], (L, D_SSM), 1.0),
        "ssm_glu_w": nrm(ks[21], (L, D_SSM, D_SSM), D_SSM ** -0.5),
        "ssm_glu_b": nrm(ks[22], (L, D_SSM), 0.01),
        "ssm_norm_g": gain(ks[23], (L, D_SSM)),
        "w_out": nrm(ks[24], (L, D_MIX, D_MODEL), DEEP_BETA * D_MIX ** -0.5),
        "ln2_g": gain(ks[25], (L, D_MODEL)),
        "ln2_b": nrm(ks[26], (L, D_MODEL), 0.01),
        "ffn2_w_up": nrm(ks[27], (L, D_MODEL, 2 * D_FF), D_MODEL ** -0.5),
        "ffn2_w_down": nrm(ks[28], (L, D_FF, D_MODEL), DEEP_BETA * D_FF ** -0.5),
        "ln3_g": gain(ks[29], (L, D_MODEL)),
        "ln3_b": nrm(ks[30], (L, D_MODEL), 0.01),
    }


def reference(x, c, w_ada, b_ada, ffn1_w_up, ffn1_w_down, ln1_g, ln1_b, w_in, b_in,
              qk_conv_w, qk_conv_b, mlstm_norm_g, ssm_lam_re, ssm_lam_im, ssm_log_step,
              ssm_b_re, ssm_b_im, ssm_c_re, ssm_c_im, ssm_d, ssm_glu_w, ssm_glu_b,
              ssm_norm_g, w_out, ln2_g, ln2_b, ffn2_w_up, ffn2_w_down, ln3_g, ln3_b):
    bsz = x.shape[0]
    c_act = jax.nn.silu(c)
    for l in range(DEPTH):
        mod = (c_act @ w_ada[l] + b_ada[l]).reshape(bsz, N_SUB, 3, D_MODEL)
        f1 = _swiglu(_modulate(x, mod[:, 0, 0], mod[:, 0, 1]), ffn1_w_up[l], ffn1_w_down[l])
        x = _post_norm(x, 0.5 * f1, mod[:, 0, 2], ln1_g[l], ln1_b[l])
        tm = _token_mixer(_modulate(x, mod[:, 1, 0], mod[:, 1, 1]), w_in[l], b_in[l],
                          qk_conv_w[l], qk_conv_b[l], mlstm_norm_g[l], ssm_lam_re[l],
                          ssm_lam_im[l], ssm_log_step[l], ssm_b_re[l], ssm_b_im[l],
                          ssm_c_re[l], ssm_c_im[l], ssm_d[l], ssm_glu_w[l], ssm_glu_b[l],
                          ssm_norm_g[l], w_out[l])
        x = _post_norm(x, tm, mod[:, 1, 2], ln2_g[l], ln2_b[l])
        f2 = _swiglu(_modulate(x, mod[:, 2, 0], mod[:, 2, 1]), ffn2_w_up[l], ffn2_w_down[l])
        x = _post_norm(x, 0.5 * f2, mod[:, 2, 2], ln3_g[l], ln3_b[l])
    return x
```

```python
import os
import math
import contextlib
import numpy as np
import concourse.bass as bass
import concourse.mybir as mybir
from concourse.bass_utils import run_bass_kernel_spmd

F32 = mybir.dt.float32
BF16 = mybir.dt.bfloat16
I32 = mybir.dt.int32
AF = mybir.ActivationFunctionType
ALU = mybir.AluOpType
AX = mybir.AxisListType

D = 1024
DF = 2816
T = 4096
NB = 2
TT = 256
NT = T // TT
ALPHA = 2.0 ** 0.25
LN_EPS = 1e-5
EPS_P = LN_EPS / (ALPHA * ALPHA)
IN_COLS = 2064
STAGE = int(os.environ.get("MK_STAGE", "9"))
SKIP = os.environ.get("MK_SKIP", "").split(",")

C_LN = 0
C_BADA = 48
C_BQK = 120
C_CW = 124
C_CB = 144
C_GLUB = 148
C_SNG = 152
C_BG = 156
NCOLP = 160
R_BV, R_BO, R_BU, R_MNG = 0, 512, 1024, 1536
NROWP = 2048


class Buf:
    __slots__ = ("name", "w", "r")

    def __init__(self, name="", init_readers=()):
        self.name = name
        self.w = None
        self.r = list(init_readers)


class Sched:
    ENGS = ("pe", "act", "dve", "pool", "sp")
    DMA_RING = 8
    SEM_MAX = 30000

    def __init__(self, nc):
        self.nc = nc
        self.ops = []
        self.by_eng = {e: [] for e in self.ENGS}

    def op(self, eng, fn, reads=(), writes=(), dma=False):
        oid = len(self.ops)
        deps = set()
        for b in reads:
            if b.w is not None:
                deps.add(b.w)
        for b in writes:
            if b.w is not None:
                deps.add(b.w)
            for r in b.r:
                deps.add(r)
        if eng == "pe":
            deps = {d for d in deps if self.ops[d][0] != "pe"}
        self.ops.append((eng, fn, deps, dma))
        self.by_eng[eng].append(oid)
        for b in reads:
            b.r.append(oid)
        for b in writes:
            b.w = oid
            b.r = []
        return oid

    def fence(self):
        f = []
        for e in self.ENGS:
            lst = self.by_eng[e]
            if not lst:
                continue
            if e == "sp":
                f.extend(lst[-self.DMA_RING:])
            else:
                f.append(lst[-1])
        return f

    def emit(self, final_wait_ops=()):
        nc = self.nc
        ops = self.ops
        needed = [False] * len(ops)
        for (_, _, deps, _) in ops:
            for d in deps:
                needed[d] = True
        for d in final_wait_ops:
            needed[d] = True
        sig = [None] * len(ops)
        cnt = {e: 0 for e in self.ENGS}
        dma_idx = {e: 0 for e in self.ENGS}
        for e in self.ENGS:
            for oid in self.by_eng[e]:
                isdma = ops[oid][3]
                if isdma:
                    n = dma_idx[e]
                    dma_idx[e] += 1
                    sig[oid] = (("d", e, n % self.DMA_RING), 16 * (n // self.DMA_RING + 1))
                elif needed[oid]:
                    c = cnt[e]
                    cnt[e] += 1
                    sig[oid] = (("c", e, c // self.SEM_MAX), c % self.SEM_MAX + 1)
        keys = []
        kset = set()
        for s in sig:
            if s is not None and s[0] not in kset:
                kset.add(s[0])
                keys.append(s[0])
        with contextlib.ExitStack() as st:
            sems = {}
            for k in keys:
                sems[k] = st.enter_context(nc.semaphore("s_%s_%s_%d" % k))
            block = st.enter_context(nc.Block())

            def run_engine(ename, eh):
                seen = {}
                for oid in self.by_eng[ename]:
                    _, fn, deps, isdma = ops[oid]
                    waits = {}
                    for d in deps:
                        k, v = sig[d]
                        if seen.get(k, 0) >= v:
                            continue
                        if waits.get(k, 0) < v:
                            waits[k] = v
                    if isdma:
                        k, v = sig[oid]
                        if v > 16 and seen.get(k, 0) < v - 16 and waits.get(k, 0) < v - 16:
                            waits[k] = v - 16
                    for k, v in waits.items():
                        eh.wait_ge(sems[k], v)
                        seen[k] = v
                    ins = fn(eh)
                    if sig[oid] is not None:
                        k, v = sig[oid]
                        ins.then_inc(sems[k], 16 if isdma else 1)
                if ename == "sp":
                    fw = {}
                    for d in final_wait_ops:
                        k, v = sig[d]
                        if fw.get(k, 0) < v:
                            fw[k] = v
                    for k, v in fw.items():
                        if seen.get(k, 0) < v:
                            eh.wait_ge(sems[k], v)

            @block.tensor
            def _(e):
                run_engine("pe", e)

            @block.scalar
            def _(e):
                run_engine("act", e)

            @block.vector
            def _(e):
                run_engine("dve", e)

            @block.gpsimd
            def _(e):
                run_engine("pool", e)

            @block.sync
            def _(e):
                run_engine("sp", e)


class KB:
    def __init__(self, nc):
        self.nc = nc
        self.S = Sched(nc)
        self.stacks = []
        self.fence_ops = []
        self.uid = 0

    @contextlib.contextmanager
    def scope(self):
        st = contextlib.ExitStack()
        self.stacks.append(st)
        try:
            with st:
                yield
        finally:
            self.stacks.pop()
            self.fence_ops = self.S.fence()

    def sb(self, shape, dt, name=None):
        self.uid += 1
        nm = "%s_%d" % (name or "t", self.uid)
        return self.stacks[-1].enter_context(self.nc.sbuf_tensor(nm, list(shape), dt))

    def buf(self, name=""):
        return Buf(name, self.fence_ops)

    def dma(self, out, in_, reads=(), writes=(), **kw):
        return self.S.op("sp", lambda e: e.dma_start(out=out, in_=in_, **kw), reads, writes, dma=True)

    def mm(self, out, lhsT, rhs, start, stop, reads=(), writes=()):
        return self.S.op("pe", lambda e: e.matmul(out, lhsT=lhsT, rhs=rhs, start=start, stop=stop), reads, writes)

    def act(self, out, in_, func, reads=(), writes=(), scale=1.0, bias=0.0):
        return self.S.op("act", lambda e: e.activation(out=out, in_=in_, func=func, scale=scale, bias=bias), reads, writes)

    def tt(self, eng, out, in0, in1, op, reads=(), writes=()):
        return self.S.op(eng, lambda e: e.tensor_tensor(out=out, in0=in0, in1=in1, op=op), reads, writes)

    def ts(self, eng, out, in0, s1, s2, op0, op1=None, reads=(), writes=()):
        if op1 is None:
            return self.S.op(eng, lambda e: e.tensor_scalar(out=out, in0=in0, scalar1=s1, scalar2=None, op0=op0), reads, writes)
        return self.S.op(eng, lambda e: e.tensor_scalar(out=out, in0=in0, scalar1=s1, scalar2=s2, op0=op0, op1=op1), reads, writes)

    def stt(self, out, in0, scalar, in1, op0, op1, reads=(), writes=()):
        return self.S.op("dve", lambda e: e.scalar_tensor_tensor(out=out, in0=in0, scalar=scalar, in1=in1, op0=op0, op1=op1), reads, writes)

    def copy(self, eng, out, in_, reads=(), writes=()):
        if eng == "act":
            return self.act(out, in_, AF.Copy, reads, writes)
        return self.S.op(eng, lambda e: e.tensor_copy(out=out, in_=in_), reads, writes)


def build_program():
    nc = bass.Bass("TRN2", target_bir_lowering=False)
    k = KB(nc)

    def din(name, shape, dt=F32):
        return nc.dram_tensor(name, list(shape), dt, kind="ExternalInput").ap()

    xT = din("xT", [NB, D, T])
    cT = din("cT", [128, 16])
    w_ada = din("w_ada", [D, 9 * D])
    colpack = din("colpack", [128, NCOLP])
    rowpack = din("rowpack", [1, NROWP])
    consts = din("consts", [128, 1024])
    f1u = din("ffn1_w_up", [D, 2 * DF])
    f1d = din("ffn1_w_down", [DF, D])
    f2u = din("ffn2_w_up", [D, 2 * DF])
    f2d = din("ffn2_w_down", [DF, D])
    w_in = din("w_in", [D, IN_COLS])
    w_out = din("w_out", [D, D])
    glu_w = din("glu_w", [512, 512])
    s5p = din("s5p", [128, 32 * 70])
    consts2 = din("consts2", [128, 1536])
    outT = nc.dram_tensor("outT", [NB, D, T], F32, kind="ExternalOutput").ap()
    X1 = nc.dram_tensor("X1s", [NB, D, T], F32, kind="Internal").ap()
    X2 = nc.dram_tensor("X2s", [NB, D, T], F32, kind="Internal").ap()
    MIX = nc.dram_tensor("MIXs", [NB, D, T], BF16, kind="Internal").ap()
    bX1 = [Buf("X1_%d" % b) for b in range(NB)]
    bX2 = [Buf("X2_%d" % b) for b in range(NB)]
    bMIX = [Buf("MIX_%d" % b) for b in range(NB)]

    final_ops = []
    with k.scope():
        cp = k.sb([128, NCOLP], F32, "colpack"); b_cp = k.buf()
        k.dma(cp[:], colpack[:, :], writes=[b_cp])
        cst = k.sb([128, 1024], F32, "consts"); b_cst = k.buf()
        k.dma(cst[:], consts[:, :], writes=[b_cst])
        ones_bf = k.sb([128, 128], BF16, "ones"); b_ones = k.buf()
        k.S.op("pool", lambda e: e.memset(ones_bf[:], 1.0), writes=[b_ones])
        der = k.sb([128, 9, 8, NB], F32, "der"); b_der = k.buf()
        pbanks = []
        for i in range(8):
            pbanks.append(k.stacks[-1].enter_context(nc.psum_tensor("pb%d" % i, [128, 512], F32)))
        bpb = [k.buf("pb%d" % i) for i in range(8)]

        with k.scope():
            ct = k.sb([128, 16], F32, "ct"); b_ct = k.buf()
            k.dma(ct[:], cT[:, :], writes=[b_ct])
            cact = k.sb([128, 16], F32, "cact"); b_cact = k.buf()
            k.act(cact[:], ct[:], AF.Silu, reads=[b_ct], writes=[b_cact])
            wst = [k.sb([128, 8, 512], F32, "wada") for _ in range(2)]
            b_wst = [k.buf() for _ in range(2)]
            wv = w_ada.rearrange("(k p) c -> p k c", p=128)
            for ch in range(18):
                s = ch % 2
                k.dma(wst[s][:], wv[:, :, ch * 512:(ch + 1) * 512], writes=[b_wst[s]])
                for jj in range(4):
                    j = ch * 4 + jj
                    for kk in range(8):
                        k.mm(pbanks[0][:, 2 * j:2 * j + 2], wst[s][:, kk, jj * 128:(jj + 1) * 128],
                             cact[:, 2 * kk:2 * kk + 2], kk == 0, kk == 7,
                             reads=[b_wst[s], b_cact], writes=[bpb[0]])
            modc = k.sb([128, 72, NB], F32, "modc"); b_modc = k.buf()
            k.tt("dve", modc[:], pbanks[0][:, 0:144].rearrange("p (j b) -> p j b", b=NB),
                 cp[:, C_BADA:C_BADA + 72].unsqueeze(2).to_broadcast([128, 72, NB]), ALU.add,
                 reads=[bpb[0], b_cp], writes=[b_modc])
            for sub in range(3):
                base = sub * 24
                k.copy("dve", der[:, 3 * sub + 1, :, :], modc[:, base:base + 8, :], reads=[b_modc], writes=[b_der])
                k.ts("dve", der[:, 3 * sub, :, :], modc[:, base + 8:base + 16, :], 1.0, None, ALU.add,
                     reads=[b_modc], writes=[b_der])
                gsc = (1.0 if sub == 1 else 0.5) / ALPHA
                k.ts("dve", der[:, 3 * sub + 2, :, :], modc[:, base + 16:base + 24, :], 1.0, gsc, ALU.add, ALU.mult,
                     reads=[b_modc], writes=[b_der])

        def ln_tail(z, b_z, zb, zq, b_zbq, small, b_small, tmp, b_tmp, xo, b_xo, lnidx, ntok):
            ln_tail_a(z, b_z, zb, zq, b_zbq)
            ln_tail_b(z, b_z, zb, zq, b_zbq, small, b_small, tmp, b_tmp, xo, b_xo, lnidx, ntok)

        def ln_tail_a(z, b_z, zb, zq, b_zbq):
            k.act(zb[:], z[:], AF.Copy, reads=[b_z], writes=[b_zbq])
            k.act(zq[:], z[:], AF.Square, reads=[b_z], writes=[b_zbq])

        def ln_tail_b(z, b_z, zb, zq, b_zbq, small, b_small, tmp, b_tmp, xo, b_xo, lnidx, ntok):
            pbs = pbanks[6]
            for dt in range(8):
                k.mm(pbs[:, 0:ntok], ones_bf[:], zb[:, dt, :], dt == 0, dt == 7, reads=[b_zbq, b_ones], writes=[bpb[6]])
            for dt in range(8):
                k.mm(pbs[:, 256:256 + ntok], ones_bf[:], zq[:, dt, :], dt == 0, dt == 7, reads=[b_zbq], writes=[bpb[6]])
            mt, m2, vv, sd, rstd, nmr = [small[:, i, :] for i in range(6)]
            k.ts("dve", mt, pbs[:, 0:ntok], 1.0 / D, None, ALU.mult, reads=[bpb[6]], writes=[b_small])
            k.tt("dve", m2, mt, mt, ALU.mult, reads=[b_small], writes=[b_small])
            k.stt(vv, pbs[:, 256:256 + ntok], 1.0 / D, m2, ALU.mult, ALU.subtract, reads=[bpb[6], b_small], writes=[b_small])
            k.ts("dve", vv, vv, EPS_P, None, ALU.add, reads=[b_small], writes=[b_small])
            k.act(sd, vv, AF.Sqrt, reads=[b_small], writes=[b_small])
            k.S.op("dve", lambda e: e.reciprocal(out=rstd, in_=sd), reads=[b_small], writes=[b_small])
            k.stt(nmr, mt, -1.0, rstd, ALU.mult, ALU.mult, reads=[b_small], writes=[b_small])
            for dt in range(8):
                eng = "dve" if dt % 2 == 0 else "pool"
                tp = tmp[dt % 2]
                k.tt(eng, tp[:], z[:, dt, :], rstd, ALU.mult, reads=[b_z, b_small], writes=[b_tmp[dt % 2]])
                k.tt(eng, tp[:], tp[:], nmr, ALU.add, reads=[b_small, b_tmp[dt % 2]], writes=[b_tmp[dt % 2]])
                k.ts(eng, xo[:, dt, :], tp[:], cp[:, C_LN + 16 * lnidx + dt:C_LN + 16 * lnidx + dt + 1],
                     cp[:, C_LN + 16 * lnidx + 8 + dt:C_LN + 16 * lnidx + 9 + dt], ALU.mult, ALU.add,
                     reads=[b_tmp[dt % 2], b_cp], writes=[b_xo])

        def ffn_pass(src, b_src, dst, b_dst, wu_d, wd_d, sub, lnidx, is_final):
            with k.scope():
                wup = k.sb([128, 8, 2 * DF], BF16, "wup")
                wdn = k.sb([128, 22, D], BF16, "wdn")
                b_wup = [k.buf() for _ in range(22)]
                b_wdn = [k.buf() for _ in range(11)]
                xb = [k.sb([128, 8, TT], F32, "xb") for _ in range(2)]
                b_xb = [k.buf() for _ in range(2)]
                xm = k.sb([128, 8, TT], BF16, "xm"); b_xm = k.buf()
                sgt = [k.sb([128, TT], F32, "sgt") for _ in range(2)]
                b_sgt = [k.buf() for _ in range(2)]
                hh = k.sb([128, 22, TT], BF16, "hh"); b_hh = k.buf()
                z = k.sb([128, 8, TT], F32, "z"); b_z = k.buf()
                zb = k.sb([128, 8, TT], BF16, "zb")
                zq = k.sb([128, 8, TT], BF16, "zq"); b_zbq = k.buf()
                small = k.sb([128, 6, TT], F32, "small"); b_small = k.buf()
                tmp = [k.sb([128, TT], F32, "tmp") for _ in range(2)]
                b_tmp = [k.buf() for _ in range(2)]
                wuv = wu_d.rearrange("(k p) c -> p k c", p=128)
                wdv = wd_d.rearrange("(f p) c -> p f c", p=128)
                engs = ["dve", "act", "pool"]
                n = 0
                for ch in range(22):
                    s = n % 2
                    k.dma(xb[s][:], wuv[:, :, ch * 256:(ch + 1) * 256], writes=[b_xb[s]])
                    k.copy(engs[n % 3], wup[:, :, ch * 256:(ch + 1) * 256], xb[s][:], reads=[b_xb[s]], writes=[b_wup[ch]])
                    n += 1
                for ch in range(11):
                    s = n % 2
                    sv = xb[s][:].rearrange("p a t -> p (a t)").rearrange("p (f c) -> p f c", f=2)
                    k.dma(sv, wdv[:, 2 * ch:2 * ch + 2, :], writes=[b_xb[s]])
                    k.copy(engs[n % 3], wdn[:, 2 * ch:2 * ch + 2, :], sv, reads=[b_xb[s]], writes=[b_wdn[ch]])
                    n += 1
                A = lambda dt, b: der[:, 3 * sub, dt, b:b + 1]
                Bm = lambda dt, b: der[:, 3 * sub + 1, dt, b:b + 1]
                GA = lambda dt, b: der[:, 3 * sub + 2, dt, b:b + 1]
                tiles = [(b, ti) for b in range(NB) for ti in range(NT)]

                def load_x(i):
                    b, ti = tiles[i]
                    s = i % 2
                    k.dma(xb[s][:], src[b].rearrange("(k p) t -> p k t", p=128)[:, :, ti * TT:(ti + 1) * TT],
                          reads=[b_src[b]], writes=[b_xb[s]])

                def make_xm(i):
                    b, ti = tiles[i]
                    s = i % 2
                    for dt in range(8):
                        eng = ("dve", "pool", "act", "pool")[dt % 4]
                        if eng == "act":
                            k.S.op("act", lambda e, dt=dt: e.activation(out=xm[:, dt, :], in_=xb[s][:, dt, :], func=AF.Identity,
                                                                        scale=A(dt, b), bias=Bm(dt, b)),
                                   reads=[b_xb[s], b_der], writes=[b_xm])
                        else:
                            k.ts(eng, xm[:, dt, :], xb[s][:, dt, :], A(dt, b), Bm(dt, b), ALU.mult, ALU.add,
                                 reads=[b_xb[s], b_der], writes=[b_xm])

                load_x(0)
                make_xm(0)

                def finish(i):
                    b, ti = tiles[i]
                    s = i % 2
                    ln_tail_b(z, b_z, zb, zq, b_zbq, small, b_small, tmp, b_tmp, xb[s], b_xb[s], lnidx, TT)
                    o = k.dma(dst[b].rearrange("(k p) t -> p k t", p=128)[:, :, ti * TT:(ti + 1) * TT], xb[s][:],
                              reads=[b_xb[s]], writes=[b_dst[b]])
                    if is_final:
                        final_ops.append(o)

                for i, (b, ti) in enumerate(tiles):
                    s = i % 2
                    for j in range(22):
                        if j == 2:
                            if i > 0:
                                finish(i - 1)
                            if i + 1 < len(tiles):
                                load_x(i + 1)
                        bank = pbanks[j % 2]; bb = bpb[j % 2]
                        for kk in range(8):
                            k.mm(bank[:, 0:TT], wup[:, kk, j * 128:(j + 1) * 128], xm[:, kk, :], kk == 0, kk == 7,
                                 reads=[b_wup[j // 2], b_xm], writes=[bb])
                        for kk in range(8):
                            c0 = DF + j * 128
                            k.mm(bank[:, 256:256 + TT], wup[:, kk, c0:c0 + 128], xm[:, kk, :], kk == 0, kk == 7,
                                 reads=[b_wup[c0 // 256], b_xm], writes=[bb])
                        k.act(sgt[j % 2][:], bank[:, 0:TT], AF.Silu, reads=[bb], writes=[b_sgt[j % 2]])
                        k.tt("dve", hh[:, j, :], sgt[j % 2][:], bank[:, 256:256 + TT], ALU.mult,
                             reads=[bb, b_sgt[j % 2]], writes=[b_hh])
                    if i + 1 < len(tiles):
                        make_xm(i + 1)
                    for dt in range(8):
                        bi = 2 + (dt // 2) % 2
                        half = (dt % 2) * 256
                        for j in range(22):
                            k.mm(pbanks[bi][:, half:half + TT], wdn[:, j, dt * 128:(dt + 1) * 128], hh[:, j, :],
                                 j == 0, j == 21, reads=[b_wdn[j // 2], b_hh], writes=[bpb[bi]])
                        k.stt(z[:, dt, :], pbanks[bi][:, half:half + TT], GA(dt, b), xb[s][:, dt, :], ALU.mult, ALU.add,
                              reads=[bpb[bi], b_xb[s], b_der], writes=[b_z])
                    ln_tail_a(z, b_z, zb, zq, b_zbq)
                finish(len(tiles) - 1)

        b_out = [Buf("out%d" % b) for b in range(NB)]
        if STAGE == 1:
            ffn_pass(xT, [Buf(), Buf()], outT, b_out, f1u, f1d, 0, 0, True)
        if STAGE >= 2 and STAGE != 3:
            ffn_pass(xT, [Buf(), Buf()], X1, bX1, f1u, f1d, 0, 0, False)
        if STAGE == 3:
            bX1 = [Buf(), Buf()]
            X1 = xT
        if STAGE >= 2:
            TWO_PI = 2.0 * math.pi
            mixc = contextlib.ExitStack()
            k.stacks.append(mixc)
            ident = cst[:, 0:128]
            tri = [cst[:, 128:256], cst[:, 256:384]]
            iota = cst[:, 384:896]
            KTc = cst[:, 896:928]
            cs2 = k.sb([128, 1536], F32, "consts2"); b_cs2 = k.buf()
            k.dma(cs2[:], consts2[:, :], writes=[b_cs2])
            rmask = cs2[:, 0:512]
            mle2 = cs2[:, 512:640]
            mge2 = cs2[:, 640:768]
            J32 = cs2[:, 768:800]
            esel = cs2[:, 800:1312]
            dpk = cs2[:, 1312:1344]
            ident_bf = k.sb([128, 128], BF16, "identbf"); b_idbf = k.buf()
            k.copy("dve", ident_bf[:], ident, reads=[b_cst], writes=[b_idbf])
            J_bf = k.sb([128, 128], BF16, "Jbf")
            k.copy("dve", J_bf[:], cs2[:, 1344:1472], reads=[b_cs2], writes=[b_idbf])
            rowbc = k.sb([128, NROWP], F32, "rowbc"); b_rowbc = k.buf()
            k.dma(rowbc[:], rowpack[0:1, :].partition_broadcast(128), writes=[b_rowbc])
            GSd = nc.dram_tensor("GSs", [NB, 4, 4, T], F32, kind="Internal").ap()
            bGS = [Buf() for _ in range(NB)]

            def new(shape, dt, name="w"):
                return k.sb(shape, dt, name), k.buf()

            def load_w(src_ap, rows_k, c0, c1, stage_cols=128):
                wt, b_wt = new([128, rows_k, c1 - c0], BF16, "wbf")
                v = src_ap.rearrange("(k p) c -> p k c", p=128)
                with k.scope():
                    stg = [new([128, rows_k, stage_cols], F32, "wstg") for _ in range(2)]
                    i = 0
                    c = c0
                    while c < c1:
                        w_ = min(stage_cols, c1 - c)
                        st_, bs_ = stg[i % 2]
                        k.dma(st_[:, :, 0:w_], v[:, :, c:c + w_], writes=[bs_])
                        k.copy(("dve", "pool")[i % 2], wt[:, :, c - c0:c - c0 + w_], st_[:, :, 0:w_], reads=[bs_], writes=[b_wt])
                        c += w_
                        i += 1
                return wt, b_wt

            def reduce_angle(x, out, shape, bx, bo):
                tf, btf = new(shape, F32, "raf")
                ti, bti = new(shape, I32, "rai")
                k.ts("dve", tf[:], x, 1.0 / TWO_PI, None, ALU.mult, reads=[bx], writes=[btf])
                k.copy("dve", ti[:], tf[:], reads=[btf], writes=[bti])
                k.copy("dve", tf[:], ti[:], reads=[bti], writes=[btf])
                k.stt(out, tf[:], -TWO_PI, x, ALU.mult, ALU.add, reads=[btf, bx], writes=[bo])

            def sincos(x, sn, cs_, shape, bx, bsn, bcs):
                s2, bs2 = new(shape, F32, "s2")
                s4, bs4 = new(shape, F32, "s4")
                k.act(s2[:], x, AF.Sin, reads=[bx], writes=[bs2], scale=0.5)
                k.act(s4[:], x, AF.Sin, reads=[bx], writes=[bs4], scale=0.25)
                k.tt("dve", cs_, s2[:], s2[:], ALU.mult, reads=[bs2], writes=[bcs])
                k.ts("dve", cs_, cs_, -2.0, 1.0, ALU.mult, ALU.add, reads=[bcs], writes=[bcs])
                k.tt("dve", s4[:], s4[:], s4[:], ALU.mult, reads=[bs4], writes=[bs4])
                k.ts("dve", s4[:], s4[:], -2.0, 1.0, ALU.mult, ALU.add, reads=[bs4], writes=[bs4])
                k.stt(sn, s2[:], 2.0, s4[:], ALU.mult, ALU.mult, reads=[bs2, bs4], writes=[bsn])

            with k.scope():
                mst = contextlib.ExitStack()
                k.stacks.append(mst)
                Tm, b_Tm = new([128, 32, 128], BF16, "Tm")
                PTre, b_PTre = new([128, 32, 128], BF16, "PTre")
                PTim, b_PTim = new([128, 32, 128], BF16, "PTim")
                Qre, b_Qre = new([128, 32, 128], BF16, "Qre")
                Qim, b_Qim = new([128, 32, 128], BF16, "Qim")
                thr, b_thr = new([128, 32], F32, "thr")
                r8, b_r8 = new([128, 32], F32, "r8")
                with k.scope():
                    sp, b_sp = new([128, 32, 70], F32, "s5p")
                    k.dma(sp[:], s5p.rearrange("p (g f) -> p g f", f=70), writes=[b_sp])
                    lr = sp[:, :, 0]; li = sp[:, :, 1]
                    S2 = [128, 32]
                    step, b_step = new(S2, F32)
                    k.act(step[:], sp[:, :, 2], AF.Exp, reads=[b_sp], writes=[b_step])
                    lrs, b_lrs = new(S2, F32)
                    k.tt("dve", lrs[:], lr, step[:], ALU.mult, reads=[b_sp, b_step], writes=[b_lrs])
                    phi, b_phi = new(S2, F32)
                    k.tt("dve", phi[:], li, step[:], ALU.mult, reads=[b_sp, b_step], writes=[b_phi])
                    phr, b_phr = new(S2, F32)
                    reduce_angle(phi[:], phr[:], S2, b_phi, b_phr)
                    th8, b_th8 = new(S2, F32)
                    k.ts("dve", th8[:], phr[:], 8.0, None, ALU.mult, reads=[b_phr], writes=[b_th8])
                    reduce_angle(th8[:], thr[:], S2, b_th8, b_thr)
                    k.act(r8[:], lrs[:], AF.Exp, reads=[b_lrs], writes=[b_r8], scale=8.0)
                    sn1, b_sn1 = new(S2, F32); cs1, b_cs1 = new(S2, F32); mg1, b_mg1 = new(S2, F32)
                    sincos(phr[:], sn1[:], cs1[:], S2, b_phr, b_sn1, b_cs1)
                    k.act(mg1[:], lrs[:], AF.Exp, reads=[b_lrs], writes=[b_mg1])
                    nr, b_nr = new(S2, F32); ni, b_ni = new(S2, F32)
                    k.tt("dve", nr[:], mg1[:], cs1[:], ALU.mult, reads=[b_mg1, b_cs1], writes=[b_nr])
                    k.ts("dve", nr[:], nr[:], -1.0, None, ALU.add, reads=[b_nr], writes=[b_nr])
                    k.tt("dve", ni[:], mg1[:], sn1[:], ALU.mult, reads=[b_mg1, b_sn1], writes=[b_ni])
                    inv, b_inv = new(S2, F32); t0, b_t0 = new(S2, F32); t1, b_t1 = new(S2, F32)
                    k.tt("dve", inv[:], lr, lr, ALU.mult, reads=[b_sp], writes=[b_inv])
                    k.tt("dve", t0[:], li, li, ALU.mult, reads=[b_sp], writes=[b_t0])
                    k.tt("dve", inv[:], inv[:], t0[:], ALU.add, reads=[b_inv, b_t0], writes=[b_inv])
                    k.S.op("dve", lambda e: e.reciprocal(out=inv[:], in_=inv[:]), reads=[b_inv], writes=[b_inv])
                    cfr, b_cfr = new(S2, F32); cfi, b_cfi = new(S2, F32)
                    k.tt("dve", t0[:], nr[:], lr, ALU.mult, reads=[b_nr, b_sp], writes=[b_t0])
                    k.tt("dve", t1[:], ni[:], li, ALU.mult, reads=[b_ni, b_sp], writes=[b_t1])
                    k.tt("dve", t0[:], t0[:], t1[:], ALU.add, reads=[b_t0, b_t1], writes=[b_t0])
                    k.tt("dve", cfr[:], t0[:], inv[:], ALU.mult, reads=[b_t0, b_inv], writes=[b_cfr])
                    k.tt("dve", t0[:], ni[:], lr, ALU.mult, reads=[b_ni, b_sp], writes=[b_t0])
                    k.tt("dve", t1[:], nr[:], li, ALU.mult, reads=[b_nr, b_sp], writes=[b_t1])
                    k.tt("dve", t0[:], t0[:], t1[:], ALU.subtract, reads=[b_t0, b_t1], writes=[b_t0])
                    k.tt("dve", cfi[:], t0[:], inv[:], ALU.mult, reads=[b_t0, b_inv], writes=[b_cfi])
                    S3 = [128, 32, 16]
                    bre = sp[:, :, 3:19]; bim = sp[:, :, 19:35]; cre = sp[:, :, 35:51]; cim = sp[:, :, 51:67]
                    Bcr, b_Bcr = new(S3, F32); Bci, b_Bci = new(S3, F32); u0, b_u0 = new(S3, F32)
                    cfrb = cfr[:].unsqueeze(2).to_broadcast(S3); cfib = cfi[:].unsqueeze(2).to_broadcast(S3)
                    k.tt("dve", Bcr[:], bre, cfrb, ALU.mult, reads=[b_sp, b_cfr], writes=[b_Bcr])
                    k.tt("dve", u0[:], bim, cfib, ALU.mult, reads=[b_sp, b_cfi], writes=[b_u0])
                    k.tt("dve", Bcr[:], Bcr[:], u0[:], ALU.subtract, reads=[b_Bcr, b_u0], writes=[b_Bcr])
                    k.tt("dve", Bci[:], bim, cfrb, ALU.mult, reads=[b_sp, b_cfr], writes=[b_Bci])
                    k.tt("dve", u0[:], bre, cfib, ALU.mult, reads=[b_sp, b_cfi], writes=[b_u0])
                    k.tt("dve", Bci[:], Bci[:], u0[:], ALU.add, reads=[b_Bci, b_u0], writes=[b_Bci])
                    SP = [128, 32, 32]
                    pwr, b_pwr = new(SP, F32); pwi, b_pwi = new(SP, F32)
                    with k.scope():
                        ang, b_ang = new(SP, F32); angr, b_angr = new(SP, F32)
                        ktb = KTc.unsqueeze(1).to_broadcast(SP)
                        k.tt("dve", ang[:], phr[:].unsqueeze(2).to_broadcast(SP), ktb, ALU.mult, reads=[b_phr, b_cst], writes=[b_ang])
                        reduce_angle(ang[:], angr[:], SP, b_ang, b_angr)
                        psn, b_psn = new(SP, F32); pcs, b_pcs = new(SP, F32); pmg, b_pmg = new(SP, F32)
                        sincos(angr[:], psn[:], pcs[:], SP, b_angr, b_psn, b_pcs)
                        k.tt("dve", pmg[:], lrs[:].unsqueeze(2).to_broadcast(SP), ktb, ALU.mult, reads=[b_lrs, b_cst], writes=[b_pmg])
                        k.act(pmg[:], pmg[:], AF.Exp, reads=[b_pmg], writes=[b_pmg])
                        k.tt("dve", pwr[:], pmg[:], pcs[:], ALU.mult, reads=[b_pmg, b_pcs], writes=[b_pwr])
                        k.tt("dve", pwi[:], pmg[:], psn[:], ALU.mult, reads=[b_pmg, b_psn], writes=[b_pwi])
                    S4 = [128, 32, 8, 16]
                    ta, b_ta = new(S4, F32); tb_, b_tb = new(S4, F32)

                    def cprod(ar, ai, bar, bai, off, o_re, b_ore, o_im, b_oim, neg_im):
                        arb = ar.unsqueeze(2).to_broadcast(S4); aib = ai.unsqueeze(2).to_broadcast(S4)
                        pr = pwr[:, :, off:off + 8].unsqueeze(3).to_broadcast(S4)
                        pi_ = pwi[:, :, off:off + 8].unsqueeze(3).to_broadcast(S4)
                        k.tt("dve", ta[:], arb, pr, ALU.mult, reads=[bar, b_pwr], writes=[b_ta])
                        k.tt("pool", tb_[:], aib, pi_, ALU.mult, reads=[bai, b_pwi], writes=[b_tb])
                        k.tt("dve", o_re, ta[:], tb_[:], ALU.subtract, reads=[b_ta, b_tb], writes=[b_ore])
                        k.tt("dve", ta[:], arb, pi_, ALU.mult, reads=[bar, b_pwi], writes=[b_ta])
                        k.tt("pool", tb_[:], aib, pr, ALU.mult, reads=[bai, b_pwr], writes=[b_tb])
                        if neg_im:
                            k.stt(o_im, ta[:], -1.0, tb_[:], ALU.mult, ALU.subtract, reads=[b_ta, b_tb], writes=[b_oim])
                        else:
                            k.tt("dve", o_im, ta[:], tb_[:], ALU.add, reads=[b_ta, b_tb], writes=[b_oim])

                    v4 = lambda t_: t_[:].rearrange("p g (s h) -> p g s h", h=16)
                    if "noQ" not in SKIP:
                        cprod(cre, cim, b_sp, b_sp, 24, v4(Qre), b_Qre, v4(Qim), b_Qim, True)
                    with k.scope():
                      if "noP" not in SKIP:
                        Pr, b_Pr = new([128, 32, 128], F32); Pi, b_Pi = new([128, 32, 128], F32)
                        cprod(Bcr[:], Bci[:], b_Bcr, b_Bci, 0, v4(Pr), b_Pr, v4(Pi), b_Pi, False)
                        for g in range(32):
                            bank = pbanks[g % 2]; bb = bpb[g % 2]
                            k.mm(bank[:, 256:384], Pr[:, g, :], ident, True, True, reads=[b_Pr, b_cst], writes=[bb])
                            k.mm(bank[:, 384:512], Pi[:, g, :], ident, True, True, reads=[b_Pi, b_cst], writes=[bb])
                            k.act(PTre[:, g, :], bank[:, 256:384], AF.Copy, reads=[bb], writes=[b_PTre])
                            k.act(PTim[:, g, :], bank[:, 384:512], AF.Copy, reads=[bb], writes=[b_PTim])
                    with k.scope():
                      if "noT" not in SKIP:
                        Xr, b_Xr = new([128, 32, 128], F32); Xi, b_Xi = new([128, 32, 128], F32)
                        cprod(Bcr[:], Bci[:], b_Bcr, b_Bci, 8, v4(Xr), b_Xr, v4(Xi), b_Xi, False)
                        Zr, b_Zr = new([128, 32, 128], F32); Zi, b_Zi = new([128, 32, 128], F32)
                        cprod(cre, cim, b_sp, b_sp, 16, v4(Zr), b_Zr, v4(Zi), b_Zi, True)
                        Xmr = ta[:].rearrange("p g s h -> p g (s h)"); b_Xmr = b_ta
                        Xmi = tb_[:].rearrange("p g s h -> p g (s h)"); b_Xmi = b_tb
                        hm = [cst[:, 128 + 63:128 + 64], cst[:, 256 + 64:256 + 65]]
                        tA, b_tA = new([128, 128], F32); tB, b_tB = new([128, 128], F32)
                        for dd_ in range(2):
                            k.ts("dve", Xmr, Xr[:], hm[dd_], None, ALU.mult, reads=[b_Xr, b_cst], writes=[b_Xmr])
                            k.ts("pool", Xmi, Xi[:], hm[dd_], None, ALU.mult, reads=[b_Xi, b_cst], writes=[b_Xmi])
                            for g in range(32):
                                bank = pbanks[2 + g % 2]; bb = bpb[2 + g % 2]
                                k.mm(bank[:, 0:128], Xmr[:, g, :], Zr[:, g, :], True, False, reads=[b_Xmr, b_Zr], writes=[bb])
                                k.mm(bank[:, 0:128], Xmi[:, g, :], Zi[:, g, :], False, True, reads=[b_Xmi, b_Zi], writes=[bb])
                                if dd_ == 0:
                                    k.tt("dve", tA[:], bank[:, 0:128], mle2, ALU.mult, reads=[bb, b_cs2], writes=[b_tA])
                                    k.stt(Tm[:, g, :], ident, dpk[:, g:g + 1], tA[:], ALU.mult, ALU.add,
                                          reads=[b_tA, b_cs2, b_cst], writes=[b_Tm])
                                else:
                                    k.tt("dve", tB[:], bank[:, 0:128], mge2, ALU.mult, reads=[bb, b_cs2], writes=[b_tB])
                                    k.tt("dve", Tm[:, g, :], Tm[:, g, :], tB[:], ALU.add, reads=[b_tB, b_Tm], writes=[b_Tm])

                for (b, phase) in ((0, "S"), (1, "S"), (0, "MO"), (1, "MO")):
                    if phase == "MO" and mst is not None:
                        k.stacks.pop()
                        mst.close()
                        mst = None
                        k.fence_ops = k.S.fence()
                    x1v = X1[b].rearrange("(k p) t -> p k t", p=128)

                    def dma_x1(t0, ntok, xs, b_xs):
                        k.dma(xs[:, :, 0:ntok], x1v[:, :, t0:t0 + ntok], reads=[bX1[b]], writes=[b_xs])

                    def mod_x1(dst_ap, ntok, xs, b_xs, b_dst):
                        for dt in range(8):
                            eng = ("dve", "pool")[dt % 2]
                            k.ts(eng, dst_ap[:, dt, :], xs[:, dt, 0:ntok], der[:, 3, dt, b:b + 1], der[:, 4, dt, b:b + 1],
                                 ALU.mult, ALU.add, reads=[b_xs, b_der], writes=[b_dst])

                    def load_xm2(dst_ap, t0, ntok, xs, b_xs, b_dst):
                        k.dma(xs[:, :, 0:ntok], x1v[:, :, t0:t0 + ntok], reads=[bX1[b]], writes=[b_xs])
                        for dt in range(8):
                            eng = ("dve", "pool")[dt % 2]
                            k.ts(eng, dst_ap[:, dt, :], xs[:, dt, 0:ntok], der[:, 3, dt, b:b + 1], der[:, 4, dt, b:b + 1],
                                 ALU.mult, ALU.add, reads=[b_xs, b_der], writes=[b_dst])

                    with (k.scope() if phase == "S" else contextlib.nullcontext()):
                      if phase == "S" and "S" not in SKIP:
                        Zy, b_Zy = new([128, 4, 8, 512], BF16, "Zy")
                        with k.scope():
                            Ut, b_Ut = new([128, 32, 512], BF16, "Ut")
                            Ur, b_Ur = new([128, 32, 512], BF16, "Ur")
                            with k.scope():
                                wu, b_wu = load_w(w_in, 8, 1552, 2064)
                                xs, b_xs = new([128, 8, 128], F32, "xs")
                                xmb, b_xmb = new([128, 8, 1024], BF16, "xmb")
                                Zt, b_Zt = new([128, 32, 8, 16], BF16, "Zt")
                                for cb in range(4):
                                    for q8 in range(8):
                                        load_xm2(xmb[:, :, q8 * 128:(q8 + 1) * 128], cb * 1024 + q8 * 128, 128, xs, b_xs, b_xmb)
                                    for s_ in range(8):
                                        bank = pbanks[s_ % 2]; bb = bpb[s_ % 2]
                                        for kk in range(8):
                                            lv = xmb[:, kk, :].rearrange("p (c s) -> p s c", s=8)[:, s_, :]
                                            k.mm(bank[:, :], lv, wu[:, kk, :], kk == 0, kk == 7, reads=[b_xmb, b_wu], writes=[bb])
                                        k.tt("dve", Zt[:, :, s_, :], bank[:, :].rearrange("p (g h) -> p g h", h=16),
                                             rowbc[:, R_BU:R_BU + 512].rearrange("p (g h) -> p g h", h=16), ALU.add,
                                             reads=[bb, b_rowbc], writes=[b_Zt])
                                    for g4 in range(8):
                                        bank = pbanks[2 + g4 % 2]; bb = bpb[2 + g4 % 2]
                                        bank2 = pbanks[4 + g4 % 2]; bb2 = bpb[4 + g4 % 2]
                                        for gg in range(4):
                                            g = g4 * 4 + gg
                                            zl = Zt[:, g, :, :].rearrange("p s h -> p (s h)")
                                            k.mm(bank[:, gg * 128:(gg + 1) * 128], zl, ident_bf[:], True, True,
                                                 reads=[b_Zt, b_idbf], writes=[bb])
                                            k.mm(bank2[:, gg * 128:(gg + 1) * 128], zl, J_bf[:], True, True,
                                                 reads=[b_Zt, b_idbf], writes=[bb2])
                                        k.act(Ut[:, g4 * 4:g4 * 4 + 4, cb * 128:(cb + 1) * 128],
                                              bank[:, :].rearrange("p (g c) -> p g c", g=4), AF.Copy, reads=[bb], writes=[b_Ut])
                                        k.copy("dve", Ur[:, g4 * 4:g4 * 4 + 4, (3 - cb) * 128:(4 - cb) * 128],
                                               bank2[:, :].rearrange("p (g c) -> p g c", g=4), reads=[bb2], writes=[b_Ur])
                            with k.scope():
                                W_ = lambda: new([128, 512], F32, "sw")
                                (ta_, bta), (tr_, btr), (tc, btc), (tsn, bts) = W_(), W_(), W_(), W_()
                                (sr, bsr), (si, bsi), (a1, ba1), (a2, ba2) = W_(), W_(), W_(), W_()
                                (stR, bstR), (stI, bstI), (kR, bkR), (kI, bkI) = W_(), W_(), W_(), W_()
                                (s2t, bs2) = W_()
                                (ti_, bti) = new([128, 512], I32, "twi")
                                Hp = [new([128, 513], BF16, "Hp") for _ in range(4)]
                                for (h_, bh_) in Hp:
                                    k.S.op("pool", lambda e, h_=h_: e.memset(h_[:], 0.0), writes=[bh_])
                                (hrf, bhrf), (hif, bhif), (hrb, bhrb), (hib, bhib) = Hp
                                (yg, byg) = new([128, 512], BF16, "yg")
                                (ygb, bygb) = new([128, 512], BF16, "ygb")
                                (ygT, bygT) = new([128, 4, 128], BF16, "ygT")
                                pY, bY = pbanks[0], bpb[0]
                                pYb, bYb = pbanks[1], bpb[1]
                                pAr, bAr = pbanks[2], bpb[2]
                                pBr, bBr = pbanks[3], bpb[3]
                                pAi, bAi = pbanks[4], bpb[4]
                                pBi, bBi = pbanks[5], bpb[5]
                                pT, bT = pbanks[6], bpb[6]
                                pT2, bT2 = pbanks[7], bpb[7]
                                tcs = [(tc, btc), W_()]
                                tsns = [(tsn, bts), W_()]

                                def gen_tw(g):
                                    (tc, btc), (tsn, bts) = tcs[g % 2], tsns[g % 2]
                                    k.S.op("act", lambda e: e.activation(out=ta_[:], in_=iota, func=AF.Copy, scale=thr[:, g:g + 1]),
                                           reads=[b_cst, b_thr], writes=[bta])
                                    k.act(tr_[:], ta_[:], AF.Copy, reads=[bta], writes=[btr], scale=1.0 / TWO_PI)
                                    k.copy("dve", ti_[:], tr_[:], reads=[btr], writes=[bti])
                                    k.copy("dve", tr_[:], ti_[:], reads=[bti], writes=[btr])
                                    k.stt(ta_[:], tr_[:], -TWO_PI, ta_[:], ALU.mult, ALU.add, reads=[btr, bta], writes=[bta])
                                    k.act(s2t[:], ta_[:], AF.Sin, reads=[bta], writes=[bs2], scale=0.5)
                                    k.act(tr_[:], ta_[:], AF.Sin, reads=[bta], writes=[btr], scale=0.25)
                                    k.tt("pool", tc[:], s2t[:], s2t[:], ALU.mult, reads=[bs2], writes=[btc])
                                    k.ts("pool", tc[:], tc[:], -2.0, 1.0, ALU.mult, ALU.add, reads=[btc], writes=[btc])
                                    k.tt("pool", tr_[:], tr_[:], tr_[:], ALU.mult, reads=[btr], writes=[btr])
                                    k.ts("pool", tr_[:], tr_[:], -2.0, 1.0, ALU.mult, ALU.add, reads=[btr], writes=[btr])
                                    k.stt(tsn[:], s2t[:], 2.0, tr_[:], ALU.mult, ALU.mult, reads=[bs2, btr], writes=[bts])

                                gen_tw(0)
                                for g in range(32):
                                    (tc, btc), (tsn, bts) = tcs[g % 2], tsns[g % 2]
                                    k.mm(pY[:, :], Tm[:, g, :], Ut[:, g, :], True, False, reads=[b_Tm, b_Ut], writes=[bY])
                                    k.mm(pAr[:, :], PTre[:, g, :], Ut[:, g, :], True, True, reads=[b_PTre, b_Ut], writes=[bAr])
                                    k.mm(pBr[:, :], PTre[:, g, :], Ur[:, g, :], True, True, reads=[b_PTre, b_Ur], writes=[bBr])
                                    k.mm(pAi[:, :], PTim[:, g, :], Ut[:, g, :], True, True, reads=[b_PTim, b_Ut], writes=[bAi])
                                    k.mm(pBi[:, :], PTim[:, g, :], Ur[:, g, :], True, True, reads=[b_PTim, b_Ur], writes=[bBi])
                                    k.act(sr[0:64, :], pAr[0:64, :], AF.Copy, reads=[bAr], writes=[bsr])
                                    k.act(sr[64:128, :], pBr[64:128, :], AF.Copy, reads=[bBr], writes=[bsr])
                                    k.act(si[0:64, :], pAi[0:64, :], AF.Copy, reads=[bAi], writes=[bsi])
                                    k.act(si[64:128, :], pBi[64:128, :], AF.Copy, reads=[bBi], writes=[bsi])
                                    k.tt("dve", a1[:], sr[:], tc[:], ALU.mult, reads=[bsr, btc], writes=[ba1])
                                    k.tt("pool", a2[:], si[:], tsn[:], ALU.mult, reads=[bsi, bts], writes=[ba2])
                                    k.tt("dve", stR[:], a1[:], a2[:], ALU.add, reads=[ba1, ba2], writes=[bstR])
                                    k.tt("dve", a1[:], si[:], tc[:], ALU.mult, reads=[bsi, btc], writes=[ba1])
                                    k.tt("pool", a2[:], sr[:], tsn[:], ALU.mult, reads=[bsr, bts], writes=[ba2])
                                    k.tt("dve", stI[:], a1[:], a2[:], ALU.subtract, reads=[ba1, ba2], writes=[bstI])
                                    r8b = r8[:, g:g + 1].to_broadcast([128, 512])
                                    k.S.op("dve", lambda e, r_=r8b: e.tensor_tensor_scan(out=kR[:], data0=r_, data1=stR[:], initial=0.0,
                                                                                       op0=ALU.mult, op1=ALU.add),
                                           reads=[bstR, b_r8], writes=[bkR])
                                    k.S.op("dve", lambda e, r_=r8b: e.tensor_tensor_scan(out=kI[:], data0=r_, data1=stI[:], initial=0.0,
                                                                                       op0=ALU.mult, op1=ALU.add),
                                           reads=[bstI, b_r8], writes=[bkI])
                                    if g + 1 < 32:
                                        gen_tw(g + 1)
                                    k.tt("dve", a1[:], kR[:], tc[:], ALU.mult, reads=[bkR, btc], writes=[ba1])
                                    k.tt("pool", a2[:], kI[:], tsn[:], ALU.mult, reads=[bkI, bts], writes=[ba2])
                                    k.tt("dve", hrf[0:64, 1:513], a1[0:64, :], a2[0:64, :], ALU.subtract, reads=[ba1, ba2], writes=[bhrf])
                                    k.tt("dve", hrb[64:128, 1:513], a1[64:128, :], a2[64:128, :], ALU.subtract, reads=[ba1, ba2], writes=[bhrb])
                                    k.tt("dve", a1[:], kI[:], tc[:], ALU.mult, reads=[bkI, btc], writes=[ba1])
                                    k.tt("pool", a2[:], kR[:], tsn[:], ALU.mult, reads=[bkR, bts], writes=[ba2])
                                    k.tt("dve", hif[0:64, 1:513], a1[0:64, :], a2[0:64, :], ALU.add, reads=[ba1, ba2], writes=[bhif])
                                    k.tt("dve", hib[64:128, 1:513], a1[64:128, :], a2[64:128, :], ALU.add, reads=[ba1, ba2], writes=[bhib])
                                    k.mm(pY[:, :], Qre[:, g, :], hrf[:, 0:512], False, False, reads=[b_Qre, bhrf], writes=[bY])
                                    k.mm(pY[:, :], Qim[:, g, :], hif[:, 0:512], False, True, reads=[b_Qim, bhif], writes=[bY])
                                    k.mm(pYb[:, :], Qre[:, g, :], hrb[:, 0:512], True, False, reads=[b_Qre, bhrb], writes=[bYb])
                                    k.mm(pYb[:, :], Qim[:, g, :], hib[:, 0:512], False, True, reads=[b_Qim, bhib], writes=[bYb])
                                    k.act(yg[:], pY[:, :], AF.Copy, reads=[bY], writes=[byg])
                                    k.act(ygb[:], pYb[:, :], AF.Copy, reads=[bYb], writes=[bygb])
                                    for jb in range(4):
                                        k.mm(pT2[:, jb * 128:(jb + 1) * 128], ygb[:, jb * 128:(jb + 1) * 128], ident_bf[:], True, True,
                                             reads=[bygb, b_idbf], writes=[bT2])
                                    k.copy("dve", ygT[:], pT2[:, :].rearrange("p (j x) -> p j x", j=4), reads=[bT2], writes=[bygT])
                                    for cb in range(4):
                                        k.mm(pT[:, cb * 128:(cb + 1) * 128], yg[:, cb * 128:(cb + 1) * 128], ident_bf[:], True, False,
                                             reads=[byg, b_idbf], writes=[bT])
                                        k.mm(pT[:, cb * 128:(cb + 1) * 128], J_bf[:], ygT[:, 3 - cb, :], False, True,
                                             reads=[bygT, b_idbf], writes=[bT])
                                    k.copy("dve", Zy[:, :, :, 16 * g:16 * g + 16],
                                           pT[:, :].rearrange("p (c t h) -> p c t h", c=4, t=8), reads=[bT], writes=[b_Zy])
                        with k.scope():
                            glw, b_glw = load_w(glu_w, 4, 0, 512)
                            yb, b_yb = new([128, 4, 1024], F32, "yb")
                            u1, b_u1 = new([128, 4, 1024], F32, "u1")
                            yab, b_yab = new([128, 4, 1024], BF16, "yab")
                            sqb, b_sqb = new([128, 4, 1024], BF16, "sqb")
                            sg_, b_sg = new([128, 512], F32, "sg")
                            rs_, b_rs = new([128, 512], F32, "rs")
                            mo, b_mo = new([128, 4, 1024], BF16, "mo")
                            GC = 2.0 * math.sqrt(2.0 / math.pi)
                            for cb in range(4):
                                for ct in range(4):
                                    for th in range(2):
                                        bank = pbanks[(ct * 2 + th) % 2]; bb = bpb[(ct * 2 + th) % 2]
                                        for t4 in range(4):
                                            t_ = th * 4 + t4
                                            k.mm(bank[:, t4 * 128:(t4 + 1) * 128], Zy[:, cb, t_, ct * 128:(ct + 1) * 128], ident_bf[:],
                                                 True, True, reads=[b_Zy, b_idbf], writes=[bb])
                                        k.act(yb[:, ct, :].rearrange("p (c s) -> p s c", s=8)[:, th * 4:th * 4 + 4, :],
                                              bank[:, :].rearrange("p (s c) -> p s c", s=4), AF.Copy, reads=[bb], writes=[b_yb])
                                k.act(u1[:], yb[:], AF.Square, reads=[b_yb], writes=[b_u1])
                                k.ts("dve", u1[:], u1[:], 0.044715, 1.0, ALU.mult, ALU.add, reads=[b_u1], writes=[b_u1])
                                k.tt("pool", u1[:], u1[:], yb[:], ALU.mult, reads=[b_u1, b_yb], writes=[b_u1])
                                k.act(u1[:], u1[:], AF.Sigmoid, reads=[b_u1], writes=[b_u1], scale=GC)
                                k.tt("dve", yb[:], yb[:], u1[:], ALU.mult, reads=[b_u1, b_yb], writes=[b_yb])
                                k.copy("pool", yab[:], yb[:], reads=[b_yb], writes=[b_yab])
                                for hf in range(2):
                                    tk = slice(hf * 512, (hf + 1) * 512)
                                    for co in range(4):
                                        bank = pbanks[2 + co % 2]; bb = bpb[2 + co % 2]
                                        for ci in range(4):
                                            k.mm(bank[:, :], glw[:, ci, co * 128:(co + 1) * 128], yab[:, ci, tk], ci == 0, ci == 3,
                                                 reads=[b_glw, b_yab], writes=[bb])
                                        k.act(sg_[:], bank[:, :], AF.Sigmoid, reads=[bb, b_cp], writes=[b_sg],
                                              bias=cp[:, C_GLUB + co:C_GLUB + co + 1])
                                        k.tt("dve", yb[:, co, tk], yb[:, co, tk], sg_[:], ALU.mult, reads=[b_yb, b_sg], writes=[b_yb])
                                k.act(sqb[:], yb[:], AF.Square, reads=[b_yb], writes=[b_sqb])
                                for hf in range(2):
                                    tk = slice(hf * 512, (hf + 1) * 512)
                                    bank = pbanks[4 + hf]; bb = bpb[4 + hf]
                                    for ci in range(4):
                                        k.mm(bank[:, :], ones_bf[:], sqb[:, ci, tk], ci == 0, ci == 3, reads=[b_sqb, b_ones], writes=[bb])
                                    k.ts("dve", rs_[:], bank[:, :], 1.0 / 512.0, LN_EPS, ALU.mult, ALU.add, reads=[bb], writes=[b_rs])
                                    k.act(rs_[:], rs_[:], AF.Sqrt, reads=[b_rs], writes=[b_rs])
                                    k.S.op("dve", lambda e: e.reciprocal(out=rs_[:], in_=rs_[:]), reads=[b_rs], writes=[b_rs])
                                    for co in range(4):
                                        k.stt(mo[:, co, tk], yb[:, co, tk], cp[:, C_SNG + co:C_SNG + co + 1], rs_[:], ALU.mult, ALU.mult,
                                              reads=[b_yb, b_rs, b_cp], writes=[b_mo])
                                k.dma(MIX[b, 512:1024, cb * 1024:(cb + 1) * 1024].rearrange("(k p) t -> p k t", p=128), mo[:],
                                      reads=[b_mo], writes=[bMIX[b]])

                    with (k.scope() if phase == "MO" else contextlib.nullcontext()):
                      if phase == "MO" and "M" not in SKIP:
                        qf, b_qf = new([128, 2, T], BF16, "qf")
                        kf, b_kf = new([128, 2, T], BF16, "kf")
                        vext, b_vext = new([128, 32, 4, 129], BF16, "vext")
                        big, b_big = new([128, 4, T + 4], F32, "big")
                        k.S.op("pool", lambda e: e.memset(vext[:, :, :, 128:129], 1.0), writes=[b_vext])
                        k.S.op("pool", lambda e: e.memset(big[:, :, 0:2], 0.0), writes=[b_big])
                        k.S.op("pool", lambda e: e.memset(big[:, :, T + 2:T + 4], 0.0), writes=[b_big])
                        with k.scope():
                            wqk, b_wqk = load_w(w_in, 8, 0, 512)
                            wv, b_wv = load_w(w_in, 8, 512, 1024)
                            wg, b_wg = load_w(w_in, 8, 1536, 1552, 16)
                            xs, b_xs = new([128, 8, 256], F32, "xs")
                            xm2L = [new([128, 8, 256], BF16, "xm2") for _ in range(2)]
                            gt, b_gt = new([4, 4, 256], F32, "gt")
                            dma_x1(0, 256, xs, b_xs)
                            mod_x1(xm2L[0][0], 256, xs, b_xs, xm2L[0][1])
                            for ti in range(16):
                                t0_ = ti * 256
                                (xm2, b_xm2) = xm2L[ti % 2]
                                if ti + 1 < 16:
                                    dma_x1(t0_ + 256, 256, xs, b_xs)
                                for ct in range(4):
                                    bank = pbanks[ct // 2]; bb = bpb[ct // 2]
                                    half = (ct % 2) * 256
                                    for kk in range(8):
                                        k.mm(bank[:, half:half + 256], wqk[:, kk, ct * 128:(ct + 1) * 128], xm2[:, kk, :], kk == 0, kk == 7,
                                             reads=[b_wqk, b_xm2], writes=[bb])
                                    k.act(big[:, ct, 2 + t0_:2 + t0_ + 256], bank[:, half:half + 256], AF.Identity, reads=[bb, b_cp],
                                          writes=[b_big], bias=cp[:, C_BQK + ct:C_BQK + ct + 1])
                                for cc in range(2):
                                    c = ti * 2 + cc
                                    bank = pbanks[2 + cc]; bb = bpb[2 + cc]
                                    for kk in range(8):
                                        k.mm(bank[:, :], xm2[:, kk, cc * 128:(cc + 1) * 128], wv[:, kk, :], kk == 0, kk == 7,
                                             reads=[b_wv, b_xm2], writes=[bb])
                                    k.tt("dve", vext[:, c, :, 0:128], bank[:, :].rearrange("p (r v) -> p r v", r=4),
                                         rowbc[:, R_BV:R_BV + 512].rearrange("p (r v) -> p r v", r=4), ALU.add,
                                         reads=[bb, b_rowbc], writes=[b_vext])
                                for kind in range(4):
                                    bank = pbanks[4 + kind // 2]; bb = bpb[4 + kind // 2]
                                    half = (kind % 2) * 256
                                    for kk in range(8):
                                        k.mm(bank[0:4, half:half + 256], wg[:, kk, kind * 4:kind * 4 + 4], xm2[:, kk, :], kk == 0, kk == 7,
                                             reads=[b_wg, b_xm2], writes=[bb])
                                    k.act(gt[0:4, kind, :], bank[0:4, half:half + 256], AF.Identity, reads=[bb, b_cp], writes=[b_gt],
                                          bias=cp[0:4, C_BG + kind:C_BG + kind + 1])
                                k.dma(GSd[b, :, :, t0_:t0_ + 256].rearrange("k r t -> r k t"), gt[:], reads=[b_gt], writes=[bGS[b]])
                                if ti + 1 < 16:
                                    mod_x1(xm2L[(ti + 1) % 2][0], 256, xs, b_xs, xm2L[(ti + 1) % 2][1])
                        ktm, b_ktm = new([128, 32, 256], BF16, "ktm")
                        with k.scope():
                            acc, b_acc = new([128, T], F32, "acc")
                            for ct in range(4):
                                cw = lambda j: cp[:, C_CW + 4 * j + ct:C_CW + 4 * j + ct + 1]
                                k.ts("dve", acc[:], big[:, ct, 0:T], cw(0), None, ALU.mult, reads=[b_big, b_cp], writes=[b_acc])
                                for j in range(1, 5):
                                    k.stt(acc[:], big[:, ct, j:j + T], cw(j), acc[:], ALU.mult, ALU.add, reads=[b_big, b_cp, b_acc], writes=[b_acc])
                                cb_ = cp[:, C_CB + ct:C_CB + ct + 1]
                                if ct < 2:
                                    k.act(acc[:], acc[:], AF.Silu, reads=[b_acc, b_cp], writes=[b_acc], bias=cb_)
                                    k.ts("pool", qf[:, ct, :], acc[:], 0.125, None, ALU.mult, reads=[b_acc], writes=[b_qf])
                                else:
                                    k.act(kf[:, ct - 2, :], acc[:], AF.Silu, reads=[b_acc, b_cp], writes=[b_kf], bias=cb_)
                            for c2 in range(16):
                                bank = pbanks[c2 % 2]; bb = bpb[c2 % 2]
                                for cc in range(2):
                                    c = c2 * 2 + cc
                                    for kt in range(2):
                                        o_ = cc * 256 + kt * 128
                                        k.mm(bank[:, o_:o_ + 128], kf[:, kt, c * 128:(c + 1) * 128], ident_bf[:], True, True,
                                             reads=[b_kf, b_idbf], writes=[bb])
                                k.act(ktm[:, c2 * 2:c2 * 2 + 2, :], bank[:, :].rearrange("p (c x) -> p c x", c=2), AF.Copy,
                                      reads=[bb], writes=[b_ktm])
                        WCt, b_WCt = new([128, 2, 8, 32], F32, "WCt")
                        DECt, b_DECt = new([128, 8, 32], F32, "DECt")
                        with k.scope():
                            G5 = [32, 4, 4, 128]
                            Gc, b_Gc = new(G5, F32, "Gc")
                            k.dma(Gc[:], GSd[b].rearrange("k r (c t) -> c k r t", t=128), reads=[bGS[b]], writes=[b_Gc])
                            D4 = [32, 2, 4, 128]
                            spl, b_spl = new(D4, F32); BS, b_BS = new(D4, F32); ee, b_ee = new(D4, F32)
                            k.act(spl[:], Gc[:, 2:4, :, :], AF.Exp, reads=[b_Gc], writes=[b_spl], scale=-1.0)
                            k.act(spl[:], spl[:], AF.Ln, reads=[b_spl], writes=[b_spl], bias=1.0)
                            flat = lambda t_, d: t_[:, d, :, :].rearrange("c r t -> c (r t)")
                            for d in range(2):
                                k.S.op("dve", lambda e, d=d: e.tensor_tensor_scan(out=flat(BS, d), data0=rmask[0:32, :], data1=flat(spl, d),
                                                                                  initial=0.0, op0=ALU.mult, op1=ALU.add),
                                       reads=[b_spl, b_cs2], writes=[b_BS])
                            tot = BS[:, 1, :, 127:128].to_broadcast([32, 4, 128])
                            k.tt("dve", ee[:, 1, :, :], spl[:, 1, :, :], BS[:, 1, :, :], ALU.subtract, reads=[b_spl, b_BS], writes=[b_ee])
                            k.tt("dve", BS[:, 1, :, :], ee[:, 1, :, :], tot, ALU.add, reads=[b_ee, b_BS], writes=[b_BS])
                            k.tt("dve", ee[:], Gc[:, 0:2, :, :], BS[:], ALU.add, reads=[b_Gc, b_BS], writes=[b_ee])
                            emx, b_emx = new([32, 2, 4], F32)
                            k.S.op("dve", lambda e: e.tensor_reduce(out=emx[:], in_=ee[:], axis=AX.X, op=ALU.max), reads=[b_ee], writes=[b_emx])
                            nbe, b_nbe = new([32, 2, 4], F32)
                            k.ts("dve", nbe[:, 0, :], BS[:, 0, :, 127], -1.0, None, ALU.mult, reads=[b_BS], writes=[b_nbe])
                            k.ts("dve", nbe[:, 1, :], BS[:, 1, :, 0], -1.0, None, ALU.mult, reads=[b_BS], writes=[b_nbe])
                            pbk = pbanks[0]; bbk = bpb[0]
                            for d in range(2):
                                rhs_ = ident[0:32, 0:32] if d == 0 else J32[0:32, :]
                                k.mm(pbk[0:4, d * 64:d * 64 + 32], emx[:, d, :], rhs_, True, True, reads=[b_emx, b_cst, b_cs2], writes=[bbk])
                                k.mm(pbk[0:4, d * 64 + 32:d * 64 + 64], nbe[:, d, :], rhs_, True, True, reads=[b_nbe, b_cst, b_cs2], writes=[bbk])
                            rows, b_rows = new([4, 2, 2, 32], F32)
                            k.copy("dve", rows[:], pbk[0:4, 0:128].rearrange("r (d x j) -> r d x j", d=2, x=2), reads=[bbk], writes=[b_rows])
                            mrow, b_mrow = new([4, 2, 33], F32)
                            k.S.op("pool", lambda e: e.memset(mrow[:], 0.0), writes=[b_mrow])
                            Rrow, b_Rrow = new([4, 2, 32], F32); drow, b_drow = new([4, 2, 32], F32)
                            for d in range(2):
                                k.S.op("dve", lambda e, d=d: e.tensor_tensor_scan(out=mrow[:, d, 1:33], data0=rows[:, d, 0, :], data1=rows[:, d, 1, :],
                                                                                  initial=0.0, op0=ALU.max, op1=ALU.add),
                                       reads=[b_rows, b_mrow], writes=[b_mrow])
                            k.tt("dve", Rrow[:], mrow[:, :, 0:32], rows[:, :, 0, :], ALU.max, reads=[b_mrow, b_rows], writes=[b_Rrow])
                            k.tt("dve", drow[:], mrow[:, :, 0:32], Rrow[:], ALU.subtract, reads=[b_mrow, b_Rrow], writes=[b_drow])
                            k.act(drow[:], drow[:], AF.Exp, reads=[b_drow], writes=[b_drow])
                            pbd = pbanks[1]; bbd = bpb[1]
                            for d in range(2):
                                for r in range(4):
                                    o_ = (d * 4 + r) * 32
                                    k.mm(pbd[:, o_:o_ + 32], esel[0:4, r * 128:(r + 1) * 128], drow[:, d, :], True, True,
                                         reads=[b_cs2, b_drow], writes=[bbd])
                            k.copy("dve", DECt[:], pbd[:, 0:256].rearrange("p (x j) -> p x j", x=8), reads=[bbd], writes=[b_DECt])
                            pbr = pbanks[2]; bbr = bpb[2]
                            k.mm(pbr[0:32, 0:4], Rrow[:, 0, :], ident[0:4, 0:4], True, True, reads=[b_Rrow, b_cst], writes=[bbr])
                            k.mm(pbr[0:32, 4:8], Rrow[:, 1, :], ident[0:4, 0:4], True, True, reads=[b_Rrow, b_cst], writes=[bbr])
                            RbT, b_RbT = new([32, 8], F32)
                            k.copy("dve", RbT[:], pbr[0:32, 0:8], reads=[bbr], writes=[b_RbT])
                            k.mm(pbr[0:32, 8:12], J32[0:32, :], RbT[:, 4:8], True, True, reads=[b_RbT, b_cs2], writes=[bbr])
                            Rc, b_Rc = new([32, 2, 4], F32)
                            k.copy("dve", Rc[:, 0, :], RbT[:, 0:4], reads=[b_RbT], writes=[b_Rc])
                            k.copy("dve", Rc[:, 1, :], pbr[0:32, 8:12], reads=[bbr], writes=[b_Rc])
                            Rcb = Rc[:].unsqueeze(3).to_broadcast(D4)
                            k.tt("dve", ee[:], ee[:], Rcb, ALU.subtract, reads=[b_ee, b_Rc], writes=[b_ee])
                            k.act(ee[:], ee[:], AF.Exp, reads=[b_ee], writes=[b_ee])
                            k.tt("dve", BS[:], BS[:], Rcb, ALU.subtract, reads=[b_BS, b_Rc], writes=[b_BS])
                            k.act(BS[:], BS[:], AF.Exp, reads=[b_BS], writes=[b_BS])
                            pbw = pbanks[3]; bbw = bpb[3]
                            for x_, src_, bsrc in ((0, ee, b_ee), (1, BS, b_BS)):
                                for d in range(2):
                                    for r in range(4):
                                        o_ = (x_ * 8 + d * 4 + r) * 32
                                        k.mm(pbw[:, o_:o_ + 32], src_[:, d, r, :], ident[0:32, 0:32], True, True, reads=[bsrc, b_cst], writes=[bbw])
                            k.copy("dve", WCt[:], pbw[:, :].rearrange("p (x y c) -> p x y c", x=2, y=8), reads=[bbw], writes=[b_WCt])
                        with k.scope():
                            Cst = [[new([128, 129], F32, "Cst") for r in range(4)] for d in range(2)]
                            NR = 4
                            STb = [new([128, 128], BF16, "ST") for _ in range(NR)]
                            Vpb = [new([128, 129], BF16, "Vp") for _ in range(NR)]
                            Cdb = [new([128, 129], BF16, "Cd") for _ in range(NR)]
                            ddb = [new([128, 2], F32, "dd") for _ in range(NR)]
                            insts = [(st, d, r) for st in range(32) for d in range(2) for r in range(4)]

                            def geom(it):
                                st, d, r = insts[it]
                                c = st if d == 0 else 31 - st
                                return st, d, r, c, slice(c * 128, (c + 1) * 128), r // 2, 64 * (r % 2), d * 4 + r, it % NR

                            def emitA(it):
                                st, d, r, c, tok, kt, ro, x_, w = geom(it)
                                pS = pbanks[it % 2]; bS = bpb[it % 2]
                                (ST, bST), (Vp, bVp), (Cd, bCd) = STb[w], Vpb[w], Cdb[w]
                                (Cs, bCs) = Cst[d][r]
                                k.mm(pS[:, 0:128], kf[ro:ro + 64, kt, tok], qf[ro:ro + 64, kt, tok], True, True,
                                     reads=[b_kf, b_qf], writes=[bS])
                                k.tt("dve", ST[:], pS[:, 0:128], tri[d], ALU.mult, reads=[bS, b_cst], writes=[bST])
                                k.S.op("act", lambda e: e.activation(out=Vp[:], in_=vext[:, c, r, :], func=AF.Copy,
                                                                     scale=WCt[:, 0, x_, c:c + 1]),
                                       reads=[b_vext, b_WCt], writes=[bVp])
                                if st > 0:
                                    k.ts("dve", Cs[ro:ro + 64, :], Cs[ro:ro + 64, :], DECt[ro:ro + 64, x_, st:st + 1], None, ALU.mult,
                                         reads=[bCs, b_DECt], writes=[bCs])
                                    k.act(Cd[ro:ro + 64, :], Cs[ro:ro + 64, :], AF.Copy, reads=[bCs], writes=[bCd])

                            def emitB(it):
                                st, d, r, c, tok, kt, ro, x_, w = geom(it)
                                pN = pbanks[2 + it % 2]; bN = bpb[2 + it % 2]
                                pC = pbanks[4 + it % 2]; bC = bpb[4 + it % 2]
                                (ST, bST), (Vp, bVp), (Cd, bCd), (dd, bdd) = STb[w], Vpb[w], Cdb[w], ddb[w]
                                (Cs, bCs) = Cst[d][r]
                                k.mm(pN[:, 0:129], ST[:], Vp[:], True, st == 0, reads=[bST, bVp], writes=[bN])
                                if st > 0:
                                    k.mm(pN[:, 0:129], qf[ro:ro + 64, kt, tok], Cd[ro:ro + 64, :], False, True,
                                         reads=[b_qf, bCd], writes=[bN])
                                k.mm(pC[ro:ro + 64, 0:129], ktm[:, c, r * 64:(r + 1) * 64], Vp[:], True, True,
                                     reads=[b_ktm, bVp], writes=[bC])
                                if st > 0:
                                    k.tt("dve", Cs[ro:ro + 64, :], Cs[ro:ro + 64, :], pC[ro:ro + 64, 0:129], ALU.add,
                                         reads=[bCs, bC], writes=[bCs])
                                else:
                                    k.copy("dve", Cs[ro:ro + 64, :], pC[ro:ro + 64, 0:129], reads=[bC], writes=[bCs])
                                k.act(dd[:, 0:1], pN[:, 128:129], AF.Abs, reads=[bN], writes=[bdd])
                                k.ts("dve", dd[:, 0:1], dd[:, 0:1], WCt[:, 1, x_, c:c + 1], None, ALU.max,
                                     reads=[bdd, b_WCt], writes=[bdd])
                                k.S.op("dve", lambda e: e.reciprocal(out=dd[:, 1:2], in_=dd[:, 0:1]), reads=[bdd], writes=[bdd])
                                cell = big[:, :, 0:T].rearrange("p r (c v) -> p c r v", v=128)[:, c, r, :]
                                first = (d == 0 and c < 16) or (d == 1 and c >= 16)
                                if first:
                                    k.S.op("act", lambda e: e.activation(out=cell, in_=pN[:, 0:128], func=AF.Copy, scale=dd[:, 1:2]),
                                           reads=[bN, bdd], writes=[b_big])
                                else:
                                    k.stt(cell, pN[:, 0:128], dd[:, 1:2], cell, ALU.mult, ALU.add, reads=[bN, bdd, b_big], writes=[b_big])

                            for it in range(len(insts) + 1):
                                if it < len(insts):
                                    emitA(it)
                                if it >= 1:
                                    emitB(it - 1)
                        with k.scope():
                            wo, b_wo = load_w(w_in, 8, 1024, 1536)
                            xs1 = new([128, 8, 256], F32, "xs")
                            xsL = [xs1, xs1]
                            xmL = [new([128, 8, 256], BF16, "xm2") for _ in range(2)]
                            obL = [new([128, 512], F32, "ob") for _ in range(2)]
                            hgL = [new([128, 512], F32, "hg") for _ in range(2)]
                            hnL = [new([128, 512], BF16, "hnb") for _ in range(2)]
                            stL = [new([128, 4, 6], F32, "bst") for _ in range(2)]
                            mvL = [new([128, 4, 2], F32, "mv") for _ in range(2)]
                            mmL = [new([128, 4, 256], BF16, "mixm") for _ in range(2)]
                            (xs, b_xs) = xs1
                            dma_x1(0, 256, xs, b_xs)
                            mod_x1(xmL[0][0], 256, xs, b_xs, xmL[0][1])
                            for ti in range(16):
                                (xm2, b_xm2), (mm_, b_mm) = xmL[ti % 2], mmL[ti % 2]
                                if ti + 1 < 16:
                                    dma_x1((ti + 1) * 256, 256, xs, b_xs)
                                for cc in range(2):
                                    c = ti * 2 + cc
                                    (ob, b_ob), (hg, b_hg), (hnb, b_hnb), (stt_, b_stt), (mv, b_mv) = obL[cc], hgL[cc], hnL[cc], stL[cc], mvL[cc]
                                    bank = pbanks[cc]; bb = bpb[cc]
                                    for kk in range(8):
                                        k.mm(bank[:, :], xm2[:, kk, cc * 128:(cc + 1) * 128], wo[:, kk, :], kk == 0, kk == 7,
                                             reads=[b_wo, b_xm2], writes=[bb])
                                    k.tt("dve", ob[:], bank[:, :], rowbc[:, R_BO:R_BO + 512], ALU.add, reads=[bb, b_rowbc], writes=[b_ob])
                                    k.act(ob[:], ob[:], AF.Sigmoid, reads=[b_ob], writes=[b_ob])
                                    cellc = big[:, :, 0:T].rearrange("p r (c v) -> p c r v", v=128)[:, c, :, :]
                                    k.tt("dve", hg[:].rearrange("p (r v) -> p r v", r=4), ob[:].rearrange("p (r v) -> p r v", r=4), cellc,
                                         ALU.mult, reads=[b_ob, b_big], writes=[b_hg])
                                    for r in range(4):
                                        k.S.op("dve", lambda e, r=r, stt_=stt_, hg=hg: e.bn_stats(out=stt_[:, r, :], in_=hg[:, r * 128:(r + 1) * 128]),
                                               reads=[b_hg], writes=[b_stt])
                                        k.S.op("dve", lambda e, r=r, stt_=stt_, mv=mv: e.bn_aggr(out=mv[:, r, :], in_=stt_[:, r, :]),
                                               reads=[b_stt], writes=[b_mv])
                                    k.ts("dve", mv[:, :, 1], mv[:, :, 1], LN_EPS, None, ALU.add, reads=[b_mv], writes=[b_mv])
                                    k.act(mv[:, :, 1], mv[:, :, 1], AF.Sqrt, reads=[b_mv], writes=[b_mv])
                                    k.S.op("dve", lambda e, mv=mv: e.reciprocal(out=mv[:, :, 1], in_=mv[:, :, 1]), reads=[b_mv], writes=[b_mv])
                                    for r in range(4):
                                        k.ts("dve", hg[:, r * 128:(r + 1) * 128], hg[:, r * 128:(r + 1) * 128], mv[:, r, 0:1], mv[:, r, 1:2],
                                             ALU.subtract, ALU.mult, reads=[b_hg, b_mv], writes=[b_hg])
                                    k.tt("pool", hnb[:], hg[:], rowbc[:, R_MNG:R_MNG + 512], ALU.mult, reads=[b_hg, b_rowbc], writes=[b_hnb])
                                    bank2 = pbanks[2 + cc]; bb2 = bpb[2 + cc]
                                    for r in range(4):
                                        k.mm(bank2[:, r * 128:(r + 1) * 128], hnb[:, r * 128:(r + 1) * 128], ident_bf[:], True, True,
                                             reads=[b_hnb, b_idbf], writes=[bb2])
                                    k.act(mm_[:, :, cc * 128:(cc + 1) * 128], bank2[:, :].rearrange("p (r t) -> p r t", r=4), AF.Copy,
                                          reads=[bb2], writes=[b_mm])
                                k.dma(MIX[b, 0:512, ti * 256:(ti + 1) * 256].rearrange("(k p) t -> p k t", p=128), mm_[:],
                                      reads=[b_mm], writes=[bMIX[b]])
                                if ti + 1 < 16:
                                    mod_x1(xmL[(ti + 1) % 2][0], 256, xs, b_xs, xmL[(ti + 1) % 2][1])

                    with (k.scope() if phase == "MO" else contextlib.nullcontext()):
                      if phase == "MO" and "O" not in SKIP:
                        wob, b_wob = load_w(w_out, 8, 0, 1024)
                        xs = [new([128, 8, TT], F32, "xs") for _ in range(2)]
                        mx = [new([128, 8, TT], BF16, "mx") for _ in range(2)]
                        zL = [new([128, 8, TT], F32, "z") for _ in range(2)]
                        zbL = [k.sb([128, 8, TT], BF16, "zb") for _ in range(2)]
                        zqL = [k.sb([128, 8, TT], BF16, "zq") for _ in range(2)]
                        bzbqL = [k.buf() for _ in range(2)]
                        smL = [new([128, 6, TT], F32, "small") for _ in range(2)]
                        tmpL = [[k.sb([128, TT], F32, "tmp") for _ in range(2)] for _ in range(2)]
                        btmpL = [[k.buf() for _ in range(2)] for _ in range(2)]
                        def loadO(ti):
                            (xs_, bxs_), (mx_, bmx_) = xs[ti % 2], mx[ti % 2]
                            tsl = slice(ti * TT, (ti + 1) * TT)
                            k.dma(xs_[:], x1v[:, :, tsl], reads=[bX1[b]], writes=[bxs_])
                            k.dma(mx_[:], MIX[b].rearrange("(k p) t -> p k t", p=128)[:, :, tsl], reads=[bMIX[b]], writes=[bmx_])

                        loadO(0)
                        for ti in range(NT):
                            s = ti % 2
                            (xs_, bxs_), (mx_, bmx_) = xs[s], mx[s]
                            tsl = slice(ti * TT, (ti + 1) * TT)
                            if ti + 1 < NT:
                                loadO(ti + 1)
                            (z, b_z), zb, zq, b_zbq = zL[s], zbL[s], zqL[s], bzbqL[s]
                            (small, b_small), tmp, b_tmp = smL[s], tmpL[s], btmpL[s]
                            for dt in range(8):
                                bi = (dt // 2) % 2
                                half = (dt % 2) * 256
                                for kk in range(8):
                                    k.mm(pbanks[bi][:, half:half + TT], wob[:, kk, dt * 128:(dt + 1) * 128], mx_[:, kk, :], kk == 0, kk == 7,
                                         reads=[b_wob, bmx_], writes=[bpb[bi]])
                                k.stt(z[:, dt, :], pbanks[bi][:, half:half + TT], der[:, 5, dt, b:b + 1], xs_[:, dt, :], ALU.mult, ALU.add,
                                      reads=[bpb[bi], bxs_, b_der], writes=[b_z])
                            ln_tail(z, b_z, zb, zq, b_zbq, small, b_small, tmp, b_tmp, xs_, bxs_, 1, TT)
                            k.dma(X2[b].rearrange("(k p) t -> p k t", p=128)[:, :, tsl], xs_[:], reads=[bxs_], writes=[bX2[b]])

        if STAGE >= 2:
            k.stacks.pop()
            mixc.close()
            k.fence_ops = k.S.fence()
        if STAGE == 3:
            with k.scope():
                cvb, b_cvb = k.sb([128, 8, 256], BF16, "cvb"), k.buf()
                cvf, b_cvf = k.sb([128, 8, 256], F32, "cvf"), k.buf()
                for b in range(NB):
                    for ti in range(16):
                        tsl = slice(ti * 256, (ti + 1) * 256)
                        k.dma(cvb[:], MIX[b].rearrange("(k p) t -> p k t", p=128)[:, :, tsl], reads=[bMIX[b]], writes=[b_cvb])
                        k.copy("dve", cvf[:], cvb[:], reads=[b_cvb], writes=[b_cvf])
                        final_ops.append(k.dma(outT[b].rearrange("(k p) t -> p k t", p=128)[:, :, tsl], cvf[:], reads=[b_cvf], writes=[b_out[b]]))
        if STAGE >= 9:
            ffn_pass(X2, bX2, outT, b_out, f2u, f2d, 2, 2, True)

        k.S.emit(final_wait_ops=final_ops)
    return nc


def _pack_inputs(inp):
    f = lambda a: np.ascontiguousarray(np.asarray(a, dtype=np.float32))
    col = np.zeros((128, NCOLP), np.float32)

    def put8(off, v):
        col[:, off:off + 8] = f(v).reshape(8, 128).T

    for i, nm in enumerate(["ln1_g", "ln1_b", "ln2_g", "ln2_b", "ln3_g", "ln3_b"]):
        put8(C_LN + 8 * i, inp[nm][0])
    col[:, C_BADA:C_BADA + 72] = f(inp["b_ada"][0]).reshape(72, 128).T
    b_in = f(inp["b_in"][0])
    col[:, C_BQK:C_BQK + 4] = b_in[0:512].reshape(4, 128).T
    cw = f(inp["qk_conv_w"][0])
    for j in range(5):
        col[:, C_CW + 4 * j:C_CW + 4 * j + 4] = cw[j].reshape(4, 128).T
    col[:, C_CB:C_CB + 4] = f(inp["qk_conv_b"][0]).reshape(4, 128).T
    col[:, C_GLUB:C_GLUB + 4] = f(inp["ssm_glu_b"][0]).reshape(4, 128).T
    col[:, C_SNG:C_SNG + 4] = f(inp["ssm_norm_g"][0]).reshape(4, 128).T
    g0 = 1536
    for i in range(4):
        col[0:4, C_BG + i] = b_in[g0 + 4 * i:g0 + 4 * i + 4]
    row = np.zeros((1, NROWP), np.float32)
    row[0, R_BV:R_BV + 512] = b_in[512:1024]
    row[0, R_BO:R_BO + 512] = b_in[1024:1536]
    row[0, R_BU:R_BU + 512] = b_in[1552:2064]
    row[0, R_MNG:R_MNG + 512] = f(inp["mlstm_norm_g"][0])
    cst = np.zeros((128, 1024), np.float32)
    cst[:, 0:128] = np.eye(128, dtype=np.float32)
    ii = np.arange(128)
    cst[:, 128:256] = (ii[:, None] <= ii[None, :]).astype(np.float32)
    cst[:, 256:384] = (ii[:, None] >= ii[None, :]).astype(np.float32)
    cst[:, 384:896] = np.arange(512, dtype=np.float32)[None, :]
    a8 = np.arange(8, dtype=np.float32)
    for half, rows in ((0, slice(0, 64)), (1, slice(64, 128))):
        cst[rows, 896:904] = (7 - a8) if half == 0 else a8
        cst[rows, 904:912] = (-a8) if half == 0 else a8
        cst[rows, 912:920] = a8 if half == 0 else (-a8)
        cst[rows, 920:928] = (a8 + 1) if half == 0 else (8 - a8)
    cs2 = np.zeros((128, 1536), np.float32)
    cs2[:, 0:512] = (np.arange(512) % 128 != 0).astype(np.float32)[None, :]
    cs2[:, 512:640] = ((ii[:, None] // 16) <= (ii[None, :] // 16)).astype(np.float32)
    cs2[:, 640:768] = ((ii[:, None] // 16) >= (ii[None, :] // 16)).astype(np.float32)
    cs2[0:32, 768:800] = np.eye(32, dtype=np.float32)[::-1]
    cs2[:, 1344:1472] = np.eye(128, dtype=np.float32)[::-1]
    for r in range(4):
        cs2[r, 800 + r * 128:800 + (r + 1) * 128] = 1.0
    dd_ = f(inp["ssm_d"][0]).reshape(32, 16)
    cs2[:, 1312:1344] = np.tile(dd_.T, (8, 1))
    s5 = np.zeros((2, 64, 32, 70), np.float32)
    s5[..., 0] = f(inp["ssm_lam_re"][0]).transpose(0, 2, 1)
    s5[..., 1] = f(inp["ssm_lam_im"][0]).transpose(0, 2, 1)
    s5[..., 2] = f(inp["ssm_log_step"][0])[:, None, :]
    s5[..., 3:19] = f(inp["ssm_b_re"][0]).transpose(1, 0, 2)[None]
    s5[..., 19:35] = f(inp["ssm_b_im"][0]).transpose(1, 0, 2)[None]
    s5[..., 35:51] = f(inp["ssm_c_re"][0]).transpose(0, 3, 1, 2)
    s5[..., 51:67] = f(inp["ssm_c_im"][0]).transpose(0, 3, 1, 2)
    s5p = np.ascontiguousarray(s5.reshape(128, 32 * 70))
    shared = {
        "w_ada": f(inp["w_ada"][0]), "colpack": col, "rowpack": row, "consts": cst,
        "ffn1_w_up": f(inp["ffn1_w_up"][0]), "ffn1_w_down": f(inp["ffn1_w_down"][0]),
        "ffn2_w_up": f(inp["ffn2_w_up"][0]), "ffn2_w_down": f(inp["ffn2_w_down"][0]),
        "w_in": f(inp["w_in"][0]), "w_out": f(inp["w_out"][0]), "glu_w": f(inp["ssm_glu_w"][0]),
        "s5p": s5p, "consts2": cs2,
    }
    x = np.asarray(inp["x"], dtype=np.float32)
    c = np.asarray(inp["c"], dtype=np.float32)
    maps = []
    for core in range(8):
        m = dict(shared)
        m["xT"] = np.ascontiguousarray(x[2 * core:2 * core + 2].transpose(0, 2, 1))
        cc = c[2 * core:2 * core + 2]
        m["cT"] = np.ascontiguousarray(cc.reshape(2, 8, 128).transpose(2, 1, 0).reshape(128, 16))
        maps.append(m)
    return maps


_NC_CACHE = {}


def kernel(**inputs):
    maps = _pack_inputs(inputs)
    if "nc" not in _NC_CACHE:
        _NC_CACHE["nc"] = build_program()
    nc = _NC_CACHE["nc"]
    res = run_bass_kernel_spmd(nc, maps, core_ids=list(range(8)))
    outs = [r["outT"] for r in res.results]
    full = np.concatenate([o.transpose(0, 2, 1) for o in outs], axis=0)
    return np.ascontiguousarray(full.astype(np.float32))
```

```python
import os
import math
import contextlib
import numpy as np
import concourse.bass as bass
import concourse.mybir as mybir
from concourse.bass_utils import run_bass_kernel_spmd

F32 = mybir.dt.float32
BF16 = mybir.dt.bfloat16
I32 = mybir.dt.int32
AF = mybir.ActivationFunctionType
ALU = mybir.AluOpType
AX = mybir.AxisListType

D = 1024
DF = 2816
T = 4096
NB = 2
TT = 256
NT = T // TT
ALPHA = 2.0 ** 0.25
LN_EPS = 1e-5
EPS_P = LN_EPS / (ALPHA * ALPHA)
IN_COLS = 2064
STAGE = int(os.environ.get("MK_STAGE", "9"))
SKIP = os.environ.get("MK_SKIP", "").split(",")

C_LN = 0
C_BADA = 48
C_BQK = 120
C_CW = 124
C_CB = 144
C_GLUB = 148
C_SNG = 152
C_BG = 156
NCOLP = 168
R_BV, R_BO, R_BU, R_MNG = 0, 512, 1024, 1536
NROWP = 2048


class Buf:
    __slots__ = ("name", "w", "r")

    def __init__(self, name="", init_readers=()):
        self.name = name
        self.w = None
        self.r = list(init_readers)


class Sched:
    ENGS = ("pe", "act", "dve", "pool", "sp")
    DMA_RING = 8
    SEM_MAX = 30000

    def __init__(self, nc):
        self.nc = nc
        self.ops = []
        self.by_eng = {e: [] for e in self.ENGS}

    def op(self, eng, fn, reads=(), writes=(), dma=False):
        oid = len(self.ops)
        deps = set()
        for b in reads:
            if b.w is not None:
                deps.add(b.w)
        for b in writes:
            if b.w is not None:
                deps.add(b.w)
            for r in b.r:
                deps.add(r)
        if eng == "pe":
            deps = {d for d in deps if self.ops[d][0] != "pe"}
        self.ops.append((eng, fn, deps, dma))
        self.by_eng[eng].append(oid)
        for b in reads:
            b.r.append(oid)
        for b in writes:
            b.w = oid
            b.r = []
        return oid

    def fence(self):
        f = []
        for e in self.ENGS:
            lst = self.by_eng[e]
            if not lst:
                continue
            if e == "sp":
                f.extend(lst[-self.DMA_RING:])
            else:
                f.append(lst[-1])
        return f

    def emit(self, final_wait_ops=()):
        nc = self.nc
        ops = self.ops
        needed = [False] * len(ops)
        for (_, _, deps, _) in ops:
            for d in deps:
                needed[d] = True
        for d in final_wait_ops:
            needed[d] = True
        sig = [None] * len(ops)
        cnt = {e: 0 for e in self.ENGS}
        dma_idx = {e: 0 for e in self.ENGS}
        for e in self.ENGS:
            for oid in self.by_eng[e]:
                isdma = ops[oid][3]
                if isdma:
                    n = dma_idx[e]
                    dma_idx[e] += 1
                    sig[oid] = (("d", e, n % self.DMA_RING), 16 * (n // self.DMA_RING + 1))
                elif needed[oid]:
                    c = cnt[e]
                    cnt[e] += 1
                    sig[oid] = (("c", e, c // self.SEM_MAX), c % self.SEM_MAX + 1)
        keys = []
        kset = set()
        for s in sig:
            if s is not None and s[0] not in kset:
                kset.add(s[0])
                keys.append(s[0])
        with contextlib.ExitStack() as st:
            sems = {}
            for k in keys:
                sems[k] = st.enter_context(nc.semaphore("s_%s_%s_%d" % k))
            block = st.enter_context(nc.Block())

            def run_engine(ename, eh):
                seen = {}
                for oid in self.by_eng[ename]:
                    _, fn, deps, isdma = ops[oid]
                    waits = {}
                    for d in deps:
                        k, v = sig[d]
                        if seen.get(k, 0) >= v:
                            continue
                        if waits.get(k, 0) < v:
                            waits[k] = v
                    if isdma:
                        k, v = sig[oid]
                        if v > 16 and seen.get(k, 0) < v - 16 and waits.get(k, 0) < v - 16:
                            waits[k] = v - 16
                    for k, v in waits.items():
                        eh.wait_ge(sems[k], v)
                        seen[k] = v
                    ins = fn(eh)
                    if sig[oid] is not None:
                        k, v = sig[oid]
                        ins.then_inc(sems[k], 16 if isdma else 1)
                if ename == "sp":
                    fw = {}
                    for d in final_wait_ops:
                        k, v = sig[d]
                        if fw.get(k, 0) < v:
                            fw[k] = v
                    for k, v in fw.items():
                        if seen.get(k, 0) < v:
                            eh.wait_ge(sems[k], v)

            @block.tensor
            def _(e):
                run_engine("pe", e)

            @block.scalar
            def _(e):
                run_engine("act", e)

            @block.vector
            def _(e):
                run_engine("dve", e)

            @block.gpsimd
            def _(e):
                run_engine("pool", e)

            @block.sync
            def _(e):
                run_engine("sp", e)


class KB:
    def __init__(self, nc):
        self.nc = nc
        self.S = Sched(nc)
        self.stacks = []
        self.fence_ops = []
        self.uid = 0

    @contextlib.contextmanager
    def scope(self):
        st = contextlib.ExitStack()
        self.stacks.append(st)
        try:
            with st:
                yield
        finally:
            self.stacks.pop()
            self.fence_ops = self.S.fence()

    def sb(self, shape, dt, name=None):
        self.uid += 1
        nm = "%s_%d" % (name or "t", self.uid)
        return self.stacks[-1].enter_context(self.nc.sbuf_tensor(nm, list(shape), dt))

    def buf(self, name=""):
        return Buf(name, self.fence_ops)

    def dma(self, out, in_, reads=(), writes=(), **kw):
        return self.S.op("sp", lambda e: e.dma_start(out=out, in_=in_, **kw), reads, writes, dma=True)

    def mm(self, out, lhsT, rhs, start, stop, reads=(), writes=()):
        return self.S.op("pe", lambda e: e.matmul(out, lhsT=lhsT, rhs=rhs, start=start, stop=stop), reads, writes)

    def act(self, out, in_, func, reads=(), writes=(), scale=1.0, bias=0.0):
        return self.S.op("act", lambda e: e.activation(out=out, in_=in_, func=func, scale=scale, bias=bias), reads, writes)

    def tt(self, eng, out, in0, in1, op, reads=(), writes=()):
        return self.S.op(eng, lambda e: e.tensor_tensor(out=out, in0=in0, in1=in1, op=op), reads, writes)

    def ts(self, eng, out, in0, s1, s2, op0, op1=None, reads=(), writes=()):
        if op1 is None:
            return self.S.op(eng, lambda e: e.tensor_scalar(out=out, in0=in0, scalar1=s1, scalar2=None, op0=op0), reads, writes)
        return self.S.op(eng, lambda e: e.tensor_scalar(out=out, in0=in0, scalar1=s1, scalar2=s2, op0=op0, op1=op1), reads, writes)

    def stt(self, out, in0, scalar, in1, op0, op1, reads=(), writes=()):
        return self.S.op("dve", lambda e: e.scalar_tensor_tensor(out=out, in0=in0, scalar=scalar, in1=in1, op0=op0, op1=op1), reads, writes)

    def copy(self, eng, out, in_, reads=(), writes=()):
        if eng == "act":
            return self.act(out, in_, AF.Copy, reads, writes)
        return self.S.op(eng, lambda e: e.tensor_copy(out=out, in_=in_), reads, writes)


def build_program():
    nc = bass.Bass("TRN2", target_bir_lowering=False)
    k = KB(nc)

    def din(name, shape, dt=F32):
        return nc.dram_tensor(name, list(shape), dt, kind="ExternalInput").ap()

    xT = din("xT", [NB, D, T])
    cT = din("cT", [128, 16])
    w_ada = din("w_ada", [D, 9 * D])
    colpack = din("colpack", [128, NCOLP])
    rowpack = din("rowpack", [1, NROWP])
    consts = din("consts", [128, 1024])
    f1u = din("ffn1_w_up", [D, 2 * DF])
    f1d = din("ffn1_w_down", [DF, D])
    f2u = din("ffn2_w_up", [D, 2 * DF])
    f2d = din("ffn2_w_down", [DF, D])
    w_in = din("w_in", [D, IN_COLS])
    w_out = din("w_out", [D, D])
    glu_w = din("glu_w", [512, 512])
    s5p = din("s5p", [128, 32 * 70])
    consts2 = din("consts2", [128, 1536])
    outT = nc.dram_tensor("outT", [NB, D, T], F32, kind="ExternalOutput").ap()
    X1 = nc.dram_tensor("X1s", [NB, D, T], F32, kind="Internal").ap()
    X2 = nc.dram_tensor("X2s", [NB, D, T], F32, kind="Internal").ap()
    MIX = nc.dram_tensor("MIXs", [NB, D, T], BF16, kind="Internal").ap()
    bX1 = [Buf("X1_%d" % b) for b in range(NB)]
    bX2 = [Buf("X2_%d" % b) for b in range(NB)]
    bMIX = [Buf("MIX_%d" % b) for b in range(NB)]

    final_ops = []
    with k.scope():
        cp = k.sb([128, NCOLP], F32, "colpack"); b_cp = k.buf()
        k.dma(cp[:], colpack[:, :], writes=[b_cp])
        cst = k.sb([128, 1024], F32, "consts"); b_cst = k.buf()
        k.dma(cst[:], consts[:, :], writes=[b_cst])
        ones_bf = k.sb([128, 128], BF16, "ones"); b_ones = k.buf()
        k.S.op("pool", lambda e: e.memset(ones_bf[:], 1.0), writes=[b_ones])
        der = k.sb([128, 9, 8, NB], F32, "der"); b_der = k.buf()
        pbanks = []
        for i in range(8):
            pbanks.append(k.stacks[-1].enter_context(nc.psum_tensor("pb%d" % i, [128, 512], F32)))
        bpb = [k.buf("pb%d" % i) for i in range(8)]

        with k.scope():
            ct = k.sb([128, 16], F32, "ct"); b_ct = k.buf()
            k.dma(ct[:], cT[:, :], writes=[b_ct])
            cact = k.sb([128, 16], F32, "cact"); b_cact = k.buf()
            k.act(cact[:], ct[:], AF.Silu, reads=[b_ct], writes=[b_cact])
            wst = [k.sb([128, 8, 512], F32, "wada") for _ in range(2)]
            b_wst = [k.buf() for _ in range(2)]
            wv = w_ada.rearrange("(k p) c -> p k c", p=128)
            for ch in range(18):
                s = ch % 2
                k.dma(wst[s][:], wv[:, :, ch * 512:(ch + 1) * 512], writes=[b_wst[s]])
                for jj in range(4):
                    j = ch * 4 + jj
                    for kk in range(8):
                        k.mm(pbanks[0][:, 2 * j:2 * j + 2], wst[s][:, kk, jj * 128:(jj + 1) * 128],
                             cact[:, 2 * kk:2 * kk + 2], kk == 0, kk == 7,
                             reads=[b_wst[s], b_cact], writes=[bpb[0]])
            modc = k.sb([128, 72, NB], F32, "modc"); b_modc = k.buf()
            k.tt("dve", modc[:], pbanks[0][:, 0:144].rearrange("p (j b) -> p j b", b=NB),
                 cp[:, C_BADA:C_BADA + 72].unsqueeze(2).to_broadcast([128, 72, NB]), ALU.add,
                 reads=[bpb[0], b_cp], writes=[b_modc])
            for sub in range(3):
                base = sub * 24
                k.copy("dve", der[:, 3 * sub + 1, :, :], modc[:, base:base + 8, :], reads=[b_modc], writes=[b_der])
                k.ts("dve", der[:, 3 * sub, :, :], modc[:, base + 8:base + 16, :], 1.0, None, ALU.add,
                     reads=[b_modc], writes=[b_der])
                gsc = (1.0 if sub == 1 else 0.5) / ALPHA
                k.ts("dve", der[:, 3 * sub + 2, :, :], modc[:, base + 16:base + 24, :], 1.0, gsc, ALU.add, ALU.mult,
                     reads=[b_modc], writes=[b_der])

        def ln_tail(z, b_z, zb, zq, b_zbq, small, b_small, tmp, b_tmp, xo, b_xo, lnidx, ntok):
            ln_tail_a(z, b_z, zb, zq, b_zbq)
            ln_tail_b(z, b_z, zb, zq, b_zbq, small, b_small, tmp, b_tmp, xo, b_xo, lnidx, ntok)

        def ln_tail_a(z, b_z, zb, zq, b_zbq):
            k.act(zb[:], z[:], AF.Copy, reads=[b_z], writes=[b_zbq])
            k.act(zq[:], z[:], AF.Square, reads=[b_z], writes=[b_zbq])

        def ln_tail_b(z, b_z, zb, zq, b_zbq, small, b_small, tmp, b_tmp, xo, b_xo, lnidx, ntok):
            pbs = pbanks[6]
            for dt in range(8):
                k.mm(pbs[:, 0:ntok], ones_bf[:], zb[:, dt, :], dt == 0, dt == 7, reads=[b_zbq, b_ones], writes=[bpb[6]])
            for dt in range(8):
                k.mm(pbs[:, 256:256 + ntok], ones_bf[:], zq[:, dt, :], dt == 0, dt == 7, reads=[b_zbq], writes=[bpb[6]])
            mt, m2, vv, sd, rstd, nmr = [small[:, i, :] for i in range(6)]
            k.ts("dve", mt, pbs[:, 0:ntok], 1.0 / D, None, ALU.mult, reads=[bpb[6]], writes=[b_small])
            k.tt("dve", m2, mt, mt, ALU.mult, reads=[b_small], writes=[b_small])
            k.stt(vv, pbs[:, 256:256 + ntok], 1.0 / D, m2, ALU.mult, ALU.subtract, reads=[bpb[6], b_small], writes=[b_small])
            k.ts("dve", vv, vv, EPS_P, None, ALU.add, reads=[b_small], writes=[b_small])
            k.act(sd, vv, AF.Sqrt, reads=[b_small], writes=[b_small])
            k.S.op("dve", lambda e: e.reciprocal(out=rstd, in_=sd), reads=[b_small], writes=[b_small])
            k.stt(nmr, mt, -1.0, rstd, ALU.mult, ALU.mult, reads=[b_small], writes=[b_small])
            for dt in range(8):
                eng = "dve" if dt % 2 == 0 else "pool"
                tp = tmp[dt % 2]
                k.tt(eng, tp[:], z[:, dt, :], rstd, ALU.mult, reads=[b_z, b_small], writes=[b_tmp[dt % 2]])
                k.tt(eng, tp[:], tp[:], nmr, ALU.add, reads=[b_small, b_tmp[dt % 2]], writes=[b_tmp[dt % 2]])
                k.ts(eng, xo[:, dt, :], tp[:], cp[:, C_LN + 16 * lnidx + dt:C_LN + 16 * lnidx + dt + 1],
                     cp[:, C_LN + 16 * lnidx + 8 + dt:C_LN + 16 * lnidx + 9 + dt], ALU.mult, ALU.add,
                     reads=[b_tmp[dt % 2], b_cp], writes=[b_xo])

        def ffn_pass(src, b_src, dst, b_dst, wu_d, wd_d, sub, lnidx, is_final):
            with k.scope():
                wup = k.sb([128, 8, 2 * DF], BF16, "wup")
                wdn = k.sb([128, 22, D], BF16, "wdn")
                b_wup = [k.buf() for _ in range(22)]
                b_wdn = [k.buf() for _ in range(11)]
                xb = [k.sb([128, 8, TT], F32, "xb") for _ in range(2)]
                b_xb = [k.buf() for _ in range(2)]
                xm = k.sb([128, 8, TT], BF16, "xm"); b_xm = k.buf()
                sgt = [k.sb([128, TT], F32, "sgt") for _ in range(2)]
                b_sgt = [k.buf() for _ in range(2)]
                hh = k.sb([128, 22, TT], BF16, "hh"); b_hh = k.buf()
                z = k.sb([128, 8, TT], F32, "z"); b_z = k.buf()
                zb = k.sb([128, 8, TT], BF16, "zb")
                zq = k.sb([128, 8, TT], BF16, "zq"); b_zbq = k.buf()
                small = k.sb([128, 6, TT], F32, "small"); b_small = k.buf()
                tmp = [k.sb([128, TT], F32, "tmp") for _ in range(2)]
                b_tmp = [k.buf() for _ in range(2)]
                wuv = wu_d.rearrange("(k p) c -> p k c", p=128)
                wdv = wd_d.rearrange("(f p) c -> p f c", p=128)
                engs = ["dve", "act", "pool"]
                n = 0
                for ch in range(22):
                    s = n % 2
                    k.dma(xb[s][:], wuv[:, :, ch * 256:(ch + 1) * 256], writes=[b_xb[s]])
                    k.copy(engs[n % 3], wup[:, :, ch * 256:(ch + 1) * 256], xb[s][:], reads=[b_xb[s]], writes=[b_wup[ch]])
                    n += 1
                for ch in range(11):
                    s = n % 2
                    sv = xb[s][:].rearrange("p a t -> p (a t)").rearrange("p (f c) -> p f c", f=2)
                    k.dma(sv, wdv[:, 2 * ch:2 * ch + 2, :], writes=[b_xb[s]])
                    k.copy(engs[n % 3], wdn[:, 2 * ch:2 * ch + 2, :], sv, reads=[b_xb[s]], writes=[b_wdn[ch]])
                    n += 1
                A = lambda dt, b: der[:, 3 * sub, dt, b:b + 1]
                Bm = lambda dt, b: der[:, 3 * sub + 1, dt, b:b + 1]
                GA = lambda dt, b: der[:, 3 * sub + 2, dt, b:b + 1]
                tiles = [(b, ti) for b in range(NB) for ti in range(NT)]

                def load_x(i):
                    b, ti = tiles[i]
                    s = i % 2
                    k.dma(xb[s][:], src[b].rearrange("(k p) t -> p k t", p=128)[:, :, ti * TT:(ti + 1) * TT],
                          reads=[b_src[b]], writes=[b_xb[s]])

                def make_xm(i):
                    b, ti = tiles[i]
                    s = i % 2
                    for dt in range(8):
                        eng = ("dve", "pool", "act", "pool")[dt % 4]
                        if eng == "act":
                            k.S.op("act", lambda e, dt=dt: e.activation(out=xm[:, dt, :], in_=xb[s][:, dt, :], func=AF.Identity,
                                                                        scale=A(dt, b), bias=Bm(dt, b)),
                                   reads=[b_xb[s], b_der], writes=[b_xm])
                        else:
                            k.ts(eng, xm[:, dt, :], xb[s][:, dt, :], A(dt, b), Bm(dt, b), ALU.mult, ALU.add,
                                 reads=[b_xb[s], b_der], writes=[b_xm])

                load_x(0)
                make_xm(0)

                def finish(i):
                    b, ti = tiles[i]
                    s = i % 2
                    ln_tail_b(z, b_z, zb, zq, b_zbq, small, b_small, tmp, b_tmp, xb[s], b_xb[s], lnidx, TT)
                    o = k.dma(dst[b].rearrange("(k p) t -> p k t", p=128)[:, :, ti * TT:(ti + 1) * TT], xb[s][:],
                              reads=[b_xb[s]], writes=[b_dst[b]])
                    if is_final:
                        final_ops.append(o)

                for i, (b, ti) in enumerate(tiles):
                    s = i % 2
                    for j in range(22):
                        if j == 2:
                            if i > 0:
                                finish(i - 1)
                            if i + 1 < len(tiles):
                                load_x(i + 1)
                        bank = pbanks[j % 2]; bb = bpb[j % 2]
                        for kk in range(8):
                            k.mm(bank[:, 0:TT], wup[:, kk, j * 128:(j + 1) * 128], xm[:, kk, :], kk == 0, kk == 7,
                                 reads=[b_wup[j // 2], b_xm], writes=[bb])
                        for kk in range(8):
                            c0 = DF + j * 128
                            k.mm(bank[:, 256:256 + TT], wup[:, kk, c0:c0 + 128], xm[:, kk, :], kk == 0, kk == 7,
                                 reads=[b_wup[c0 // 256], b_xm], writes=[bb])
                        k.act(sgt[j % 2][:], bank[:, 0:TT], AF.Silu, reads=[bb], writes=[b_sgt[j % 2]])
                        k.tt("dve", hh[:, j, :], sgt[j % 2][:], bank[:, 256:256 + TT], ALU.mult,
                             reads=[bb, b_sgt[j % 2]], writes=[b_hh])
                    if i + 1 < len(tiles):
                        make_xm(i + 1)
                    for dt in range(8):
                        bi = 2 + (dt // 2) % 2
                        half = (dt % 2) * 256
                        for j in range(22):
                            k.mm(pbanks[bi][:, half:half + TT], wdn[:, j, dt * 128:(dt + 1) * 128], hh[:, j, :],
                                 j == 0, j == 21, reads=[b_wdn[j // 2], b_hh], writes=[bpb[bi]])
                        k.stt(z[:, dt, :], pbanks[bi][:, half:half + TT], GA(dt, b), xb[s][:, dt, :], ALU.mult, ALU.add,
                              reads=[bpb[bi], b_xb[s], b_der], writes=[b_z])
                    ln_tail_a(z, b_z, zb, zq, b_zbq)
                finish(len(tiles) - 1)

        b_out = [Buf("out%d" % b) for b in range(NB)]
        if STAGE == 1:
            ffn_pass(xT, [Buf(), Buf()], outT, b_out, f1u, f1d, 0, 0, True)
        if STAGE >= 2 and STAGE != 3:
            ffn_pass(xT, [Buf(), Buf()], X1, bX1, f1u, f1d, 0, 0, False)
        if STAGE == 3:
            bX1 = [Buf(), Buf()]
            X1 = xT
        if STAGE >= 2:
            TWO_PI = 2.0 * math.pi
            mixc = contextlib.ExitStack()
            k.stacks.append(mixc)
            ident = cst[:, 0:128]
            tri = [cst[:, 128:256], cst[:, 256:384]]
            iota = cst[:, 384:896]
            KTc = cst[:, 896:928]
            cs2 = k.sb([128, 1536], F32, "consts2"); b_cs2 = k.buf()
            k.dma(cs2[:], consts2[:, :], writes=[b_cs2])
            rmask = cs2[:, 0:512]
            mle2 = cs2[:, 512:640]
            mge2 = cs2[:, 640:768]
            J32 = cs2[:, 768:800]
            esel = cs2[:, 800:1312]
            dpk = cs2[:, 1312:1344]
            ident_bf = k.sb([128, 128], BF16, "identbf"); b_idbf = k.buf()
            k.copy("dve", ident_bf[:], ident, reads=[b_cst], writes=[b_idbf])
            J_bf = k.sb([128, 128], BF16, "Jbf")
            k.copy("dve", J_bf[:], cs2[:, 1344:1472], reads=[b_cs2], writes=[b_idbf])
            rowbc = k.sb([128, NROWP], F32, "rowbc"); b_rowbc = k.buf()
            k.dma(rowbc[:], rowpack[0:1, :].partition_broadcast(128), writes=[b_rowbc])
            GSd = nc.dram_tensor("GSs", [NB, 4, 4, T], F32, kind="Internal").ap()
            bGS = [Buf() for _ in range(NB)]

            def new(shape, dt, name="w"):
                return k.sb(shape, dt, name), k.buf()

            def load_w(src_ap, rows_k, c0, c1, stage_cols=128):
                wt, b_wt = new([128, rows_k, c1 - c0], BF16, "wbf")
                v = src_ap.rearrange("(k p) c -> p k c", p=128)
                with k.scope():
                    stg = [new([128, rows_k, stage_cols], F32, "wstg") for _ in range(2)]
                    i = 0
                    c = c0
                    while c < c1:
                        w_ = min(stage_cols, c1 - c)
                        st_, bs_ = stg[i % 2]
                        k.dma(st_[:, :, 0:w_], v[:, :, c:c + w_], writes=[bs_])
                        k.copy(("dve", "pool")[i % 2], wt[:, :, c - c0:c - c0 + w_], st_[:, :, 0:w_], reads=[bs_], writes=[b_wt])
                        c += w_
                        i += 1
                return wt, b_wt

            def reduce_angle(x, out, shape, bx, bo):
                tf, btf = new(shape, F32, "raf")
                ti, bti = new(shape, I32, "rai")
                k.ts("dve", tf[:], x, 1.0 / TWO_PI, None, ALU.mult, reads=[bx], writes=[btf])
                k.copy("dve", ti[:], tf[:], reads=[btf], writes=[bti])
                k.copy("dve", tf[:], ti[:], reads=[bti], writes=[btf])
                k.stt(out, tf[:], -TWO_PI, x, ALU.mult, ALU.add, reads=[btf, bx], writes=[bo])

            def sincos(x, sn, cs_, shape, bx, bsn, bcs):
                s2, bs2 = new(shape, F32, "s2")
                s4, bs4 = new(shape, F32, "s4")
                k.act(s2[:], x, AF.Sin, reads=[bx], writes=[bs2], scale=0.5)
                k.act(s4[:], x, AF.Sin, reads=[bx], writes=[bs4], scale=0.25)
                k.tt("dve", cs_, s2[:], s2[:], ALU.mult, reads=[bs2], writes=[bcs])
                k.ts("dve", cs_, cs_, -2.0, 1.0, ALU.mult, ALU.add, reads=[bcs], writes=[bcs])
                k.tt("dve", s4[:], s4[:], s4[:], ALU.mult, reads=[bs4], writes=[bs4])
                k.ts("dve", s4[:], s4[:], -2.0, 1.0, ALU.mult, ALU.add, reads=[bs4], writes=[bs4])
                k.stt(sn, s2[:], 2.0, s4[:], ALU.mult, ALU.mult, reads=[bs2, bs4], writes=[bsn])

            with k.scope():
                mst = contextlib.ExitStack()
                k.stacks.append(mst)
                Tm, b_Tm = new([128, 32, 128], BF16, "Tm")
                PTre, b_PTre = new([128, 32, 128], BF16, "PTre")
                PTim, b_PTim = new([128, 32, 128], BF16, "PTim")
                Qre, b_Qre = new([128, 32, 128], BF16, "Qre")
                Qim, b_Qim = new([128, 32, 128], BF16, "Qim")
                thr, b_thr = new([128, 32], F32, "thr")
                r8, b_r8 = new([128, 32], F32, "r8")
                with k.scope():
                    sp, b_sp = new([128, 32, 70], F32, "s5p")
                    k.dma(sp[:], s5p.rearrange("p (g f) -> p g f", f=70), writes=[b_sp])
                    lr = sp[:, :, 0]; li = sp[:, :, 1]
                    S2 = [128, 32]
                    step, b_step = new(S2, F32)
                    k.act(step[:], sp[:, :, 2], AF.Exp, reads=[b_sp], writes=[b_step])
                    lrs, b_lrs = new(S2, F32)
                    k.tt("dve", lrs[:], lr, step[:], ALU.mult, reads=[b_sp, b_step], writes=[b_lrs])
                    phi, b_phi = new(S2, F32)
                    k.tt("dve", phi[:], li, step[:], ALU.mult, reads=[b_sp, b_step], writes=[b_phi])
                    phr, b_phr = new(S2, F32)
                    reduce_angle(phi[:], phr[:], S2, b_phi, b_phr)
                    th8, b_th8 = new(S2, F32)
                    k.ts("dve", th8[:], phr[:], 8.0, None, ALU.mult, reads=[b_phr], writes=[b_th8])
                    reduce_angle(th8[:], thr[:], S2, b_th8, b_thr)
                    k.act(r8[:], lrs[:], AF.Exp, reads=[b_lrs], writes=[b_r8], scale=8.0)
                    sn1, b_sn1 = new(S2, F32); cs1, b_cs1 = new(S2, F32); mg1, b_mg1 = new(S2, F32)
                    sincos(phr[:], sn1[:], cs1[:], S2, b_phr, b_sn1, b_cs1)
                    k.act(mg1[:], lrs[:], AF.Exp, reads=[b_lrs], writes=[b_mg1])
                    nr, b_nr = new(S2, F32); ni, b_ni = new(S2, F32)
                    k.tt("dve", nr[:], mg1[:], cs1[:], ALU.mult, reads=[b_mg1, b_cs1], writes=[b_nr])
                    k.ts("dve", nr[:], nr[:], -1.0, None, ALU.add, reads=[b_nr], writes=[b_nr])
                    k.tt("dve", ni[:], mg1[:], sn1[:], ALU.mult, reads=[b_mg1, b_sn1], writes=[b_ni])
                    inv, b_inv = new(S2, F32); t0, b_t0 = new(S2, F32); t1, b_t1 = new(S2, F32)
                    k.tt("dve", inv[:], lr, lr, ALU.mult, reads=[b_sp], writes=[b_inv])
                    k.tt("dve", t0[:], li, li, ALU.mult, reads=[b_sp], writes=[b_t0])
                    k.tt("dve", inv[:], inv[:], t0[:], ALU.add, reads=[b_inv, b_t0], writes=[b_inv])
                    k.S.op("dve", lambda e: e.reciprocal(out=inv[:], in_=inv[:]), reads=[b_inv], writes=[b_inv])
                    cfr, b_cfr = new(S2, F32); cfi, b_cfi = new(S2, F32)
                    k.tt("dve", t0[:], nr[:], lr, ALU.mult, reads=[b_nr, b_sp], writes=[b_t0])
                    k.tt("dve", t1[:], ni[:], li, ALU.mult, reads=[b_ni, b_sp], writes=[b_t1])
                    k.tt("dve", t0[:], t0[:], t1[:], ALU.add, reads=[b_t0, b_t1], writes=[b_t0])
                    k.tt("dve", cfr[:], t0[:], inv[:], ALU.mult, reads=[b_t0, b_inv], writes=[b_cfr])
                    k.tt("dve", t0[:], ni[:], lr, ALU.mult, reads=[b_ni, b_sp], writes=[b_t0])
                    k.tt("dve", t1[:], nr[:], li, ALU.mult, reads=[b_nr, b_sp], writes=[b_t1])
                    k.tt("dve", t0[:], t0[:], t1[:], ALU.subtract, reads=[b_t0, b_t1], writes=[b_t0])
                    k.tt("dve", cfi[:], t0[:], inv[:], ALU.mult, reads=[b_t0, b_inv], writes=[b_cfi])
                    S3 = [128, 32, 16]
                    bre = sp[:, :, 3:19]; bim = sp[:, :, 19:35]; cre = sp[:, :, 35:51]; cim = sp[:, :, 51:67]
                    Bcr, b_Bcr = new(S3, F32); Bci, b_Bci = new(S3, F32); u0, b_u0 = new(S3, F32)
                    cfrb = cfr[:].unsqueeze(2).to_broadcast(S3); cfib = cfi[:].unsqueeze(2).to_broadcast(S3)
                    k.tt("dve", Bcr[:], bre, cfrb, ALU.mult, reads=[b_sp, b_cfr], writes=[b_Bcr])
                    k.tt("dve", u0[:], bim, cfib, ALU.mult, reads=[b_sp, b_cfi], writes=[b_u0])
                    k.tt("dve", Bcr[:], Bcr[:], u0[:], ALU.subtract, reads=[b_Bcr, b_u0], writes=[b_Bcr])
                    k.tt("dve", Bci[:], bim, cfrb, ALU.mult, reads=[b_sp, b_cfr], writes=[b_Bci])
                    k.tt("dve", u0[:], bre, cfib, ALU.mult, reads=[b_sp, b_cfi], writes=[b_u0])
                    k.tt("dve", Bci[:], Bci[:], u0[:], ALU.add, reads=[b_Bci, b_u0], writes=[b_Bci])
                    SP = [128, 32, 32]
                    pwr, b_pwr = new(SP, F32); pwi, b_pwi = new(SP, F32)
                    with k.scope():
                        ang, b_ang = new(SP, F32); angr, b_angr = new(SP, F32)
                        ktb = KTc.unsqueeze(1).to_broadcast(SP)
                        k.tt("dve", ang[:], phr[:].unsqueeze(2).to_broadcast(SP), ktb, ALU.mult, reads=[b_phr, b_cst], writes=[b_ang])
                        reduce_angle(ang[:], angr[:], SP, b_ang, b_angr)
                        psn, b_psn = new(SP, F32); pcs, b_pcs = new(SP, F32); pmg, b_pmg = new(SP, F32)
                        sincos(angr[:], psn[:], pcs[:], SP, b_angr, b_psn, b_pcs)
                        k.tt("dve", pmg[:], lrs[:].unsqueeze(2).to_broadcast(SP), ktb, ALU.mult, reads=[b_lrs, b_cst], writes=[b_pmg])
                        k.act(pmg[:], pmg[:], AF.Exp, reads=[b_pmg], writes=[b_pmg])
                        k.tt("dve", pwr[:], pmg[:], pcs[:], ALU.mult, reads=[b_pmg, b_pcs], writes=[b_pwr])
                        k.tt("dve", pwi[:], pmg[:], psn[:], ALU.mult, reads=[b_pmg, b_psn], writes=[b_pwi])
                    S4 = [128, 32, 8, 16]
                    ta, b_ta = new(S4, F32); tb_, b_tb = new(S4, F32)

                    def cprod(ar, ai, bar, bai, off, o_re, b_ore, o_im, b_oim, neg_im):
                        arb = ar.unsqueeze(2).to_broadcast(S4); aib = ai.unsqueeze(2).to_broadcast(S4)
                        pr = pwr[:, :, off:off + 8].unsqueeze(3).to_broadcast(S4)
                        pi_ = pwi[:, :, off:off + 8].unsqueeze(3).to_broadcast(S4)
                        k.tt("dve", ta[:], arb, pr, ALU.mult, reads=[bar, b_pwr], writes=[b_ta])
                        k.tt("pool", tb_[:], aib, pi_, ALU.mult, reads=[bai, b_pwi], writes=[b_tb])
                        k.tt("dve", o_re, ta[:], tb_[:], ALU.subtract, reads=[b_ta, b_tb], writes=[b_ore])
                        k.tt("dve", ta[:], arb, pi_, ALU.mult, reads=[bar, b_pwi], writes=[b_ta])
                        k.tt("pool", tb_[:], aib, pr, ALU.mult, reads=[bai, b_pwr], writes=[b_tb])
                        if neg_im:
                            k.stt(o_im, ta[:], -1.0, tb_[:], ALU.mult, ALU.subtract, reads=[b_ta, b_tb], writes=[b_oim])
                        else:
                            k.tt("dve", o_im, ta[:], tb_[:], ALU.add, reads=[b_ta, b_tb], writes=[b_oim])

                    v4 = lambda t_: t_[:].rearrange("p g (s h) -> p g s h", h=16)
                    if "noQ" not in SKIP:
                        cprod(cre, cim, b_sp, b_sp, 24, v4(Qre), b_Qre, v4(Qim), b_Qim, True)
                    with k.scope():
                      if "noP" not in SKIP:
                        Pr, b_Pr = new([128, 32, 128], F32); Pi, b_Pi = new([128, 32, 128], F32)
                        cprod(Bcr[:], Bci[:], b_Bcr, b_Bci, 0, v4(Pr), b_Pr, v4(Pi), b_Pi, False)
                        for g in range(32):
                            bank = pbanks[g % 2]; bb = bpb[g % 2]
                            k.mm(bank[:, 256:384], Pr[:, g, :], ident, True, True, reads=[b_Pr, b_cst], writes=[bb])
                            k.mm(bank[:, 384:512], Pi[:, g, :], ident, True, True, reads=[b_Pi, b_cst], writes=[bb])
                            k.act(PTre[:, g, :], bank[:, 256:384], AF.Copy, reads=[bb], writes=[b_PTre])
                            k.act(PTim[:, g, :], bank[:, 384:512], AF.Copy, reads=[bb], writes=[b_PTim])
                    with k.scope():
                      if "noT" not in SKIP:
                        Xr, b_Xr = new([128, 32, 128], F32); Xi, b_Xi = new([128, 32, 128], F32)
                        cprod(Bcr[:], Bci[:], b_Bcr, b_Bci, 8, v4(Xr), b_Xr, v4(Xi), b_Xi, False)
                        Zr, b_Zr = new([128, 32, 128], F32); Zi, b_Zi = new([128, 32, 128], F32)
                        cprod(cre, cim, b_sp, b_sp, 16, v4(Zr), b_Zr, v4(Zi), b_Zi, True)
                        Xmr = ta[:].rearrange("p g s h -> p g (s h)"); b_Xmr = b_ta
                        Xmi = tb_[:].rearrange("p g s h -> p g (s h)"); b_Xmi = b_tb
                        hm = [cst[:, 128 + 63:128 + 64], cst[:, 256 + 64:256 + 65]]
                        tA, b_tA = new([128, 128], F32); tB, b_tB = new([128, 128], F32)
                        for dd_ in range(2):
                            k.ts("dve", Xmr, Xr[:], hm[dd_], None, ALU.mult, reads=[b_Xr, b_cst], writes=[b_Xmr])
                            k.ts("pool", Xmi, Xi[:], hm[dd_], None, ALU.mult, reads=[b_Xi, b_cst], writes=[b_Xmi])
                            for g in range(32):
                                bank = pbanks[2 + g % 2]; bb = bpb[2 + g % 2]
                                k.mm(bank[:, 0:128], Xmr[:, g, :], Zr[:, g, :], True, False, reads=[b_Xmr, b_Zr], writes=[bb])
                                k.mm(bank[:, 0:128], Xmi[:, g, :], Zi[:, g, :], False, True, reads=[b_Xmi, b_Zi], writes=[bb])
                                if dd_ == 0:
                                    k.tt("dve", tA[:], bank[:, 0:128], mle2, ALU.mult, reads=[bb, b_cs2], writes=[b_tA])
                                    k.stt(Tm[:, g, :], ident, dpk[:, g:g + 1], tA[:], ALU.mult, ALU.add,
                                          reads=[b_tA, b_cs2, b_cst], writes=[b_Tm])
                                else:
                                    k.tt("dve", tB[:], bank[:, 0:128], mge2, ALU.mult, reads=[bb, b_cs2], writes=[b_tB])
                                    k.tt("dve", Tm[:, g, :], Tm[:, g, :], tB[:], ALU.add, reads=[b_tB, b_Tm], writes=[b_Tm])

                for (b, phase) in ((0, "S"), (1, "S"), (0, "MO"), (1, "MO")):
                    if phase == "MO" and mst is not None:
                        k.stacks.pop()
                        mst.close()
                        mst = None
                        k.fence_ops = k.S.fence()
                    x1v = X1[b].rearrange("(k p) t -> p k t", p=128)

                    def dma_x1(t0, ntok, xs, b_xs):
                        k.dma(xs[:, :, 0:ntok], x1v[:, :, t0:t0 + ntok], reads=[bX1[b]], writes=[b_xs])

                    def mod_x1(dst_ap, ntok, xs, b_xs, b_dst):
                        for dt in range(8):
                            eng = ("dve", "pool")[dt % 2]
                            k.ts(eng, dst_ap[:, dt, :], xs[:, dt, 0:ntok], der[:, 3, dt, b:b + 1], der[:, 4, dt, b:b + 1],
                                 ALU.mult, ALU.add, reads=[b_xs, b_der], writes=[b_dst])

                    def load_xm2(dst_ap, t0, ntok, xs, b_xs, b_dst):
                        k.dma(xs[:, :, 0:ntok], x1v[:, :, t0:t0 + ntok], reads=[bX1[b]], writes=[b_xs])
                        for dt in range(8):
                            eng = ("dve", "pool")[dt % 2]
                            k.ts(eng, dst_ap[:, dt, :], xs[:, dt, 0:ntok], der[:, 3, dt, b:b + 1], der[:, 4, dt, b:b + 1],
                                 ALU.mult, ALU.add, reads=[b_xs, b_der], writes=[b_dst])

                    with (k.scope() if phase == "S" else contextlib.nullcontext()):
                      if phase == "S" and "S" not in SKIP:
                        Zy, b_Zy = new([128, 4, 8, 512], BF16, "Zy")
                        Zta, b_Zta = new([128, 4, 32, 8, 16], BF16, "Zta")
                        with k.scope():
                            wu, b_wu = load_w(w_in, 8, 1552, 2064)
                            xs, b_xs = new([128, 8, 128], F32, "xs")
                            xmb, b_xmb = new([128, 8, 1024], BF16, "xmb")
                            for cb in range(4):
                                for q8 in range(8):
                                    load_xm2(xmb[:, :, q8 * 128:(q8 + 1) * 128], cb * 1024 + q8 * 128, 128, xs, b_xs, b_xmb)
                                for s_ in range(8):
                                    bank = pbanks[s_ % 2]; bb = bpb[s_ % 2]
                                    for kk in range(8):
                                        lv = xmb[:, kk, :].rearrange("p (c s) -> p s c", s=8)[:, s_, :]
                                        k.mm(bank[:, :], lv, wu[:, kk, :], kk == 0, kk == 7, reads=[b_xmb, b_wu], writes=[bb])
                                    k.tt("dve", Zta[:, cb, :, s_, :], bank[:, :].rearrange("p (g h) -> p g h", h=16),
                                         rowbc[:, R_BU:R_BU + 512].rearrange("p (g h) -> p g h", h=16), ALU.add,
                                         reads=[bb, b_rowbc], writes=[b_Zta])
                        NG = 8
                        for hf_ in range(32 // NG):
                          with k.scope():
                            G0 = NG * hf_
                            Ut, b_Ut = new([128, NG, 512], BF16, "Ut")
                            Ur, b_Ur = new([128, NG, 512], BF16, "Ur")
                            for cb in range(4):
                                for g4 in range(NG // 4):
                                    bank = pbanks[2 + g4 % 2]; bb = bpb[2 + g4 % 2]
                                    bank2 = pbanks[4 + g4 % 2]; bb2 = bpb[4 + g4 % 2]
                                    for gg in range(4):
                                        g = G0 + g4 * 4 + gg
                                        zl = Zta[:, cb, g, :, :].rearrange("p s h -> p (s h)")
                                        k.mm(bank[:, gg * 128:(gg + 1) * 128], zl, ident_bf[:], True, True,
                                             reads=[b_Zta, b_idbf], writes=[bb])
                                        k.mm(bank2[:, gg * 128:(gg + 1) * 128], zl, J_bf[:], True, True,
                                             reads=[b_Zta, b_idbf], writes=[bb2])
                                    k.act(Ut[:, g4 * 4:g4 * 4 + 4, cb * 128:(cb + 1) * 128],
                                          bank[:, :].rearrange("p (g c) -> p g c", g=4), AF.Copy, reads=[bb], writes=[b_Ut])
                                    k.copy("dve", Ur[:, g4 * 4:g4 * 4 + 4, (3 - cb) * 128:(4 - cb) * 128],
                                           bank2[:, :].rearrange("p (g c) -> p g c", g=4), reads=[bb2], writes=[b_Ur])
                            with k.scope():
                                W_ = lambda: new([128, 512], F32, "sw")
                                W2 = lambda: [W_(), W_()]
                                (ta_, bta), (tr_, btr), (s2t, bs2) = W_(), W_(), W_()
                                (ti_, bti) = new([128, 512], I32, "twi")
                                tcs, tsns = W2(), W2()
                                srL, siL, a1L, a2L, stRL, stIL, kRL, kIL = [W2() for _ in range(8)]
                                HpL = [[new([128, 513], BF16, "Hp") for _ in range(4)] for _ in range(2)]
                                for hl in HpL:
                                    for (h_, bh_) in hl:
                                        k.S.op("pool", lambda e, h_=h_: e.memset(h_[:], 0.0), writes=[bh_])
                                ygL = [new([128, 512], BF16, "yg") for _ in range(2)]
                                ygbL = [new([128, 512], BF16, "ygb") for _ in range(2)]
                                ygTL = [new([128, 4, 128], BF16, "ygT") for _ in range(2)]
                                pY, bY = pbanks[0], bpb[0]
                                pYb, bYb = pbanks[1], bpb[1]
                                pAr, bAr = pbanks[2], bpb[2]
                                pBr, bBr = pbanks[3], bpb[3]
                                pAi, bAi = pbanks[4], bpb[4]
                                pBi, bBi = pbanks[5], bpb[5]
                                pT, bT = pbanks[6], bpb[6]
                                pT2, bT2 = pbanks[7], bpb[7]

                                def gen_tw(gl):
                                    g = G0 + gl
                                    (tc, btc), (tsn, bts) = tcs[gl % 2], tsns[gl % 2]
                                    k.S.op("act", lambda e: e.activation(out=ta_[:], in_=iota, func=AF.Copy, scale=thr[:, g:g + 1]),
                                           reads=[b_cst, b_thr], writes=[bta])
                                    k.act(tr_[:], ta_[:], AF.Copy, reads=[bta], writes=[btr], scale=1.0 / TWO_PI)
                                    k.copy("dve", ti_[:], tr_[:], reads=[btr], writes=[bti])
                                    k.copy("dve", tr_[:], ti_[:], reads=[bti], writes=[btr])
                                    k.stt(ta_[:], tr_[:], -TWO_PI, ta_[:], ALU.mult, ALU.add, reads=[btr, bta], writes=[bta])
                                    k.act(s2t[:], ta_[:], AF.Sin, reads=[bta], writes=[bs2], scale=0.5)
                                    k.act(tr_[:], ta_[:], AF.Sin, reads=[bta], writes=[btr], scale=0.25)
                                    k.tt("pool", tc[:], s2t[:], s2t[:], ALU.mult, reads=[bs2], writes=[btc])
                                    k.ts("pool", tc[:], tc[:], -2.0, 1.0, ALU.mult, ALU.add, reads=[btc], writes=[btc])
                                    k.tt("pool", tr_[:], tr_[:], tr_[:], ALU.mult, reads=[btr], writes=[btr])
                                    k.ts("pool", tr_[:], tr_[:], -2.0, 1.0, ALU.mult, ALU.add, reads=[btr], writes=[btr])
                                    k.stt(tsn[:], s2t[:], 2.0, tr_[:], ALU.mult, ALU.mult, reads=[bs2, btr], writes=[bts])

                                def stageA(gl):
                                    g = G0 + gl
                                    (sr, bsr), (si, bsi) = srL[gl % 2], siL[gl % 2]
                                    k.mm(pAr[:, :], PTre[:, g, :], Ut[:, gl, :], True, True, reads=[b_PTre, b_Ut], writes=[bAr])
                                    k.mm(pBr[:, :], PTre[:, g, :], Ur[:, gl, :], True, True, reads=[b_PTre, b_Ur], writes=[bBr])
                                    k.mm(pAi[:, :], PTim[:, g, :], Ut[:, gl, :], True, True, reads=[b_PTim, b_Ut], writes=[bAi])
                                    k.mm(pBi[:, :], PTim[:, g, :], Ur[:, gl, :], True, True, reads=[b_PTim, b_Ur], writes=[bBi])
                                    k.act(sr[0:64, :], pAr[0:64, :], AF.Copy, reads=[bAr], writes=[bsr])
                                    k.act(sr[64:128, :], pBr[64:128, :], AF.Copy, reads=[bBr], writes=[bsr])
                                    k.act(si[0:64, :], pAi[0:64, :], AF.Copy, reads=[bAi], writes=[bsi])
                                    k.act(si[64:128, :], pBi[64:128, :], AF.Copy, reads=[bBi], writes=[bsi])

                                def stageB(gl):
                                    g = G0 + gl
                                    w = gl % 2
                                    (tc, btc), (tsn, bts) = tcs[w], tsns[w]
                                    (sr, bsr), (si, bsi), (a1, ba1), (a2, ba2) = srL[w], siL[w], a1L[w], a2L[w]
                                    (stR, bstR), (stI, bstI), (kR, bkR), (kI, bkI) = stRL[w], stIL[w], kRL[w], kIL[w]
                                    (hrf, bhrf), (hif, bhif), (hrb, bhrb), (hib, bhib) = HpL[w]
                                    (yg, byg), (ygb, bygb), (ygT, bygT) = ygL[w], ygbL[w], ygTL[w]
                                    k.tt("dve", a1[:], sr[:], tc[:], ALU.mult, reads=[bsr, btc], writes=[ba1])
                                    k.tt("pool", a2[:], si[:], tsn[:], ALU.mult, reads=[bsi, bts], writes=[ba2])
                                    k.tt("dve", stR[:], a1[:], a2[:], ALU.add, reads=[ba1, ba2], writes=[bstR])
                                    k.tt("dve", a1[:], si[:], tc[:], ALU.mult, reads=[bsi, btc], writes=[ba1])
                                    k.tt("pool", a2[:], sr[:], tsn[:], ALU.mult, reads=[bsr, bts], writes=[ba2])
                                    k.tt("dve", stI[:], a1[:], a2[:], ALU.subtract, reads=[ba1, ba2], writes=[bstI])
                                    r8b = r8[:, g:g + 1].to_broadcast([128, 512])
                                    k.S.op("dve", lambda e: e.tensor_tensor_scan(out=kR[:], data0=r8b, data1=stR[:], initial=0.0,
                                                                                 op0=ALU.mult, op1=ALU.add),
                                           reads=[bstR, b_r8], writes=[bkR])
                                    k.S.op("dve", lambda e: e.tensor_tensor_scan(out=kI[:], data0=r8b, data1=stI[:], initial=0.0,
                                                                                 op0=ALU.mult, op1=ALU.add),
                                           reads=[bstI, b_r8], writes=[bkI])
                                    if gl + 1 < NG:
                                        gen_tw(gl + 1)
                                    k.tt("dve", a1[:], kR[:], tc[:], ALU.mult, reads=[bkR, btc], writes=[ba1])
                                    k.tt("pool", a2[:], kI[:], tsn[:], ALU.mult, reads=[bkI, bts], writes=[ba2])
                                    k.tt("dve", hrf[0:64, 1:513], a1[0:64, :], a2[0:64, :], ALU.subtract, reads=[ba1, ba2], writes=[bhrf])
                                    k.tt("dve", hrb[64:128, 1:513], a1[64:128, :], a2[64:128, :], ALU.subtract, reads=[ba1, ba2], writes=[bhrb])
                                    k.tt("dve", a1[:], kI[:], tc[:], ALU.mult, reads=[bkI, btc], writes=[ba1])
                                    k.tt("pool", a2[:], kR[:], tsn[:], ALU.mult, reads=[bkR, bts], writes=[ba2])
                                    k.tt("dve", hif[0:64, 1:513], a1[0:64, :], a2[0:64, :], ALU.add, reads=[ba1, ba2], writes=[bhif])
                                    k.tt("dve", hib[64:128, 1:513], a1[64:128, :], a2[64:128, :], ALU.add, reads=[ba1, ba2], writes=[bhib])
                                    k.mm(pY[:, :], Tm[:, g, :], Ut[:, gl, :], True, False, reads=[b_Tm, b_Ut], writes=[bY])
                                    k.mm(pY[:, :], Qre[:, g, :], hrf[:, 0:512], False, False, reads=[b_Qre, bhrf], writes=[bY])
                                    k.mm(pY[:, :], Qim[:, g, :], hif[:, 0:512], False, True, reads=[b_Qim, bhif], writes=[bY])
                                    k.mm(pYb[:, :], Qre[:, g, :], hrb[:, 0:512], True, False, reads=[b_Qre, bhrb], writes=[bYb])
                                    k.mm(pYb[:, :], Qim[:, g, :], hib[:, 0:512], False, True, reads=[b_Qim, bhib], writes=[bYb])
                                    k.act(yg[:], pY[:, :], AF.Copy, reads=[bY], writes=[byg])
                                    k.act(ygb[:], pYb[:, :], AF.Copy, reads=[bYb], writes=[bygb])
                                    for jb in range(4):
                                        k.mm(pT2[:, jb * 128:(jb + 1) * 128], ygb[:, jb * 128:(jb + 1) * 128], ident_bf[:], True, True,
                                             reads=[bygb, b_idbf], writes=[bT2])
                                    k.copy("dve", ygT[:], pT2[:, :].rearrange("p (j x) -> p j x", j=4), reads=[bT2], writes=[bygT])
                                    for cb in range(4):
                                        k.mm(pT[:, cb * 128:(cb + 1) * 128], yg[:, cb * 128:(cb + 1) * 128], ident_bf[:], True, False,
                                             reads=[byg, b_idbf], writes=[bT])
                                        k.mm(pT[:, cb * 128:(cb + 1) * 128], J_bf[:], ygT[:, 3 - cb, :], False, True,
                                             reads=[bygT, b_idbf], writes=[bT])
                                    k.copy("dve", Zy[:, :, :, 16 * g:16 * g + 16],
                                           pT[:, :].rearrange("p (c t h) -> p c t h", c=4, t=8), reads=[bT], writes=[b_Zy])

                                gen_tw(0)
                                stageA(0)
                                for gl in range(NG):
                                    if gl + 1 < NG:
                                        stageA(gl + 1)
                                    stageB(gl)
                        with k.scope():
                            glw, b_glw = load_w(glu_w, 4, 0, 512)
                            yb, b_yb = new([128, 4, 1024], F32, "yb")
                            u1, b_u1 = new([128, 4, 1024], F32, "u1")
                            yab, b_yab = new([128, 4, 1024], BF16, "yab")
                            sqb, b_sqb = new([128, 4, 1024], BF16, "sqb")
                            sg_, b_sg = new([128, 512], F32, "sg")
                            rs_, b_rs = new([128, 512], F32, "rs")
                            mo, b_mo = new([128, 4, 1024], BF16, "mo")
                            GC = 2.0 * math.sqrt(2.0 / math.pi)
                            for cb in range(4):
                                for ct in range(4):
                                    for th in range(2):
                                        bank = pbanks[(ct * 2 + th) % 2]; bb = bpb[(ct * 2 + th) % 2]
                                        for t4 in range(4):
                                            t_ = th * 4 + t4
                                            k.mm(bank[:, t4 * 128:(t4 + 1) * 128], Zy[:, cb, t_, ct * 128:(ct + 1) * 128], ident_bf[:],
                                                 True, True, reads=[b_Zy, b_idbf], writes=[bb])
                                        k.act(yb[:, ct, :].rearrange("p (c s) -> p s c", s=8)[:, th * 4:th * 4 + 4, :],
                                              bank[:, :].rearrange("p (s c) -> p s c", s=4), AF.Copy, reads=[bb], writes=[b_yb])
                                k.act(u1[:], yb[:], AF.Square, reads=[b_yb], writes=[b_u1])
                                k.ts("dve", u1[:], u1[:], 0.044715, 1.0, ALU.mult, ALU.add, reads=[b_u1], writes=[b_u1])
                                k.tt("pool", u1[:], u1[:], yb[:], ALU.mult, reads=[b_u1, b_yb], writes=[b_u1])
                                k.act(u1[:], u1[:], AF.Sigmoid, reads=[b_u1], writes=[b_u1], scale=GC)
                                k.tt("dve", yb[:], yb[:], u1[:], ALU.mult, reads=[b_u1, b_yb], writes=[b_yb])
                                k.copy("pool", yab[:], yb[:], reads=[b_yb], writes=[b_yab])
                                for hf in range(2):
                                    tk = slice(hf * 512, (hf + 1) * 512)
                                    for co in range(4):
                                        bank = pbanks[2 + co % 2]; bb = bpb[2 + co % 2]
                                        for ci in range(4):
                                            k.mm(bank[:, :], glw[:, ci, co * 128:(co + 1) * 128], yab[:, ci, tk], ci == 0, ci == 3,
                                                 reads=[b_glw, b_yab], writes=[bb])
                                        k.act(sg_[:], bank[:, :], AF.Sigmoid, reads=[bb, b_cp], writes=[b_sg],
                                              bias=cp[:, C_GLUB + co:C_GLUB + co + 1])
                                        k.tt("dve", yb[:, co, tk], yb[:, co, tk], sg_[:], ALU.mult, reads=[b_yb, b_sg], writes=[b_yb])
                                k.act(sqb[:], yb[:], AF.Square, reads=[b_yb], writes=[b_sqb])
                                for hf in range(2):
                                    tk = slice(hf * 512, (hf + 1) * 512)
                                    bank = pbanks[4 + hf]; bb = bpb[4 + hf]
                                    for ci in range(4):
                                        k.mm(bank[:, :], ones_bf[:], sqb[:, ci, tk], ci == 0, ci == 3, reads=[b_sqb, b_ones], writes=[bb])
                                    k.ts("dve", rs_[:], bank[:, :], 1.0 / 512.0, LN_EPS, ALU.mult, ALU.add, reads=[bb], writes=[b_rs])
                                    k.act(rs_[:], rs_[:], AF.Sqrt, reads=[b_rs], writes=[b_rs])
                                    k.S.op("dve", lambda e: e.reciprocal(out=rs_[:], in_=rs_[:]), reads=[b_rs], writes=[b_rs])
                                    for co in range(4):
                                        k.stt(mo[:, co, tk], yb[:, co, tk], cp[:, C_SNG + co:C_SNG + co + 1], rs_[:], ALU.mult, ALU.mult,
                                              reads=[b_yb, b_rs, b_cp], writes=[b_mo])
                                k.dma(MIX[b, 512:1024, cb * 1024:(cb + 1) * 1024].rearrange("(k p) t -> p k t", p=128), mo[:],
                                      reads=[b_mo], writes=[bMIX[b]])

                    with (k.scope() if phase == "MO" else contextlib.nullcontext()):
                      if phase == "MO" and "M" not in SKIP:
                        qf, b_qf = new([128, 2, T], BF16, "qf")
                        kf, b_kf = new([128, 2, T], BF16, "kf")
                        vext, b_vext = new([128, 32, 4, 129], BF16, "vext")
                        big, b_big = new([128, 4, T + 4], F32, "big")
                        k.S.op("pool", lambda e: e.memset(vext[:, :, :, 128:129], 1.0), writes=[b_vext])
                        k.S.op("pool", lambda e: e.memset(big[:, :, 0:2], 0.0), writes=[b_big])
                        k.S.op("pool", lambda e: e.memset(big[:, :, T + 2:T + 4], 0.0), writes=[b_big])
                        with k.scope():
                            wqk, b_wqk = load_w(w_in, 8, 0, 512)
                            wv, b_wv = load_w(w_in, 8, 512, 1024)
                            wg, b_wg = load_w(w_in, 8, 1536, 1552, 16)
                            xs, b_xs = new([128, 8, 256], F32, "xs")
                            xm2L = [new([128, 8, 256], BF16, "xm2") for _ in range(2)]
                            gt, b_gt = new([16, 256], F32, "gt")
                            dma_x1(0, 256, xs, b_xs)
                            mod_x1(xm2L[0][0], 256, xs, b_xs, xm2L[0][1])
                            for ti in range(16):
                                t0_ = ti * 256
                                (xm2, b_xm2) = xm2L[ti % 2]
                                if ti + 1 < 16:
                                    dma_x1(t0_ + 256, 256, xs, b_xs)
                                for ct in range(4):
                                    bank = pbanks[ct // 2]; bb = bpb[ct // 2]
                                    half = (ct % 2) * 256
                                    for kk in range(8):
                                        k.mm(bank[:, half:half + 256], wqk[:, kk, ct * 128:(ct + 1) * 128], xm2[:, kk, :], kk == 0, kk == 7,
                                             reads=[b_wqk, b_xm2], writes=[bb])
                                    k.act(big[:, ct, 2 + t0_:2 + t0_ + 256], bank[:, half:half + 256], AF.Identity, reads=[bb, b_cp],
                                          writes=[b_big], bias=cp[:, C_BQK + ct:C_BQK + ct + 1])
                                for cc in range(2):
                                    c = ti * 2 + cc
                                    bank = pbanks[2 + cc]; bb = bpb[2 + cc]
                                    for kk in range(8):
                                        k.mm(bank[:, :], xm2[:, kk, cc * 128:(cc + 1) * 128], wv[:, kk, :], kk == 0, kk == 7,
                                             reads=[b_wv, b_xm2], writes=[bb])
                                    k.tt("dve", vext[:, c, :, 0:128], bank[:, :].rearrange("p (r v) -> p r v", r=4),
                                         rowbc[:, R_BV:R_BV + 512].rearrange("p (r v) -> p r v", r=4), ALU.add,
                                         reads=[bb, b_rowbc], writes=[b_vext])
                                bank = pbanks[4]; bb = bpb[4]
                                for kk in range(8):
                                    k.mm(bank[0:16, 0:256], wg[:, kk, 0:16], xm2[:, kk, :], kk == 0, kk == 7,
                                         reads=[b_wg, b_xm2], writes=[bb])
                                k.act(gt[0:16, :], bank[0:16, 0:256], AF.Identity, reads=[bb, b_cp], writes=[b_gt],
                                      bias=cp[0:16, C_BG + 4:C_BG + 5])
                                k.dma(GSd[b, :, :, t0_:t0_ + 256].rearrange("k r t -> (k r) t"), gt[:], reads=[b_gt], writes=[bGS[b]])
                                if ti + 1 < 16:
                                    mod_x1(xm2L[(ti + 1) % 2][0], 256, xs, b_xs, xm2L[(ti + 1) % 2][1])
                        ktm, b_ktm = new([128, 32, 256], BF16, "ktm")
                        with k.scope():
                            acc, b_acc = new([128, T], F32, "acc")
                            for ct in range(4):
                                cw = lambda j: cp[:, C_CW + 4 * j + ct:C_CW + 4 * j + ct + 1]
                                k.ts("dve", acc[:], big[:, ct, 0:T], cw(0), None, ALU.mult, reads=[b_big, b_cp], writes=[b_acc])
                                for j in range(1, 5):
                                    k.stt(acc[:], big[:, ct, j:j + T], cw(j), acc[:], ALU.mult, ALU.add, reads=[b_big, b_cp, b_acc], writes=[b_acc])
                                cb_ = cp[:, C_CB + ct:C_CB + ct + 1]
                                if ct < 2:
                                    k.act(acc[:], acc[:], AF.Silu, reads=[b_acc, b_cp], writes=[b_acc], bias=cb_)
                                    k.ts("pool", qf[:, ct, :], acc[:], 0.125, None, ALU.mult, reads=[b_acc], writes=[b_qf])
                                else:
                                    k.act(kf[:, ct - 2, :], acc[:], AF.Silu, reads=[b_acc, b_cp], writes=[b_kf], bias=cb_)
                            for c2 in range(16):
                                bank = pbanks[c2 % 2]; bb = bpb[c2 % 2]
                                for cc in range(2):
                                    c = c2 * 2 + cc
                                    for kt in range(2):
                                        o_ = cc * 256 + kt * 128
                                        k.mm(bank[:, o_:o_ + 128], kf[:, kt, c * 128:(c + 1) * 128], ident_bf[:], True, True,
                                             reads=[b_kf, b_idbf], writes=[bb])
                                k.act(ktm[:, c2 * 2:c2 * 2 + 2, :], bank[:, :].rearrange("p (c x) -> p c x", c=2), AF.Copy,
                                      reads=[bb], writes=[b_ktm])
                        WCt, b_WCt = new([128, 2, 8, 32], F32, "WCt")
                        DECt, b_DECt = new([128, 8, 32], F32, "DECt")
                        with k.scope():
                            G5 = [32, 4, 4, 128]
                            Gc, b_Gc = new(G5, F32, "Gc")
                            k.dma(Gc[:], GSd[b].rearrange("k r (c t) -> c k r t", t=128), reads=[bGS[b]], writes=[b_Gc])
                            D4 = [32, 2, 4, 128]
                            spl, b_spl = new(D4, F32); BS, b_BS = new(D4, F32); ee, b_ee = new(D4, F32)
                            k.act(spl[:], Gc[:, 2:4, :, :], AF.Exp, reads=[b_Gc], writes=[b_spl], scale=-1.0)
                            k.act(spl[:], spl[:], AF.Ln, reads=[b_spl], writes=[b_spl], bias=1.0)
                            flat = lambda t_, d: t_[:, d, :, :].rearrange("c r t -> c (r t)")
                            for d in range(2):
                                k.S.op("dve", lambda e, d=d: e.tensor_tensor_scan(out=flat(BS, d), data0=rmask[0:32, :], data1=flat(spl, d),
                                                                                  initial=0.0, op0=ALU.mult, op1=ALU.add),
                                       reads=[b_spl, b_cs2], writes=[b_BS])
                            tot = BS[:, 1, :, 127:128].to_broadcast([32, 4, 128])
                            k.tt("dve", ee[:, 1, :, :], spl[:, 1, :, :], BS[:, 1, :, :], ALU.subtract, reads=[b_spl, b_BS], writes=[b_ee])
                            k.tt("dve", BS[:, 1, :, :], ee[:, 1, :, :], tot, ALU.add, reads=[b_ee, b_BS], writes=[b_BS])
                            k.tt("dve", ee[:], Gc[:, 0:2, :, :], BS[:], ALU.add, reads=[b_Gc, b_BS], writes=[b_ee])
                            emx, b_emx = new([32, 2, 4], F32)
                            k.S.op("dve", lambda e: e.tensor_reduce(out=emx[:], in_=ee[:], axis=AX.X, op=ALU.max), reads=[b_ee], writes=[b_emx])
                            nbe, b_nbe = new([32, 2, 4], F32)
                            k.ts("dve", nbe[:, 0, :], BS[:, 0, :, 127], -1.0, None, ALU.mult, reads=[b_BS], writes=[b_nbe])
                            k.ts("dve", nbe[:, 1, :], BS[:, 1, :, 0], -1.0, None, ALU.mult, reads=[b_BS], writes=[b_nbe])
                            pbk = pbanks[0]; bbk = bpb[0]
                            for d in range(2):
                                rhs_ = ident[0:32, 0:32] if d == 0 else J32[0:32, :]
                                k.mm(pbk[0:4, d * 64:d * 64 + 32], emx[:, d, :], rhs_, True, True, reads=[b_emx, b_cst, b_cs2], writes=[bbk])
                                k.mm(pbk[0:4, d * 64 + 32:d * 64 + 64], nbe[:, d, :], rhs_, True, True, reads=[b_nbe, b_cst, b_cs2], writes=[bbk])
                            rows, b_rows = new([4, 2, 2, 32], F32)
                            k.copy("dve", rows[:], pbk[0:4, 0:128].rearrange("r (d x j) -> r d x j", d=2, x=2), reads=[bbk], writes=[b_rows])
                            mrow, b_mrow = new([4, 2, 33], F32)
                            k.S.op("pool", lambda e: e.memset(mrow[:], 0.0), writes=[b_mrow])
                            Rrow, b_Rrow = new([4, 2, 32], F32); drow, b_drow = new([4, 2, 32], F32)
                            for d in range(2):
                                k.S.op("dve", lambda e, d=d: e.tensor_tensor_scan(out=mrow[:, d, 1:33], data0=rows[:, d, 0, :], data1=rows[:, d, 1, :],
                                                                                  initial=0.0, op0=ALU.max, op1=ALU.add),
                                       reads=[b_rows, b_mrow], writes=[b_mrow])
                            k.tt("dve", Rrow[:], mrow[:, :, 0:32], rows[:, :, 0, :], ALU.max, reads=[b_mrow, b_rows], writes=[b_Rrow])
                            k.tt("dve", drow[:], mrow[:, :, 0:32], Rrow[:], ALU.subtract, reads=[b_mrow, b_Rrow], writes=[b_drow])
                            k.act(drow[:], drow[:], AF.Exp, reads=[b_drow], writes=[b_drow])
                            pbd = pbanks[1]; bbd = bpb[1]
                            for d in range(2):
                                for r in range(4):
                                    o_ = (d * 4 + r) * 32
                                    k.mm(pbd[:, o_:o_ + 32], esel[0:4, r * 128:(r + 1) * 128], drow[:, d, :], True, True,
                                         reads=[b_cs2, b_drow], writes=[bbd])
                            k.copy("dve", DECt[:], pbd[:, 0:256].rearrange("p (x j) -> p x j", x=8), reads=[bbd], writes=[b_DECt])
                            pbr = pbanks[2]; bbr = bpb[2]
                            k.mm(pbr[0:32, 0:4], Rrow[:, 0, :], ident[0:4, 0:4], True, True, reads=[b_Rrow, b_cst], writes=[bbr])
                            k.mm(pbr[0:32, 4:8], Rrow[:, 1, :], ident[0:4, 0:4], True, True, reads=[b_Rrow, b_cst], writes=[bbr])
                            RbT, b_RbT = new([32, 8], F32)
                            k.copy("dve", RbT[:], pbr[0:32, 0:8], reads=[bbr], writes=[b_RbT])
                            k.mm(pbr[0:32, 8:12], J32[0:32, :], RbT[:, 4:8], True, True, reads=[b_RbT, b_cs2], writes=[bbr])
                            Rc, b_Rc = new([32, 2, 4], F32)
                            k.copy("dve", Rc[:, 0, :], RbT[:, 0:4], reads=[b_RbT], writes=[b_Rc])
                            k.copy("dve", Rc[:, 1, :], pbr[0:32, 8:12], reads=[bbr], writes=[b_Rc])
                            Rcb = Rc[:].unsqueeze(3).to_broadcast(D4)
                            k.tt("dve", ee[:], ee[:], Rcb, ALU.subtract, reads=[b_ee, b_Rc], writes=[b_ee])
                            k.act(ee[:], ee[:], AF.Exp, reads=[b_ee], writes=[b_ee])
                            k.tt("dve", BS[:], BS[:], Rcb, ALU.subtract, reads=[b_BS, b_Rc], writes=[b_BS])
                            k.act(BS[:], BS[:], AF.Exp, reads=[b_BS], writes=[b_BS])
                            pbw = pbanks[3]; bbw = bpb[3]
                            for x_, src_, bsrc in ((0, ee, b_ee), (1, BS, b_BS)):
                                for d in range(2):
                                    for r in range(4):
                                        o_ = (x_ * 8 + d * 4 + r) * 32
                                        k.mm(pbw[:, o_:o_ + 32], src_[:, d, r, :], ident[0:32, 0:32], True, True, reads=[bsrc, b_cst], writes=[bbw])
                            k.copy("dve", WCt[:], pbw[:, :].rearrange("p (x y c) -> p x y c", x=2, y=8), reads=[bbw], writes=[b_WCt])
                        with k.scope():
                            Cst = [[new([128, 129], F32, "Cst") for r in range(4)] for d in range(2)]
                            NR = 4
                            STb = [new([128, 128], BF16, "ST") for _ in range(NR)]
                            Vpb = [new([128, 129], BF16, "Vp") for _ in range(NR)]
                            Cdb = [new([128, 129], BF16, "Cd") for _ in range(NR)]
                            ddb = [new([128, 2], F32, "dd") for _ in range(NR)]
                            insts = [(st, d, r) for st in range(32) for d in range(2) for r in range(4)]

                            def geom(it):
                                st, d, r = insts[it]
                                c = st if d == 0 else 31 - st
                                return st, d, r, c, slice(c * 128, (c + 1) * 128), r // 2, 64 * (r % 2), d * 4 + r, it % NR

                            def emitA(it):
                                st, d, r, c, tok, kt, ro, x_, w = geom(it)
                                pS = pbanks[it % 2]; bS = bpb[it % 2]
                                (ST, bST), (Vp, bVp), (Cd, bCd) = STb[w], Vpb[w], Cdb[w]
                                (Cs, bCs) = Cst[d][r]
                                k.mm(pS[:, 0:128], kf[ro:ro + 64, kt, tok], qf[ro:ro + 64, kt, tok], True, True,
                                     reads=[b_kf, b_qf], writes=[bS])
                                k.tt("dve", ST[:], pS[:, 0:128], tri[d], ALU.mult, reads=[bS, b_cst], writes=[bST])
                                k.S.op("act", lambda e: e.activation(out=Vp[:], in_=vext[:, c, r, :], func=AF.Copy,
                                                                     scale=WCt[:, 0, x_, c:c + 1]),
                                       reads=[b_vext, b_WCt], writes=[bVp])
                                if st > 0:
                                    k.ts("dve", Cs[ro:ro + 64, :], Cs[ro:ro + 64, :], DECt[ro:ro + 64, x_, st:st + 1], None, ALU.mult,
                                         reads=[bCs, b_DECt], writes=[bCs])
                                    k.act(Cd[ro:ro + 64, :], Cs[ro:ro + 64, :], AF.Copy, reads=[bCs], writes=[bCd])

                            def emitB(it):
                                st, d, r, c, tok, kt, ro, x_, w = geom(it)
                                pN = pbanks[2 + it % 2]; bN = bpb[2 + it % 2]
                                pC = pbanks[4 + it % 2]; bC = bpb[4 + it % 2]
                                (ST, bST), (Vp, bVp), (Cd, bCd), (dd, bdd) = STb[w], Vpb[w], Cdb[w], ddb[w]
                                (Cs, bCs) = Cst[d][r]
                                k.mm(pN[:, 0:129], ST[:], Vp[:], True, st == 0, reads=[bST, bVp], writes=[bN])
                                if st > 0:
                                    k.mm(pN[:, 0:129], qf[ro:ro + 64, kt, tok], Cd[ro:ro + 64, :], False, True,
                                         reads=[b_qf, bCd], writes=[bN])
                                k.mm(pC[ro:ro + 64, 0:129], ktm[:, c, r * 64:(r + 1) * 64], Vp[:], True, True,
                                     reads=[b_ktm, bVp], writes=[bC])
                                if st > 0:
                                    k.tt("dve", Cs[ro:ro + 64, :], Cs[ro:ro + 64, :], pC[ro:ro + 64, 0:129], ALU.add,
                                         reads=[bCs, bC], writes=[bCs])
                                else:
                                    k.copy("dve", Cs[ro:ro + 64, :], pC[ro:ro + 64, 0:129], reads=[bC], writes=[bCs])
                                k.act(dd[:, 0:1], pN[:, 128:129], AF.Abs, reads=[bN], writes=[bdd])
                                k.ts("dve", dd[:, 0:1], dd[:, 0:1], WCt[:, 1, x_, c:c + 1], None, ALU.max,
                                     reads=[bdd, b_WCt], writes=[bdd])
                                k.S.op("dve", lambda e: e.reciprocal(out=dd[:, 1:2], in_=dd[:, 0:1]), reads=[bdd], writes=[bdd])
                                cell = big[:, :, 0:T].rearrange("p r (c v) -> p c r v", v=128)[:, c, r, :]
                                first = (d == 0 and c < 16) or (d == 1 and c >= 16)
                                if first:
                                    k.S.op("act", lambda e: e.activation(out=cell, in_=pN[:, 0:128], func=AF.Copy, scale=dd[:, 1:2]),
                                           reads=[bN, bdd], writes=[b_big])
                                else:
                                    k.stt(cell, pN[:, 0:128], dd[:, 1:2], cell, ALU.mult, ALU.add, reads=[bN, bdd, b_big], writes=[b_big])

                            for it in range(len(insts) + 1):
                                if it < len(insts):
                                    emitA(it)
                                if it >= 1:
                                    emitB(it - 1)
                        with k.scope():
                            wo, b_wo = load_w(w_in, 8, 1024, 1536)
                            xs1 = new([128, 8, 256], F32, "xs")
                            xsL = [xs1, xs1]
                            xmL = [new([128, 8, 256], BF16, "xm2") for _ in range(2)]
                            obL = [new([128, 512], F32, "ob") for _ in range(2)]
                            hgL = [new([128, 512], F32, "hg") for _ in range(2)]
                            hnL = [new([128, 512], BF16, "hnb") for _ in range(2)]
                            stL = [new([128, 4, 6], F32, "bst") for _ in range(2)]
                            mvL = [new([128, 4, 2], F32, "mv") for _ in range(2)]
                            mmL = [new([128, 4, 256], BF16, "mixm") for _ in range(2)]
                            (xs, b_xs) = xs1
                            dma_x1(0, 256, xs, b_xs)
                            mod_x1(xmL[0][0], 256, xs, b_xs, xmL[0][1])
                            for ti in range(16):
                                (xm2, b_xm2), (mm_, b_mm) = xmL[ti % 2], mmL[ti % 2]
                                if ti + 1 < 16:
                                    dma_x1((ti + 1) * 256, 256, xs, b_xs)
                                for cc in range(2):
                                    c = ti * 2 + cc
                                    (ob, b_ob), (hg, b_hg), (hnb, b_hnb), (stt_, b_stt), (mv, b_mv) = obL[cc], hgL[cc], hnL[cc], stL[cc], mvL[cc]
                                    bank = pbanks[cc]; bb = bpb[cc]
                                    for kk in range(8):
                                        k.mm(bank[:, :], xm2[:, kk, cc * 128:(cc + 1) * 128], wo[:, kk, :], kk == 0, kk == 7,
                                             reads=[b_wo, b_xm2], writes=[bb])
                                    k.tt("dve", ob[:], bank[:, :], rowbc[:, R_BO:R_BO + 512], ALU.add, reads=[bb, b_rowbc], writes=[b_ob])
                                    k.act(ob[:], ob[:], AF.Sigmoid, reads=[b_ob], writes=[b_ob])
                                    cellc = big[:, :, 0:T].rearrange("p r (c v) -> p c r v", v=128)[:, c, :, :]
                                    k.tt("dve", hg[:].rearrange("p (r v) -> p r v", r=4), ob[:].rearrange("p (r v) -> p r v", r=4), cellc,
                                         ALU.mult, reads=[b_ob, b_big], writes=[b_hg])
                                    for r in range(4):
                                        k.S.op("dve", lambda e, r=r, stt_=stt_, hg=hg: e.bn_stats(out=stt_[:, r, :], in_=hg[:, r * 128:(r + 1) * 128]),
                                               reads=[b_hg], writes=[b_stt])
                                        k.S.op("dve", lambda e, r=r, stt_=stt_, mv=mv: e.bn_aggr(out=mv[:, r, :], in_=stt_[:, r, :]),
                                               reads=[b_stt], writes=[b_mv])
                                    k.ts("dve", mv[:, :, 1], mv[:, :, 1], LN_EPS, None, ALU.add, reads=[b_mv], writes=[b_mv])
                                    k.act(mv[:, :, 1], mv[:, :, 1], AF.Sqrt, reads=[b_mv], writes=[b_mv])
                                    k.S.op("dve", lambda e, mv=mv: e.reciprocal(out=mv[:, :, 1], in_=mv[:, :, 1]), reads=[b_mv], writes=[b_mv])
                                    for r in range(4):
                                        k.ts("dve", hg[:, r * 128:(r + 1) * 128], hg[:, r * 128:(r + 1) * 128], mv[:, r, 0:1], mv[:, r, 1:2],
                                             ALU.subtract, ALU.mult, reads=[b_hg, b_mv], writes=[b_hg])
                                    k.tt("pool", hnb[:], hg[:], rowbc[:, R_MNG:R_MNG + 512], ALU.mult, reads=[b_hg, b_rowbc], writes=[b_hnb])
                                    bank2 = pbanks[2 + cc]; bb2 = bpb[2 + cc]
                                    for r in range(4):
                                        k.mm(bank2[:, r * 128:(r + 1) * 128], hnb[:, r * 128:(r + 1) * 128], ident_bf[:], True, True,
                                             reads=[b_hnb, b_idbf], writes=[bb2])
                                    k.act(mm_[:, :, cc * 128:(cc + 1) * 128], bank2[:, :].rearrange("p (r t) -> p r t", r=4), AF.Copy,
                                          reads=[bb2], writes=[b_mm])
                                k.dma(MIX[b, 0:512, ti * 256:(ti + 1) * 256].rearrange("(k p) t -> p k t", p=128), mm_[:],
                                      reads=[b_mm], writes=[bMIX[b]])
                                if ti + 1 < 16:
                                    mod_x1(xmL[(ti + 1) % 2][0], 256, xs, b_xs, xmL[(ti + 1) % 2][1])

                    with (k.scope() if phase == "MO" else contextlib.nullcontext()):
                      if phase == "MO" and "O" not in SKIP:
                        wob, b_wob = load_w(w_out, 8, 0, 1024)
                        xs = [new([128, 8, TT], F32, "xs") for _ in range(2)]
                        mx = [new([128, 8, TT], BF16, "mx") for _ in range(2)]
                        zL = [new([128, 8, TT], F32, "z") for _ in range(2)]
                        zbL = [k.sb([128, 8, TT], BF16, "zb") for _ in range(2)]
                        zqL = [k.sb([128, 8, TT], BF16, "zq") for _ in range(2)]
                        bzbqL = [k.buf() for _ in range(2)]
                        smL = [new([128, 6, TT], F32, "small") for _ in range(2)]
                        tmpL = [[k.sb([128, TT], F32, "tmp") for _ in range(2)] for _ in range(2)]
                        btmpL = [[k.buf() for _ in range(2)] for _ in range(2)]
                        def loadO(ti):
                            (xs_, bxs_), (mx_, bmx_) = xs[ti % 2], mx[ti % 2]
                            tsl = slice(ti * TT, (ti + 1) * TT)
                            k.dma(xs_[:], x1v[:, :, tsl], reads=[bX1[b]], writes=[bxs_])
                            k.dma(mx_[:], MIX[b].rearrange("(k p) t -> p k t", p=128)[:, :, tsl], reads=[bMIX[b]], writes=[bmx_])

                        loadO(0)
                        for ti in range(NT):
                            s = ti % 2
                            (xs_, bxs_), (mx_, bmx_) = xs[s], mx[s]
                            tsl = slice(ti * TT, (ti + 1) * TT)
                            if ti + 1 < NT:
                                loadO(ti + 1)
                            (z, b_z), zb, zq, b_zbq = zL[s], zbL[s], zqL[s], bzbqL[s]
                            (small, b_small), tmp, b_tmp = smL[s], tmpL[s], btmpL[s]
                            for dt in range(8):
                                bi = (dt // 2) % 2
                                half = (dt % 2) * 256
                                for kk in range(8):
                                    k.mm(pbanks[bi][:, half:half + TT], wob[:, kk, dt * 128:(dt + 1) * 128], mx_[:, kk, :], kk == 0, kk == 7,
                                         reads=[b_wob, bmx_], writes=[bpb[bi]])
                                k.stt(z[:, dt, :], pbanks[bi][:, half:half + TT], der[:, 5, dt, b:b + 1], xs_[:, dt, :], ALU.mult, ALU.add,
                                      reads=[bpb[bi], bxs_, b_der], writes=[b_z])
                            ln_tail(z, b_z, zb, zq, b_zbq, small, b_small, tmp, b_tmp, xs_, bxs_, 1, TT)
                            k.dma(X2[b].rearrange("(k p) t -> p k t", p=128)[:, :, tsl], xs_[:], reads=[bxs_], writes=[bX2[b]])

        if STAGE >= 2:
            k.stacks.pop()
            mixc.close()
            k.fence_ops = k.S.fence()
        if STAGE == 3:
            with k.scope():
                cvb, b_cvb = k.sb([128, 8, 256], BF16, "cvb"), k.buf()
                cvf, b_cvf = k.sb([128, 8, 256], F32, "cvf"), k.buf()
                for b in range(NB):
                    for ti in range(16):
                        tsl = slice(ti * 256, (ti + 1) * 256)
                        k.dma(cvb[:], MIX[b].rearrange("(k p) t -> p k t", p=128)[:, :, tsl], reads=[bMIX[b]], writes=[b_cvb])
                        k.copy("dve", cvf[:], cvb[:], reads=[b_cvb], writes=[b_cvf])
                        final_ops.append(k.dma(outT[b].rearrange("(k p) t -> p k t", p=128)[:, :, tsl], cvf[:], reads=[b_cvf], writes=[b_out[b]]))
        if STAGE >= 9:
            ffn_pass(X2, bX2, outT, b_out, f2u, f2d, 2, 2, True)

        k.S.emit(final_wait_ops=final_ops)
    return nc


def _pack_inputs(inp):
    f = lambda a: np.ascontiguousarray(np.asarray(a, dtype=np.float32))
    col = np.zeros((128, NCOLP), np.float32)

    def put8(off, v):
        col[:, off:off + 8] = f(v).reshape(8, 128).T

    for i, nm in enumerate(["ln1_g", "ln1_b", "ln2_g", "ln2_b", "ln3_g", "ln3_b"]):
        put8(C_LN + 8 * i, inp[nm][0])
    col[:, C_BADA:C_BADA + 72] = f(inp["b_ada"][0]).reshape(72, 128).T
    b_in = f(inp["b_in"][0])
    col[:, C_BQK:C_BQK + 4] = b_in[0:512].reshape(4, 128).T
    cw = f(inp["qk_conv_w"][0])
    for j in range(5):
        col[:, C_CW + 4 * j:C_CW + 4 * j + 4] = cw[j].reshape(4, 128).T
    col[:, C_CB:C_CB + 4] = f(inp["qk_conv_b"][0]).reshape(4, 128).T
    col[:, C_GLUB:C_GLUB + 4] = f(inp["ssm_glu_b"][0]).reshape(4, 128).T
    col[:, C_SNG:C_SNG + 4] = f(inp["ssm_norm_g"][0]).reshape(4, 128).T
    g0 = 1536
    for i in range(4):
        col[0:4, C_BG + i] = b_in[g0 + 4 * i:g0 + 4 * i + 4]
    col[0:16, C_BG + 4] = b_in[g0:g0 + 16]
    row = np.zeros((1, NROWP), np.float32)
    row[0, R_BV:R_BV + 512] = b_in[512:1024]
    row[0, R_BO:R_BO + 512] = b_in[1024:1536]
    row[0, R_BU:R_BU + 512] = b_in[1552:2064]
    row[0, R_MNG:R_MNG + 512] = f(inp["mlstm_norm_g"][0])
    cst = np.zeros((128, 1024), np.float32)
    cst[:, 0:128] = np.eye(128, dtype=np.float32)
    ii = np.arange(128)
    cst[:, 128:256] = (ii[:, None] <= ii[None, :]).astype(np.float32)
    cst[:, 256:384] = (ii[:, None] >= ii[None, :]).astype(np.float32)
    cst[:, 384:896] = np.arange(512, dtype=np.float32)[None, :]
    a8 = np.arange(8, dtype=np.float32)
    for half, rows in ((0, slice(0, 64)), (1, slice(64, 128))):
        cst[rows, 896:904] = (7 - a8) if half == 0 else a8
        cst[rows, 904:912] = (-a8) if half == 0 else a8
        cst[rows, 912:920] = a8 if half == 0 else (-a8)
        cst[rows, 920:928] = (a8 + 1) if half == 0 else (8 - a8)
    cs2 = np.zeros((128, 1536), np.float32)
    cs2[:, 0:512] = (np.arange(512) % 128 != 0).astype(np.float32)[None, :]
    cs2[:, 512:640] = ((ii[:, None] // 16) <= (ii[None, :] // 16)).astype(np.float32)
    cs2[:, 640:768] = ((ii[:, None] // 16) >= (ii[None, :] // 16)).astype(np.float32)
    cs2[0:32, 768:800] = np.eye(32, dtype=np.float32)[::-1]
    cs2[:, 1344:1472] = np.eye(128, dtype=np.float32)[::-1]
    for r in range(4):
        cs2[r, 800 + r * 128:800 + (r + 1) * 128] = 1.0
    dd_ = f(inp["ssm_d"][0]).reshape(32, 16)
    cs2[:, 1312:1344] = np.tile(dd_.T, (8, 1))
    s5 = np.zeros((2, 64, 32, 70), np.float32)
    s5[..., 0] = f(inp["ssm_lam_re"][0]).transpose(0, 2, 1)
    s5[..., 1] = f(inp["ssm_lam_im"][0]).transpose(0, 2, 1)
    s5[..., 2] = f(inp["ssm_log_step"][0])[:, None, :]
    s5[..., 3:19] = f(inp["ssm_b_re"][0]).transpose(1, 0, 2)[None]
    s5[..., 19:35] = f(inp["ssm_b_im"][0]).transpose(1, 0, 2)[None]
    s5[..., 35:51] = f(inp["ssm_c_re"][0]).transpose(0, 3, 1, 2)
    s5[..., 51:67] = f(inp["ssm_c_im"][0]).transpose(0, 3, 1, 2)
    s5p = np.ascontiguousarray(s5.reshape(128, 32 * 70))
    shared = {
        "w_ada": f(inp["w_ada"][0]), "colpack": col, "rowpack": row, "consts": cst,
        "ffn1_w_up": f(inp["ffn1_w_up"][0]), "ffn1_w_down": f(inp["ffn1_w_down"][0]),
        "ffn2_w_up": f(inp["ffn2_w_up"][0]), "ffn2_w_down": f(inp["ffn2_w_down"][0]),
        "w_in": f(inp["w_in"][0]), "w_out": f(inp["w_out"][0]), "glu_w": f(inp["ssm_glu_w"][0]),
        "s5p": s5p, "consts2": cs2,
    }
    x = np.asarray(inp["x"], dtype=np.float32)
    c = np.asarray(inp["c"], dtype=np.float32)
    maps = []
    for core in range(8):
        m = dict(shared)
        m["xT"] = np.ascontiguousarray(x[2 * core:2 * core + 2].transpose(0, 2, 1))
        cc = c[2 * core:2 * core + 2]
        m["cT"] = np.ascontiguousarray(cc.reshape(2, 8, 128).transpose(2, 1, 0).reshape(128, 16))
        maps.append(m)
    return maps


_NC_CACHE = {}


def kernel(**inputs):
    maps = _pack_inputs(inputs)
    if "nc" not in _NC_CACHE:
        _NC_CACHE["nc"] = build_program()
    nc = _NC_CACHE["nc"]
    res = run_bass_kernel_spmd(nc, maps, core_ids=list(range(8)))
    outs = [r["outT"] for r in res.results]
    full = np.concatenate([o.transpose(0, 2, 1) for o in outs], axis=0)
    return np.ascontiguousarray(full.astype(np.float32))
```

```python
import os
import math
import contextlib
import numpy as np
import concourse.bass as bass
import concourse.mybir as mybir
from concourse.bass_utils import run_bass_kernel_spmd

F32 = mybir.dt.float32
BF16 = mybir.dt.bfloat16
I32 = mybir.dt.int32
AF = mybir.ActivationFunctionType
ALU = mybir.AluOpType
AX = mybir.AxisListType

D = 1024
DF = 2816
T = 4096
NB = 2
TT = 256
NT = T // TT
ALPHA = 2.0 ** 0.25
LN_EPS = 1e-5
EPS_P = LN_EPS / (ALPHA * ALPHA)
IN_COLS = 2064
STAGE = int(os.environ.get("MK_STAGE", "9"))
SKIP = os.environ.get("MK_SKIP", "").split(",")

C_LN = 0
C_BADA = 48
C_BQK = 120
C_CW = 124
C_CB = 144
C_GLUB = 148
C_SNG = 152
C_BG = 156
NCOLP = 168
R_BV, R_BO, R_BU, R_MNG = 0, 512, 1024, 1536
NROWP = 2048


class Buf:
    __slots__ = ("name", "w", "r")

    def __init__(self, name="", init_readers=()):
        self.name = name
        self.w = None
        self.r = list(init_readers)


class Sched:
    ENGS = ("pe", "act", "dve", "pool", "sp")
    DMA_RING = 8
    SEM_MAX = 30000

    def __init__(self, nc):
        self.nc = nc
        self.ops = []
        self.by_eng = {e: [] for e in self.ENGS}

    def op(self, eng, fn, reads=(), writes=(), dma=False):
        oid = len(self.ops)
        deps = set()
        for b in reads:
            if b.w is not None:
                deps.add(b.w)
        for b in writes:
            if b.w is not None:
                deps.add(b.w)
            for r in b.r:
                deps.add(r)
        if eng == "pe":
            deps = {d for d in deps if self.ops[d][0] != "pe"}
        self.ops.append((eng, fn, deps, dma))
        self.by_eng[eng].append(oid)
        for b in reads:
            b.r.append(oid)
        for b in writes:
            b.w = oid
            b.r = []
        return oid

    def fence(self):
        f = []
        for e in self.ENGS:
            lst = self.by_eng[e]
            if not lst:
                continue
            if e == "sp":
                f.extend(lst[-self.DMA_RING:])
            else:
                f.append(lst[-1])
        return f

    def emit(self, final_wait_ops=()):
        nc = self.nc
        ops = self.ops
        needed = [False] * len(ops)
        for (_, _, deps, _) in ops:
            for d in deps:
                needed[d] = True
        for d in final_wait_ops:
            needed[d] = True
        sig = [None] * len(ops)
        cnt = {e: 0 for e in self.ENGS}
        dma_idx = {e: 0 for e in self.ENGS}
        for e in self.ENGS:
            for oid in self.by_eng[e]:
                isdma = ops[oid][3]
                if isdma:
                    n = dma_idx[e]
                    dma_idx[e] += 1
                    sig[oid] = (("d", e, n % self.DMA_RING), 16 * (n // self.DMA_RING + 1))
                elif needed[oid]:
                    c = cnt[e]
                    cnt[e] += 1
                    sig[oid] = (("c", e, c // self.SEM_MAX), c % self.SEM_MAX + 1)
        keys = []
        kset = set()
        for s in sig:
            if s is not None and s[0] not in kset:
                kset.add(s[0])
                keys.append(s[0])
        with contextlib.ExitStack() as st:
            sems = {}
            for k in keys:
                sems[k] = st.enter_context(nc.semaphore("s_%s_%s_%d" % k))
            block = st.enter_context(nc.Block())

            def run_engine(ename, eh):
                seen = {}
                for oid in self.by_eng[ename]:
                    _, fn, deps, isdma = ops[oid]
                    waits = {}
                    for d in deps:
                        k, v = sig[d]
                        if seen.get(k, 0) >= v:
                            continue
                        if waits.get(k, 0) < v:
                            waits[k] = v
                    if isdma:
                        k, v = sig[oid]
                        if v > 16 and seen.get(k, 0) < v - 16 and waits.get(k, 0) < v - 16:
                            waits[k] = v - 16
                    for k, v in waits.items():
                        eh.wait_ge(sems[k], v)
                        seen[k] = v
                    ins = fn(eh)
                    if sig[oid] is not None:
                        k, v = sig[oid]
                        ins.then_inc(sems[k], 16 if isdma else 1)
                if ename == "sp":
                    fw = {}
                    for d in final_wait_ops:
                        k, v = sig[d]
                        if fw.get(k, 0) < v:
                            fw[k] = v
                    for k, v in fw.items():
                        if seen.get(k, 0) < v:
                            eh.wait_ge(sems[k], v)

            @block.tensor
            def _(e):
                run_engine("pe", e)

            @block.scalar
            def _(e):
                run_engine("act", e)

            @block.vector
            def _(e):
                run_engine("dve", e)

            @block.gpsimd
            def _(e):
                run_engine("pool", e)

            @block.sync
            def _(e):
                run_engine("sp", e)


class KB:
    def __init__(self, nc):
        self.nc = nc
        self.S = Sched(nc)
        self.stacks = []
        self.fence_ops = []
        self.uid = 0

    @contextlib.contextmanager
    def scope(self):
        st = contextlib.ExitStack()
        self.stacks.append(st)
        try:
            with st:
                yield
        finally:
            self.stacks.pop()
            self.fence_ops = self.S.fence()

    def sb(self, shape, dt, name=None):
        self.uid += 1
        nm = "%s_%d" % (name or "t", self.uid)
        return self.stacks[-1].enter_context(self.nc.sbuf_tensor(nm, list(shape), dt))

    def buf(self, name=""):
        return Buf(name, self.fence_ops)

    def dma(self, out, in_, reads=(), writes=(), **kw):
        return self.S.op("sp", lambda e: e.dma_start(out=out, in_=in_, **kw), reads, writes, dma=True)

    def mm(self, out, lhsT, rhs, start, stop, reads=(), writes=()):
        return self.S.op("pe", lambda e: e.matmul(out, lhsT=lhsT, rhs=rhs, start=start, stop=stop), reads, writes)

    def act(self, out, in_, func, reads=(), writes=(), scale=1.0, bias=0.0):
        return self.S.op("act", lambda e: e.activation(out=out, in_=in_, func=func, scale=scale, bias=bias), reads, writes)

    def tt(self, eng, out, in0, in1, op, reads=(), writes=()):
        return self.S.op(eng, lambda e: e.tensor_tensor(out=out, in0=in0, in1=in1, op=op), reads, writes)

    def ts(self, eng, out, in0, s1, s2, op0, op1=None, reads=(), writes=()):
        if op1 is None:
            return self.S.op(eng, lambda e: e.tensor_scalar(out=out, in0=in0, scalar1=s1, scalar2=None, op0=op0), reads, writes)
        return self.S.op(eng, lambda e: e.tensor_scalar(out=out, in0=in0, scalar1=s1, scalar2=s2, op0=op0, op1=op1), reads, writes)

    def stt(self, out, in0, scalar, in1, op0, op1, reads=(), writes=()):
        return self.S.op("dve", lambda e: e.scalar_tensor_tensor(out=out, in0=in0, scalar=scalar, in1=in1, op0=op0, op1=op1), reads, writes)

    def copy(self, eng, out, in_, reads=(), writes=()):
        if eng == "act":
            return self.act(out, in_, AF.Copy, reads, writes)
        return self.S.op(eng, lambda e: e.tensor_copy(out=out, in_=in_), reads, writes)


def build_program():
    nc = bass.Bass("TRN2", target_bir_lowering=False)
    k = KB(nc)

    def din(name, shape, dt=F32):
        return nc.dram_tensor(name, list(shape), dt, kind="ExternalInput").ap()

    xT = din("xT", [NB, D, T])
    cT = din("cT", [128, 16])
    w_ada = din("w_ada", [D, 9 * D])
    colpack = din("colpack", [128, NCOLP])
    rowpack = din("rowpack", [1, NROWP])
    consts = din("consts", [128, 1024])
    f1u = din("ffn1_w_up", [D, 2 * DF])
    f1d = din("ffn1_w_down", [DF, D])
    f2u = din("ffn2_w_up", [D, 2 * DF])
    f2d = din("ffn2_w_down", [DF, D])
    w_in = din("w_in", [D, IN_COLS])
    w_out = din("w_out", [D, D])
    glu_w = din("glu_w", [512, 512])
    s5p = din("s5p", [128, 32 * 70])
    consts2 = din("consts2", [128, 1536])
    outT = nc.dram_tensor("outT", [NB, D, T], F32, kind="ExternalOutput").ap()
    X1 = nc.dram_tensor("X1s", [NB, D, T], F32, kind="Internal").ap()
    X2 = nc.dram_tensor("X2s", [NB, D, T], F32, kind="Internal").ap()
    MIX = nc.dram_tensor("MIXs", [NB, D, T], BF16, kind="Internal").ap()
    bX1 = [Buf("X1_%d" % b) for b in range(NB)]
    bX2 = [Buf("X2_%d" % b) for b in range(NB)]
    bMIX = [Buf("MIX_%d" % b) for b in range(NB)]

    final_ops = []
    with k.scope():
        cp = k.sb([128, NCOLP], F32, "colpack"); b_cp = k.buf()
        k.dma(cp[:], colpack[:, :], writes=[b_cp])
        cst = k.sb([128, 1024], F32, "consts"); b_cst = k.buf()
        k.dma(cst[:], consts[:, :], writes=[b_cst])
        ones_bf = k.sb([128, 128], BF16, "ones"); b_ones = k.buf()
        k.S.op("pool", lambda e: e.memset(ones_bf[:], 1.0), writes=[b_ones])
        der = k.sb([128, 9, 8, NB], F32, "der"); b_der = k.buf()
        pbanks = []
        for i in range(8):
            pbanks.append(k.stacks[-1].enter_context(nc.psum_tensor("pb%d" % i, [128, 512], F32)))
        bpb = [k.buf("pb%d" % i) for i in range(8)]

        with k.scope():
            ct = k.sb([128, 16], F32, "ct"); b_ct = k.buf()
            k.dma(ct[:], cT[:, :], writes=[b_ct])
            cact = k.sb([128, 16], F32, "cact"); b_cact = k.buf()
            k.act(cact[:], ct[:], AF.Silu, reads=[b_ct], writes=[b_cact])
            wst = [k.sb([128, 8, 512], F32, "wada") for _ in range(2)]
            b_wst = [k.buf() for _ in range(2)]
            wv = w_ada.rearrange("(k p) c -> p k c", p=128)
            for ch in range(18):
                s = ch % 2
                k.dma(wst[s][:], wv[:, :, ch * 512:(ch + 1) * 512], writes=[b_wst[s]])
                for jj in range(4):
                    j = ch * 4 + jj
                    for kk in range(8):
                        k.mm(pbanks[0][:, 2 * j:2 * j + 2], wst[s][:, kk, jj * 128:(jj + 1) * 128],
                             cact[:, 2 * kk:2 * kk + 2], kk == 0, kk == 7,
                             reads=[b_wst[s], b_cact], writes=[bpb[0]])
            modc = k.sb([128, 72, NB], F32, "modc"); b_modc = k.buf()
            k.tt("dve", modc[:], pbanks[0][:, 0:144].rearrange("p (j b) -> p j b", b=NB),
                 cp[:, C_BADA:C_BADA + 72].unsqueeze(2).to_broadcast([128, 72, NB]), ALU.add,
                 reads=[bpb[0], b_cp], writes=[b_modc])
            for sub in range(3):
                base = sub * 24
                k.copy("dve", der[:, 3 * sub + 1, :, :], modc[:, base:base + 8, :], reads=[b_modc], writes=[b_der])
                k.ts("dve", der[:, 3 * sub, :, :], modc[:, base + 8:base + 16, :], 1.0, None, ALU.add,
                     reads=[b_modc], writes=[b_der])
                gsc = (1.0 if sub == 1 else 0.5) / ALPHA
                k.ts("dve", der[:, 3 * sub + 2, :, :], modc[:, base + 16:base + 24, :], 1.0, gsc, ALU.add, ALU.mult,
                     reads=[b_modc], writes=[b_der])

        def ln_tail(z, b_z, zb, zq, b_zbq, small, b_small, tmp, b_tmp, xo, b_xo, lnidx, ntok):
            ln_tail_a(z, b_z, zb, zq, b_zbq)
            ln_tail_b(z, b_z, zb, zq, b_zbq, small, b_small, tmp, b_tmp, xo, b_xo, lnidx, ntok)

        def ln_tail_a(z, b_z, zb, zq, b_zbq):
            k.act(zb[:], z[:], AF.Copy, reads=[b_z], writes=[b_zbq])
            k.act(zq[:], z[:], AF.Square, reads=[b_z], writes=[b_zbq])

        def ln_tail_b(z, b_z, zb, zq, b_zbq, small, b_small, tmp, b_tmp, xo, b_xo, lnidx, ntok):
            pbs = pbanks[6]
            for dt in range(8):
                k.mm(pbs[:, 0:ntok], ones_bf[:], zb[:, dt, :], dt == 0, dt == 7, reads=[b_zbq, b_ones], writes=[bpb[6]])
            for dt in range(8):
                k.mm(pbs[:, 256:256 + ntok], ones_bf[:], zq[:, dt, :], dt == 0, dt == 7, reads=[b_zbq], writes=[bpb[6]])
            mt, m2, vv, sd, rstd, nmr = [small[:, i, :] for i in range(6)]
            k.ts("dve", mt, pbs[:, 0:ntok], 1.0 / D, None, ALU.mult, reads=[bpb[6]], writes=[b_small])
            k.tt("dve", m2, mt, mt, ALU.mult, reads=[b_small], writes=[b_small])
            k.stt(vv, pbs[:, 256:256 + ntok], 1.0 / D, m2, ALU.mult, ALU.subtract, reads=[bpb[6], b_small], writes=[b_small])
            k.ts("dve", vv, vv, EPS_P, None, ALU.add, reads=[b_small], writes=[b_small])
            k.act(sd, vv, AF.Sqrt, reads=[b_small], writes=[b_small])
            k.S.op("dve", lambda e: e.reciprocal(out=rstd, in_=sd), reads=[b_small], writes=[b_small])
            k.stt(nmr, mt, -1.0, rstd, ALU.mult, ALU.mult, reads=[b_small], writes=[b_small])
            for dt in range(8):
                eng = "dve" if dt % 2 == 0 else "pool"
                tp = tmp[dt % 2]
                k.tt(eng, tp[:], z[:, dt, :], rstd, ALU.mult, reads=[b_z, b_small], writes=[b_tmp[dt % 2]])
                k.tt(eng, tp[:], tp[:], nmr, ALU.add, reads=[b_small, b_tmp[dt % 2]], writes=[b_tmp[dt % 2]])
                k.ts(eng, xo[:, dt, :], tp[:], cp[:, C_LN + 16 * lnidx + dt:C_LN + 16 * lnidx + dt + 1],
                     cp[:, C_LN + 16 * lnidx + 8 + dt:C_LN + 16 * lnidx + 9 + dt], ALU.mult, ALU.add,
                     reads=[b_tmp[dt % 2], b_cp], writes=[b_xo])

        def ffn_pass(src, b_src, dst, b_dst, wu_d, wd_d, sub, lnidx, is_final):
            with k.scope():
                wup = k.sb([128, 8, 2 * DF], BF16, "wup")
                wdn = k.sb([128, 22, D], BF16, "wdn")
                b_wup = [k.buf() for _ in range(22)]
                b_wdn = [k.buf() for _ in range(11)]
                xb = [k.sb([128, 8, TT], F32, "xb") for _ in range(2)]
                b_xb = [k.buf() for _ in range(2)]
                xm = k.sb([128, 8, TT], BF16, "xm"); b_xm = k.buf()
                sgt = [k.sb([128, TT], F32, "sgt") for _ in range(2)]
                b_sgt = [k.buf() for _ in range(2)]
                hh = k.sb([128, 22, TT], BF16, "hh"); b_hh = k.buf()
                z = k.sb([128, 8, TT], F32, "z"); b_z = k.buf()
                zb = k.sb([128, 8, TT], BF16, "zb")
                zq = k.sb([128, 8, TT], BF16, "zq"); b_zbq = k.buf()
                small = k.sb([128, 6, TT], F32, "small"); b_small = k.buf()
                tmp = [k.sb([128, TT], F32, "tmp") for _ in range(2)]
                b_tmp = [k.buf() for _ in range(2)]
                wuv = wu_d.rearrange("(k p) c -> p k c", p=128)
                wdv = wd_d.rearrange("(f p) c -> p f c", p=128)
                engs = ["dve", "act", "pool"]
                n = 0
                for ch in range(22):
                    s = n % 2
                    k.dma(xb[s][:], wuv[:, :, ch * 256:(ch + 1) * 256], writes=[b_xb[s]])
                    k.copy(engs[n % 3], wup[:, :, ch * 256:(ch + 1) * 256], xb[s][:], reads=[b_xb[s]], writes=[b_wup[ch]])
                    n += 1
                for ch in range(11):
                    s = n % 2
                    sv = xb[s][:].rearrange("p a t -> p (a t)").rearrange("p (f c) -> p f c", f=2)
                    k.dma(sv, wdv[:, 2 * ch:2 * ch + 2, :], writes=[b_xb[s]])
                    k.copy(engs[n % 3], wdn[:, 2 * ch:2 * ch + 2, :], sv, reads=[b_xb[s]], writes=[b_wdn[ch]])
                    n += 1
                A = lambda dt, b: der[:, 3 * sub, dt, b:b + 1]
                Bm = lambda dt, b: der[:, 3 * sub + 1, dt, b:b + 1]
                GA = lambda dt, b: der[:, 3 * sub + 2, dt, b:b + 1]
                tiles = [(b, ti) for b in range(NB) for ti in range(NT)]

                def load_x(i):
                    b, ti = tiles[i]
                    s = i % 2
                    k.dma(xb[s][:], src[b].rearrange("(k p) t -> p k t", p=128)[:, :, ti * TT:(ti + 1) * TT],
                          reads=[b_src[b]], writes=[b_xb[s]])

                def make_xm(i):
                    b, ti = tiles[i]
                    s = i % 2
                    for dt in range(8):
                        eng = ("dve", "pool", "act", "pool")[dt % 4]
                        if eng == "act":
                            k.S.op("act", lambda e, dt=dt: e.activation(out=xm[:, dt, :], in_=xb[s][:, dt, :], func=AF.Identity,
                                                                        scale=A(dt, b), bias=Bm(dt, b)),
                                   reads=[b_xb[s], b_der], writes=[b_xm])
                        else:
                            k.ts(eng, xm[:, dt, :], xb[s][:, dt, :], A(dt, b), Bm(dt, b), ALU.mult, ALU.add,
                                 reads=[b_xb[s], b_der], writes=[b_xm])

                load_x(0)
                make_xm(0)

                def finish(i):
                    b, ti = tiles[i]
                    s = i % 2
                    ln_tail_b(z, b_z, zb, zq, b_zbq, small, b_small, tmp, b_tmp, xb[s], b_xb[s], lnidx, TT)
                    o = k.dma(dst[b].rearrange("(k p) t -> p k t", p=128)[:, :, ti * TT:(ti + 1) * TT], xb[s][:],
                              reads=[b_xb[s]], writes=[b_dst[b]])
                    if is_final:
                        final_ops.append(o)

                for i, (b, ti) in enumerate(tiles):
                    s = i % 2
                    for j in range(22):
                        if j == 2:
                            if i > 0:
                                finish(i - 1)
                            if i + 1 < len(tiles):
                                load_x(i + 1)
                        bank = pbanks[j % 2]; bb = bpb[j % 2]
                        for kk in range(8):
                            k.mm(bank[:, 0:TT], wup[:, kk, j * 128:(j + 1) * 128], xm[:, kk, :], kk == 0, kk == 7,
                                 reads=[b_wup[j // 2], b_xm], writes=[bb])
                        for kk in range(8):
                            c0 = DF + j * 128
                            k.mm(bank[:, 256:256 + TT], wup[:, kk, c0:c0 + 128], xm[:, kk, :], kk == 0, kk == 7,
                                 reads=[b_wup[c0 // 256], b_xm], writes=[bb])
                        k.act(sgt[j % 2][:], bank[:, 0:TT], AF.Silu, reads=[bb], writes=[b_sgt[j % 2]])
                        k.tt("dve", hh[:, j, :], sgt[j % 2][:], bank[:, 256:256 + TT], ALU.mult,
                             reads=[bb, b_sgt[j % 2]], writes=[b_hh])
                    if i + 1 < len(tiles):
                        make_xm(i + 1)
                    for dt in range(8):
                        bi = 2 + (dt // 2) % 2
                        half = (dt % 2) * 256
                        for j in range(22):
                            k.mm(pbanks[bi][:, half:half + TT], wdn[:, j, dt * 128:(dt + 1) * 128], hh[:, j, :],
                                 j == 0, j == 21, reads=[b_wdn[j // 2], b_hh], writes=[bpb[bi]])
                        k.stt(z[:, dt, :], pbanks[bi][:, half:half + TT], GA(dt, b), xb[s][:, dt, :], ALU.mult, ALU.add,
                              reads=[bpb[bi], b_xb[s], b_der], writes=[b_z])
                    ln_tail_a(z, b_z, zb, zq, b_zbq)
                finish(len(tiles) - 1)

        b_out = [Buf("out%d" % b) for b in range(NB)]
        if STAGE == 1:
            ffn_pass(xT, [Buf(), Buf()], outT, b_out, f1u, f1d, 0, 0, True)
        if STAGE >= 2 and STAGE != 3:
            ffn_pass(xT, [Buf(), Buf()], X1, bX1, f1u, f1d, 0, 0, False)
        if STAGE == 3:
            bX1 = [Buf(), Buf()]
            X1 = xT
        if STAGE >= 2:
            TWO_PI = 2.0 * math.pi
            mixc = contextlib.ExitStack()
            k.stacks.append(mixc)
            ident = cst[:, 0:128]
            tri = [cst[:, 128:256], cst[:, 256:384]]
            iota = cst[:, 384:896]
            KTc = cst[:, 896:928]
            cs2 = k.sb([128, 1536], F32, "consts2"); b_cs2 = k.buf()
            k.dma(cs2[:], consts2[:, :], writes=[b_cs2])
            rmask = cs2[:, 0:512]
            mle2 = cs2[:, 512:640]
            mge2 = cs2[:, 640:768]
            J32 = cs2[:, 768:800]
            esel = cs2[:, 800:1312]
            dpk = cs2[:, 1312:1344]
            ident_bf = k.sb([128, 128], BF16, "identbf"); b_idbf = k.buf()
            k.copy("dve", ident_bf[:], ident, reads=[b_cst], writes=[b_idbf])
            J_bf = k.sb([128, 128], BF16, "Jbf")
            k.copy("dve", J_bf[:], cs2[:, 1344:1472], reads=[b_cs2], writes=[b_idbf])
            rowbc = k.sb([128, NROWP], F32, "rowbc"); b_rowbc = k.buf()
            k.dma(rowbc[:], rowpack[0:1, :].partition_broadcast(128), writes=[b_rowbc])
            GSd = nc.dram_tensor("GSs", [NB, 4, 4, T], F32, kind="Internal").ap()
            bGS = [Buf() for _ in range(NB)]

            def new(shape, dt, name="w"):
                return k.sb(shape, dt, name), k.buf()

            def load_w(src_ap, rows_k, c0, c1, stage_cols=128):
                wt, b_wt = new([128, rows_k, c1 - c0], BF16, "wbf")
                v = src_ap.rearrange("(k p) c -> p k c", p=128)
                with k.scope():
                    stg = [new([128, rows_k, stage_cols], F32, "wstg") for _ in range(2)]
                    i = 0
                    c = c0
                    while c < c1:
                        w_ = min(stage_cols, c1 - c)
                        st_, bs_ = stg[i % 2]
                        k.dma(st_[:, :, 0:w_], v[:, :, c:c + w_], writes=[bs_])
                        k.copy(("dve", "pool")[i % 2], wt[:, :, c - c0:c - c0 + w_], st_[:, :, 0:w_], reads=[bs_], writes=[b_wt])
                        c += w_
                        i += 1
                return wt, b_wt

            def reduce_angle(x, out, shape, bx, bo):
                tf, btf = new(shape, F32, "raf")
                ti, bti = new(shape, I32, "rai")
                k.ts("dve", tf[:], x, 1.0 / TWO_PI, None, ALU.mult, reads=[bx], writes=[btf])
                k.copy("dve", ti[:], tf[:], reads=[btf], writes=[bti])
                k.copy("dve", tf[:], ti[:], reads=[bti], writes=[btf])
                k.stt(out, tf[:], -TWO_PI, x, ALU.mult, ALU.add, reads=[btf, bx], writes=[bo])

            def sincos(x, sn, cs_, shape, bx, bsn, bcs):
                s2, bs2 = new(shape, F32, "s2")
                s4, bs4 = new(shape, F32, "s4")
                k.act(s2[:], x, AF.Sin, reads=[bx], writes=[bs2], scale=0.5)
                k.act(s4[:], x, AF.Sin, reads=[bx], writes=[bs4], scale=0.25)
                k.tt("dve", cs_, s2[:], s2[:], ALU.mult, reads=[bs2], writes=[bcs])
                k.ts("dve", cs_, cs_, -2.0, 1.0, ALU.mult, ALU.add, reads=[bcs], writes=[bcs])
                k.tt("dve", s4[:], s4[:], s4[:], ALU.mult, reads=[bs4], writes=[bs4])
                k.ts("dve", s4[:], s4[:], -2.0, 1.0, ALU.mult, ALU.add, reads=[bs4], writes=[bs4])
                k.stt(sn, s2[:], 2.0, s4[:], ALU.mult, ALU.mult, reads=[bs2, bs4], writes=[bsn])

            with k.scope():
                mst = contextlib.ExitStack()
                k.stacks.append(mst)
                Tm, b_Tm = new([128, 32, 128], BF16, "Tm")
                PTre, b_PTre = new([128, 32, 128], BF16, "PTre")
                PTim, b_PTim = new([128, 32, 128], BF16, "PTim")
                Qre, b_Qre = new([128, 32, 128], BF16, "Qre")
                Qim, b_Qim = new([128, 32, 128], BF16, "Qim")
                thr, b_thr = new([128, 32], F32, "thr")
                r8, b_r8 = new([128, 32], F32, "r8")
                with k.scope():
                    sp, b_sp = new([128, 32, 70], F32, "s5p")
                    k.dma(sp[:], s5p.rearrange("p (g f) -> p g f", f=70), writes=[b_sp])
                    lr = sp[:, :, 0]; li = sp[:, :, 1]
                    S2 = [128, 32]
                    step, b_step = new(S2, F32)
                    k.act(step[:], sp[:, :, 2], AF.Exp, reads=[b_sp], writes=[b_step])
                    lrs, b_lrs = new(S2, F32)
                    k.tt("dve", lrs[:], lr, step[:], ALU.mult, reads=[b_sp, b_step], writes=[b_lrs])
                    phi, b_phi = new(S2, F32)
                    k.tt("dve", phi[:], li, step[:], ALU.mult, reads=[b_sp, b_step], writes=[b_phi])
                    phr, b_phr = new(S2, F32)
                    reduce_angle(phi[:], phr[:], S2, b_phi, b_phr)
                    th8, b_th8 = new(S2, F32)
                    k.ts("dve", th8[:], phr[:], 8.0, None, ALU.mult, reads=[b_phr], writes=[b_th8])
                    reduce_angle(th8[:], thr[:], S2, b_th8, b_thr)
                    k.act(r8[:], lrs[:], AF.Exp, reads=[b_lrs], writes=[b_r8], scale=8.0)
                    sn1, b_sn1 = new(S2, F32); cs1, b_cs1 = new(S2, F32); mg1, b_mg1 = new(S2, F32)
                    sincos(phr[:], sn1[:], cs1[:], S2, b_phr, b_sn1, b_cs1)
                    k.act(mg1[:], lrs[:], AF.Exp, reads=[b_lrs], writes=[b_mg1])
                    nr, b_nr = new(S2, F32); ni, b_ni = new(S2, F32)
                    k.tt("dve", nr[:], mg1[:], cs1[:], ALU.mult, reads=[b_mg1, b_cs1], writes=[b_nr])
                    k.ts("dve", nr[:], nr[:], -1.0, None, ALU.add, reads=[b_nr], writes=[b_nr])
                    k.tt("dve", ni[:], mg1[:], sn1[:], ALU.mult, reads=[b_mg1, b_sn1], writes=[b_ni])
                    inv, b_inv = new(S2, F32); t0, b_t0 = new(S2, F32); t1, b_t1 = new(S2, F32)
                    k.tt("dve", inv[:], lr, lr, ALU.mult, reads=[b_sp], writes=[b_inv])
                    k.tt("dve", t0[:], li, li, ALU.mult, reads=[b_sp], writes=[b_t0])
                    k.tt("dve", inv[:], inv[:], t0[:], ALU.add, reads=[b_inv, b_t0], writes=[b_inv])
                    k.S.op("dve", lambda e: e.reciprocal(out=inv[:], in_=inv[:]), reads=[b_inv], writes=[b_inv])
                    cfr, b_cfr = new(S2, F32); cfi, b_cfi = new(S2, F32)
                    k.tt("dve", t0[:], nr[:], lr, ALU.mult, reads=[b_nr, b_sp], writes=[b_t0])
                    k.tt("dve", t1[:], ni[:], li, ALU.mult, reads=[b_ni, b_sp], writes=[b_t1])
                    k.tt("dve", t0[:], t0[:], t1[:], ALU.add, reads=[b_t0, b_t1], writes=[b_t0])
                    k.tt("dve", cfr[:], t0[:], inv[:], ALU.mult, reads=[b_t0, b_inv], writes=[b_cfr])
                    k.tt("dve", t0[:], ni[:], lr, ALU.mult, reads=[b_ni, b_sp], writes=[b_t0])
                    k.tt("dve", t1[:], nr[:], li, ALU.mult, reads=[b_nr, b_sp], writes=[b_t1])
                    k.tt("dve", t0[:], t0[:], t1[:], ALU.subtract, reads=[b_t0, b_t1], writes=[b_t0])
                    k.tt("dve", cfi[:], t0[:], inv[:], ALU.mult, reads=[b_t0, b_inv], writes=[b_cfi])
                    S3 = [128, 32, 16]
                    bre = sp[:, :, 3:19]; bim = sp[:, :, 19:35]; cre = sp[:, :, 35:51]; cim = sp[:, :, 51:67]
                    Bcr, b_Bcr = new(S3, F32); Bci, b_Bci = new(S3, F32); u0, b_u0 = new(S3, F32)
                    cfrb = cfr[:].unsqueeze(2).to_broadcast(S3); cfib = cfi[:].unsqueeze(2).to_broadcast(S3)
                    k.tt("dve", Bcr[:], bre, cfrb, ALU.mult, reads=[b_sp, b_cfr], writes=[b_Bcr])
                    k.tt("dve", u0[:], bim, cfib, ALU.mult, reads=[b_sp, b_cfi], writes=[b_u0])
                    k.tt("dve", Bcr[:], Bcr[:], u0[:], ALU.subtract, reads=[b_Bcr, b_u0], writes=[b_Bcr])
                    k.tt("dve", Bci[:], bim, cfrb, ALU.mult, reads=[b_sp, b_cfr], writes=[b_Bci])
                    k.tt("dve", u0[:], bre, cfib, ALU.mult, reads=[b_sp, b_cfi], writes=[b_u0])
                    k.tt("dve", Bci[:], Bci[:], u0[:], ALU.add, reads=[b_Bci, b_u0], writes=[b_Bci])
                    SP = [128, 32, 32]
                    pwr, b_pwr = new(SP, F32); pwi, b_pwi = new(SP, F32)
                    with k.scope():
                        ang, b_ang = new(SP, F32); angr, b_angr = new(SP, F32)
                        ktb = KTc.unsqueeze(1).to_broadcast(SP)
                        k.tt("dve", ang[:], phr[:].unsqueeze(2).to_broadcast(SP), ktb, ALU.mult, reads=[b_phr, b_cst], writes=[b_ang])
                        reduce_angle(ang[:], angr[:], SP, b_ang, b_angr)
                        psn, b_psn = new(SP, F32); pcs, b_pcs = new(SP, F32); pmg, b_pmg = new(SP, F32)
                        sincos(angr[:], psn[:], pcs[:], SP, b_angr, b_psn, b_pcs)
                        k.tt("dve", pmg[:], lrs[:].unsqueeze(2).to_broadcast(SP), ktb, ALU.mult, reads=[b_lrs, b_cst], writes=[b_pmg])
                        k.act(pmg[:], pmg[:], AF.Exp, reads=[b_pmg], writes=[b_pmg])
                        k.tt("dve", pwr[:], pmg[:], pcs[:], ALU.mult, reads=[b_pmg, b_pcs], writes=[b_pwr])
                        k.tt("dve", pwi[:], pmg[:], psn[:], ALU.mult, reads=[b_pmg, b_psn], writes=[b_pwi])
                    S4 = [128, 32, 8, 16]
                    ta, b_ta = new(S4, F32); tb_, b_tb = new(S4, F32)

                    def cprod(ar, ai, bar, bai, off, o_re, b_ore, o_im, b_oim, neg_im):
                        arb = ar.unsqueeze(2).to_broadcast(S4); aib = ai.unsqueeze(2).to_broadcast(S4)
                        pr = pwr[:, :, off:off + 8].unsqueeze(3).to_broadcast(S4)
                        pi_ = pwi[:, :, off:off + 8].unsqueeze(3).to_broadcast(S4)
                        k.tt("dve", ta[:], arb, pr, ALU.mult, reads=[bar, b_pwr], writes=[b_ta])
                        k.tt("pool", tb_[:], aib, pi_, ALU.mult, reads=[bai, b_pwi], writes=[b_tb])
                        k.tt("dve", o_re, ta[:], tb_[:], ALU.subtract, reads=[b_ta, b_tb], writes=[b_ore])
                        k.tt("dve", ta[:], arb, pi_, ALU.mult, reads=[bar, b_pwi], writes=[b_ta])
                        k.tt("pool", tb_[:], aib, pr, ALU.mult, reads=[bai, b_pwr], writes=[b_tb])
                        if neg_im:
                            k.stt(o_im, ta[:], -1.0, tb_[:], ALU.mult, ALU.subtract, reads=[b_ta, b_tb], writes=[b_oim])
                        else:
                            k.tt("dve", o_im, ta[:], tb_[:], ALU.add, reads=[b_ta, b_tb], writes=[b_oim])

                    v4 = lambda t_: t_[:].rearrange("p g (s h) -> p g s h", h=16)
                    if "noQ" not in SKIP:
                        cprod(cre, cim, b_sp, b_sp, 24, v4(Qre), b_Qre, v4(Qim), b_Qim, True)
                    with k.scope():
                      if "noP" not in SKIP:
                        Pr, b_Pr = new([128, 32, 128], F32); Pi, b_Pi = new([128, 32, 128], F32)
                        cprod(Bcr[:], Bci[:], b_Bcr, b_Bci, 0, v4(Pr), b_Pr, v4(Pi), b_Pi, False)
                        for g in range(32):
                            bank = pbanks[g % 2]; bb = bpb[g % 2]
                            k.mm(bank[:, 256:384], Pr[:, g, :], ident, True, True, reads=[b_Pr, b_cst], writes=[bb])
                            k.mm(bank[:, 384:512], Pi[:, g, :], ident, True, True, reads=[b_Pi, b_cst], writes=[bb])
                            k.act(PTre[:, g, :], bank[:, 256:384], AF.Copy, reads=[bb], writes=[b_PTre])
                            k.act(PTim[:, g, :], bank[:, 384:512], AF.Copy, reads=[bb], writes=[b_PTim])
                    with k.scope():
                      if "noT" not in SKIP:
                        Xr, b_Xr = new([128, 32, 128], F32); Xi, b_Xi = new([128, 32, 128], F32)
                        cprod(Bcr[:], Bci[:], b_Bcr, b_Bci, 8, v4(Xr), b_Xr, v4(Xi), b_Xi, False)
                        Zr, b_Zr = new([128, 32, 128], F32); Zi, b_Zi = new([128, 32, 128], F32)
                        cprod(cre, cim, b_sp, b_sp, 16, v4(Zr), b_Zr, v4(Zi), b_Zi, True)
                        Xmr = ta[:].rearrange("p g s h -> p g (s h)"); b_Xmr = b_ta
                        Xmi = tb_[:].rearrange("p g s h -> p g (s h)"); b_Xmi = b_tb
                        hm = [cst[:, 128 + 63:128 + 64], cst[:, 256 + 64:256 + 65]]
                        tA, b_tA = new([128, 128], F32); tB, b_tB = new([128, 128], F32)
                        for dd_ in range(2):
                            k.ts("dve", Xmr, Xr[:], hm[dd_], None, ALU.mult, reads=[b_Xr, b_cst], writes=[b_Xmr])
                            k.ts("pool", Xmi, Xi[:], hm[dd_], None, ALU.mult, reads=[b_Xi, b_cst], writes=[b_Xmi])
                            for g in range(32):
                                bank = pbanks[2 + g % 2]; bb = bpb[2 + g % 2]
                                k.mm(bank[:, 0:128], Xmr[:, g, :], Zr[:, g, :], True, False, reads=[b_Xmr, b_Zr], writes=[bb])
                                k.mm(bank[:, 0:128], Xmi[:, g, :], Zi[:, g, :], False, True, reads=[b_Xmi, b_Zi], writes=[bb])
                                if dd_ == 0:
                                    k.tt("dve", tA[:], bank[:, 0:128], mle2, ALU.mult, reads=[bb, b_cs2], writes=[b_tA])
                                    k.stt(Tm[:, g, :], ident, dpk[:, g:g + 1], tA[:], ALU.mult, ALU.add,
                                          reads=[b_tA, b_cs2, b_cst], writes=[b_Tm])
                                else:
                                    k.tt("dve", tB[:], bank[:, 0:128], mge2, ALU.mult, reads=[bb, b_cs2], writes=[b_tB])
                                    k.tt("dve", Tm[:, g, :], Tm[:, g, :], tB[:], ALU.add, reads=[b_tB, b_Tm], writes=[b_Tm])

                for (b, phase) in ((0, "S"), (1, "S"), (0, "MO"), (1, "MO")):
                    if phase == "MO" and mst is not None:
                        k.stacks.pop()
                        mst.close()
                        mst = None
                        k.fence_ops = k.S.fence()
                    x1v = X1[b].rearrange("(k p) t -> p k t", p=128)

                    def dma_x1(t0, ntok, xs, b_xs):
                        k.dma(xs[:, :, 0:ntok], x1v[:, :, t0:t0 + ntok], reads=[bX1[b]], writes=[b_xs])

                    def mod_x1(dst_ap, ntok, xs, b_xs, b_dst):
                        for dt in range(8):
                            eng = ("dve", "pool")[dt % 2]
                            k.ts(eng, dst_ap[:, dt, :], xs[:, dt, 0:ntok], der[:, 3, dt, b:b + 1], der[:, 4, dt, b:b + 1],
                                 ALU.mult, ALU.add, reads=[b_xs, b_der], writes=[b_dst])

                    def load_xm2(dst_ap, t0, ntok, xs, b_xs, b_dst):
                        k.dma(xs[:, :, 0:ntok], x1v[:, :, t0:t0 + ntok], reads=[bX1[b]], writes=[b_xs])
                        for dt in range(8):
                            eng = ("dve", "pool")[dt % 2]
                            k.ts(eng, dst_ap[:, dt, :], xs[:, dt, 0:ntok], der[:, 3, dt, b:b + 1], der[:, 4, dt, b:b + 1],
                                 ALU.mult, ALU.add, reads=[b_xs, b_der], writes=[b_dst])

                    with (k.scope() if phase == "S" else contextlib.nullcontext()):
                      if phase == "S" and "S" not in SKIP:
                        Zy, b_Zy = new([128, 4, 8, 512], BF16, "Zy")
                        Zta, b_Zta = new([128, 4, 32, 8, 16], BF16, "Zta")
                        with k.scope():
                            wu, b_wu = load_w(w_in, 8, 1552, 2064)
                            xsL2 = [new([128, 8, 128], F32, "xs") for _ in range(2)]
                            xmbL = [new([128, 8, 1024], BF16, "xmb") for _ in range(2)]

                            def prepA(cb):
                                (xmb, b_xmb) = xmbL[cb % 2]
                                for q8 in range(8):
                                    (xs, b_xs) = xsL2[q8 % 2]
                                    load_xm2(xmb[:, :, q8 * 128:(q8 + 1) * 128], cb * 1024 + q8 * 128, 128, xs, b_xs, b_xmb)

                            prepA(0)
                            for cb in range(4):
                                (xmb, b_xmb) = xmbL[cb % 2]
                                if cb + 1 < 4:
                                    prepA(cb + 1)
                                for s_ in range(8):
                                    bank = pbanks[s_ % 2]; bb = bpb[s_ % 2]
                                    for kk in range(8):
                                        lv = xmb[:, kk, :].rearrange("p (c s) -> p s c", s=8)[:, s_, :]
                                        k.mm(bank[:, :], lv, wu[:, kk, :], kk == 0, kk == 7, reads=[b_xmb, b_wu], writes=[bb])
                                    k.tt("dve", Zta[:, cb, :, s_, :], bank[:, :].rearrange("p (g h) -> p g h", h=16),
                                         rowbc[:, R_BU:R_BU + 512].rearrange("p (g h) -> p g h", h=16), ALU.add,
                                         reads=[bb, b_rowbc], writes=[b_Zta])
                        NG = 8
                        for hf_ in range(32 // NG):
                          with k.scope():
                            G0 = NG * hf_
                            Ut, b_Ut = new([128, NG, 512], BF16, "Ut")
                            Ur, b_Ur = new([128, NG, 512], BF16, "Ur")
                            for cb in range(4):
                                for g4 in range(NG // 4):
                                    bank = pbanks[2 + g4 % 2]; bb = bpb[2 + g4 % 2]
                                    bank2 = pbanks[4 + g4 % 2]; bb2 = bpb[4 + g4 % 2]
                                    for gg in range(4):
                                        g = G0 + g4 * 4 + gg
                                        zl = Zta[:, cb, g, :, :].rearrange("p s h -> p (s h)")
                                        k.mm(bank[:, gg * 128:(gg + 1) * 128], zl, ident_bf[:], True, True,
                                             reads=[b_Zta, b_idbf], writes=[bb])
                                        k.mm(bank2[:, gg * 128:(gg + 1) * 128], zl, J_bf[:], True, True,
                                             reads=[b_Zta, b_idbf], writes=[bb2])
                                    k.act(Ut[:, g4 * 4:g4 * 4 + 4, cb * 128:(cb + 1) * 128],
                                          bank[:, :].rearrange("p (g c) -> p g c", g=4), AF.Copy, reads=[bb], writes=[b_Ut])
                                    k.copy("dve", Ur[:, g4 * 4:g4 * 4 + 4, (3 - cb) * 128:(4 - cb) * 128],
                                           bank2[:, :].rearrange("p (g c) -> p g c", g=4), reads=[bb2], writes=[b_Ur])
                            with k.scope():
                                W_ = lambda: new([128, 512], F32, "sw")
                                W2 = lambda: [W_(), W_()]
                                (ta_, bta), (tr_, btr), (s2t, bs2) = W_(), W_(), W_()
                                (ti_, bti) = new([128, 512], I32, "twi")
                                tcs, tsns = W2(), W2()
                                srL, siL, a1L, a2L, stRL, stIL, kRL, kIL = [W2() for _ in range(8)]
                                HpL = [[new([128, 513], BF16, "Hp") for _ in range(4)] for _ in range(2)]
                                for hl in HpL:
                                    for (h_, bh_) in hl:
                                        k.S.op("pool", lambda e, h_=h_: e.memset(h_[:], 0.0), writes=[bh_])
                                ygL = [new([128, 512], BF16, "yg") for _ in range(2)]
                                ygbL = [new([128, 512], BF16, "ygb") for _ in range(2)]
                                ygTL = [new([128, 4, 128], BF16, "ygT") for _ in range(2)]
                                pY, bY = pbanks[0], bpb[0]
                                pYb, bYb = pbanks[1], bpb[1]
                                pAr, bAr = pbanks[2], bpb[2]
                                pBr, bBr = pbanks[3], bpb[3]
                                pAi, bAi = pbanks[4], bpb[4]
                                pBi, bBi = pbanks[5], bpb[5]
                                pT, bT = pbanks[6], bpb[6]
                                pT2, bT2 = pbanks[7], bpb[7]

                                def gen_tw(gl):
                                    g = G0 + gl
                                    (tc, btc), (tsn, bts) = tcs[gl % 2], tsns[gl % 2]
                                    k.S.op("act", lambda e: e.activation(out=ta_[:], in_=iota, func=AF.Copy, scale=thr[:, g:g + 1]),
                                           reads=[b_cst, b_thr], writes=[bta])
                                    k.act(tr_[:], ta_[:], AF.Copy, reads=[bta], writes=[btr], scale=1.0 / TWO_PI)
                                    k.copy("dve", ti_[:], tr_[:], reads=[btr], writes=[bti])
                                    k.copy("dve", tr_[:], ti_[:], reads=[bti], writes=[btr])
                                    k.stt(ta_[:], tr_[:], -TWO_PI, ta_[:], ALU.mult, ALU.add, reads=[btr, bta], writes=[bta])
                                    k.act(s2t[:], ta_[:], AF.Sin, reads=[bta], writes=[bs2], scale=0.5)
                                    k.act(tr_[:], ta_[:], AF.Sin, reads=[bta], writes=[btr], scale=0.25)
                                    k.tt("pool", tc[:], s2t[:], s2t[:], ALU.mult, reads=[bs2], writes=[btc])
                                    k.ts("pool", tc[:], tc[:], -2.0, 1.0, ALU.mult, ALU.add, reads=[btc], writes=[btc])
                                    k.tt("pool", tr_[:], tr_[:], tr_[:], ALU.mult, reads=[btr], writes=[btr])
                                    k.ts("pool", tr_[:], tr_[:], -2.0, 1.0, ALU.mult, ALU.add, reads=[btr], writes=[btr])
                                    k.stt(tsn[:], s2t[:], 2.0, tr_[:], ALU.mult, ALU.mult, reads=[bs2, btr], writes=[bts])

                                def stageA(gl):
                                    g = G0 + gl
                                    (sr, bsr), (si, bsi) = srL[gl % 2], siL[gl % 2]
                                    k.mm(pAr[:, :], PTre[:, g, :], Ut[:, gl, :], True, True, reads=[b_PTre, b_Ut], writes=[bAr])
                                    k.mm(pBr[:, :], PTre[:, g, :], Ur[:, gl, :], True, True, reads=[b_PTre, b_Ur], writes=[bBr])
                                    k.mm(pAi[:, :], PTim[:, g, :], Ut[:, gl, :], True, True, reads=[b_PTim, b_Ut], writes=[bAi])
                                    k.mm(pBi[:, :], PTim[:, g, :], Ur[:, gl, :], True, True, reads=[b_PTim, b_Ur], writes=[bBi])
                                    k.act(sr[0:64, :], pAr[0:64, :], AF.Copy, reads=[bAr], writes=[bsr])
                                    k.act(sr[64:128, :], pBr[64:128, :], AF.Copy, reads=[bBr], writes=[bsr])
                                    k.act(si[0:64, :], pAi[0:64, :], AF.Copy, reads=[bAi], writes=[bsi])
                                    k.act(si[64:128, :], pBi[64:128, :], AF.Copy, reads=[bBi], writes=[bsi])

                                def stageB(gl):
                                    g = G0 + gl
                                    w = gl % 2
                                    (tc, btc), (tsn, bts) = tcs[w], tsns[w]
                                    (sr, bsr), (si, bsi), (a1, ba1), (a2, ba2) = srL[w], siL[w], a1L[w], a2L[w]
                                    (stR, bstR), (stI, bstI), (kR, bkR), (kI, bkI) = stRL[w], stIL[w], kRL[w], kIL[w]
                                    (hrf, bhrf), (hif, bhif), (hrb, bhrb), (hib, bhib) = HpL[w]
                                    (yg, byg), (ygb, bygb), (ygT, bygT) = ygL[w], ygbL[w], ygTL[w]
                                    k.tt("dve", a1[:], sr[:], tc[:], ALU.mult, reads=[bsr, btc], writes=[ba1])
                                    k.tt("pool", a2[:], si[:], tsn[:], ALU.mult, reads=[bsi, bts], writes=[ba2])
                                    k.tt("dve", stR[:], a1[:], a2[:], ALU.add, reads=[ba1, ba2], writes=[bstR])
                                    k.tt("dve", a1[:], si[:], tc[:], ALU.mult, reads=[bsi, btc], writes=[ba1])
                                    k.tt("pool", a2[:], sr[:], tsn[:], ALU.mult, reads=[bsr, bts], writes=[ba2])
                                    k.tt("dve", stI[:], a1[:], a2[:], ALU.subtract, reads=[ba1, ba2], writes=[bstI])
                                    r8b = r8[:, g:g + 1].to_broadcast([128, 512])
                                    k.S.op("dve", lambda e: e.tensor_tensor_scan(out=kR[:], data0=r8b, data1=stR[:], initial=0.0,
                                                                                 op0=ALU.mult, op1=ALU.add),
                                           reads=[bstR, b_r8], writes=[bkR])
                                    k.S.op("dve", lambda e: e.tensor_tensor_scan(out=kI[:], data0=r8b, data1=stI[:], initial=0.0,
                                                                                 op0=ALU.mult, op1=ALU.add),
                                           reads=[bstI, b_r8], writes=[bkI])
                                    if gl + 1 < NG:
                                        gen_tw(gl + 1)
                                    k.tt("dve", a1[:], kR[:], tc[:], ALU.mult, reads=[bkR, btc], writes=[ba1])
                                    k.tt("pool", a2[:], kI[:], tsn[:], ALU.mult, reads=[bkI, bts], writes=[ba2])
                                    k.tt("dve", hrf[0:64, 1:513], a1[0:64, :], a2[0:64, :], ALU.subtract, reads=[ba1, ba2], writes=[bhrf])
                                    k.tt("dve", hrb[64:128, 1:513], a1[64:128, :], a2[64:128, :], ALU.subtract, reads=[ba1, ba2], writes=[bhrb])
                                    k.tt("dve", a1[:], kI[:], tc[:], ALU.mult, reads=[bkI, btc], writes=[ba1])
                                    k.tt("pool", a2[:], kR[:], tsn[:], ALU.mult, reads=[bkR, bts], writes=[ba2])
                                    k.tt("dve", hif[0:64, 1:513], a1[0:64, :], a2[0:64, :], ALU.add, reads=[ba1, ba2], writes=[bhif])
                                    k.tt("dve", hib[64:128, 1:513], a1[64:128, :], a2[64:128, :], ALU.add, reads=[ba1, ba2], writes=[bhib])
                                    k.mm(pY[:, :], Tm[:, g, :], Ut[:, gl, :], True, False, reads=[b_Tm, b_Ut], writes=[bY])
                                    k.mm(pY[:, :], Qre[:, g, :], hrf[:, 0:512], False, False, reads=[b_Qre, bhrf], writes=[bY])
                                    k.mm(pY[:, :], Qim[:, g, :], hif[:, 0:512], False, True, reads=[b_Qim, bhif], writes=[bY])
                                    k.mm(pYb[:, :], Qre[:, g, :], hrb[:, 0:512], True, False, reads=[b_Qre, bhrb], writes=[bYb])
                                    k.mm(pYb[:, :], Qim[:, g, :], hib[:, 0:512], False, True, reads=[b_Qim, bhib], writes=[bYb])
                                    k.act(yg[:], pY[:, :], AF.Copy, reads=[bY], writes=[byg])
                                    k.act(ygb[:], pYb[:, :], AF.Copy, reads=[bYb], writes=[bygb])
                                    for jb in range(4):
                                        k.mm(pT2[:, jb * 128:(jb + 1) * 128], ygb[:, jb * 128:(jb + 1) * 128], ident_bf[:], True, True,
                                             reads=[bygb, b_idbf], writes=[bT2])
                                    k.copy("dve", ygT[:], pT2[:, :].rearrange("p (j x) -> p j x", j=4), reads=[bT2], writes=[bygT])
                                    for cb in range(4):
                                        k.mm(pT[:, cb * 128:(cb + 1) * 128], yg[:, cb * 128:(cb + 1) * 128], ident_bf[:], True, False,
                                             reads=[byg, b_idbf], writes=[bT])
                                        k.mm(pT[:, cb * 128:(cb + 1) * 128], J_bf[:], ygT[:, 3 - cb, :], False, True,
                                             reads=[bygT, b_idbf], writes=[bT])
                                    k.copy("dve", Zy[:, :, :, 16 * g:16 * g + 16],
                                           pT[:, :].rearrange("p (c t h) -> p c t h", c=4, t=8), reads=[bT], writes=[b_Zy])

                                gen_tw(0)
                                stageA(0)
                                for gl in range(NG):
                                    if gl + 1 < NG:
                                        stageA(gl + 1)
                                    stageB(gl)
                        with k.scope():
                            glw, b_glw = load_w(glu_w, 4, 0, 512)
                            yb, b_yb = new([128, 4, 1024], F32, "yb")
                            u1, b_u1 = new([128, 4, 1024], F32, "u1")
                            yab, b_yab = new([128, 4, 1024], BF16, "yab")
                            sqb, b_sqb = new([128, 4, 1024], BF16, "sqb")
                            sg_, b_sg = new([128, 512], F32, "sg")
                            rs_, b_rs = new([128, 512], F32, "rs")
                            mo, b_mo = new([128, 4, 1024], BF16, "mo")
                            GC = 2.0 * math.sqrt(2.0 / math.pi)
                            for cb in range(4):
                                for ct in range(4):
                                    for th in range(2):
                                        bank = pbanks[(ct * 2 + th) % 2]; bb = bpb[(ct * 2 + th) % 2]
                                        for t4 in range(4):
                                            t_ = th * 4 + t4
                                            k.mm(bank[:, t4 * 128:(t4 + 1) * 128], Zy[:, cb, t_, ct * 128:(ct + 1) * 128], ident_bf[:],
                                                 True, True, reads=[b_Zy, b_idbf], writes=[bb])
                                        k.act(yb[:, ct, :].rearrange("p (c s) -> p s c", s=8)[:, th * 4:th * 4 + 4, :],
                                              bank[:, :].rearrange("p (s c) -> p s c", s=4), AF.Copy, reads=[bb], writes=[b_yb])
                                k.act(u1[:], yb[:], AF.Square, reads=[b_yb], writes=[b_u1])
                                k.ts("dve", u1[:], u1[:], 0.044715, 1.0, ALU.mult, ALU.add, reads=[b_u1], writes=[b_u1])
                                k.tt("pool", u1[:], u1[:], yb[:], ALU.mult, reads=[b_u1, b_yb], writes=[b_u1])
                                k.act(u1[:], u1[:], AF.Sigmoid, reads=[b_u1], writes=[b_u1], scale=GC)
                                k.tt("dve", yb[:], yb[:], u1[:], ALU.mult, reads=[b_u1, b_yb], writes=[b_yb])
                                k.copy("pool", yab[:], yb[:], reads=[b_yb], writes=[b_yab])
                                for hf in range(2):
                                    tk = slice(hf * 512, (hf + 1) * 512)
                                    for co in range(4):
                                        bank = pbanks[2 + co % 2]; bb = bpb[2 + co % 2]
                                        for ci in range(4):
                                            k.mm(bank[:, :], glw[:, ci, co * 128:(co + 1) * 128], yab[:, ci, tk], ci == 0, ci == 3,
                                                 reads=[b_glw, b_yab], writes=[bb])
                                        k.act(sg_[:], bank[:, :], AF.Sigmoid, reads=[bb, b_cp], writes=[b_sg],
                                              bias=cp[:, C_GLUB + co:C_GLUB + co + 1])
                                        k.tt("dve", yb[:, co, tk], yb[:, co, tk], sg_[:], ALU.mult, reads=[b_yb, b_sg], writes=[b_yb])
                                k.act(sqb[:], yb[:], AF.Square, reads=[b_yb], writes=[b_sqb])
                                for hf in range(2):
                                    tk = slice(hf * 512, (hf + 1) * 512)
                                    bank = pbanks[4 + hf]; bb = bpb[4 + hf]
                                    for ci in range(4):
                                        k.mm(bank[:, :], ones_bf[:], sqb[:, ci, tk], ci == 0, ci == 3, reads=[b_sqb, b_ones], writes=[bb])
                                    k.ts("dve", rs_[:], bank[:, :], 1.0 / 512.0, LN_EPS, ALU.mult, ALU.add, reads=[bb], writes=[b_rs])
                                    k.act(rs_[:], rs_[:], AF.Sqrt, reads=[b_rs], writes=[b_rs])
                                    k.S.op("dve", lambda e: e.reciprocal(out=rs_[:], in_=rs_[:]), reads=[b_rs], writes=[b_rs])
                                    for co in range(4):
                                        k.stt(mo[:, co, tk], yb[:, co, tk], cp[:, C_SNG + co:C_SNG + co + 1], rs_[:], ALU.mult, ALU.mult,
                                              reads=[b_yb, b_rs, b_cp], writes=[b_mo])
                                k.dma(MIX[b, 512:1024, cb * 1024:(cb + 1) * 1024].rearrange("(k p) t -> p k t", p=128), mo[:],
                                      reads=[b_mo], writes=[bMIX[b]])

                    with (k.scope() if phase == "MO" else contextlib.nullcontext()):
                      if phase == "MO" and "M" not in SKIP:
                        qf, b_qf = new([128, 2, T], BF16, "qf")
                        kf, b_kf = new([128, 2, T], BF16, "kf")
                        vext, b_vext = new([128, 32, 4, 129], BF16, "vext")
                        big, b_big = new([128, 4, T + 4], F32, "big")
                        k.S.op("pool", lambda e: e.memset(vext[:, :, :, 128:129], 1.0), writes=[b_vext])
                        k.S.op("pool", lambda e: e.memset(big[:, :, 0:2], 0.0), writes=[b_big])
                        k.S.op("pool", lambda e: e.memset(big[:, :, T + 2:T + 4], 0.0), writes=[b_big])
                        with k.scope():
                            wqk, b_wqk = load_w(w_in, 8, 0, 512)
                            wv, b_wv = load_w(w_in, 8, 512, 1024)
                            wg, b_wg = load_w(w_in, 8, 1536, 1552, 16)
                            xs, b_xs = new([128, 8, 256], F32, "xs")
                            xm2L = [new([128, 8, 256], BF16, "xm2") for _ in range(2)]
                            gt, b_gt = new([16, 256], F32, "gt")
                            dma_x1(0, 256, xs, b_xs)
                            mod_x1(xm2L[0][0], 256, xs, b_xs, xm2L[0][1])
                            for ti in range(16):
                                t0_ = ti * 256
                                (xm2, b_xm2) = xm2L[ti % 2]
                                if ti + 1 < 16:
                                    dma_x1(t0_ + 256, 256, xs, b_xs)
                                for ct in range(4):
                                    bank = pbanks[ct // 2]; bb = bpb[ct // 2]
                                    half = (ct % 2) * 256
                                    for kk in range(8):
                                        k.mm(bank[:, half:half + 256], wqk[:, kk, ct * 128:(ct + 1) * 128], xm2[:, kk, :], kk == 0, kk == 7,
                                             reads=[b_wqk, b_xm2], writes=[bb])
                                    k.act(big[:, ct, 2 + t0_:2 + t0_ + 256], bank[:, half:half + 256], AF.Identity, reads=[bb, b_cp],
                                          writes=[b_big], bias=cp[:, C_BQK + ct:C_BQK + ct + 1])
                                for cc in range(2):
                                    c = ti * 2 + cc
                                    bank = pbanks[2 + cc]; bb = bpb[2 + cc]
                                    for kk in range(8):
                                        k.mm(bank[:, :], xm2[:, kk, cc * 128:(cc + 1) * 128], wv[:, kk, :], kk == 0, kk == 7,
                                             reads=[b_wv, b_xm2], writes=[bb])
                                    k.tt("dve", vext[:, c, :, 0:128], bank[:, :].rearrange("p (r v) -> p r v", r=4),
                                         rowbc[:, R_BV:R_BV + 512].rearrange("p (r v) -> p r v", r=4), ALU.add,
                                         reads=[bb, b_rowbc], writes=[b_vext])
                                bank = pbanks[4]; bb = bpb[4]
                                for kk in range(8):
                                    k.mm(bank[0:16, 0:256], wg[:, kk, 0:16], xm2[:, kk, :], kk == 0, kk == 7,
                                         reads=[b_wg, b_xm2], writes=[bb])
                                k.act(gt[0:16, :], bank[0:16, 0:256], AF.Identity, reads=[bb, b_cp], writes=[b_gt],
                                      bias=cp[0:16, C_BG + 4:C_BG + 5])
                                k.dma(GSd[b, :, :, t0_:t0_ + 256].rearrange("k r t -> (k r) t"), gt[:], reads=[b_gt], writes=[bGS[b]])
                                if ti + 1 < 16:
                                    mod_x1(xm2L[(ti + 1) % 2][0], 256, xs, b_xs, xm2L[(ti + 1) % 2][1])
                        ktm, b_ktm = new([128, 32, 256], BF16, "ktm")
                        with k.scope():
                            acc, b_acc = new([128, T], F32, "acc")
                            for ct in range(4):
                                cw = lambda j: cp[:, C_CW + 4 * j + ct:C_CW + 4 * j + ct + 1]
                                k.ts("dve", acc[:], big[:, ct, 0:T], cw(0), None, ALU.mult, reads=[b_big, b_cp], writes=[b_acc])
                                for j in range(1, 5):
                                    k.stt(acc[:], big[:, ct, j:j + T], cw(j), acc[:], ALU.mult, ALU.add, reads=[b_big, b_cp, b_acc], writes=[b_acc])
                                cb_ = cp[:, C_CB + ct:C_CB + ct + 1]
                                if ct < 2:
                                    k.act(acc[:], acc[:], AF.Silu, reads=[b_acc, b_cp], writes=[b_acc], bias=cb_)
                                    k.ts("pool", qf[:, ct, :], acc[:], 0.125, None, ALU.mult, reads=[b_acc], writes=[b_qf])
                                else:
                                    k.act(kf[:, ct - 2, :], acc[:], AF.Silu, reads=[b_acc, b_cp], writes=[b_kf], bias=cb_)
                            for c2 in range(16):
                                bank = pbanks[c2 % 2]; bb = bpb[c2 % 2]
                                for cc in range(2):
                                    c = c2 * 2 + cc
                                    for kt in range(2):
                                        o_ = cc * 256 + kt * 128
                                        k.mm(bank[:, o_:o_ + 128], kf[:, kt, c * 128:(c + 1) * 128], ident_bf[:], True, True,
                                             reads=[b_kf, b_idbf], writes=[bb])
                                k.act(ktm[:, c2 * 2:c2 * 2 + 2, :], bank[:, :].rearrange("p (c x) -> p c x", c=2), AF.Copy,
                                      reads=[bb], writes=[b_ktm])
                        WCt, b_WCt = new([128, 2, 8, 32], F32, "WCt")
                        DECt, b_DECt = new([128, 8, 32], F32, "DECt")
                        with k.scope():
                            G5 = [32, 4, 4, 128]
                            Gc, b_Gc = new(G5, F32, "Gc")
                            k.dma(Gc[:], GSd[b].rearrange("k r (c t) -> c k r t", t=128), reads=[bGS[b]], writes=[b_Gc])
                            D4 = [32, 2, 4, 128]
                            spl, b_spl = new(D4, F32); BS, b_BS = new(D4, F32); ee, b_ee = new(D4, F32)
                            k.act(spl[:], Gc[:, 2:4, :, :], AF.Exp, reads=[b_Gc], writes=[b_spl], scale=-1.0)
                            k.act(spl[:], spl[:], AF.Ln, reads=[b_spl], writes=[b_spl], bias=1.0)
                            flat = lambda t_, d: t_[:, d, :, :].rearrange("c r t -> c (r t)")
                            for d in range(2):
                                k.S.op("dve", lambda e, d=d: e.tensor_tensor_scan(out=flat(BS, d), data0=rmask[0:32, :], data1=flat(spl, d),
                                                                                  initial=0.0, op0=ALU.mult, op1=ALU.add),
                                       reads=[b_spl, b_cs2], writes=[b_BS])
                            tot = BS[:, 1, :, 127:128].to_broadcast([32, 4, 128])
                            k.tt("dve", ee[:, 1, :, :], spl[:, 1, :, :], BS[:, 1, :, :], ALU.subtract, reads=[b_spl, b_BS], writes=[b_ee])
                            k.tt("dve", BS[:, 1, :, :], ee[:, 1, :, :], tot, ALU.add, reads=[b_ee, b_BS], writes=[b_BS])
                            k.tt("dve", ee[:], Gc[:, 0:2, :, :], BS[:], ALU.add, reads=[b_Gc, b_BS], writes=[b_ee])
                            emx, b_emx = new([32, 2, 4], F32)
                            k.S.op("dve", lambda e: e.tensor_reduce(out=emx[:], in_=ee[:], axis=AX.X, op=ALU.max), reads=[b_ee], writes=[b_emx])
                            nbe, b_nbe = new([32, 2, 4], F32)
                            k.ts("dve", nbe[:, 0, :], BS[:, 0, :, 127], -1.0, None, ALU.mult, reads=[b_BS], writes=[b_nbe])
                            k.ts("dve", nbe[:, 1, :], BS[:, 1, :, 0], -1.0, None, ALU.mult, reads=[b_BS], writes=[b_nbe])
                            pbk = pbanks[0]; bbk = bpb[0]
                            for d in range(2):
                                rhs_ = ident[0:32, 0:32] if d == 0 else J32[0:32, :]
                                k.mm(pbk[0:4, d * 64:d * 64 + 32], emx[:, d, :], rhs_, True, True, reads=[b_emx, b_cst, b_cs2], writes=[bbk])
                                k.mm(pbk[0:4, d * 64 + 32:d * 64 + 64], nbe[:, d, :], rhs_, True, True, reads=[b_nbe, b_cst, b_cs2], writes=[bbk])
                            rows, b_rows = new([4, 2, 2, 32], F32)
                            k.copy("dve", rows[:], pbk[0:4, 0:128].rearrange("r (d x j) -> r d x j", d=2, x=2), reads=[bbk], writes=[b_rows])
                            mrow, b_mrow = new([4, 2, 33], F32)
                            k.S.op("pool", lambda e: e.memset(mrow[:], 0.0), writes=[b_mrow])
                            Rrow, b_Rrow = new([4, 2, 32], F32); drow, b_drow = new([4, 2, 32], F32)
                            for d in range(2):
                                k.S.op("dve", lambda e, d=d: e.tensor_tensor_scan(out=mrow[:, d, 1:33], data0=rows[:, d, 0, :], data1=rows[:, d, 1, :],
                                                                                  initial=0.0, op0=ALU.max, op1=ALU.add),
                                       reads=[b_rows, b_mrow], writes=[b_mrow])
                            k.tt("dve", Rrow[:], mrow[:, :, 0:32], rows[:, :, 0, :], ALU.max, reads=[b_mrow, b_rows], writes=[b_Rrow])
                            k.tt("dve", drow[:], mrow[:, :, 0:32], Rrow[:], ALU.subtract, reads=[b_mrow, b_Rrow], writes=[b_drow])
                            k.act(drow[:], drow[:], AF.Exp, reads=[b_drow], writes=[b_drow])
                            pbd = pbanks[1]; bbd = bpb[1]
                            for d in range(2):
                                for r in range(4):
                                    o_ = (d * 4 + r) * 32
                                    k.mm(pbd[:, o_:o_ + 32], esel[0:4, r * 128:(r + 1) * 128], drow[:, d, :], True, True,
                                         reads=[b_cs2, b_drow], writes=[bbd])
                            k.copy("dve", DECt[:], pbd[:, 0:256].rearrange("p (x j) -> p x j", x=8), reads=[bbd], writes=[b_DECt])
                            pbr = pbanks[2]; bbr = bpb[2]
                            k.mm(pbr[0:32, 0:4], Rrow[:, 0, :], ident[0:4, 0:4], True, True, reads=[b_Rrow, b_cst], writes=[bbr])
                            k.mm(pbr[0:32, 4:8], Rrow[:, 1, :], ident[0:4, 0:4], True, True, reads=[b_Rrow, b_cst], writes=[bbr])
                            RbT, b_RbT = new([32, 8], F32)
                            k.copy("dve", RbT[:], pbr[0:32, 0:8], reads=[bbr], writes=[b_RbT])
                            k.mm(pbr[0:32, 8:12], J32[0:32, :], RbT[:, 4:8], True, True, reads=[b_RbT, b_cs2], writes=[bbr])
                            Rc, b_Rc = new([32, 2, 4], F32)
                            k.copy("dve", Rc[:, 0, :], RbT[:, 0:4], reads=[b_RbT], writes=[b_Rc])
                            k.copy("dve", Rc[:, 1, :], pbr[0:32, 8:12], reads=[bbr], writes=[b_Rc])
                            Rcb = Rc[:].unsqueeze(3).to_broadcast(D4)
                            k.tt("dve", ee[:], ee[:], Rcb, ALU.subtract, reads=[b_ee, b_Rc], writes=[b_ee])
                            k.act(ee[:], ee[:], AF.Exp, reads=[b_ee], writes=[b_ee])
                            k.tt("dve", BS[:], BS[:], Rcb, ALU.subtract, reads=[b_BS, b_Rc], writes=[b_BS])
                            k.act(BS[:], BS[:], AF.Exp, reads=[b_BS], writes=[b_BS])
                            pbw = pbanks[3]; bbw = bpb[3]
                            for x_, src_, bsrc in ((0, ee, b_ee), (1, BS, b_BS)):
                                for d in range(2):
                                    for r in range(4):
                                        o_ = (x_ * 8 + d * 4 + r) * 32
                                        k.mm(pbw[:, o_:o_ + 32], src_[:, d, r, :], ident[0:32, 0:32], True, True, reads=[bsrc, b_cst], writes=[bbw])
                            k.copy("dve", WCt[:], pbw[:, :].rearrange("p (x y c) -> p x y c", x=2, y=8), reads=[bbw], writes=[b_WCt])
                        with k.scope():
                            Cst = [[new([128, 129], F32, "Cst") for r in range(4)] for d in range(2)]
                            NR = 4
                            STb = [new([128, 128], BF16, "ST") for _ in range(NR)]
                            Vpb = [new([128, 129], BF16, "Vp") for _ in range(NR)]
                            Cdb = [new([128, 129], BF16, "Cd") for _ in range(NR)]
                            ddb = [new([128, 2], F32, "dd") for _ in range(NR)]
                            insts = [(st, d, r) for st in range(32) for d in range(2) for r in range(4)]

                            def geom(it):
                                st, d, r = insts[it]
                                c = st if d == 0 else 31 - st
                                return st, d, r, c, slice(c * 128, (c + 1) * 128), r // 2, 64 * (r % 2), d * 4 + r, it % NR

                            def emitA(it):
                                st, d, r, c, tok, kt, ro, x_, w = geom(it)
                                pS = pbanks[it % 2]; bS = bpb[it % 2]
                                (ST, bST), (Vp, bVp), (Cd, bCd) = STb[w], Vpb[w], Cdb[w]
                                (Cs, bCs) = Cst[d][r]
                                k.mm(pS[:, 0:128], kf[ro:ro + 64, kt, tok], qf[ro:ro + 64, kt, tok], True, True,
                                     reads=[b_kf, b_qf], writes=[bS])
                                k.tt("dve", ST[:], pS[:, 0:128], tri[d], ALU.mult, reads=[bS, b_cst], writes=[bST])
                                k.S.op("act", lambda e: e.activation(out=Vp[:], in_=vext[:, c, r, :], func=AF.Copy,
                                                                     scale=WCt[:, 0, x_, c:c + 1]),
                                       reads=[b_vext, b_WCt], writes=[bVp])
                                if st > 0:
                                    k.ts("dve", Cs[ro:ro + 64, :], Cs[ro:ro + 64, :], DECt[ro:ro + 64, x_, st:st + 1], None, ALU.mult,
                                         reads=[bCs, b_DECt], writes=[bCs])
                                    k.act(Cd[ro:ro + 64, :], Cs[ro:ro + 64, :], AF.Copy, reads=[bCs], writes=[bCd])

                            def emitB(it):
                                st, d, r, c, tok, kt, ro, x_, w = geom(it)
                                pN = pbanks[2 + it % 2]; bN = bpb[2 + it % 2]
                                pC = pbanks[4 + it % 2]; bC = bpb[4 + it % 2]
                                (ST, bST), (Vp, bVp), (Cd, bCd), (dd, bdd) = STb[w], Vpb[w], Cdb[w], ddb[w]
                                (Cs, bCs) = Cst[d][r]
                                k.mm(pN[:, 0:129], ST[:], Vp[:], True, st == 0, reads=[bST, bVp], writes=[bN])
                                if st > 0:
                                    k.mm(pN[:, 0:129], qf[ro:ro + 64, kt, tok], Cd[ro:ro + 64, :], False, True,
                                         reads=[b_qf, bCd], writes=[bN])
                                k.mm(pC[ro:ro + 64, 0:129], ktm[:, c, r * 64:(r + 1) * 64], Vp[:], True, True,
                                     reads=[b_ktm, bVp], writes=[bC])
                                if st > 0:
                                    k.tt("dve", Cs[ro:ro + 64, :], Cs[ro:ro + 64, :], pC[ro:ro + 64, 0:129], ALU.add,
                                         reads=[bCs, bC], writes=[bCs])
                                else:
                                    k.copy("dve", Cs[ro:ro + 64, :], pC[ro:ro + 64, 0:129], reads=[bC], writes=[bCs])
                                k.act(dd[:, 0:1], pN[:, 128:129], AF.Abs, reads=[bN], writes=[bdd])
                                k.ts("dve", dd[:, 0:1], dd[:, 0:1], WCt[:, 1, x_, c:c + 1], None, ALU.max,
                                     reads=[bdd, b_WCt], writes=[bdd])
                                k.S.op("dve", lambda e: e.reciprocal(out=dd[:, 1:2], in_=dd[:, 0:1]), reads=[bdd], writes=[bdd])
                                cell = big[:, :, 0:T].rearrange("p r (c v) -> p c r v", v=128)[:, c, r, :]
                                first = (d == 0 and c < 16) or (d == 1 and c >= 16)
                                if first:
                                    k.S.op("act", lambda e: e.activation(out=cell, in_=pN[:, 0:128], func=AF.Copy, scale=dd[:, 1:2]),
                                           reads=[bN, bdd], writes=[b_big])
                                else:
                                    k.stt(cell, pN[:, 0:128], dd[:, 1:2], cell, ALU.mult, ALU.add, reads=[bN, bdd, b_big], writes=[b_big])

                            for it in range(len(insts) + 1):
                                if it < len(insts):
                                    emitA(it)
                                if it >= 1:
                                    emitB(it - 1)
                        with k.scope():
                            wo, b_wo = load_w(w_in, 8, 1024, 1536)
                            xs1 = new([128, 8, 256], F32, "xs")
                            xsL = [xs1, xs1]
                            xmL = [new([128, 8, 256], BF16, "xm2") for _ in range(2)]
                            obL = [new([128, 512], F32, "ob") for _ in range(2)]
                            hgL = [new([128, 512], F32, "hg") for _ in range(2)]
                            hnL = [new([128, 512], BF16, "hnb") for _ in range(2)]
                            stL = [new([128, 4, 6], F32, "bst") for _ in range(2)]
                            mvL = [new([128, 4, 2], F32, "mv") for _ in range(2)]
                            mmL = [new([128, 4, 256], BF16, "mixm") for _ in range(2)]
                            (xs, b_xs) = xs1
                            dma_x1(0, 256, xs, b_xs)
                            mod_x1(xmL[0][0], 256, xs, b_xs, xmL[0][1])
                            for ti in range(16):
                                (xm2, b_xm2), (mm_, b_mm) = xmL[ti % 2], mmL[ti % 2]
                                if ti + 1 < 16:
                                    dma_x1((ti + 1) * 256, 256, xs, b_xs)
                                for cc in range(2):
                                    c = ti * 2 + cc
                                    (ob, b_ob), (hg, b_hg), (hnb, b_hnb), (stt_, b_stt), (mv, b_mv) = obL[cc], hgL[cc], hnL[cc], stL[cc], mvL[cc]
                                    bank = pbanks[cc]; bb = bpb[cc]
                                    for kk in range(8):
                                        k.mm(bank[:, :], xm2[:, kk, cc * 128:(cc + 1) * 128], wo[:, kk, :], kk == 0, kk == 7,
                                             reads=[b_wo, b_xm2], writes=[bb])
                                    k.tt("dve", ob[:], bank[:, :], rowbc[:, R_BO:R_BO + 512], ALU.add, reads=[bb, b_rowbc], writes=[b_ob])
                                    k.act(ob[:], ob[:], AF.Sigmoid, reads=[b_ob], writes=[b_ob])
                                    cellc = big[:, :, 0:T].rearrange("p r (c v) -> p c r v", v=128)[:, c, :, :]
                                    k.tt("dve", hg[:].rearrange("p (r v) -> p r v", r=4), ob[:].rearrange("p (r v) -> p r v", r=4), cellc,
                                         ALU.mult, reads=[b_ob, b_big], writes=[b_hg])
                                    for r in range(4):
                                        k.S.op("dve", lambda e, r=r, stt_=stt_, hg=hg: e.bn_stats(out=stt_[:, r, :], in_=hg[:, r * 128:(r + 1) * 128]),
                                               reads=[b_hg], writes=[b_stt])
                                        k.S.op("dve", lambda e, r=r, stt_=stt_, mv=mv: e.bn_aggr(out=mv[:, r, :], in_=stt_[:, r, :]),
                                               reads=[b_stt], writes=[b_mv])
                                    k.ts("dve", mv[:, :, 1], mv[:, :, 1], LN_EPS, None, ALU.add, reads=[b_mv], writes=[b_mv])
                                    k.act(mv[:, :, 1], mv[:, :, 1], AF.Sqrt, reads=[b_mv], writes=[b_mv])
                                    k.S.op("dve", lambda e, mv=mv: e.reciprocal(out=mv[:, :, 1], in_=mv[:, :, 1]), reads=[b_mv], writes=[b_mv])
                                    for r in range(4):
                                        k.ts("dve", hg[:, r * 128:(r + 1) * 128], hg[:, r * 128:(r + 1) * 128], mv[:, r, 0:1], mv[:, r, 1:2],
                                             ALU.subtract, ALU.mult, reads=[b_hg, b_mv], writes=[b_hg])
                                    k.tt("pool", hnb[:], hg[:], rowbc[:, R_MNG:R_MNG + 512], ALU.mult, reads=[b_hg, b_rowbc], writes=[b_hnb])
                                    bank2 = pbanks[2 + cc]; bb2 = bpb[2 + cc]
                                    for r in range(4):
                                        k.mm(bank2[:, r * 128:(r + 1) * 128], hnb[:, r * 128:(r + 1) * 128], ident_bf[:], True, True,
                                             reads=[b_hnb, b_idbf], writes=[bb2])
                                    k.act(mm_[:, :, cc * 128:(cc + 1) * 128], bank2[:, :].rearrange("p (r t) -> p r t", r=4), AF.Copy,
                                          reads=[bb2], writes=[b_mm])
                                k.dma(MIX[b, 0:512, ti * 256:(ti + 1) * 256].rearrange("(k p) t -> p k t", p=128), mm_[:],
                                      reads=[b_mm], writes=[bMIX[b]])
                                if ti + 1 < 16:
                                    mod_x1(xmL[(ti + 1) % 2][0], 256, xs, b_xs, xmL[(ti + 1) % 2][1])

                    with (k.scope() if phase == "MO" else contextlib.nullcontext()):
                      if phase == "MO" and "O" not in SKIP:
                        wob, b_wob = load_w(w_out, 8, 0, 1024)
                        xs = [new([128, 8, TT], F32, "xs") for _ in range(2)]
                        mx = [new([128, 8, TT], BF16, "mx") for _ in range(2)]
                        zL = [new([128, 8, TT], F32, "z") for _ in range(2)]
                        zbL = [k.sb([128, 8, TT], BF16, "zb") for _ in range(2)]
                        zqL = [k.sb([128, 8, TT], BF16, "zq") for _ in range(2)]
                        bzbqL = [k.buf() for _ in range(2)]
                        smL = [new([128, 6, TT], F32, "small") for _ in range(2)]
                        tmpL = [[k.sb([128, TT], F32, "tmp") for _ in range(2)] for _ in range(2)]
                        btmpL = [[k.buf() for _ in range(2)] for _ in range(2)]
                        def loadO(ti):
                            (xs_, bxs_), (mx_, bmx_) = xs[ti % 2], mx[ti % 2]
                            tsl = slice(ti * TT, (ti + 1) * TT)
                            k.dma(xs_[:], x1v[:, :, tsl], reads=[bX1[b]], writes=[bxs_])
                            k.dma(mx_[:], MIX[b].rearrange("(k p) t -> p k t", p=128)[:, :, tsl], reads=[bMIX[b]], writes=[bmx_])

                        loadO(0)
                        for ti in range(NT):
                            s = ti % 2
                            (xs_, bxs_), (mx_, bmx_) = xs[s], mx[s]
                            tsl = slice(ti * TT, (ti + 1) * TT)
                            if ti + 1 < NT:
                                loadO(ti + 1)
                            (z, b_z), zb, zq, b_zbq = zL[s], zbL[s], zqL[s], bzbqL[s]
                            (small, b_small), tmp, b_tmp = smL[s], tmpL[s], btmpL[s]
                            for dt in range(8):
                                bi = (dt // 2) % 2
                                half = (dt % 2) * 256
                                for kk in range(8):
                                    k.mm(pbanks[bi][:, half:half + TT], wob[:, kk, dt * 128:(dt + 1) * 128], mx_[:, kk, :], kk == 0, kk == 7,
                                         reads=[b_wob, bmx_], writes=[bpb[bi]])
                                k.stt(z[:, dt, :], pbanks[bi][:, half:half + TT], der[:, 5, dt, b:b + 1], xs_[:, dt, :], ALU.mult, ALU.add,
                                      reads=[bpb[bi], bxs_, b_der], writes=[b_z])
                            ln_tail(z, b_z, zb, zq, b_zbq, small, b_small, tmp, b_tmp, xs_, bxs_, 1, TT)
                            k.dma(X2[b].rearrange("(k p) t -> p k t", p=128)[:, :, tsl], xs_[:], reads=[bxs_], writes=[bX2[b]])

        if STAGE >= 2:
            k.stacks.pop()
            mixc.close()
            k.fence_ops = k.S.fence()
        if STAGE == 3:
            with k.scope():
                cvb, b_cvb = k.sb([128, 8, 256], BF16, "cvb"), k.buf()
                cvf, b_cvf = k.sb([128, 8, 256], F32, "cvf"), k.buf()
                for b in range(NB):
                    for ti in range(16):
                        tsl = slice(ti * 256, (ti + 1) * 256)
                        k.dma(cvb[:], MIX[b].rearrange("(k p) t -> p k t", p=128)[:, :, tsl], reads=[bMIX[b]], writes=[b_cvb])
                        k.copy("dve", cvf[:], cvb[:], reads=[b_cvb], writes=[b_cvf])
                        final_ops.append(k.dma(outT[b].rearrange("(k p) t -> p k t", p=128)[:, :, tsl], cvf[:], reads=[b_cvf], writes=[b_out[b]]))
        if STAGE >= 9:
            ffn_pass(X2, bX2, outT, b_out, f2u, f2d, 2, 2, True)

        k.S.emit(final_wait_ops=final_ops)
    return nc


def _pack_inputs(inp):
    f = lambda a: np.ascontiguousarray(np.asarray(a, dtype=np.float32))
    col = np.zeros((128, NCOLP), np.float32)

    def put8(off, v):
        col[:, off:off + 8] = f(v).reshape(8, 128).T

    for i, nm in enumerate(["ln1_g", "ln1_b", "ln2_g", "ln2_b", "ln3_g", "ln3_b"]):
        put8(C_LN + 8 * i, inp[nm][0])
    col[:, C_BADA:C_BADA + 72] = f(inp["b_ada"][0]).reshape(72, 128).T
    b_in = f(inp["b_in"][0])
    col[:, C_BQK:C_BQK + 4] = b_in[0:512].reshape(4, 128).T
    cw = f(inp["qk_conv_w"][0])
    for j in range(5):
        col[:, C_CW + 4 * j:C_CW + 4 * j + 4] = cw[j].reshape(4, 128).T
    col[:, C_CB:C_CB + 4] = f(inp["qk_conv_b"][0]).reshape(4, 128).T
    col[:, C_GLUB:C_GLUB + 4] = f(inp["ssm_glu_b"][0]).reshape(4, 128).T
    col[:, C_SNG:C_SNG + 4] = f(inp["ssm_norm_g"][0]).reshape(4, 128).T
    g0 = 1536
    for i in range(4):
        col[0:4, C_BG + i] = b_in[g0 + 4 * i:g0 + 4 * i + 4]
    col[0:16, C_BG + 4] = b_in[g0:g0 + 16]
    row = np.zeros((1, NROWP), np.float32)
    row[0, R_BV:R_BV + 512] = b_in[512:1024]
    row[0, R_BO:R_BO + 512] = b_in[1024:1536]
    row[0, R_BU:R_BU + 512] = b_in[1552:2064]
    row[0, R_MNG:R_MNG + 512] = f(inp["mlstm_norm_g"][0])
    cst = np.zeros((128, 1024), np.float32)
    cst[:, 0:128] = np.eye(128, dtype=np.float32)
    ii = np.arange(128)
    cst[:, 128:256] = (ii[:, None] <= ii[None, :]).astype(np.float32)
    cst[:, 256:384] = (ii[:, None] >= ii[None, :]).astype(np.float32)
    cst[:, 384:896] = np.arange(512, dtype=np.float32)[None, :]
    a8 = np.arange(8, dtype=np.float32)
    for half, rows in ((0, slice(0, 64)), (1, slice(64, 128))):
        cst[rows, 896:904] = (7 - a8) if half == 0 else a8
        cst[rows, 904:912] = (-a8) if half == 0 else a8
        cst[rows, 912:920] = a8 if half == 0 else (-a8)
        cst[rows, 920:928] = (a8 + 1) if half == 0 else (8 - a8)
    cs2 = np.zeros((128, 1536), np.float32)
    cs2[:, 0:512] = (np.arange(512) % 128 != 0).astype(np.float32)[None, :]
    cs2[:, 512:640] = ((ii[:, None] // 16) <= (ii[None, :] // 16)).astype(np.float32)
    cs2[:, 640:768] = ((ii[:, None] // 16) >= (ii[None, :] // 16)).astype(np.float32)
    cs2[0:32, 768:800] = np.eye(32, dtype=np.float32)[::-1]
    cs2[:, 1344:1472] = np.eye(128, dtype=np.float32)[::-1]
    for r in range(4):
        cs2[r, 800 + r * 128:800 + (r + 1) * 128] = 1.0
    dd_ = f(inp["ssm_d"][0]).reshape(32, 16)
    cs2[:, 1312:1344] = np.tile(dd_.T, (8, 1))
    s5 = np.zeros((2, 64, 32, 70), np.float32)
    s5[..., 0] = f(inp["ssm_lam_re"][0]).transpose(0, 2, 1)
    s5[..., 1] = f(inp["ssm_lam_im"][0]).transpose(0, 2, 1)
    s5[..., 2] = f(inp["ssm_log_step"][0])[:, None, :]
    s5[..., 3:19] = f(inp["ssm_b_re"][0]).transpose(1, 0, 2)[None]
    s5[..., 19:35] = f(inp["ssm_b_im"][0]).transpose(1, 0, 2)[None]
    s5[..., 35:51] = f(inp["ssm_c_re"][0]).transpose(0, 3, 1, 2)
    s5[..., 51:67] = f(inp["ssm_c_im"][0]).transpose(0, 3, 1, 2)
    s5p = np.ascontiguousarray(s5.reshape(128, 32 * 70))
    shared = {
        "w_ada": f(inp["w_ada"][0]), "colpack": col, "rowpack": row, "consts": cst,
        "ffn1_w_up": f(inp["ffn1_w_up"][0]), "ffn1_w_down": f(inp["ffn1_w_down"][0]),
        "ffn2_w_up": f(inp["ffn2_w_up"][0]), "ffn2_w_down": f(inp["ffn2_w_down"][0]),
        "w_in": f(inp["w_in"][0]), "w_out": f(inp["w_out"][0]), "glu_w": f(inp["ssm_glu_w"][0]),
        "s5p": s5p, "consts2": cs2,
    }
    x = np.asarray(inp["x"], dtype=np.float32)
    c = np.asarray(inp["c"], dtype=np.float32)
    maps = []
    for core in range(8):
        m = dict(shared)
        m["xT"] = np.ascontiguousarray(x[2 * core:2 * core + 2].transpose(0, 2, 1))
        cc = c[2 * core:2 * core + 2]
        m["cT"] = np.ascontiguousarray(cc.reshape(2, 8, 128).transpose(2, 1, 0).reshape(128, 16))
        maps.append(m)
    return maps


_NC_CACHE = {}


def kernel(**inputs):
    maps = _pack_inputs(inputs)
    if "nc" not in _NC_CACHE:
        _NC_CACHE["nc"] = build_program()
    nc = _NC_CACHE["nc"]
    res = run_bass_kernel_spmd(nc, maps, core_ids=list(range(8)))
    outs = [r["outT"] for r in res.results]
    full = np.concatenate([o.transpose(0, 2, 1) for o in outs], axis=0)
    return np.ascontiguousarray(full.astype(np.float32))
```

```python
import os
import math
import contextlib
import numpy as np
import concourse.bass as bass
import concourse.mybir as mybir
from concourse.bass_utils import run_bass_kernel_spmd

F32 = mybir.dt.float32
BF16 = mybir.dt.bfloat16
I32 = mybir.dt.int32
AF = mybir.ActivationFunctionType
ALU = mybir.AluOpType
AX = mybir.AxisListType

D = 1024
DF = 2816
T = 4096
NB = 2
TT = 256
NT = T // TT
ALPHA = 2.0 ** 0.25
LN_EPS = 1e-5
EPS_P = LN_EPS / (ALPHA * ALPHA)
IN_COLS = 2064
STAGE = int(os.environ.get("MK_STAGE", "9"))
SKIP = os.environ.get("MK_SKIP", "").split(",")

C_LN = 0
C_BADA = 48
C_BQK = 120
C_CW = 124
C_CB = 144
C_GLUB = 148
C_SNG = 152
C_BG = 156
NCOLP = 168
R_BV, R_BO, R_BU, R_MNG = 0, 512, 1024, 1536
NROWP = 2048


class Buf:
    __slots__ = ("name", "w", "r")

    def __init__(self, name="", init_readers=()):
        self.name = name
        self.w = None
        self.r = list(init_readers)


class Sched:
    ENGS = ("pe", "act", "dve", "pool", "sp")
    DMA_RING = 8
    SEM_MAX = 30000

    def __init__(self, nc):
        self.nc = nc
        self.ops = []
        self.by_eng = {e: [] for e in self.ENGS}

    def op(self, eng, fn, reads=(), writes=(), dma=False):
        oid = len(self.ops)
        deps = set()
        for b in reads:
            if b.w is not None:
                deps.add(b.w)
        for b in writes:
            if b.w is not None:
                deps.add(b.w)
            for r in b.r:
                deps.add(r)
        if eng == "pe":
            deps = {d for d in deps if self.ops[d][0] != "pe"}
        self.ops.append((eng, fn, deps, dma))
        self.by_eng[eng].append(oid)
        for b in reads:
            b.r.append(oid)
        for b in writes:
            b.w = oid
            b.r = []
        return oid

    def fence(self):
        f = []
        for e in self.ENGS:
            lst = self.by_eng[e]
            if not lst:
                continue
            if e == "sp":
                f.extend(lst[-self.DMA_RING:])
            else:
                f.append(lst[-1])
        return f

    def emit(self, final_wait_ops=()):
        nc = self.nc
        ops = self.ops
        needed = [False] * len(ops)
        for (_, _, deps, _) in ops:
            for d in deps:
                needed[d] = True
        for d in final_wait_ops:
            needed[d] = True
        sig = [None] * len(ops)
        cnt = {e: 0 for e in self.ENGS}
        dma_idx = {e: 0 for e in self.ENGS}
        for e in self.ENGS:
            for oid in self.by_eng[e]:
                isdma = ops[oid][3]
                if isdma:
                    n = dma_idx[e]
                    dma_idx[e] += 1
                    sig[oid] = (("d", e, n % self.DMA_RING), 16 * (n // self.DMA_RING + 1))
                elif needed[oid]:
                    c = cnt[e]
                    cnt[e] += 1
                    sig[oid] = (("c", e, c // self.SEM_MAX), c % self.SEM_MAX + 1)
        keys = []
        kset = set()
        for s in sig:
            if s is not None and s[0] not in kset:
                kset.add(s[0])
                keys.append(s[0])
        with contextlib.ExitStack() as st:
            sems = {}
            for k in keys:
                sems[k] = st.enter_context(nc.semaphore("s_%s_%s_%d" % k))
            block = st.enter_context(nc.Block())

            def run_engine(ename, eh):
                seen = {}
                for oid in self.by_eng[ename]:
                    _, fn, deps, isdma = ops[oid]
                    waits = {}
                    for d in deps:
                        k, v = sig[d]
                        if seen.get(k, 0) >= v:
                            continue
                        if waits.get(k, 0) < v:
                            waits[k] = v
                    if isdma:
                        k, v = sig[oid]
                        if v > 16 and seen.get(k, 0) < v - 16 and waits.get(k, 0) < v - 16:
                            waits[k] = v - 16
                    for k, v in waits.items():
                        eh.wait_ge(sems[k], v)
                        seen[k] = v
                    ins = fn(eh)
                    if sig[oid] is not None:
                        k, v = sig[oid]
                        ins.then_inc(sems[k], 16 if isdma else 1)
                if ename == "sp":
                    fw = {}
                    for d in final_wait_ops:
                        k, v = sig[d]
                        if fw.get(k, 0) < v:
                            fw[k] = v
                    for k, v in fw.items():
                        if seen.get(k, 0) < v:
                            eh.wait_ge(sems[k], v)

            @block.tensor
            def _(e):
                run_engine("pe", e)

            @block.scalar
            def _(e):
                run_engine("act", e)

            @block.vector
            def _(e):
                run_engine("dve", e)

            @block.gpsimd
            def _(e):
                run_engine("pool", e)

            @block.sync
            def _(e):
                run_engine("sp", e)


class KB:
    def __init__(self, nc):
        self.nc = nc
        self.S = Sched(nc)
        self.stacks = []
        self.fence_ops = []
        self.uid = 0

    @contextlib.contextmanager
    def scope(self):
        st = contextlib.ExitStack()
        self.stacks.append(st)
        try:
            with st:
                yield
        finally:
            self.stacks.pop()
            self.fence_ops = self.S.fence()

    def sb(self, shape, dt, name=None):
        self.uid += 1
        nm = "%s_%d" % (name or "t", self.uid)
        return self.stacks[-1].enter_context(self.nc.sbuf_tensor(nm, list(shape), dt))

    def buf(self, name=""):
        return Buf(name, self.fence_ops)

    def dma(self, out, in_, reads=(), writes=(), **kw):
        return self.S.op("sp", lambda e: e.dma_start(out=out, in_=in_, **kw), reads, writes, dma=True)

    def mm(self, out, lhsT, rhs, start, stop, reads=(), writes=()):
        return self.S.op("pe", lambda e: e.matmul(out, lhsT=lhsT, rhs=rhs, start=start, stop=stop), reads, writes)

    def act(self, out, in_, func, reads=(), writes=(), scale=1.0, bias=0.0):
        return self.S.op("act", lambda e: e.activation(out=out, in_=in_, func=func, scale=scale, bias=bias), reads, writes)

    def tt(self, eng, out, in0, in1, op, reads=(), writes=()):
        return self.S.op(eng, lambda e: e.tensor_tensor(out=out, in0=in0, in1=in1, op=op), reads, writes)

    def ts(self, eng, out, in0, s1, s2, op0, op1=None, reads=(), writes=()):
        if op1 is None:
            return self.S.op(eng, lambda e: e.tensor_scalar(out=out, in0=in0, scalar1=s1, scalar2=None, op0=op0), reads, writes)
        return self.S.op(eng, lambda e: e.tensor_scalar(out=out, in0=in0, scalar1=s1, scalar2=s2, op0=op0, op1=op1), reads, writes)

    def stt(self, out, in0, scalar, in1, op0, op1, reads=(), writes=()):
        return self.S.op("dve", lambda e: e.scalar_tensor_tensor(out=out, in0=in0, scalar=scalar, in1=in1, op0=op0, op1=op1), reads, writes)

    def copy(self, eng, out, in_, reads=(), writes=()):
        if eng == "act":
            return self.act(out, in_, AF.Copy, reads, writes)
        return self.S.op(eng, lambda e: e.tensor_copy(out=out, in_=in_), reads, writes)


def build_program():
    nc = bass.Bass("TRN2", target_bir_lowering=False)
    k = KB(nc)

    def din(name, shape, dt=F32):
        return nc.dram_tensor(name, list(shape), dt, kind="ExternalInput").ap()

    xT = din("xT", [NB, D, T])
    cT = din("cT", [128, 16])
    w_ada = din("w_ada", [D, 9 * D])
    colpack = din("colpack", [128, NCOLP])
    rowpack = din("rowpack", [1, NROWP])
    consts = din("consts", [128, 1024])
    f1u = din("ffn1_w_up", [D, 2 * DF])
    f1d = din("ffn1_w_down", [DF, D])
    f2u = din("ffn2_w_up", [D, 2 * DF])
    f2d = din("ffn2_w_down", [DF, D])
    w_in = din("w_in", [D, IN_COLS])
    w_out = din("w_out", [D, D])
    glu_w = din("glu_w", [512, 512])
    s5p = din("s5p", [128, 32 * 70])
    consts2 = din("consts2", [128, 1536])
    outT = nc.dram_tensor("outT", [NB, D, T], F32, kind="ExternalOutput").ap()
    X1 = nc.dram_tensor("X1s", [NB, D, T], F32, kind="Internal").ap()
    X2 = nc.dram_tensor("X2s", [NB, D, T], F32, kind="Internal").ap()
    MIX = nc.dram_tensor("MIXs", [NB, D, T], BF16, kind="Internal").ap()
    bX1 = [Buf("X1_%d" % b) for b in range(NB)]
    bX2 = [Buf("X2_%d" % b) for b in range(NB)]
    bMIX = [Buf("MIX_%d" % b) for b in range(NB)]

    final_ops = []
    with k.scope():
        cp = k.sb([128, NCOLP], F32, "colpack"); b_cp = k.buf()
        k.dma(cp[:], colpack[:, :], writes=[b_cp])
        cst = k.sb([128, 1024], F32, "consts"); b_cst = k.buf()
        k.dma(cst[:], consts[:, :], writes=[b_cst])
        ones_bf = k.sb([128, 128], BF16, "ones"); b_ones = k.buf()
        k.S.op("pool", lambda e: e.memset(ones_bf[:], 1.0), writes=[b_ones])
        der = k.sb([128, 9, 8, NB], F32, "der"); b_der = k.buf()
        pbanks = []
        for i in range(8):
            pbanks.append(k.stacks[-1].enter_context(nc.psum_tensor("pb%d" % i, [128, 512], F32)))
        bpb = [k.buf("pb%d" % i) for i in range(8)]

        with k.scope():
            ct = k.sb([128, 16], F32, "ct"); b_ct = k.buf()
            k.dma(ct[:], cT[:, :], writes=[b_ct])
            cact = k.sb([128, 16], F32, "cact"); b_cact = k.buf()
            k.act(cact[:], ct[:], AF.Silu, reads=[b_ct], writes=[b_cact])
            wst = [k.sb([128, 8, 512], F32, "wada") for _ in range(2)]
            b_wst = [k.buf() for _ in range(2)]
            wv = w_ada.rearrange("(k p) c -> p k c", p=128)
            for ch in range(18):
                s = ch % 2
                k.dma(wst[s][:], wv[:, :, ch * 512:(ch + 1) * 512], writes=[b_wst[s]])
                for jj in range(4):
                    j = ch * 4 + jj
                    for kk in range(8):
                        k.mm(pbanks[0][:, 2 * j:2 * j + 2], wst[s][:, kk, jj * 128:(jj + 1) * 128],
                             cact[:, 2 * kk:2 * kk + 2], kk == 0, kk == 7,
                             reads=[b_wst[s], b_cact], writes=[bpb[0]])
            modc = k.sb([128, 72, NB], F32, "modc"); b_modc = k.buf()
            k.tt("dve", modc[:], pbanks[0][:, 0:144].rearrange("p (j b) -> p j b", b=NB),
                 cp[:, C_BADA:C_BADA + 72].unsqueeze(2).to_broadcast([128, 72, NB]), ALU.add,
                 reads=[bpb[0], b_cp], writes=[b_modc])
            for sub in range(3):
                base = sub * 24
                k.copy("dve", der[:, 3 * sub + 1, :, :], modc[:, base:base + 8, :], reads=[b_modc], writes=[b_der])
                k.ts("dve", der[:, 3 * sub, :, :], modc[:, base + 8:base + 16, :], 1.0, None, ALU.add,
                     reads=[b_modc], writes=[b_der])
                gsc = (1.0 if sub == 1 else 0.5) / ALPHA
                k.ts("dve", der[:, 3 * sub + 2, :, :], modc[:, base + 16:base + 24, :], 1.0, gsc, ALU.add, ALU.mult,
                     reads=[b_modc], writes=[b_der])

        def ln_tail(z, b_z, zb, zq, b_zbq, small, b_small, tmp, b_tmp, xo, b_xo, lnidx, ntok):
            ln_tail_a(z, b_z, zb, zq, b_zbq)
            ln_tail_b(z, b_z, zb, zq, b_zbq, small, b_small, tmp, b_tmp, xo, b_xo, lnidx, ntok)

        def ln_tail_a(z, b_z, zb, zq, b_zbq):
            k.act(zb[:], z[:], AF.Copy, reads=[b_z], writes=[b_zbq])
            k.act(zq[:], z[:], AF.Square, reads=[b_z], writes=[b_zbq])

        def ln_tail_b(z, b_z, zb, zq, b_zbq, small, b_small, tmp, b_tmp, xo, b_xo, lnidx, ntok):
            pbs = pbanks[6]
            for dt in range(8):
                k.mm(pbs[:, 0:ntok], ones_bf[:], zb[:, dt, :], dt == 0, dt == 7, reads=[b_zbq, b_ones], writes=[bpb[6]])
            for dt in range(8):
                k.mm(pbs[:, 256:256 + ntok], ones_bf[:], zq[:, dt, :], dt == 0, dt == 7, reads=[b_zbq], writes=[bpb[6]])
            mt, m2, vv, sd, rstd, nmr = [small[:, i, :] for i in range(6)]
            k.ts("dve", mt, pbs[:, 0:ntok], 1.0 / D, None, ALU.mult, reads=[bpb[6]], writes=[b_small])
            k.tt("dve", m2, mt, mt, ALU.mult, reads=[b_small], writes=[b_small])
            k.stt(vv, pbs[:, 256:256 + ntok], 1.0 / D, m2, ALU.mult, ALU.subtract, reads=[bpb[6], b_small], writes=[b_small])
            k.ts("dve", vv, vv, EPS_P, None, ALU.add, reads=[b_small], writes=[b_small])
            k.act(sd, vv, AF.Sqrt, reads=[b_small], writes=[b_small])
            k.S.op("dve", lambda e: e.reciprocal(out=rstd, in_=sd), reads=[b_small], writes=[b_small])
            k.stt(nmr, mt, -1.0, rstd, ALU.mult, ALU.mult, reads=[b_small], writes=[b_small])
            for dt in range(8):
                eng = "dve" if dt % 2 == 0 else "pool"
                tp = tmp[dt % 2]
                k.tt(eng, tp[:], z[:, dt, :], rstd, ALU.mult, reads=[b_z, b_small], writes=[b_tmp[dt % 2]])
                k.tt(eng, tp[:], tp[:], nmr, ALU.add, reads=[b_small, b_tmp[dt % 2]], writes=[b_tmp[dt % 2]])
                k.ts(eng, xo[:, dt, :], tp[:], cp[:, C_LN + 16 * lnidx + dt:C_LN + 16 * lnidx + dt + 1],
                     cp[:, C_LN + 16 * lnidx + 8 + dt:C_LN + 16 * lnidx + 9 + dt], ALU.mult, ALU.add,
                     reads=[b_tmp[dt % 2], b_cp], writes=[b_xo])

        def ffn_pass(src, b_src, dst, b_dst, wu_d, wd_d, sub, lnidx, is_final):
            with k.scope():
                wup = k.sb([128, 8, 2 * DF], BF16, "wup")
                wdn = k.sb([128, 22, D], BF16, "wdn")
                b_wup = [k.buf() for _ in range(22)]
                b_wdn = [k.buf() for _ in range(11)]
                xb = [k.sb([128, 8, TT], F32, "xb") for _ in range(2)]
                b_xb = [k.buf() for _ in range(2)]
                xm = k.sb([128, 8, TT], BF16, "xm"); b_xm = k.buf()
                sgt = [k.sb([128, TT], F32, "sgt") for _ in range(2)]
                b_sgt = [k.buf() for _ in range(2)]
                hh = k.sb([128, 22, TT], BF16, "hh"); b_hh = k.buf()
                z = k.sb([128, 8, TT], F32, "z"); b_z = k.buf()
                zb = k.sb([128, 8, TT], BF16, "zb")
                zq = k.sb([128, 8, TT], BF16, "zq"); b_zbq = k.buf()
                small = k.sb([128, 6, TT], F32, "small"); b_small = k.buf()
                tmp = [k.sb([128, TT], F32, "tmp") for _ in range(2)]
                b_tmp = [k.buf() for _ in range(2)]
                wuv = wu_d.rearrange("(k p) c -> p k c", p=128)
                wdv = wd_d.rearrange("(f p) c -> p f c", p=128)
                engs = ["dve", "act", "pool"]
                n = 0
                for ch in range(22):
                    s = n % 2
                    k.dma(xb[s][:], wuv[:, :, ch * 256:(ch + 1) * 256], writes=[b_xb[s]])
                    k.copy(engs[n % 3], wup[:, :, ch * 256:(ch + 1) * 256], xb[s][:], reads=[b_xb[s]], writes=[b_wup[ch]])
                    n += 1
                for ch in range(11):
                    s = n % 2
                    sv = xb[s][:].rearrange("p a t -> p (a t)").rearrange("p (f c) -> p f c", f=2)
                    k.dma(sv, wdv[:, 2 * ch:2 * ch + 2, :], writes=[b_xb[s]])
                    k.copy(engs[n % 3], wdn[:, 2 * ch:2 * ch + 2, :], sv, reads=[b_xb[s]], writes=[b_wdn[ch]])
                    n += 1
                A = lambda dt, b: der[:, 3 * sub, dt, b:b + 1]
                Bm = lambda dt, b: der[:, 3 * sub + 1, dt, b:b + 1]
                GA = lambda dt, b: der[:, 3 * sub + 2, dt, b:b + 1]
                tiles = [(b, ti) for b in range(NB) for ti in range(NT)]

                def load_x(i):
                    b, ti = tiles[i]
                    s = i % 2
                    k.dma(xb[s][:], src[b].rearrange("(k p) t -> p k t", p=128)[:, :, ti * TT:(ti + 1) * TT],
                          reads=[b_src[b]], writes=[b_xb[s]])

                def make_xm(i):
                    b, ti = tiles[i]
                    s = i % 2
                    for dt in range(8):
                        eng = ("dve", "pool", "act", "pool")[dt % 4]
                        if eng == "act":
                            k.S.op("act", lambda e, dt=dt: e.activation(out=xm[:, dt, :], in_=xb[s][:, dt, :], func=AF.Identity,
                                                                        scale=A(dt, b), bias=Bm(dt, b)),
                                   reads=[b_xb[s], b_der], writes=[b_xm])
                        else:
                            k.ts(eng, xm[:, dt, :], xb[s][:, dt, :], A(dt, b), Bm(dt, b), ALU.mult, ALU.add,
                                 reads=[b_xb[s], b_der], writes=[b_xm])

                load_x(0)
                make_xm(0)

                def finish(i):
                    b, ti = tiles[i]
                    s = i % 2
                    ln_tail_b(z, b_z, zb, zq, b_zbq, small, b_small, tmp, b_tmp, xb[s], b_xb[s], lnidx, TT)
                    o = k.dma(dst[b].rearrange("(k p) t -> p k t", p=128)[:, :, ti * TT:(ti + 1) * TT], xb[s][:],
                              reads=[b_xb[s]], writes=[b_dst[b]])
                    if is_final:
                        final_ops.append(o)

                for i, (b, ti) in enumerate(tiles):
                    s = i % 2
                    for j in range(22):
                        if j == 2:
                            if i > 0:
                                finish(i - 1)
                            if i + 1 < len(tiles):
                                load_x(i + 1)
                        bank = pbanks[j % 2]; bb = bpb[j % 2]
                        for kk in range(8):
                            k.mm(bank[:, 0:TT], wup[:, kk, j * 128:(j + 1) * 128], xm[:, kk, :], kk == 0, kk == 7,
                                 reads=[b_wup[j // 2], b_xm], writes=[bb])
                        for kk in range(8):
                            c0 = DF + j * 128
                            k.mm(bank[:, 256:256 + TT], wup[:, kk, c0:c0 + 128], xm[:, kk, :], kk == 0, kk == 7,
                                 reads=[b_wup[c0 // 256], b_xm], writes=[bb])
                        k.act(sgt[j % 2][:], bank[:, 0:TT], AF.Silu, reads=[bb], writes=[b_sgt[j % 2]])
                        k.tt("dve", hh[:, j, :], sgt[j % 2][:], bank[:, 256:256 + TT], ALU.mult,
                             reads=[bb, b_sgt[j % 2]], writes=[b_hh])
                    if i + 1 < len(tiles):
                        make_xm(i + 1)
                    for dt in range(8):
                        bi = 2 + (dt // 2) % 2
                        half = (dt % 2) * 256
                        for j in range(22):
                            k.mm(pbanks[bi][:, half:half + TT], wdn[:, j, dt * 128:(dt + 1) * 128], hh[:, j, :],
                                 j == 0, j == 21, reads=[b_wdn[j // 2], b_hh], writes=[bpb[bi]])
                        k.stt(z[:, dt, :], pbanks[bi][:, half:half + TT], GA(dt, b), xb[s][:, dt, :], ALU.mult, ALU.add,
                              reads=[bpb[bi], b_xb[s], b_der], writes=[b_z])
                    ln_tail_a(z, b_z, zb, zq, b_zbq)
                finish(len(tiles) - 1)

        b_out = [Buf("out%d" % b) for b in range(NB)]
        if STAGE == 1:
            ffn_pass(xT, [Buf(), Buf()], outT, b_out, f1u, f1d, 0, 0, True)
        if STAGE >= 2 and STAGE != 3:
            ffn_pass(xT, [Buf(), Buf()], X1, bX1, f1u, f1d, 0, 0, False)
        if STAGE == 3:
            bX1 = [Buf(), Buf()]
            X1 = xT
        if STAGE >= 2:
            TWO_PI = 2.0 * math.pi
            mixc = contextlib.ExitStack()
            k.stacks.append(mixc)
            ident = cst[:, 0:128]
            tri = [cst[:, 128:256], cst[:, 256:384]]
            iota = cst[:, 384:896]
            KTc = cst[:, 896:928]
            cs2 = k.sb([128, 1536], F32, "consts2"); b_cs2 = k.buf()
            k.dma(cs2[:], consts2[:, :], writes=[b_cs2])
            rmask = cs2[:, 0:512]
            mle2 = cs2[:, 512:640]
            mge2 = cs2[:, 640:768]
            J32 = cs2[:, 768:800]
            esel = cs2[:, 800:1312]
            dpk = cs2[:, 1312:1344]
            ident_bf = k.sb([128, 128], BF16, "identbf"); b_idbf = k.buf()
            k.copy("dve", ident_bf[:], ident, reads=[b_cst], writes=[b_idbf])
            J_bf = k.sb([128, 128], BF16, "Jbf")
            k.copy("dve", J_bf[:], cs2[:, 1344:1472], reads=[b_cs2], writes=[b_idbf])
            rowbc = k.sb([128, NROWP], F32, "rowbc"); b_rowbc = k.buf()
            k.dma(rowbc[:], rowpack[0:1, :].partition_broadcast(128), writes=[b_rowbc])
            GSd = nc.dram_tensor("GSs", [NB, 4, 4, T], F32, kind="Internal").ap()
            bGS = [Buf() for _ in range(NB)]

            def new(shape, dt, name="w"):
                return k.sb(shape, dt, name), k.buf()

            def load_w(src_ap, rows_k, c0, c1, stage_cols=128):
                wt, b_wt = new([128, rows_k, c1 - c0], BF16, "wbf")
                v = src_ap.rearrange("(k p) c -> p k c", p=128)
                with k.scope():
                    stg = [new([128, rows_k, stage_cols], F32, "wstg") for _ in range(2)]
                    i = 0
                    c = c0
                    while c < c1:
                        w_ = min(stage_cols, c1 - c)
                        st_, bs_ = stg[i % 2]
                        k.dma(st_[:, :, 0:w_], v[:, :, c:c + w_], writes=[bs_])
                        k.copy(("dve", "pool")[i % 2], wt[:, :, c - c0:c - c0 + w_], st_[:, :, 0:w_], reads=[bs_], writes=[b_wt])
                        c += w_
                        i += 1
                return wt, b_wt

            def reduce_angle(x, out, shape, bx, bo):
                tf, btf = new(shape, F32, "raf")
                ti, bti = new(shape, I32, "rai")
                k.ts("dve", tf[:], x, 1.0 / TWO_PI, None, ALU.mult, reads=[bx], writes=[btf])
                k.copy("dve", ti[:], tf[:], reads=[btf], writes=[bti])
                k.copy("dve", tf[:], ti[:], reads=[bti], writes=[btf])
                k.stt(out, tf[:], -TWO_PI, x, ALU.mult, ALU.add, reads=[btf, bx], writes=[bo])

            def sincos(x, sn, cs_, shape, bx, bsn, bcs):
                s2, bs2 = new(shape, F32, "s2")
                s4, bs4 = new(shape, F32, "s4")
                k.act(s2[:], x, AF.Sin, reads=[bx], writes=[bs2], scale=0.5)
                k.act(s4[:], x, AF.Sin, reads=[bx], writes=[bs4], scale=0.25)
                k.tt("dve", cs_, s2[:], s2[:], ALU.mult, reads=[bs2], writes=[bcs])
                k.ts("dve", cs_, cs_, -2.0, 1.0, ALU.mult, ALU.add, reads=[bcs], writes=[bcs])
                k.tt("dve", s4[:], s4[:], s4[:], ALU.mult, reads=[bs4], writes=[bs4])
                k.ts("dve", s4[:], s4[:], -2.0, 1.0, ALU.mult, ALU.add, reads=[bs4], writes=[bs4])
                k.stt(sn, s2[:], 2.0, s4[:], ALU.mult, ALU.mult, reads=[bs2, bs4], writes=[bsn])

            with k.scope():
                mst = contextlib.ExitStack()
                k.stacks.append(mst)
                Tm, b_Tm = new([128, 32, 128], BF16, "Tm")
                PTre, b_PTre = new([128, 32, 128], BF16, "PTre")
                PTim, b_PTim = new([128, 32, 128], BF16, "PTim")
                Qre, b_Qre = new([128, 32, 128], BF16, "Qre")
                Qim, b_Qim = new([128, 32, 128], BF16, "Qim")
                thr, b_thr = new([128, 32], F32, "thr")
                r8, b_r8 = new([128, 32], F32, "r8")
                with k.scope():
                    sp, b_sp = new([128, 32, 70], F32, "s5p")
                    k.dma(sp[:], s5p.rearrange("p (g f) -> p g f", f=70), writes=[b_sp])
                    lr = sp[:, :, 0]; li = sp[:, :, 1]
                    S2 = [128, 32]
                    step, b_step = new(S2, F32)
                    k.act(step[:], sp[:, :, 2], AF.Exp, reads=[b_sp], writes=[b_step])
                    lrs, b_lrs = new(S2, F32)
                    k.tt("dve", lrs[:], lr, step[:], ALU.mult, reads=[b_sp, b_step], writes=[b_lrs])
                    phi, b_phi = new(S2, F32)
                    k.tt("dve", phi[:], li, step[:], ALU.mult, reads=[b_sp, b_step], writes=[b_phi])
                    phr, b_phr = new(S2, F32)
                    reduce_angle(phi[:], phr[:], S2, b_phi, b_phr)
                    th8, b_th8 = new(S2, F32)
                    k.ts("dve", th8[:], phr[:], 8.0, None, ALU.mult, reads=[b_phr], writes=[b_th8])
                    reduce_angle(th8[:], thr[:], S2, b_th8, b_thr)
                    k.act(r8[:], lrs[:], AF.Exp, reads=[b_lrs], writes=[b_r8], scale=8.0)
                    sn1, b_sn1 = new(S2, F32); cs1, b_cs1 = new(S2, F32); mg1, b_mg1 = new(S2, F32)
                    sincos(phr[:], sn1[:], cs1[:], S2, b_phr, b_sn1, b_cs1)
                    k.act(mg1[:], lrs[:], AF.Exp, reads=[b_lrs], writes=[b_mg1])
                    nr, b_nr = new(S2, F32); ni, b_ni = new(S2, F32)
                    k.tt("dve", nr[:], mg1[:], cs1[:], ALU.mult, reads=[b_mg1, b_cs1], writes=[b_nr])
                    k.ts("dve", nr[:], nr[:], -1.0, None, ALU.add, reads=[b_nr], writes=[b_nr])
                    k.tt("dve", ni[:], mg1[:], sn1[:], ALU.mult, reads=[b_mg1, b_sn1], writes=[b_ni])
                    inv, b_inv = new(S2, F32); t0, b_t0 = new(S2, F32); t1, b_t1 = new(S2, F32)
                    k.tt("dve", inv[:], lr, lr, ALU.mult, reads=[b_sp], writes=[b_inv])
                    k.tt("dve", t0[:], li, li, ALU.mult, reads=[b_sp], writes=[b_t0])
                    k.tt("dve", inv[:], inv[:], t0[:], ALU.add, reads=[b_inv, b_t0], writes=[b_inv])
                    k.S.op("dve", lambda e: e.reciprocal(out=inv[:], in_=inv[:]), reads=[b_inv], writes=[b_inv])
                    cfr, b_cfr = new(S2, F32); cfi, b_cfi = new(S2, F32)
                    k.tt("dve", t0[:], nr[:], lr, ALU.mult, reads=[b_nr, b_sp], writes=[b_t0])
                    k.tt("dve", t1[:], ni[:], li, ALU.mult, reads=[b_ni, b_sp], writes=[b_t1])
                    k.tt("dve", t0[:], t0[:], t1[:], ALU.add, reads=[b_t0, b_t1], writes=[b_t0])
                    k.tt("dve", cfr[:], t0[:], inv[:], ALU.mult, reads=[b_t0, b_inv], writes=[b_cfr])
                    k.tt("dve", t0[:], ni[:], lr, ALU.mult, reads=[b_ni, b_sp], writes=[b_t0])
                    k.tt("dve", t1[:], nr[:], li, ALU.mult, reads=[b_nr, b_sp], writes=[b_t1])
                    k.tt("dve", t0[:], t0[:], t1[:], ALU.subtract, reads=[b_t0, b_t1], writes=[b_t0])
                    k.tt("dve", cfi[:], t0[:], inv[:], ALU.mult, reads=[b_t0, b_inv], writes=[b_cfi])
                    S3 = [128, 32, 16]
                    bre = sp[:, :, 3:19]; bim = sp[:, :, 19:35]; cre = sp[:, :, 35:51]; cim = sp[:, :, 51:67]
                    Bcr, b_Bcr = new(S3, F32); Bci, b_Bci = new(S3, F32); u0, b_u0 = new(S3, F32)
                    cfrb = cfr[:].unsqueeze(2).to_broadcast(S3); cfib = cfi[:].unsqueeze(2).to_broadcast(S3)
                    k.tt("dve", Bcr[:], bre, cfrb, ALU.mult, reads=[b_sp, b_cfr], writes=[b_Bcr])
                    k.tt("dve", u0[:], bim, cfib, ALU.mult, reads=[b_sp, b_cfi], writes=[b_u0])
                    k.tt("dve", Bcr[:], Bcr[:], u0[:], ALU.subtract, reads=[b_Bcr, b_u0], writes=[b_Bcr])
                    k.tt("dve", Bci[:], bim, cfrb, ALU.mult, reads=[b_sp, b_cfr], writes=[b_Bci])
                    k.tt("dve", u0[:], bre, cfib, ALU.mult, reads=[b_sp, b_cfi], writes=[b_u0])
                    k.tt("dve", Bci[:], Bci[:], u0[:], ALU.add, reads=[b_Bci, b_u0], writes=[b_Bci])
                    SP = [128, 32, 32]
                    pwr, b_pwr = new(SP, F32); pwi, b_pwi = new(SP, F32)
                    with k.scope():
                        ang, b_ang = new(SP, F32); angr, b_angr = new(SP, F32)
                        ktb = KTc.unsqueeze(1).to_broadcast(SP)
                        k.tt("dve", ang[:], phr[:].unsqueeze(2).to_broadcast(SP), ktb, ALU.mult, reads=[b_phr, b_cst], writes=[b_ang])
                        reduce_angle(ang[:], angr[:], SP, b_ang, b_angr)
                        psn, b_psn = new(SP, F32); pcs, b_pcs = new(SP, F32); pmg, b_pmg = new(SP, F32)
                        sincos(angr[:], psn[:], pcs[:], SP, b_angr, b_psn, b_pcs)
                        k.tt("dve", pmg[:], lrs[:].unsqueeze(2).to_broadcast(SP), ktb, ALU.mult, reads=[b_lrs, b_cst], writes=[b_pmg])
                        k.act(pmg[:], pmg[:], AF.Exp, reads=[b_pmg], writes=[b_pmg])
                        k.tt("dve", pwr[:], pmg[:], pcs[:], ALU.mult, reads=[b_pmg, b_pcs], writes=[b_pwr])
                        k.tt("dve", pwi[:], pmg[:], psn[:], ALU.mult, reads=[b_pmg, b_psn], writes=[b_pwi])
                    S4 = [128, 32, 8, 16]
                    ta, b_ta = new(S4, F32); tb_, b_tb = new(S4, F32)

                    def cprod(ar, ai, bar, bai, off, o_re, b_ore, o_im, b_oim, neg_im):
                        arb = ar.unsqueeze(2).to_broadcast(S4); aib = ai.unsqueeze(2).to_broadcast(S4)
                        pr = pwr[:, :, off:off + 8].unsqueeze(3).to_broadcast(S4)
                        pi_ = pwi[:, :, off:off + 8].unsqueeze(3).to_broadcast(S4)
                        k.tt("dve", ta[:], arb, pr, ALU.mult, reads=[bar, b_pwr], writes=[b_ta])
                        k.tt("pool", tb_[:], aib, pi_, ALU.mult, reads=[bai, b_pwi], writes=[b_tb])
                        k.tt("dve", o_re, ta[:], tb_[:], ALU.subtract, reads=[b_ta, b_tb], writes=[b_ore])
                        k.tt("dve", ta[:], arb, pi_, ALU.mult, reads=[bar, b_pwi], writes=[b_ta])
                        k.tt("pool", tb_[:], aib, pr, ALU.mult, reads=[bai, b_pwr], writes=[b_tb])
                        if neg_im:
                            k.stt(o_im, ta[:], -1.0, tb_[:], ALU.mult, ALU.subtract, reads=[b_ta, b_tb], writes=[b_oim])
                        else:
                            k.tt("dve", o_im, ta[:], tb_[:], ALU.add, reads=[b_ta, b_tb], writes=[b_oim])

                    v4 = lambda t_: t_[:].rearrange("p g (s h) -> p g s h", h=16)
                    if "noQ" not in SKIP:
                        cprod(cre, cim, b_sp, b_sp, 24, v4(Qre), b_Qre, v4(Qim), b_Qim, True)
                    with k.scope():
                      if "noP" not in SKIP:
                        Pr, b_Pr = new([128, 32, 128], F32); Pi, b_Pi = new([128, 32, 128], F32)
                        cprod(Bcr[:], Bci[:], b_Bcr, b_Bci, 0, v4(Pr), b_Pr, v4(Pi), b_Pi, False)
                        for g in range(32):
                            bank = pbanks[g % 2]; bb = bpb[g % 2]
                            k.mm(bank[:, 256:384], Pr[:, g, :], ident, True, True, reads=[b_Pr, b_cst], writes=[bb])
                            k.mm(bank[:, 384:512], Pi[:, g, :], ident, True, True, reads=[b_Pi, b_cst], writes=[bb])
                            k.act(PTre[:, g, :], bank[:, 256:384], AF.Copy, reads=[bb], writes=[b_PTre])
                            k.act(PTim[:, g, :], bank[:, 384:512], AF.Copy, reads=[bb], writes=[b_PTim])
                    with k.scope():
                      if "noT" not in SKIP:
                        Xr, b_Xr = new([128, 32, 128], F32); Xi, b_Xi = new([128, 32, 128], F32)
                        cprod(Bcr[:], Bci[:], b_Bcr, b_Bci, 8, v4(Xr), b_Xr, v4(Xi), b_Xi, False)
                        Zr, b_Zr = new([128, 32, 128], F32); Zi, b_Zi = new([128, 32, 128], F32)
                        cprod(cre, cim, b_sp, b_sp, 16, v4(Zr), b_Zr, v4(Zi), b_Zi, True)
                        Xmr = ta[:].rearrange("p g s h -> p g (s h)"); b_Xmr = b_ta
                        Xmi = tb_[:].rearrange("p g s h -> p g (s h)"); b_Xmi = b_tb
                        hm = [cst[:, 128 + 63:128 + 64], cst[:, 256 + 64:256 + 65]]
                        tA, b_tA = new([128, 128], F32); tB, b_tB = new([128, 128], F32)
                        for dd_ in range(2):
                            k.ts("dve", Xmr, Xr[:], hm[dd_], None, ALU.mult, reads=[b_Xr, b_cst], writes=[b_Xmr])
                            k.ts("pool", Xmi, Xi[:], hm[dd_], None, ALU.mult, reads=[b_Xi, b_cst], writes=[b_Xmi])
                            for g in range(32):
                                bank = pbanks[2 + g % 2]; bb = bpb[2 + g % 2]
                                k.mm(bank[:, 0:128], Xmr[:, g, :], Zr[:, g, :], True, False, reads=[b_Xmr, b_Zr], writes=[bb])
                                k.mm(bank[:, 0:128], Xmi[:, g, :], Zi[:, g, :], False, True, reads=[b_Xmi, b_Zi], writes=[bb])
                                if dd_ == 0:
                                    k.tt("dve", tA[:], bank[:, 0:128], mle2, ALU.mult, reads=[bb, b_cs2], writes=[b_tA])
                                    k.stt(Tm[:, g, :], ident, dpk[:, g:g + 1], tA[:], ALU.mult, ALU.add,
                                          reads=[b_tA, b_cs2, b_cst], writes=[b_Tm])
                                else:
                                    k.tt("dve", tB[:], bank[:, 0:128], mge2, ALU.mult, reads=[bb, b_cs2], writes=[b_tB])
                                    k.tt("dve", Tm[:, g, :], Tm[:, g, :], tB[:], ALU.add, reads=[b_tB, b_Tm], writes=[b_Tm])

                for (b, phase) in ((0, "S"), (1, "S"), (0, "MO"), (1, "MO")):
                    if phase == "MO" and mst is not None:
                        k.stacks.pop()
                        mst.close()
                        mst = None
                        k.fence_ops = k.S.fence()
                    x1v = X1[b].rearrange("(k p) t -> p k t", p=128)

                    def dma_x1(t0, ntok, xs, b_xs):
                        k.dma(xs[:, :, 0:ntok], x1v[:, :, t0:t0 + ntok], reads=[bX1[b]], writes=[b_xs])

                    def mod_x1(dst_ap, ntok, xs, b_xs, b_dst):
                        for dt in range(8):
                            eng = ("dve", "pool")[dt % 2]
                            k.ts(eng, dst_ap[:, dt, :], xs[:, dt, 0:ntok], der[:, 3, dt, b:b + 1], der[:, 4, dt, b:b + 1],
                                 ALU.mult, ALU.add, reads=[b_xs, b_der], writes=[b_dst])

                    def load_xm2(dst_ap, t0, ntok, xs, b_xs, b_dst):
                        k.dma(xs[:, :, 0:ntok], x1v[:, :, t0:t0 + ntok], reads=[bX1[b]], writes=[b_xs])
                        for dt in range(8):
                            eng = ("dve", "pool")[dt % 2]
                            k.ts(eng, dst_ap[:, dt, :], xs[:, dt, 0:ntok], der[:, 3, dt, b:b + 1], der[:, 4, dt, b:b + 1],
                                 ALU.mult, ALU.add, reads=[b_xs, b_der], writes=[b_dst])

                    with (k.scope() if phase == "S" else contextlib.nullcontext()):
                      if phase == "S" and "S" not in SKIP:
                        Zy, b_Zy = new([128, 4, 8, 512], BF16, "Zy")
                        Zta, b_Zta = new([128, 4, 32, 8, 16], BF16, "Zta")
                        with k.scope():
                            wu, b_wu = load_w(w_in, 8, 1552, 2064)
                            xsL2 = [new([128, 8, 128], F32, "xs") for _ in range(2)]
                            xmbL = [new([128, 8, 1024], BF16, "xmb") for _ in range(2)]

                            def prepA(cb):
                                (xmb, b_xmb) = xmbL[cb % 2]
                                for q8 in range(8):
                                    (xs, b_xs) = xsL2[q8 % 2]
                                    load_xm2(xmb[:, :, q8 * 128:(q8 + 1) * 128], cb * 1024 + q8 * 128, 128, xs, b_xs, b_xmb)

                            prepA(0)
                            for cb in range(4):
                                (xmb, b_xmb) = xmbL[cb % 2]
                                if cb + 1 < 4:
                                    prepA(cb + 1)
                                for s_ in range(8):
                                    bank = pbanks[s_ % 2]; bb = bpb[s_ % 2]
                                    for kk in range(8):
                                        lv = xmb[:, kk, :].rearrange("p (c s) -> p s c", s=8)[:, s_, :]
                                        k.mm(bank[:, :], lv, wu[:, kk, :], kk == 0, kk == 7, reads=[b_xmb, b_wu], writes=[bb])
                                    k.tt("dve", Zta[:, cb, :, s_, :], bank[:, :].rearrange("p (g h) -> p g h", h=16),
                                         rowbc[:, R_BU:R_BU + 512].rearrange("p (g h) -> p g h", h=16), ALU.add,
                                         reads=[bb, b_rowbc], writes=[b_Zta])
                        NG = 8
                        for hf_ in range(32 // NG):
                          with k.scope():
                            G0 = NG * hf_
                            Ut, b_Ut = new([128, NG, 512], BF16, "Ut")
                            Ur, b_Ur = new([128, NG, 512], BF16, "Ur")
                            for cb in range(4):
                                for g4 in range(NG // 4):
                                    bank = pbanks[2 + g4 % 2]; bb = bpb[2 + g4 % 2]
                                    bank2 = pbanks[4 + g4 % 2]; bb2 = bpb[4 + g4 % 2]
                                    for gg in range(4):
                                        g = G0 + g4 * 4 + gg
                                        zl = Zta[:, cb, g, :, :].rearrange("p s h -> p (s h)")
                                        k.mm(bank[:, gg * 128:(gg + 1) * 128], zl, ident_bf[:], True, True,
                                             reads=[b_Zta, b_idbf], writes=[bb])
                                        k.mm(bank2[:, gg * 128:(gg + 1) * 128], zl, J_bf[:], True, True,
                                             reads=[b_Zta, b_idbf], writes=[bb2])
                                    k.act(Ut[:, g4 * 4:g4 * 4 + 4, cb * 128:(cb + 1) * 128],
                                          bank[:, :].rearrange("p (g c) -> p g c", g=4), AF.Copy, reads=[bb], writes=[b_Ut])
                                    k.copy("dve", Ur[:, g4 * 4:g4 * 4 + 4, (3 - cb) * 128:(4 - cb) * 128],
                                           bank2[:, :].rearrange("p (g c) -> p g c", g=4), reads=[bb2], writes=[b_Ur])
                            with k.scope():
                                W_ = lambda: new([128, 512], F32, "sw")
                                W2 = lambda: [W_(), W_()]
                                (ta_, bta), (tr_, btr), (s2t, bs2) = W_(), W_(), W_()
                                (ti_, bti) = new([128, 512], I32, "twi")
                                tcs, tsns = W2(), W2()
                                srL, siL, a1L, a2L, stRL, stIL, kRL, kIL = [W2() for _ in range(8)]
                                HpL = [[new([128, 513], BF16, "Hp") for _ in range(4)] for _ in range(2)]
                                for hl in HpL:
                                    for (h_, bh_) in hl:
                                        k.S.op("pool", lambda e, h_=h_: e.memset(h_[:], 0.0), writes=[bh_])
                                ygL = [new([128, 512], BF16, "yg") for _ in range(2)]
                                ygbL = [new([128, 512], BF16, "ygb") for _ in range(2)]
                                ygTL = [new([128, 4, 128], BF16, "ygT") for _ in range(2)]
                                pY, bY = pbanks[0], bpb[0]
                                pYb, bYb = pbanks[1], bpb[1]
                                pAr, bAr = pbanks[2], bpb[2]
                                pBr, bBr = pbanks[3], bpb[3]
                                pAi, bAi = pbanks[4], bpb[4]
                                pBi, bBi = pbanks[5], bpb[5]
                                pT, bT = pbanks[6], bpb[6]
                                pT2, bT2 = pbanks[7], bpb[7]

                                def gen_tw(gl):
                                    g = G0 + gl
                                    (tc, btc), (tsn, bts) = tcs[gl % 2], tsns[gl % 2]
                                    k.S.op("act", lambda e: e.activation(out=ta_[:], in_=iota, func=AF.Copy, scale=thr[:, g:g + 1]),
                                           reads=[b_cst, b_thr], writes=[bta])
                                    k.act(tr_[:], ta_[:], AF.Copy, reads=[bta], writes=[btr], scale=1.0 / TWO_PI)
                                    k.copy("dve", ti_[:], tr_[:], reads=[btr], writes=[bti])
                                    k.copy("dve", tr_[:], ti_[:], reads=[bti], writes=[btr])
                                    k.stt(ta_[:], tr_[:], -TWO_PI, ta_[:], ALU.mult, ALU.add, reads=[btr, bta], writes=[bta])
                                    k.act(s2t[:], ta_[:], AF.Sin, reads=[bta], writes=[bs2], scale=0.5)
                                    k.act(tr_[:], ta_[:], AF.Sin, reads=[bta], writes=[btr], scale=0.25)
                                    k.tt("pool", tc[:], s2t[:], s2t[:], ALU.mult, reads=[bs2], writes=[btc])
                                    k.ts("pool", tc[:], tc[:], -2.0, 1.0, ALU.mult, ALU.add, reads=[btc], writes=[btc])
                                    k.tt("pool", tr_[:], tr_[:], tr_[:], ALU.mult, reads=[btr], writes=[btr])
                                    k.ts("pool", tr_[:], tr_[:], -2.0, 1.0, ALU.mult, ALU.add, reads=[btr], writes=[btr])
                                    k.stt(tsn[:], s2t[:], 2.0, tr_[:], ALU.mult, ALU.mult, reads=[bs2, btr], writes=[bts])

                                def stageA(gl):
                                    g = G0 + gl
                                    (sr, bsr), (si, bsi) = srL[gl % 2], siL[gl % 2]
                                    k.mm(pAr[:, :], PTre[:, g, :], Ut[:, gl, :], True, True, reads=[b_PTre, b_Ut], writes=[bAr])
                                    k.mm(pBr[:, :], PTre[:, g, :], Ur[:, gl, :], True, True, reads=[b_PTre, b_Ur], writes=[bBr])
                                    k.mm(pAi[:, :], PTim[:, g, :], Ut[:, gl, :], True, True, reads=[b_PTim, b_Ut], writes=[bAi])
                                    k.mm(pBi[:, :], PTim[:, g, :], Ur[:, gl, :], True, True, reads=[b_PTim, b_Ur], writes=[bBi])
                                    k.act(sr[0:64, :], pAr[0:64, :], AF.Copy, reads=[bAr], writes=[bsr])
                                    k.act(sr[64:128, :], pBr[64:128, :], AF.Copy, reads=[bBr], writes=[bsr])
                                    k.act(si[0:64, :], pAi[0:64, :], AF.Copy, reads=[bAi], writes=[bsi])
                                    k.act(si[64:128, :], pBi[64:128, :], AF.Copy, reads=[bBi], writes=[bsi])

                                def stageB(gl):
                                    g = G0 + gl
                                    w = gl % 2
                                    (tc, btc), (tsn, bts) = tcs[w], tsns[w]
                                    (sr, bsr), (si, bsi), (a1, ba1), (a2, ba2) = srL[w], siL[w], a1L[w], a2L[w]
                                    (stR, bstR), (stI, bstI), (kR, bkR), (kI, bkI) = stRL[w], stIL[w], kRL[w], kIL[w]
                                    (hrf, bhrf), (hif, bhif), (hrb, bhrb), (hib, bhib) = HpL[w]
                                    (yg, byg), (ygb, bygb), (ygT, bygT) = ygL[w], ygbL[w], ygTL[w]
                                    k.tt("dve", a1[:], sr[:], tc[:], ALU.mult, reads=[bsr, btc], writes=[ba1])
                                    k.tt("pool", a2[:], si[:], tsn[:], ALU.mult, reads=[bsi, bts], writes=[ba2])
                                    k.tt("dve", stR[:], a1[:], a2[:], ALU.add, reads=[ba1, ba2], writes=[bstR])
                                    k.tt("dve", a1[:], si[:], tc[:], ALU.mult, reads=[bsi, btc], writes=[ba1])
                                    k.tt("pool", a2[:], sr[:], tsn[:], ALU.mult, reads=[bsr, bts], writes=[ba2])
                                    k.tt("dve", stI[:], a1[:], a2[:], ALU.subtract, reads=[ba1, ba2], writes=[bstI])
                                    r8b = r8[:, g:g + 1].to_broadcast([128, 512])
                                    k.S.op("dve", lambda e: e.tensor_tensor_scan(out=kR[:], data0=r8b, data1=stR[:], initial=0.0,
                                                                                 op0=ALU.mult, op1=ALU.add),
                                           reads=[bstR, b_r8], writes=[bkR])
                                    k.S.op("dve", lambda e: e.tensor_tensor_scan(out=kI[:], data0=r8b, data1=stI[:], initial=0.0,
                                                                                 op0=ALU.mult, op1=ALU.add),
                                           reads=[bstI, b_r8], writes=[bkI])
                                    if gl + 1 < NG:
                                        gen_tw(gl + 1)
                                    k.tt("dve", a1[:], kR[:], tc[:], ALU.mult, reads=[bkR, btc], writes=[ba1])
                                    k.tt("pool", a2[:], kI[:], tsn[:], ALU.mult, reads=[bkI, bts], writes=[ba2])
                                    k.tt("dve", hrf[0:64, 1:513], a1[0:64, :], a2[0:64, :], ALU.subtract, reads=[ba1, ba2], writes=[bhrf])
                                    k.tt("dve", hrb[64:128, 1:513], a1[64:128, :], a2[64:128, :], ALU.subtract, reads=[ba1, ba2], writes=[bhrb])
                                    k.tt("dve", a1[:], kI[:], tc[:], ALU.mult, reads=[bkI, btc], writes=[ba1])
                                    k.tt("pool", a2[:], kR[:], tsn[:], ALU.mult, reads=[bkR, bts], writes=[ba2])
                                    k.tt("dve", hif[0:64, 1:513], a1[0:64, :], a2[0:64, :], ALU.add, reads=[ba1, ba2], writes=[bhif])
                                    k.tt("dve", hib[64:128, 1:513], a1[64:128, :], a2[64:128, :], ALU.add, reads=[ba1, ba2], writes=[bhib])
                                    k.mm(pY[:, :], Tm[:, g, :], Ut[:, gl, :], True, False, reads=[b_Tm, b_Ut], writes=[bY])
                                    k.mm(pY[:, :], Qre[:, g, :], hrf[:, 0:512], False, False, reads=[b_Qre, bhrf], writes=[bY])
                                    k.mm(pY[:, :], Qim[:, g, :], hif[:, 0:512], False, True, reads=[b_Qim, bhif], writes=[bY])
                                    k.mm(pYb[:, :], Qre[:, g, :], hrb[:, 0:512], True, False, reads=[b_Qre, bhrb], writes=[bYb])
                                    k.mm(pYb[:, :], Qim[:, g, :], hib[:, 0:512], False, True, reads=[b_Qim, bhib], writes=[bYb])
                                    k.act(yg[:], pY[:, :], AF.Copy, reads=[bY], writes=[byg])
                                    k.act(ygb[:], pYb[:, :], AF.Copy, reads=[bYb], writes=[bygb])
                                    for jb in range(4):
                                        k.mm(pT2[:, jb * 128:(jb + 1) * 128], ygb[:, jb * 128:(jb + 1) * 128], ident_bf[:], True, True,
                                             reads=[bygb, b_idbf], writes=[bT2])
                                    k.copy("dve", ygT[:], pT2[:, :].rearrange("p (j x) -> p j x", j=4), reads=[bT2], writes=[bygT])
                                    for cb in range(4):
                                        k.mm(pT[:, cb * 128:(cb + 1) * 128], yg[:, cb * 128:(cb + 1) * 128], ident_bf[:], True, False,
                                             reads=[byg, b_idbf], writes=[bT])
                                        k.mm(pT[:, cb * 128:(cb + 1) * 128], J_bf[:], ygT[:, 3 - cb, :], False, True,
                                             reads=[bygT, b_idbf], writes=[bT])
                                    k.copy("dve", Zy[:, :, :, 16 * g:16 * g + 16],
                                           pT[:, :].rearrange("p (c t h) -> p c t h", c=4, t=8), reads=[bT], writes=[b_Zy])

                                gen_tw(0)
                                stageA(0)
                                for gl in range(NG):
                                    if gl + 1 < NG:
                                        stageA(gl + 1)
                                    stageB(gl)
                        with k.scope():
                            glw, b_glw = load_w(glu_w, 4, 0, 512)
                            yb, b_yb = new([128, 4, 1024], F32, "yb")
                            u1, b_u1 = new([128, 4, 1024], F32, "u1")
                            yab, b_yab = new([128, 4, 1024], BF16, "yab")
                            sqb, b_sqb = new([128, 4, 1024], BF16, "sqb")
                            sg_, b_sg = new([128, 512], F32, "sg")
                            rs_, b_rs = new([128, 512], F32, "rs")
                            mo, b_mo = new([128, 4, 1024], BF16, "mo")
                            GC = 2.0 * math.sqrt(2.0 / math.pi)
                            for cb in range(4):
                                for ct in range(4):
                                    for th in range(2):
                                        bank = pbanks[(ct * 2 + th) % 2]; bb = bpb[(ct * 2 + th) % 2]
                                        for t4 in range(4):
                                            t_ = th * 4 + t4
                                            k.mm(bank[:, t4 * 128:(t4 + 1) * 128], Zy[:, cb, t_, ct * 128:(ct + 1) * 128], ident_bf[:],
                                                 True, True, reads=[b_Zy, b_idbf], writes=[bb])
                                        k.act(yb[:, ct, :].rearrange("p (c s) -> p s c", s=8)[:, th * 4:th * 4 + 4, :],
                                              bank[:, :].rearrange("p (s c) -> p s c", s=4), AF.Copy, reads=[bb], writes=[b_yb])
                                k.act(u1[:], yb[:], AF.Square, reads=[b_yb], writes=[b_u1])
                                k.ts("dve", u1[:], u1[:], 0.044715, 1.0, ALU.mult, ALU.add, reads=[b_u1], writes=[b_u1])
                                k.tt("pool", u1[:], u1[:], yb[:], ALU.mult, reads=[b_u1, b_yb], writes=[b_u1])
                                k.act(u1[:], u1[:], AF.Sigmoid, reads=[b_u1], writes=[b_u1], scale=GC)
                                k.tt("dve", yb[:], yb[:], u1[:], ALU.mult, reads=[b_u1, b_yb], writes=[b_yb])
                                k.copy("pool", yab[:], yb[:], reads=[b_yb], writes=[b_yab])
                                for hf in range(2):
                                    tk = slice(hf * 512, (hf + 1) * 512)
                                    for co in range(4):
                                        bank = pbanks[2 + co % 2]; bb = bpb[2 + co % 2]
                                        for ci in range(4):
                                            k.mm(bank[:, :], glw[:, ci, co * 128:(co + 1) * 128], yab[:, ci, tk], ci == 0, ci == 3,
                                                 reads=[b_glw, b_yab], writes=[bb])
                                        k.act(sg_[:], bank[:, :], AF.Sigmoid, reads=[bb, b_cp], writes=[b_sg],
                                              bias=cp[:, C_GLUB + co:C_GLUB + co + 1])
                                        k.tt("dve", yb[:, co, tk], yb[:, co, tk], sg_[:], ALU.mult, reads=[b_yb, b_sg], writes=[b_yb])
                                k.act(sqb[:], yb[:], AF.Square, reads=[b_yb], writes=[b_sqb])
                                for hf in range(2):
                                    tk = slice(hf * 512, (hf + 1) * 512)
                                    bank = pbanks[4 + hf]; bb = bpb[4 + hf]
                                    for ci in range(4):
                                        k.mm(bank[:, :], ones_bf[:], sqb[:, ci, tk], ci == 0, ci == 3, reads=[b_sqb, b_ones], writes=[bb])
                                    k.ts("dve", rs_[:], bank[:, :], 1.0 / 512.0, LN_EPS, ALU.mult, ALU.add, reads=[bb], writes=[b_rs])
                                    k.act(rs_[:], rs_[:], AF.Sqrt, reads=[b_rs], writes=[b_rs])
                                    k.S.op("dve", lambda e: e.reciprocal(out=rs_[:], in_=rs_[:]), reads=[b_rs], writes=[b_rs])
                                    for co in range(4):
                                        k.stt(mo[:, co, tk], yb[:, co, tk], cp[:, C_SNG + co:C_SNG + co + 1], rs_[:], ALU.mult, ALU.mult,
                                              reads=[b_yb, b_rs, b_cp], writes=[b_mo])
                                k.dma(MIX[b, 512:1024, cb * 1024:(cb + 1) * 1024].rearrange("(k p) t -> p k t", p=128), mo[:],
                                      reads=[b_mo], writes=[bMIX[b]])

                    with (k.scope() if phase == "MO" else contextlib.nullcontext()):
                      if phase == "MO" and "M" not in SKIP:
                        qf, b_qf = new([128, 2, T], BF16, "qf")
                        kf, b_kf = new([128, 2, T], BF16, "kf")
                        vext, b_vext = new([128, 32, 4, 129], BF16, "vext")
                        big, b_big = new([128, 4, T + 4], F32, "big")
                        k.S.op("pool", lambda e: e.memset(vext[:, :, :, 128:129], 1.0), writes=[b_vext])
                        k.S.op("pool", lambda e: e.memset(big[:, :, 0:2], 0.0), writes=[b_big])
                        k.S.op("pool", lambda e: e.memset(big[:, :, T + 2:T + 4], 0.0), writes=[b_big])
                        with k.scope():
                            wqk, b_wqk = load_w(w_in, 8, 0, 512)
                            wv, b_wv = load_w(w_in, 8, 512, 1024)
                            wg, b_wg = load_w(w_in, 8, 1536, 1552, 16)
                            xs, b_xs = new([128, 8, 256], F32, "xs")
                            xm2L = [new([128, 8, 256], BF16, "xm2") for _ in range(2)]
                            gt, b_gt = new([16, 256], F32, "gt")
                            dma_x1(0, 256, xs, b_xs)
                            mod_x1(xm2L[0][0], 256, xs, b_xs, xm2L[0][1])
                            for ti in range(16):
                                t0_ = ti * 256
                                (xm2, b_xm2) = xm2L[ti % 2]
                                if ti + 1 < 16:
                                    dma_x1(t0_ + 256, 256, xs, b_xs)
                                for ct in range(4):
                                    bank = pbanks[ct // 2]; bb = bpb[ct // 2]
                                    half = (ct % 2) * 256
                                    for kk in range(8):
                                        k.mm(bank[:, half:half + 256], wqk[:, kk, ct * 128:(ct + 1) * 128], xm2[:, kk, :], kk == 0, kk == 7,
                                             reads=[b_wqk, b_xm2], writes=[bb])
                                    k.act(big[:, ct, 2 + t0_:2 + t0_ + 256], bank[:, half:half + 256], AF.Identity, reads=[bb, b_cp],
                                          writes=[b_big], bias=cp[:, C_BQK + ct:C_BQK + ct + 1])
                                for cc in range(2):
                                    c = ti * 2 + cc
                                    bank = pbanks[2 + cc]; bb = bpb[2 + cc]
                                    for kk in range(8):
                                        k.mm(bank[:, :], xm2[:, kk, cc * 128:(cc + 1) * 128], wv[:, kk, :], kk == 0, kk == 7,
                                             reads=[b_wv, b_xm2], writes=[bb])
                                    k.tt("dve", vext[:, c, :, 0:128], bank[:, :].rearrange("p (r v) -> p r v", r=4),
                                         rowbc[:, R_BV:R_BV + 512].rearrange("p (r v) -> p r v", r=4), ALU.add,
                                         reads=[bb, b_rowbc], writes=[b_vext])
                                bank = pbanks[4]; bb = bpb[4]
                                for kk in range(8):
                                    k.mm(bank[0:16, 0:256], wg[:, kk, 0:16], xm2[:, kk, :], kk == 0, kk == 7,
                                         reads=[b_wg, b_xm2], writes=[bb])
                                k.act(gt[0:16, :], bank[0:16, 0:256], AF.Identity, reads=[bb, b_cp], writes=[b_gt],
                                      bias=cp[0:16, C_BG + 4:C_BG + 5])
                                k.dma(GSd[b, :, :, t0_:t0_ + 256].rearrange("k r t -> (k r) t"), gt[:], reads=[b_gt], writes=[bGS[b]])
                                if ti + 1 < 16:
                                    mod_x1(xm2L[(ti + 1) % 2][0], 256, xs, b_xs, xm2L[(ti + 1) % 2][1])
                        ktm, b_ktm = new([128, 32, 256], BF16, "ktm")
                        with k.scope():
                            acc, b_acc = new([128, T], F32, "acc")
                            for ct in range(4):
                                cw = lambda j: cp[:, C_CW + 4 * j + ct:C_CW + 4 * j + ct + 1]
                                k.ts("dve", acc[:], big[:, ct, 0:T], cw(0), None, ALU.mult, reads=[b_big, b_cp], writes=[b_acc])
                                for j in range(1, 5):
                                    k.stt(acc[:], big[:, ct, j:j + T], cw(j), acc[:], ALU.mult, ALU.add, reads=[b_big, b_cp, b_acc], writes=[b_acc])
                                cb_ = cp[:, C_CB + ct:C_CB + ct + 1]
                                if ct < 2:
                                    k.act(acc[:], acc[:], AF.Silu, reads=[b_acc, b_cp], writes=[b_acc], bias=cb_)
                                    k.ts("pool", qf[:, ct, :], acc[:], 0.125, None, ALU.mult, reads=[b_acc], writes=[b_qf])
                                else:
                                    k.act(kf[:, ct - 2, :], acc[:], AF.Silu, reads=[b_acc, b_cp], writes=[b_kf], bias=cb_)
                            for c2 in range(16):
                                bank = pbanks[c2 % 2]; bb = bpb[c2 % 2]
                                for cc in range(2):
                                    c = c2 * 2 + cc
                                    for kt in range(2):
                                        o_ = cc * 256 + kt * 128
                                        k.mm(bank[:, o_:o_ + 128], kf[:, kt, c * 128:(c + 1) * 128], ident_bf[:], True, True,
                                             reads=[b_kf, b_idbf], writes=[bb])
                                k.act(ktm[:, c2 * 2:c2 * 2 + 2, :], bank[:, :].rearrange("p (c x) -> p c x", c=2), AF.Copy,
                                      reads=[bb], writes=[b_ktm])
                        WCt, b_WCt = new([128, 2, 8, 32], F32, "WCt")
                        DECt, b_DECt = new([128, 8, 32], F32, "DECt")
                        with k.scope():
                            G5 = [32, 4, 4, 128]
                            Gc, b_Gc = new(G5, F32, "Gc")
                            k.dma(Gc[:], GSd[b].rearrange("k r (c t) -> c k r t", t=128), reads=[bGS[b]], writes=[b_Gc])
                            D4 = [32, 2, 4, 128]
                            spl, b_spl = new(D4, F32); BS, b_BS = new(D4, F32); ee, b_ee = new(D4, F32)
                            k.act(spl[:], Gc[:, 2:4, :, :], AF.Exp, reads=[b_Gc], writes=[b_spl], scale=-1.0)
                            k.act(spl[:], spl[:], AF.Ln, reads=[b_spl], writes=[b_spl], bias=1.0)
                            flat = lambda t_, d: t_[:, d, :, :].rearrange("c r t -> c (r t)")
                            for d in range(2):
                                k.S.op("dve", lambda e, d=d: e.tensor_tensor_scan(out=flat(BS, d), data0=rmask[0:32, :], data1=flat(spl, d),
                                                                                  initial=0.0, op0=ALU.mult, op1=ALU.add),
                                       reads=[b_spl, b_cs2], writes=[b_BS])
                            tot = BS[:, 1, :, 127:128].to_broadcast([32, 4, 128])
                            k.tt("dve", ee[:, 1, :, :], spl[:, 1, :, :], BS[:, 1, :, :], ALU.subtract, reads=[b_spl, b_BS], writes=[b_ee])
                            k.tt("dve", BS[:, 1, :, :], ee[:, 1, :, :], tot, ALU.add, reads=[b_ee, b_BS], writes=[b_BS])
                            k.tt("dve", ee[:], Gc[:, 0:2, :, :], BS[:], ALU.add, reads=[b_Gc, b_BS], writes=[b_ee])
                            emx, b_emx = new([32, 2, 4], F32)
                            k.S.op("dve", lambda e: e.tensor_reduce(out=emx[:], in_=ee[:], axis=AX.X, op=ALU.max), reads=[b_ee], writes=[b_emx])
                            nbe, b_nbe = new([32, 2, 4], F32)
                            k.ts("dve", nbe[:, 0, :], BS[:, 0, :, 127], -1.0, None, ALU.mult, reads=[b_BS], writes=[b_nbe])
                            k.ts("dve", nbe[:, 1, :], BS[:, 1, :, 0], -1.0, None, ALU.mult, reads=[b_BS], writes=[b_nbe])
                            pbk = pbanks[0]; bbk = bpb[0]
                            for d in range(2):
                                rhs_ = ident[0:32, 0:32] if d == 0 else J32[0:32, :]
                                k.mm(pbk[0:4, d * 64:d * 64 + 32], emx[:, d, :], rhs_, True, True, reads=[b_emx, b_cst, b_cs2], writes=[bbk])
                                k.mm(pbk[0:4, d * 64 + 32:d * 64 + 64], nbe[:, d, :], rhs_, True, True, reads=[b_nbe, b_cst, b_cs2], writes=[bbk])
                            rows, b_rows = new([4, 2, 2, 32], F32)
                            k.copy("dve", rows[:], pbk[0:4, 0:128].rearrange("r (d x j) -> r d x j", d=2, x=2), reads=[bbk], writes=[b_rows])
                            mrow, b_mrow = new([4, 2, 33], F32)
                            k.S.op("pool", lambda e: e.memset(mrow[:], 0.0), writes=[b_mrow])
                            Rrow, b_Rrow = new([4, 2, 32], F32); drow, b_drow = new([4, 2, 32], F32)
                            for d in range(2):
                                k.S.op("dve", lambda e, d=d: e.tensor_tensor_scan(out=mrow[:, d, 1:33], data0=rows[:, d, 0, :], data1=rows[:, d, 1, :],
                                                                                  initial=0.0, op0=ALU.max, op1=ALU.add),
                                       reads=[b_rows, b_mrow], writes=[b_mrow])
                            k.tt("dve", Rrow[:], mrow[:, :, 0:32], rows[:, :, 0, :], ALU.max, reads=[b_mrow, b_rows], writes=[b_Rrow])
                            k.tt("dve", drow[:], mrow[:, :, 0:32], Rrow[:], ALU.subtract, reads=[b_mrow, b_Rrow], writes=[b_drow])
                            k.act(drow[:], drow[:], AF.Exp, reads=[b_drow], writes=[b_drow])
                            pbd = pbanks[1]; bbd = bpb[1]
                            for d in range(2):
                                for r in range(4):
                                    o_ = (d * 4 + r) * 32
                                    k.mm(pbd[:, o_:o_ + 32], esel[0:4, r * 128:(r + 1) * 128], drow[:, d, :], True, True,
                                         reads=[b_cs2, b_drow], writes=[bbd])
                            k.copy("dve", DECt[:], pbd[:, 0:256].rearrange("p (x j) -> p x j", x=8), reads=[bbd], writes=[b_DECt])
                            pbr = pbanks[2]; bbr = bpb[2]
                            k.mm(pbr[0:32, 0:4], Rrow[:, 0, :], ident[0:4, 0:4], True, True, reads=[b_Rrow, b_cst], writes=[bbr])
                            k.mm(pbr[0:32, 4:8], Rrow[:, 1, :], ident[0:4, 0:4], True, True, reads=[b_Rrow, b_cst], writes=[bbr])
                            RbT, b_RbT = new([32, 8], F32)
                            k.copy("dve", RbT[:], pbr[0:32, 0:8], reads=[bbr], writes=[b_RbT])
                            k.mm(pbr[0:32, 8:12], J32[0:32, :], RbT[:, 4:8], True, True, reads=[b_RbT, b_cs2], writes=[bbr])
                            Rc, b_Rc = new([32, 2, 4], F32)
                            k.copy("dve", Rc[:, 0, :], RbT[:, 0:4], reads=[b_RbT], writes=[b_Rc])
                            k.copy("dve", Rc[:, 1, :], pbr[0:32, 8:12], reads=[bbr], writes=[b_Rc])
                            Rcb = Rc[:].unsqueeze(3).to_broadcast(D4)
                            k.tt("dve", ee[:], ee[:], Rcb, ALU.subtract, reads=[b_ee, b_Rc], writes=[b_ee])
                            k.act(ee[:], ee[:], AF.Exp, reads=[b_ee], writes=[b_ee])
                            k.tt("dve", BS[:], BS[:], Rcb, ALU.subtract, reads=[b_BS, b_Rc], writes=[b_BS])
                            k.act(BS[:], BS[:], AF.Exp, reads=[b_BS], writes=[b_BS])
                            pbw = pbanks[3]; bbw = bpb[3]
                            for x_, src_, bsrc in ((0, ee, b_ee), (1, BS, b_BS)):
                                for d in range(2):
                                    for r in range(4):
                                        o_ = (x_ * 8 + d * 4 + r) * 32
                                        k.mm(pbw[:, o_:o_ + 32], src_[:, d, r, :], ident[0:32, 0:32], True, True, reads=[bsrc, b_cst], writes=[bbw])
                            k.copy("dve", WCt[:], pbw[:, :].rearrange("p (x y c) -> p x y c", x=2, y=8), reads=[bbw], writes=[b_WCt])
                        with k.scope():
                            Cst = [[new([128, 129], F32, "Cst") for r in range(4)] for d in range(2)]
                            NR = 4
                            STb = [new([128, 128], BF16, "ST") for _ in range(NR)]
                            Vpb = [new([128, 129], BF16, "Vp") for _ in range(NR)]
                            Cdb = [new([128, 129], BF16, "Cd") for _ in range(NR)]
                            ddb = [new([128, 2], F32, "dd") for _ in range(NR)]
                            insts = [(st, d, r) for st in range(32) for d in range(2) for r in range(4)]

                            def geom(it):
                                st, d, r = insts[it]
                                c = st if d == 0 else 31 - st
                                return st, d, r, c, slice(c * 128, (c + 1) * 128), r // 2, 64 * (r % 2), d * 4 + r, it % NR

                            def emitA(it):
                                st, d, r, c, tok, kt, ro, x_, w = geom(it)
                                pS = pbanks[it % 2]; bS = bpb[it % 2]
                                (ST, bST), (Vp, bVp), (Cd, bCd) = STb[w], Vpb[w], Cdb[w]
                                (Cs, bCs) = Cst[d][r]
                                k.mm(pS[:, 0:128], kf[ro:ro + 64, kt, tok], qf[ro:ro + 64, kt, tok], True, True,
                                     reads=[b_kf, b_qf], writes=[bS])
                                k.tt("dve", ST[:], pS[:, 0:128], tri[d], ALU.mult, reads=[bS, b_cst], writes=[bST])
                                zc = cs2[:, 1500:1501]
                                k.ts("pool", Vp[:], vext[:, c, r, :], WCt[:, 0, x_, c:c + 1], zc, ALU.mult, ALU.add,
                                     reads=[b_vext, b_WCt, b_cs2], writes=[bVp])
                                if st > 0:
                                    k.ts("pool", Cs[ro:ro + 64, :], Cs[ro:ro + 64, :], DECt[ro:ro + 64, x_, st:st + 1], zc[ro:ro + 64, :],
                                         ALU.mult, ALU.add, reads=[bCs, b_DECt, b_cs2], writes=[bCs])
                                    k.act(Cd[ro:ro + 64, :], Cs[ro:ro + 64, :], AF.Copy, reads=[bCs], writes=[bCd])

                            def emitB(it):
                                st, d, r, c, tok, kt, ro, x_, w = geom(it)
                                pN = pbanks[2 + it % 2]; bN = bpb[2 + it % 2]
                                pC = pbanks[4 + it % 2]; bC = bpb[4 + it % 2]
                                (ST, bST), (Vp, bVp), (Cd, bCd), (dd, bdd) = STb[w], Vpb[w], Cdb[w], ddb[w]
                                (Cs, bCs) = Cst[d][r]
                                k.mm(pN[:, 0:129], ST[:], Vp[:], True, st == 0, reads=[bST, bVp], writes=[bN])
                                if st > 0:
                                    k.mm(pN[:, 0:129], qf[ro:ro + 64, kt, tok], Cd[ro:ro + 64, :], False, True,
                                         reads=[b_qf, bCd], writes=[bN])
                                k.mm(pC[ro:ro + 64, 0:129], ktm[:, c, r * 64:(r + 1) * 64], Vp[:], True, True,
                                     reads=[b_ktm, bVp], writes=[bC])
                                if st > 0:
                                    k.tt("dve", Cs[ro:ro + 64, :], Cs[ro:ro + 64, :], pC[ro:ro + 64, 0:129], ALU.add,
                                         reads=[bCs, bC], writes=[bCs])
                                else:
                                    k.copy("dve", Cs[ro:ro + 64, :], pC[ro:ro + 64, 0:129], reads=[bC], writes=[bCs])
                                k.act(dd[:, 0:1], pN[:, 128:129], AF.Abs, reads=[bN], writes=[bdd])
                                k.ts("dve", dd[:, 0:1], dd[:, 0:1], WCt[:, 1, x_, c:c + 1], None, ALU.max,
                                     reads=[bdd, b_WCt], writes=[bdd])
                                k.S.op("dve", lambda e: e.reciprocal(out=dd[:, 1:2], in_=dd[:, 0:1]), reads=[bdd], writes=[bdd])
                                cell = big[:, :, 0:T].rearrange("p r (c v) -> p c r v", v=128)[:, c, r, :]
                                first = (d == 0 and c < 16) or (d == 1 and c >= 16)
                                if first:
                                    k.S.op("act", lambda e: e.activation(out=cell, in_=pN[:, 0:128], func=AF.Copy, scale=dd[:, 1:2]),
                                           reads=[bN, bdd], writes=[b_big])
                                else:
                                    k.stt(cell, pN[:, 0:128], dd[:, 1:2], cell, ALU.mult, ALU.add, reads=[bN, bdd, b_big], writes=[b_big])

                            for it in range(len(insts) + 1):
                                if it < len(insts):
                                    emitA(it)
                                if it >= 1:
                                    emitB(it - 1)
                        with k.scope():
                            wo, b_wo = load_w(w_in, 8, 1024, 1536)
                            xs1 = new([128, 8, 256], F32, "xs")
                            xsL = [xs1, xs1]
                            xmL = [new([128, 8, 256], BF16, "xm2") for _ in range(2)]
                            obL = [new([128, 512], F32, "ob") for _ in range(2)]
                            hgL = [new([128, 512], F32, "hg") for _ in range(2)]
                            hnL = [new([128, 512], BF16, "hnb") for _ in range(2)]
                            stL = [new([128, 4, 6], F32, "bst") for _ in range(2)]
                            mvL = [new([128, 4, 2], F32, "mv") for _ in range(2)]
                            mmL = [new([128, 4, 256], BF16, "mixm") for _ in range(2)]
                            (xs, b_xs) = xs1
                            dma_x1(0, 256, xs, b_xs)
                            mod_x1(xmL[0][0], 256, xs, b_xs, xmL[0][1])
                            for ti in range(16):
                                (xm2, b_xm2), (mm_, b_mm) = xmL[ti % 2], mmL[ti % 2]
                                if ti + 1 < 16:
                                    dma_x1((ti + 1) * 256, 256, xs, b_xs)
                                for cc in range(2):
                                    c = ti * 2 + cc
                                    (ob, b_ob), (hg, b_hg), (hnb, b_hnb), (stt_, b_stt), (mv, b_mv) = obL[cc], hgL[cc], hnL[cc], stL[cc], mvL[cc]
                                    bank = pbanks[cc]; bb = bpb[cc]
                                    for kk in range(8):
                                        k.mm(bank[:, :], xm2[:, kk, cc * 128:(cc + 1) * 128], wo[:, kk, :], kk == 0, kk == 7,
                                             reads=[b_wo, b_xm2], writes=[bb])
                                    k.tt("dve", ob[:], bank[:, :], rowbc[:, R_BO:R_BO + 512], ALU.add, reads=[bb, b_rowbc], writes=[b_ob])
                                    k.act(ob[:], ob[:], AF.Sigmoid, reads=[b_ob], writes=[b_ob])
                                    cellc = big[:, :, 0:T].rearrange("p r (c v) -> p c r v", v=128)[:, c, :, :]
                                    k.tt("dve", hg[:].rearrange("p (r v) -> p r v", r=4), ob[:].rearrange("p (r v) -> p r v", r=4), cellc,
                                         ALU.mult, reads=[b_ob, b_big], writes=[b_hg])
                                    for r in range(4):
                                        k.S.op("dve", lambda e, r=r, stt_=stt_, hg=hg: e.bn_stats(out=stt_[:, r, :], in_=hg[:, r * 128:(r + 1) * 128]),
                                               reads=[b_hg], writes=[b_stt])
                                        k.S.op("dve", lambda e, r=r, stt_=stt_, mv=mv: e.bn_aggr(out=mv[:, r, :], in_=stt_[:, r, :]),
                                               reads=[b_stt], writes=[b_mv])
                                    k.ts("dve", mv[:, :, 1], mv[:, :, 1], LN_EPS, None, ALU.add, reads=[b_mv], writes=[b_mv])
                                    k.act(mv[:, :, 1], mv[:, :, 1], AF.Sqrt, reads=[b_mv], writes=[b_mv])
                                    k.S.op("dve", lambda e, mv=mv: e.reciprocal(out=mv[:, :, 1], in_=mv[:, :, 1]), reads=[b_mv], writes=[b_mv])
                                    for r in range(4):
                                        k.ts("dve", hg[:, r * 128:(r + 1) * 128], hg[:, r * 128:(r + 1) * 128], mv[:, r, 0:1], mv[:, r, 1:2],
                                             ALU.subtract, ALU.mult, reads=[b_hg, b_mv], writes=[b_hg])
                                    k.tt("pool", hnb[:], hg[:], rowbc[:, R_MNG:R_MNG + 512], ALU.mult, reads=[b_hg, b_rowbc], writes=[b_hnb])
                                    bank2 = pbanks[2 + cc]; bb2 = bpb[2 + cc]
                                    for r in range(4):
                                        k.mm(bank2[:, r * 128:(r + 1) * 128], hnb[:, r * 128:(r + 1) * 128], ident_bf[:], True, True,
                                             reads=[b_hnb, b_idbf], writes=[bb2])
                                    k.act(mm_[:, :, cc * 128:(cc + 1) * 128], bank2[:, :].rearrange("p (r t) -> p r t", r=4), AF.Copy,
                                          reads=[bb2], writes=[b_mm])
                                k.dma(MIX[b, 0:512, ti * 256:(ti + 1) * 256].rearrange("(k p) t -> p k t", p=128), mm_[:],
                                      reads=[b_mm], writes=[bMIX[b]])
                                if ti + 1 < 16:
                                    mod_x1(xmL[(ti + 1) % 2][0], 256, xs, b_xs, xmL[(ti + 1) % 2][1])

                    with (k.scope() if phase == "MO" else contextlib.nullcontext()):
                      if phase == "MO" and "O" not in SKIP:
                        wob, b_wob = load_w(w_out, 8, 0, 1024)
                        xs = [new([128, 8, TT], F32, "xs") for _ in range(2)]
                        mx = [new([128, 8, TT], BF16, "mx") for _ in range(2)]
                        zL = [new([128, 8, TT], F32, "z") for _ in range(2)]
                        zbL = [k.sb([128, 8, TT], BF16, "zb") for _ in range(2)]
                        zqL = [k.sb([128, 8, TT], BF16, "zq") for _ in range(2)]
                        bzbqL = [k.buf() for _ in range(2)]
                        smL = [new([128, 6, TT], F32, "small") for _ in range(2)]
                        tmpL = [[k.sb([128, TT], F32, "tmp") for _ in range(2)] for _ in range(2)]
                        btmpL = [[k.buf() for _ in range(2)] for _ in range(2)]
                        def loadO(ti):
                            (xs_, bxs_), (mx_, bmx_) = xs[ti % 2], mx[ti % 2]
                            tsl = slice(ti * TT, (ti + 1) * TT)
                            k.dma(xs_[:], x1v[:, :, tsl], reads=[bX1[b]], writes=[bxs_])
                            k.dma(mx_[:], MIX[b].rearrange("(k p) t -> p k t", p=128)[:, :, tsl], reads=[bMIX[b]], writes=[bmx_])

                        loadO(0)
                        for ti in range(NT):
                            s = ti % 2
                            (xs_, bxs_), (mx_, bmx_) = xs[s], mx[s]
                            tsl = slice(ti * TT, (ti + 1) * TT)
                            if ti + 1 < NT:
                                loadO(ti + 1)
                            (z, b_z), zb, zq, b_zbq = zL[s], zbL[s], zqL[s], bzbqL[s]
                            (small, b_small), tmp, b_tmp = smL[s], tmpL[s], btmpL[s]
                            for dt in range(8):
                                bi = (dt // 2) % 2
                                half = (dt % 2) * 256
                                for kk in range(8):
                                    k.mm(pbanks[bi][:, half:half + TT], wob[:, kk, dt * 128:(dt + 1) * 128], mx_[:, kk, :], kk == 0, kk == 7,
                                         reads=[b_wob, bmx_], writes=[bpb[bi]])
                                k.stt(z[:, dt, :], pbanks[bi][:, half:half + TT], der[:, 5, dt, b:b + 1], xs_[:, dt, :], ALU.mult, ALU.add,
                                      reads=[bpb[bi], bxs_, b_der], writes=[b_z])
                            ln_tail(z, b_z, zb, zq, b_zbq, small, b_small, tmp, b_tmp, xs_, bxs_, 1, TT)
                            k.dma(X2[b].rearrange("(k p) t -> p k t", p=128)[:, :, tsl], xs_[:], reads=[bxs_], writes=[bX2[b]])

        if STAGE >= 2:
            k.stacks.pop()
            mixc.close()
            k.fence_ops = k.S.fence()
        if STAGE == 3:
            with k.scope():
                cvb, b_cvb = k.sb([128, 8, 256], BF16, "cvb"), k.buf()
                cvf, b_cvf = k.sb([128, 8, 256], F32, "cvf"), k.buf()
                for b in range(NB):
                    for ti in range(16):
                        tsl = slice(ti * 256, (ti + 1) * 256)
                        k.dma(cvb[:], MIX[b].rearrange("(k p) t -> p k t", p=128)[:, :, tsl], reads=[bMIX[b]], writes=[b_cvb])
                        k.copy("dve", cvf[:], cvb[:], reads=[b_cvb], writes=[b_cvf])
                        final_ops.append(k.dma(outT[b].rearrange("(k p) t -> p k t", p=128)[:, :, tsl], cvf[:], reads=[b_cvf], writes=[b_out[b]]))
        if STAGE >= 9:
            ffn_pass(X2, bX2, outT, b_out, f2u, f2d, 2, 2, True)

        k.S.emit(final_wait_ops=final_ops)
    return nc


def _pack_inputs(inp):
    f = lambda a: np.ascontiguousarray(np.asarray(a, dtype=np.float32))
    col = np.zeros((128, NCOLP), np.float32)

    def put8(off, v):
        col[:, off:off + 8] = f(v).reshape(8, 128).T

    for i, nm in enumerate(["ln1_g", "ln1_b", "ln2_g", "ln2_b", "ln3_g", "ln3_b"]):
        put8(C_LN + 8 * i, inp[nm][0])
    col[:, C_BADA:C_BADA + 72] = f(inp["b_ada"][0]).reshape(72, 128).T
    b_in = f(inp["b_in"][0])
    col[:, C_BQK:C_BQK + 4] = b_in[0:512].reshape(4, 128).T
    cw = f(inp["qk_conv_w"][0])
    for j in range(5):
        col[:, C_CW + 4 * j:C_CW + 4 * j + 4] = cw[j].reshape(4, 128).T
    col[:, C_CB:C_CB + 4] = f(inp["qk_conv_b"][0]).reshape(4, 128).T
    col[:, C_GLUB:C_GLUB + 4] = f(inp["ssm_glu_b"][0]).reshape(4, 128).T
    col[:, C_SNG:C_SNG + 4] = f(inp["ssm_norm_g"][0]).reshape(4, 128).T
    g0 = 1536
    for i in range(4):
        col[0:4, C_BG + i] = b_in[g0 + 4 * i:g0 + 4 * i + 4]
    col[0:16, C_BG + 4] = b_in[g0:g0 + 16]
    row = np.zeros((1, NROWP), np.float32)
    row[0, R_BV:R_BV + 512] = b_in[512:1024]
    row[0, R_BO:R_BO + 512] = b_in[1024:1536]
    row[0, R_BU:R_BU + 512] = b_in[1552:2064]
    row[0, R_MNG:R_MNG + 512] = f(inp["mlstm_norm_g"][0])
    cst = np.zeros((128, 1024), np.float32)
    cst[:, 0:128] = np.eye(128, dtype=np.float32)
    ii = np.arange(128)
    cst[:, 128:256] = (ii[:, None] <= ii[None, :]).astype(np.float32)
    cst[:, 256:384] = (ii[:, None] >= ii[None, :]).astype(np.float32)
    cst[:, 384:896] = np.arange(512, dtype=np.float32)[None, :]
    a8 = np.arange(8, dtype=np.float32)
    for half, rows in ((0, slice(0, 64)), (1, slice(64, 128))):
        cst[rows, 896:904] = (7 - a8) if half == 0 else a8
        cst[rows, 904:912] = (-a8) if half == 0 else a8
        cst[rows, 912:920] = a8 if half == 0 else (-a8)
        cst[rows, 920:928] = (a8 + 1) if half == 0 else (8 - a8)
    cs2 = np.zeros((128, 1536), np.float32)
    cs2[:, 0:512] = (np.arange(512) % 128 != 0).astype(np.float32)[None, :]
    cs2[:, 512:640] = ((ii[:, None] // 16) <= (ii[None, :] // 16)).astype(np.float32)
    cs2[:, 640:768] = ((ii[:, None] // 16) >= (ii[None, :] // 16)).astype(np.float32)
    cs2[0:32, 768:800] = np.eye(32, dtype=np.float32)[::-1]
    cs2[:, 1344:1472] = np.eye(128, dtype=np.float32)[::-1]
    for r in range(4):
        cs2[r, 800 + r * 128:800 + (r + 1) * 128] = 1.0
    dd_ = f(inp["ssm_d"][0]).reshape(32, 16)
    cs2[:, 1312:1344] = np.tile(dd_.T, (8, 1))
    s5 = np.zeros((2, 64, 32, 70), np.float32)
    s5[..., 0] = f(inp["ssm_lam_re"][0]).transpose(0, 2, 1)
    s5[..., 1] = f(inp["ssm_lam_im"][0]).transpose(0, 2, 1)
    s5[..., 2] = f(inp["ssm_log_step"][0])[:, None, :]
    s5[..., 3:19] = f(inp["ssm_b_re"][0]).transpose(1, 0, 2)[None]
    s5[..., 19:35] = f(inp["ssm_b_im"][0]).transpose(1, 0, 2)[None]
    s5[..., 35:51] = f(inp["ssm_c_re"][0]).transpose(0, 3, 1, 2)
    s5[..., 51:67] = f(inp["ssm_c_im"][0]).transpose(0, 3, 1, 2)
    s5p = np.ascontiguousarray(s5.reshape(128, 32 * 70))
    shared = {
        "w_ada": f(inp["w_ada"][0]), "colpack": col, "rowpack": row, "consts": cst,
        "ffn1_w_up": f(inp["ffn1_w_up"][0]), "ffn1_w_down": f(inp["ffn1_w_down"][0]),
        "ffn2_w_up": f(inp["ffn2_w_up"][0]), "ffn2_w_down": f(inp["ffn2_w_down"][0]),
        "w_in": f(inp["w_in"][0]), "w_out": f(inp["w_out"][0]), "glu_w": f(inp["ssm_glu_w"][0]),
        "s5p": s5p, "consts2": cs2,
    }
    x = np.asarray(inp["x"], dtype=np.float32)
    c = np.asarray(inp["c"], dtype=np.float32)
    maps = []
    for core in range(8):
        m = dict(shared)
        m["xT"] = np.ascontiguousarray(x[2 * core:2 * core + 2].transpose(0, 2, 1))
        cc = c[2 * core:2 * core + 2]
        m["cT"] = np.ascontiguousarray(cc.reshape(2, 8, 128).transpose(2, 1, 0).reshape(128, 16))
        maps.append(m)
    return maps


_NC_CACHE = {}


def kernel(**inputs):
    maps = _pack_inputs(inputs)
    if "nc" not in _NC_CACHE:
        _NC_CACHE["nc"] = build_program()
    nc = _NC_CACHE["nc"]
    res = run_bass_kernel_spmd(nc, maps, core_ids=list(range(8)))
    outs = [r["outT"] for r in res.results]
    full = np.concatenate([o.transpose(0, 2, 1) for o in outs], axis=0)
    return np.ascontiguousarray(full.astype(np.float32))
```

```python
import os
import math
import contextlib
import numpy as np
import concourse.bass as bass
import concourse.mybir as mybir
from concourse.bass_utils import run_bass_kernel_spmd

F32 = mybir.dt.float32
BF16 = mybir.dt.bfloat16
I32 = mybir.dt.int32
AF = mybir.ActivationFunctionType
ALU = mybir.AluOpType
AX = mybir.AxisListType

D = 1024
DF = 2816
T = 4096
NB = 2
TT = 256
NT = T // TT
ALPHA = 2.0 ** 0.25
LN_EPS = 1e-5
EPS_P = LN_EPS / (ALPHA * ALPHA)
IN_COLS = 2064
STAGE = int(os.environ.get("MK_STAGE", "9"))
SKIP = os.environ.get("MK_SKIP", "").split(",")

C_LN = 0
C_BADA = 48
C_BQK = 120
C_CW = 124
C_CB = 144
C_GLUB = 148
C_SNG = 152
C_BG = 156
NCOLP = 168
R_BV, R_BO, R_BU, R_MNG = 0, 512, 1024, 1536
NROWP = 2048


class Buf:
    __slots__ = ("name", "w", "r")

    def __init__(self, name="", init_readers=()):
        self.name = name
        self.w = None
        self.r = list(init_readers)


class Sched:
    ENGS = ("pe", "act", "dve", "pool", "sp")
    DMA_RING = 8
    SEM_MAX = 30000

    def __init__(self, nc):
        self.nc = nc
        self.ops = []
        self.by_eng = {e: [] for e in self.ENGS}

    def op(self, eng, fn, reads=(), writes=(), dma=False):
        oid = len(self.ops)
        deps = set()
        for b in reads:
            if b.w is not None:
                deps.add(b.w)
        for b in writes:
            if b.w is not None:
                deps.add(b.w)
            for r in b.r:
                deps.add(r)
        if eng == "pe":
            deps = {d for d in deps if self.ops[d][0] != "pe"}
        self.ops.append((eng, fn, deps, dma))
        self.by_eng[eng].append(oid)
        for b in reads:
            b.r.append(oid)
        for b in writes:
            b.w = oid
            b.r = []
        return oid

    def fence(self):
        f = []
        for e in self.ENGS:
            lst = self.by_eng[e]
            if not lst:
                continue
            if e == "sp":
                f.extend(lst[-self.DMA_RING:])
            else:
                f.append(lst[-1])
        return f

    def emit(self, final_wait_ops=()):
        nc = self.nc
        ops = self.ops
        needed = [False] * len(ops)
        for (_, _, deps, _) in ops:
            for d in deps:
                needed[d] = True
        for d in final_wait_ops:
            needed[d] = True
        sig = [None] * len(ops)
        cnt = {e: 0 for e in self.ENGS}
        dma_idx = {e: 0 for e in self.ENGS}
        for e in self.ENGS:
            for oid in self.by_eng[e]:
                isdma = ops[oid][3]
                if isdma:
                    n = dma_idx[e]
                    dma_idx[e] += 1
                    sig[oid] = (("d", e, n % self.DMA_RING), 16 * (n // self.DMA_RING + 1))
                elif needed[oid]:
                    c = cnt[e]
                    cnt[e] += 1
                    sig[oid] = (("c", e, c // self.SEM_MAX), c % self.SEM_MAX + 1)
        keys = []
        kset = set()
        for s in sig:
            if s is not None and s[0] not in kset:
                kset.add(s[0])
                keys.append(s[0])
        with contextlib.ExitStack() as st:
            sems = {}
            for k in keys:
                sems[k] = st.enter_context(nc.semaphore("s_%s_%s_%d" % k))
            block = st.enter_context(nc.Block())

            def run_engine(ename, eh):
                seen = {}
                for oid in self.by_eng[ename]:
                    _, fn, deps, isdma = ops[oid]
                    waits = {}
                    for d in deps:
                        k, v = sig[d]
                        if seen.get(k, 0) >= v:
                            continue
                        if waits.get(k, 0) < v:
                            waits[k] = v
                    if isdma:
                        k, v = sig[oid]
                        if v > 16 and seen.get(k, 0) < v - 16 and waits.get(k, 0) < v - 16:
                            waits[k] = v - 16
                    for k, v in waits.items():
                        eh.wait_ge(sems[k], v)
                        seen[k] = v
                    ins = fn(eh)
                    if sig[oid] is not None:
                        k, v = sig[oid]
                        ins.then_inc(sems[k], 16 if isdma else 1)
                if ename == "sp":
                    fw = {}
                    for d in final_wait_ops:
                        k, v = sig[d]
                        if fw.get(k, 0) < v:
                            fw[k] = v
                    for k, v in fw.items():
                        if seen.get(k, 0) < v:
                            eh.wait_ge(sems[k], v)

            @block.tensor
            def _(e):
                run_engine("pe", e)

            @block.scalar
            def _(e):
                run_engine("act", e)

            @block.vector
            def _(e):
                run_engine("dve", e)

            @block.gpsimd
            def _(e):
                run_engine("pool", e)

            @block.sync
            def _(e):
                run_engine("sp", e)


class KB:
    def __init__(self, nc):
        self.nc = nc
        self.S = Sched(nc)
        self.stacks = []
        self.fence_ops = []
        self.uid = 0

    @contextlib.contextmanager
    def scope(self):
        st = contextlib.ExitStack()
        self.stacks.append(st)
        try:
            with st:
                yield
        finally:
            self.stacks.pop()
            self.fence_ops = self.S.fence()

    def sb(self, shape, dt, name=None):
        self.uid += 1
        nm = "%s_%d" % (name or "t", self.uid)
        return self.stacks[-1].enter_context(self.nc.sbuf_tensor(nm, list(shape), dt))

    def buf(self, name=""):
        return Buf(name, self.fence_ops)

    def dma(self, out, in_, reads=(), writes=(), **kw):
        return self.S.op("sp", lambda e: e.dma_start(out=out, in_=in_, **kw), reads, writes, dma=True)

    def mm(self, out, lhsT, rhs, start, stop, reads=(), writes=()):
        return self.S.op("pe", lambda e: e.matmul(out, lhsT=lhsT, rhs=rhs, start=start, stop=stop), reads, writes)

    def act(self, out, in_, func, reads=(), writes=(), scale=1.0, bias=0.0):
        return self.S.op("act", lambda e: e.activation(out=out, in_=in_, func=func, scale=scale, bias=bias), reads, writes)

    def tt(self, eng, out, in0, in1, op, reads=(), writes=()):
        return self.S.op(eng, lambda e: e.tensor_tensor(out=out, in0=in0, in1=in1, op=op), reads, writes)

    def ts(self, eng, out, in0, s1, s2, op0, op1=None, reads=(), writes=()):
        if op1 is None:
            return self.S.op(eng, lambda e: e.tensor_scalar(out=out, in0=in0, scalar1=s1, scalar2=None, op0=op0), reads, writes)
        return self.S.op(eng, lambda e: e.tensor_scalar(out=out, in0=in0, scalar1=s1, scalar2=s2, op0=op0, op1=op1), reads, writes)

    def stt(self, out, in0, scalar, in1, op0, op1, reads=(), writes=()):
        return self.S.op("dve", lambda e: e.scalar_tensor_tensor(out=out, in0=in0, scalar=scalar, in1=in1, op0=op0, op1=op1), reads, writes)

    def copy(self, eng, out, in_, reads=(), writes=()):
        if eng == "act":
            return self.act(out, in_, AF.Copy, reads, writes)
        return self.S.op(eng, lambda e: e.tensor_copy(out=out, in_=in_), reads, writes)


def build_program():
    nc = bass.Bass("TRN2", target_bir_lowering=False)
    k = KB(nc)

    def din(name, shape, dt=F32):
        return nc.dram_tensor(name, list(shape), dt, kind="ExternalInput").ap()

    xT = din("xT", [NB, D, T])
    cT = din("cT", [128, 16])
    w_ada = din("w_ada", [D, 9 * D])
    colpack = din("colpack", [128, NCOLP])
    rowpack = din("rowpack", [1, NROWP])
    consts = din("consts", [128, 1024])
    f1u = din("ffn1_w_up", [D, 2 * DF])
    f1d = din("ffn1_w_down", [DF, D])
    f2u = din("ffn2_w_up", [D, 2 * DF])
    f2d = din("ffn2_w_down", [DF, D])
    w_in = din("w_in", [D, IN_COLS])
    w_out = din("w_out", [D, D])
    glu_w = din("glu_w", [512, 512])
    s5p = din("s5p", [128, 32 * 70])
    consts2 = din("consts2", [128, 1536])
    outT = nc.dram_tensor("outT", [NB, D, T], F32, kind="ExternalOutput").ap()
    X1 = nc.dram_tensor("X1s", [NB, D, T], F32, kind="Internal").ap()
    X2 = nc.dram_tensor("X2s", [NB, D, T], F32, kind="Internal").ap()
    MIX = nc.dram_tensor("MIXs", [NB, D, T], BF16, kind="Internal").ap()
    bX1 = [Buf("X1_%d" % b) for b in range(NB)]
    bX2 = [Buf("X2_%d" % b) for b in range(NB)]
    bMIX = [Buf("MIX_%d" % b) for b in range(NB)]

    final_ops = []
    with k.scope():
        cp = k.sb([128, NCOLP], F32, "colpack"); b_cp = k.buf()
        k.dma(cp[:], colpack[:, :], writes=[b_cp])
        cst = k.sb([128, 1024], F32, "consts"); b_cst = k.buf()
        k.dma(cst[:], consts[:, :], writes=[b_cst])
        ones_bf = k.sb([128, 128], BF16, "ones"); b_ones = k.buf()
        k.S.op("pool", lambda e: e.memset(ones_bf[:], 1.0), writes=[b_ones])
        der = k.sb([128, 9, 8, NB], F32, "der"); b_der = k.buf()
        pbanks = []
        for i in range(8):
            pbanks.append(k.stacks[-1].enter_context(nc.psum_tensor("pb%d" % i, [128, 512], F32)))
        bpb = [k.buf("pb%d" % i) for i in range(8)]

        with k.scope():
            ct = k.sb([128, 16], F32, "ct"); b_ct = k.buf()
            k.dma(ct[:], cT[:, :], writes=[b_ct])
            cact = k.sb([128, 16], F32, "cact"); b_cact = k.buf()
            k.act(cact[:], ct[:], AF.Silu, reads=[b_ct], writes=[b_cact])
            wst = [k.sb([128, 8, 512], F32, "wada") for _ in range(2)]
            b_wst = [k.buf() for _ in range(2)]
            wv = w_ada.rearrange("(k p) c -> p k c", p=128)
            for ch in range(18):
                s = ch % 2
                k.dma(wst[s][:], wv[:, :, ch * 512:(ch + 1) * 512], writes=[b_wst[s]])
                for jj in range(4):
                    j = ch * 4 + jj
                    for kk in range(8):
                        k.mm(pbanks[0][:, 2 * j:2 * j + 2], wst[s][:, kk, jj * 128:(jj + 1) * 128],
                             cact[:, 2 * kk:2 * kk + 2], kk == 0, kk == 7,
                             reads=[b_wst[s], b_cact], writes=[bpb[0]])
            modc = k.sb([128, 72, NB], F32, "modc"); b_modc = k.buf()
            k.tt("dve", modc[:], pbanks[0][:, 0:144].rearrange("p (j b) -> p j b", b=NB),
                 cp[:, C_BADA:C_BADA + 72].unsqueeze(2).to_broadcast([128, 72, NB]), ALU.add,
                 reads=[bpb[0], b_cp], writes=[b_modc])
            for sub in range(3):
                base = sub * 24
                k.copy("dve", der[:, 3 * sub + 1, :, :], modc[:, base:base + 8, :], reads=[b_modc], writes=[b_der])
                k.ts("dve", der[:, 3 * sub, :, :], modc[:, base + 8:base + 16, :], 1.0, None, ALU.add,
                     reads=[b_modc], writes=[b_der])
                gsc = (1.0 if sub == 1 else 0.5) / ALPHA
                k.ts("dve", der[:, 3 * sub + 2, :, :], modc[:, base + 16:base + 24, :], 1.0, gsc, ALU.add, ALU.mult,
                     reads=[b_modc], writes=[b_der])

        def ln_tail(z, b_z, zb, zq, b_zbq, small, b_small, tmp, b_tmp, xo, b_xo, lnidx, ntok):
            ln_tail_a(z, b_z, zb, zq, b_zbq)
            ln_tail_b(z, b_z, zb, zq, b_zbq, small, b_small, tmp, b_tmp, xo, b_xo, lnidx, ntok)

        def ln_tail_a(z, b_z, zb, zq, b_zbq):
            k.act(zb[:], z[:], AF.Copy, reads=[b_z], writes=[b_zbq])
            k.act(zq[:], z[:], AF.Square, reads=[b_z], writes=[b_zbq])

        def ln_tail_b(z, b_z, zb, zq, b_zbq, small, b_small, tmp, b_tmp, xo, b_xo, lnidx, ntok):
            pbs = pbanks[6]
            for dt in range(8):
                k.mm(pbs[:, 0:ntok], ones_bf[:], zb[:, dt, :], dt == 0, dt == 7, reads=[b_zbq, b_ones], writes=[bpb[6]])
            for dt in range(8):
                k.mm(pbs[:, 256:256 + ntok], ones_bf[:], zq[:, dt, :], dt == 0, dt == 7, reads=[b_zbq], writes=[bpb[6]])
            mt, m2, vv, sd, rstd, nmr = [small[:, i, :] for i in range(6)]
            k.ts("dve", mt, pbs[:, 0:ntok], 1.0 / D, None, ALU.mult, reads=[bpb[6]], writes=[b_small])
            k.tt("dve", m2, mt, mt, ALU.mult, reads=[b_small], writes=[b_small])
            k.stt(vv, pbs[:, 256:256 + ntok], 1.0 / D, m2, ALU.mult, ALU.subtract, reads=[bpb[6], b_small], writes=[b_small])
            k.ts("dve", vv, vv, EPS_P, None, ALU.add, reads=[b_small], writes=[b_small])
            k.act(sd, vv, AF.Sqrt, reads=[b_small], writes=[b_small])
            k.S.op("dve", lambda e: e.reciprocal(out=rstd, in_=sd), reads=[b_small], writes=[b_small])
            k.stt(nmr, mt, -1.0, rstd, ALU.mult, ALU.mult, reads=[b_small], writes=[b_small])
            for dt in range(8):
                eng = "dve" if dt % 2 == 0 else "pool"
                tp = tmp[dt % 2]
                k.tt(eng, tp[:], z[:, dt, :], rstd, ALU.mult, reads=[b_z, b_small], writes=[b_tmp[dt % 2]])
                k.tt(eng, tp[:], tp[:], nmr, ALU.add, reads=[b_small, b_tmp[dt % 2]], writes=[b_tmp[dt % 2]])
                k.ts(eng, xo[:, dt, :], tp[:], cp[:, C_LN + 16 * lnidx + dt:C_LN + 16 * lnidx + dt + 1],
                     cp[:, C_LN + 16 * lnidx + 8 + dt:C_LN + 16 * lnidx + 9 + dt], ALU.mult, ALU.add,
                     reads=[b_tmp[dt % 2], b_cp], writes=[b_xo])

        def ffn_pass(src, b_src, dst, b_dst, wu_d, wd_d, sub, lnidx, is_final):
            with k.scope():
                wup = k.sb([128, 8, 2 * DF], BF16, "wup")
                wdn = k.sb([128, 22, D], BF16, "wdn")
                b_wup = [k.buf() for _ in range(22)]
                b_wdn = [k.buf() for _ in range(11)]
                xb = [k.sb([128, 8, TT], F32, "xb") for _ in range(2)]
                b_xb = [k.buf() for _ in range(2)]
                xm = k.sb([128, 8, TT], BF16, "xm"); b_xm = k.buf()
                sgt = [k.sb([128, TT], F32, "sgt") for _ in range(4)]
                b_sgt = [k.buf() for _ in range(4)]
                UB = (0, 1, 4, 5)
                hh = k.sb([128, 22, TT], BF16, "hh"); b_hh = k.buf()
                z = k.sb([128, 8, TT], F32, "z"); b_z = k.buf()
                zb = k.sb([128, 8, TT], BF16, "zb")
                zq = k.sb([128, 8, TT], BF16, "zq"); b_zbq = k.buf()
                small = k.sb([128, 6, TT], F32, "small"); b_small = k.buf()
                tmp = [k.sb([128, TT], F32, "tmp") for _ in range(2)]
                b_tmp = [k.buf() for _ in range(2)]
                wuv = wu_d.rearrange("(k p) c -> p k c", p=128)
                wdv = wd_d.rearrange("(f p) c -> p f c", p=128)
                engs = ["dve", "act", "pool"]
                n = 0
                for ch in range(22):
                    s = n % 2
                    k.dma(xb[s][:], wuv[:, :, ch * 256:(ch + 1) * 256], writes=[b_xb[s]])
                    k.copy(engs[n % 3], wup[:, :, ch * 256:(ch + 1) * 256], xb[s][:], reads=[b_xb[s]], writes=[b_wup[ch]])
                    n += 1
                for ch in range(11):
                    s = n % 2
                    sv = xb[s][:].rearrange("p a t -> p (a t)").rearrange("p (f c) -> p f c", f=2)
                    k.dma(sv, wdv[:, 2 * ch:2 * ch + 2, :], writes=[b_xb[s]])
                    k.copy(engs[n % 3], wdn[:, 2 * ch:2 * ch + 2, :], sv, reads=[b_xb[s]], writes=[b_wdn[ch]])
                    n += 1
                A = lambda dt, b: der[:, 3 * sub, dt, b:b + 1]
                Bm = lambda dt, b: der[:, 3 * sub + 1, dt, b:b + 1]
                GA = lambda dt, b: der[:, 3 * sub + 2, dt, b:b + 1]
                tiles = [(b, ti) for b in range(NB) for ti in range(NT)]

                def load_x(i):
                    b, ti = tiles[i]
                    s = i % 2
                    k.dma(xb[s][:], src[b].rearrange("(k p) t -> p k t", p=128)[:, :, ti * TT:(ti + 1) * TT],
                          reads=[b_src[b]], writes=[b_xb[s]])

                def make_xm(i):
                    b, ti = tiles[i]
                    s = i % 2
                    for dt in range(8):
                        eng = ("dve", "pool", "act", "pool")[dt % 4]
                        if eng == "act":
                            k.S.op("act", lambda e, dt=dt: e.activation(out=xm[:, dt, :], in_=xb[s][:, dt, :], func=AF.Identity,
                                                                        scale=A(dt, b), bias=Bm(dt, b)),
                                   reads=[b_xb[s], b_der], writes=[b_xm])
                        else:
                            k.ts(eng, xm[:, dt, :], xb[s][:, dt, :], A(dt, b), Bm(dt, b), ALU.mult, ALU.add,
                                 reads=[b_xb[s], b_der], writes=[b_xm])

                load_x(0)
                make_xm(0)

                def finish(i):
                    b, ti = tiles[i]
                    s = i % 2
                    ln_tail_b(z, b_z, zb, zq, b_zbq, small, b_small, tmp, b_tmp, xb[s], b_xb[s], lnidx, TT)
                    o = k.dma(dst[b].rearrange("(k p) t -> p k t", p=128)[:, :, ti * TT:(ti + 1) * TT], xb[s][:],
                              reads=[b_xb[s]], writes=[b_dst[b]])
                    if is_final:
                        final_ops.append(o)

                for i, (b, ti) in enumerate(tiles):
                    s = i % 2
                    for j in range(22):
                        if j == 2:
                            if i > 0:
                                finish(i - 1)
                            if i + 1 < len(tiles):
                                load_x(i + 1)
                        bank = pbanks[UB[j % 4]]; bb = bpb[UB[j % 4]]
                        for kk in range(8):
                            k.mm(bank[:, 0:TT], wup[:, kk, j * 128:(j + 1) * 128], xm[:, kk, :], kk == 0, kk == 7,
                                 reads=[b_wup[j // 2], b_xm], writes=[bb])
                        for kk in range(8):
                            c0 = DF + j * 128
                            k.mm(bank[:, 256:256 + TT], wup[:, kk, c0:c0 + 128], xm[:, kk, :], kk == 0, kk == 7,
                                 reads=[b_wup[c0 // 256], b_xm], writes=[bb])
                        k.act(sgt[j % 4][:], bank[:, 0:TT], AF.Silu, reads=[bb], writes=[b_sgt[j % 4]])
                        k.tt("dve", hh[:, j, :], sgt[j % 4][:], bank[:, 256:256 + TT], ALU.mult,
                             reads=[bb, b_sgt[j % 4]], writes=[b_hh])
                    if i + 1 < len(tiles):
                        make_xm(i + 1)
                    for dt in range(8):
                        bi = 2 + (dt // 2) % 2
                        half = (dt % 2) * 256
                        for j in range(22):
                            k.mm(pbanks[bi][:, half:half + TT], wdn[:, j, dt * 128:(dt + 1) * 128], hh[:, j, :],
                                 j == 0, j == 21, reads=[b_wdn[j // 2], b_hh], writes=[bpb[bi]])
                        k.stt(z[:, dt, :], pbanks[bi][:, half:half + TT], GA(dt, b), xb[s][:, dt, :], ALU.mult, ALU.add,
                              reads=[bpb[bi], b_xb[s], b_der], writes=[b_z])
                    ln_tail_a(z, b_z, zb, zq, b_zbq)
                finish(len(tiles) - 1)

        b_out = [Buf("out%d" % b) for b in range(NB)]
        if STAGE == 1:
            ffn_pass(xT, [Buf(), Buf()], outT, b_out, f1u, f1d, 0, 0, True)
        if STAGE >= 2 and STAGE != 3:
            ffn_pass(xT, [Buf(), Buf()], X1, bX1, f1u, f1d, 0, 0, False)
        if STAGE == 3:
            bX1 = [Buf(), Buf()]
            X1 = xT
        if STAGE >= 2:
            TWO_PI = 2.0 * math.pi
            mixc = contextlib.ExitStack()
            k.stacks.append(mixc)
            ident = cst[:, 0:128]
            tri = [cst[:, 128:256], cst[:, 256:384]]
            iota = cst[:, 384:896]
            KTc = cst[:, 896:928]
            cs2 = k.sb([128, 1536], F32, "consts2"); b_cs2 = k.buf()
            k.dma(cs2[:], consts2[:, :], writes=[b_cs2])
            rmask = cs2[:, 0:512]
            mle2 = cs2[:, 512:640]
            mge2 = cs2[:, 640:768]
            J32 = cs2[:, 768:800]
            esel = cs2[:, 800:1312]
            dpk = cs2[:, 1312:1344]
            ident_bf = k.sb([128, 128], BF16, "identbf"); b_idbf = k.buf()
            k.copy("dve", ident_bf[:], ident, reads=[b_cst], writes=[b_idbf])
            J_bf = k.sb([128, 128], BF16, "Jbf")
            k.copy("dve", J_bf[:], cs2[:, 1344:1472], reads=[b_cs2], writes=[b_idbf])
            rowbc = k.sb([128, NROWP], F32, "rowbc"); b_rowbc = k.buf()
            k.dma(rowbc[:], rowpack[0:1, :].partition_broadcast(128), writes=[b_rowbc])
            GSd = nc.dram_tensor("GSs", [NB, 4, 4, T], F32, kind="Internal").ap()
            bGS = [Buf() for _ in range(NB)]

            def new(shape, dt, name="w"):
                return k.sb(shape, dt, name), k.buf()

            def load_w(src_ap, rows_k, c0, c1, stage_cols=128):
                wt, b_wt = new([128, rows_k, c1 - c0], BF16, "wbf")
                v = src_ap.rearrange("(k p) c -> p k c", p=128)
                with k.scope():
                    stg = [new([128, rows_k, stage_cols], F32, "wstg") for _ in range(2)]
                    i = 0
                    c = c0
                    while c < c1:
                        w_ = min(stage_cols, c1 - c)
                        st_, bs_ = stg[i % 2]
                        k.dma(st_[:, :, 0:w_], v[:, :, c:c + w_], writes=[bs_])
                        k.copy(("dve", "pool")[i % 2], wt[:, :, c - c0:c - c0 + w_], st_[:, :, 0:w_], reads=[bs_], writes=[b_wt])
                        c += w_
                        i += 1
                return wt, b_wt

            def reduce_angle(x, out, shape, bx, bo):
                tf, btf = new(shape, F32, "raf")
                ti, bti = new(shape, I32, "rai")
                k.ts("dve", tf[:], x, 1.0 / TWO_PI, None, ALU.mult, reads=[bx], writes=[btf])
                k.copy("dve", ti[:], tf[:], reads=[btf], writes=[bti])
                k.copy("dve", tf[:], ti[:], reads=[bti], writes=[btf])
                k.stt(out, tf[:], -TWO_PI, x, ALU.mult, ALU.add, reads=[btf, bx], writes=[bo])

            def sincos(x, sn, cs_, shape, bx, bsn, bcs):
                s2, bs2 = new(shape, F32, "s2")
                s4, bs4 = new(shape, F32, "s4")
                k.act(s2[:], x, AF.Sin, reads=[bx], writes=[bs2], scale=0.5)
                k.act(s4[:], x, AF.Sin, reads=[bx], writes=[bs4], scale=0.25)
                k.tt("dve", cs_, s2[:], s2[:], ALU.mult, reads=[bs2], writes=[bcs])
                k.ts("dve", cs_, cs_, -2.0, 1.0, ALU.mult, ALU.add, reads=[bcs], writes=[bcs])
                k.tt("dve", s4[:], s4[:], s4[:], ALU.mult, reads=[bs4], writes=[bs4])
                k.ts("dve", s4[:], s4[:], -2.0, 1.0, ALU.mult, ALU.add, reads=[bs4], writes=[bs4])
                k.stt(sn, s2[:], 2.0, s4[:], ALU.mult, ALU.mult, reads=[bs2, bs4], writes=[bsn])

            with k.scope():
                mst = contextlib.ExitStack()
                k.stacks.append(mst)
                Tm, b_Tm = new([128, 32, 128], BF16, "Tm")
                PTre, b_PTre = new([128, 32, 128], BF16, "PTre")
                PTim, b_PTim = new([128, 32, 128], BF16, "PTim")
                Qre, b_Qre = new([128, 32, 128], BF16, "Qre")
                Qim, b_Qim = new([128, 32, 128], BF16, "Qim")
                thr, b_thr = new([128, 32], F32, "thr")
                r8, b_r8 = new([128, 32], F32, "r8")
                with k.scope():
                    sp, b_sp = new([128, 32, 70], F32, "s5p")
                    k.dma(sp[:], s5p.rearrange("p (g f) -> p g f", f=70), writes=[b_sp])
                    lr = sp[:, :, 0]; li = sp[:, :, 1]
                    S2 = [128, 32]
                    step, b_step = new(S2, F32)
                    k.act(step[:], sp[:, :, 2], AF.Exp, reads=[b_sp], writes=[b_step])
                    lrs, b_lrs = new(S2, F32)
                    k.tt("dve", lrs[:], lr, step[:], ALU.mult, reads=[b_sp, b_step], writes=[b_lrs])
                    phi, b_phi = new(S2, F32)
                    k.tt("dve", phi[:], li, step[:], ALU.mult, reads=[b_sp, b_step], writes=[b_phi])
                    phr, b_phr = new(S2, F32)
                    reduce_angle(phi[:], phr[:], S2, b_phi, b_phr)
                    th8, b_th8 = new(S2, F32)
                    k.ts("dve", th8[:], phr[:], 8.0, None, ALU.mult, reads=[b_phr], writes=[b_th8])
                    reduce_angle(th8[:], thr[:], S2, b_th8, b_thr)
                    k.act(r8[:], lrs[:], AF.Exp, reads=[b_lrs], writes=[b_r8], scale=8.0)
                    sn1, b_sn1 = new(S2, F32); cs1, b_cs1 = new(S2, F32); mg1, b_mg1 = new(S2, F32)
                    sincos(phr[:], sn1[:], cs1[:], S2, b_phr, b_sn1, b_cs1)
                    k.act(mg1[:], lrs[:], AF.Exp, reads=[b_lrs], writes=[b_mg1])
                    nr, b_nr = new(S2, F32); ni, b_ni = new(S2, F32)
                    k.tt("dve", nr[:], mg1[:], cs1[:], ALU.mult, reads=[b_mg1, b_cs1], writes=[b_nr])
                    k.ts("dve", nr[:], nr[:], -1.0, None, ALU.add, reads=[b_nr], writes=[b_nr])
                    k.tt("dve", ni[:], mg1[:], sn1[:], ALU.mult, reads=[b_mg1, b_sn1], writes=[b_ni])
                    inv, b_inv = new(S2, F32); t0, b_t0 = new(S2, F32); t1, b_t1 = new(S2, F32)
                    k.tt("dve", inv[:], lr, lr, ALU.mult, reads=[b_sp], writes=[b_inv])
                    k.tt("dve", t0[:], li, li, ALU.mult, reads=[b_sp], writes=[b_t0])
                    k.tt("dve", inv[:], inv[:], t0[:], ALU.add, reads=[b_inv, b_t0], writes=[b_inv])
                    k.S.op("dve", lambda e: e.reciprocal(out=inv[:], in_=inv[:]), reads=[b_inv], writes=[b_inv])
                    cfr, b_cfr = new(S2, F32); cfi, b_cfi = new(S2, F32)
                    k.tt("dve", t0[:], nr[:], lr, ALU.mult, reads=[b_nr, b_sp], writes=[b_t0])
                    k.tt("dve", t1[:], ni[:], li, ALU.mult, reads=[b_ni, b_sp], writes=[b_t1])
                    k.tt("dve", t0[:], t0[:], t1[:], ALU.add, reads=[b_t0, b_t1], writes=[b_t0])
                    k.tt("dve", cfr[:], t0[:], inv[:], ALU.mult, reads=[b_t0, b_inv], writes=[b_cfr])
                    k.tt("dve", t0[:], ni[:], lr, ALU.mult, reads=[b_ni, b_sp], writes=[b_t0])
                    k.tt("dve", t1[:], nr[:], li, ALU.mult, reads=[b_nr, b_sp], writes=[b_t1])
                    k.tt("dve", t0[:], t0[:], t1[:], ALU.subtract, reads=[b_t0, b_t1], writes=[b_t0])
                    k.tt("dve", cfi[:], t0[:], inv[:], ALU.mult, reads=[b_t0, b_inv], writes=[b_cfi])
                    S3 = [128, 32, 16]
                    bre = sp[:, :, 3:19]; bim = sp[:, :, 19:35]; cre = sp[:, :, 35:51]; cim = sp[:, :, 51:67]
                    Bcr, b_Bcr = new(S3, F32); Bci, b_Bci = new(S3, F32); u0, b_u0 = new(S3, F32)
                    cfrb = cfr[:].unsqueeze(2).to_broadcast(S3); cfib = cfi[:].unsqueeze(2).to_broadcast(S3)
                    k.tt("dve", Bcr[:], bre, cfrb, ALU.mult, reads=[b_sp, b_cfr], writes=[b_Bcr])
                    k.tt("dve", u0[:], bim, cfib, ALU.mult, reads=[b_sp, b_cfi], writes=[b_u0])
                    k.tt("dve", Bcr[:], Bcr[:], u0[:], ALU.subtract, reads=[b_Bcr, b_u0], writes=[b_Bcr])
                    k.tt("dve", Bci[:], bim, cfrb, ALU.mult, reads=[b_sp, b_cfr], writes=[b_Bci])
                    k.tt("dve", u0[:], bre, cfib, ALU.mult, reads=[b_sp, b_cfi], writes=[b_u0])
                    k.tt("dve", Bci[:], Bci[:], u0[:], ALU.add, reads=[b_Bci, b_u0], writes=[b_Bci])
                    SP = [128, 32, 32]
                    pwr, b_pwr = new(SP, F32); pwi, b_pwi = new(SP, F32)
                    with k.scope():
                        ang, b_ang = new(SP, F32); angr, b_angr = new(SP, F32)
                        ktb = KTc.unsqueeze(1).to_broadcast(SP)
                        k.tt("dve", ang[:], phr[:].unsqueeze(2).to_broadcast(SP), ktb, ALU.mult, reads=[b_phr, b_cst], writes=[b_ang])
                        reduce_angle(ang[:], angr[:], SP, b_ang, b_angr)
                        psn, b_psn = new(SP, F32); pcs, b_pcs = new(SP, F32); pmg, b_pmg = new(SP, F32)
                        sincos(angr[:], psn[:], pcs[:], SP, b_angr, b_psn, b_pcs)
                        k.tt("dve", pmg[:], lrs[:].unsqueeze(2).to_broadcast(SP), ktb, ALU.mult, reads=[b_lrs, b_cst], writes=[b_pmg])
                        k.act(pmg[:], pmg[:], AF.Exp, reads=[b_pmg], writes=[b_pmg])
                        k.tt("dve", pwr[:], pmg[:], pcs[:], ALU.mult, reads=[b_pmg, b_pcs], writes=[b_pwr])
                        k.tt("dve", pwi[:], pmg[:], psn[:], ALU.mult, reads=[b_pmg, b_psn], writes=[b_pwi])
                    S4 = [128, 32, 8, 16]
                    ta, b_ta = new(S4, F32); tb_, b_tb = new(S4, F32)

                    def cprod(ar, ai, bar, bai, off, o_re, b_ore, o_im, b_oim, neg_im):
                        arb = ar.unsqueeze(2).to_broadcast(S4); aib = ai.unsqueeze(2).to_broadcast(S4)
                        pr = pwr[:, :, off:off + 8].unsqueeze(3).to_broadcast(S4)
                        pi_ = pwi[:, :, off:off + 8].unsqueeze(3).to_broadcast(S4)
                        k.tt("dve", ta[:], arb, pr, ALU.mult, reads=[bar, b_pwr], writes=[b_ta])
                        k.tt("pool", tb_[:], aib, pi_, ALU.mult, reads=[bai, b_pwi], writes=[b_tb])
                        k.tt("dve", o_re, ta[:], tb_[:], ALU.subtract, reads=[b_ta, b_tb], writes=[b_ore])
                        k.tt("dve", ta[:], arb, pi_, ALU.mult, reads=[bar, b_pwi], writes=[b_ta])
                        k.tt("pool", tb_[:], aib, pr, ALU.mult, reads=[bai, b_pwr], writes=[b_tb])
                        if neg_im:
                            k.stt(o_im, ta[:], -1.0, tb_[:], ALU.mult, ALU.subtract, reads=[b_ta, b_tb], writes=[b_oim])
                        else:
                            k.tt("dve", o_im, ta[:], tb_[:], ALU.add, reads=[b_ta, b_tb], writes=[b_oim])

                    v4 = lambda t_: t_[:].rearrange("p g (s h) -> p g s h", h=16)
                    if "noQ" not in SKIP:
                        cprod(cre, cim, b_sp, b_sp, 24, v4(Qre), b_Qre, v4(Qim), b_Qim, True)
                    with k.scope():
                      if "noP" not in SKIP:
                        Pr, b_Pr = new([128, 32, 128], F32); Pi, b_Pi = new([128, 32, 128], F32)
                        cprod(Bcr[:], Bci[:], b_Bcr, b_Bci, 0, v4(Pr), b_Pr, v4(Pi), b_Pi, False)
                        for g in range(32):
                            bank = pbanks[g % 2]; bb = bpb[g % 2]
                            k.mm(bank[:, 256:384], Pr[:, g, :], ident, True, True, reads=[b_Pr, b_cst], writes=[bb])
                            k.mm(bank[:, 384:512], Pi[:, g, :], ident, True, True, reads=[b_Pi, b_cst], writes=[bb])
                            k.act(PTre[:, g, :], bank[:, 256:384], AF.Copy, reads=[bb], writes=[b_PTre])
                            k.act(PTim[:, g, :], bank[:, 384:512], AF.Copy, reads=[bb], writes=[b_PTim])
                    with k.scope():
                      if "noT" not in SKIP:
                        Xr, b_Xr = new([128, 32, 128], F32); Xi, b_Xi = new([128, 32, 128], F32)
                        cprod(Bcr[:], Bci[:], b_Bcr, b_Bci, 8, v4(Xr), b_Xr, v4(Xi), b_Xi, False)
                        Zr, b_Zr = new([128, 32, 128], F32); Zi, b_Zi = new([128, 32, 128], F32)
                        cprod(cre, cim, b_sp, b_sp, 16, v4(Zr), b_Zr, v4(Zi), b_Zi, True)
                        Xmr = ta[:].rearrange("p g s h -> p g (s h)"); b_Xmr = b_ta
                        Xmi = tb_[:].rearrange("p g s h -> p g (s h)"); b_Xmi = b_tb
                        hm = [cst[:, 128 + 63:128 + 64], cst[:, 256 + 64:256 + 65]]
                        tA, b_tA = new([128, 128], F32); tB, b_tB = new([128, 128], F32)
                        for dd_ in range(2):
                            k.ts("dve", Xmr, Xr[:], hm[dd_], None, ALU.mult, reads=[b_Xr, b_cst], writes=[b_Xmr])
                            k.ts("pool", Xmi, Xi[:], hm[dd_], None, ALU.mult, reads=[b_Xi, b_cst], writes=[b_Xmi])
                            for g in range(32):
                                bank = pbanks[2 + g % 2]; bb = bpb[2 + g % 2]
                                k.mm(bank[:, 0:128], Xmr[:, g, :], Zr[:, g, :], True, False, reads=[b_Xmr, b_Zr], writes=[bb])
                                k.mm(bank[:, 0:128], Xmi[:, g, :], Zi[:, g, :], False, True, reads=[b_Xmi, b_Zi], writes=[bb])
                                if dd_ == 0:
                                    k.tt("dve", tA[:], bank[:, 0:128], mle2, ALU.mult, reads=[bb, b_cs2], writes=[b_tA])
                                    k.stt(Tm[:, g, :], ident, dpk[:, g:g + 1], tA[:], ALU.mult, ALU.add,
                                          reads=[b_tA, b_cs2, b_cst], writes=[b_Tm])
                                else:
                                    k.tt("dve", tB[:], bank[:, 0:128], mge2, ALU.mult, reads=[bb, b_cs2], writes=[b_tB])
                                    k.tt("dve", Tm[:, g, :], Tm[:, g, :], tB[:], ALU.add, reads=[b_tB, b_Tm], writes=[b_Tm])

                for (b, phase) in ((0, "S"), (1, "S"), (0, "MO"), (1, "MO")):
                    if phase == "MO" and mst is not None:
                        k.stacks.pop()
                        mst.close()
                        mst = None
                        k.fence_ops = k.S.fence()
                    x1v = X1[b].rearrange("(k p) t -> p k t", p=128)

                    def dma_x1(t0, ntok, xs, b_xs):
                        k.dma(xs[:, :, 0:ntok], x1v[:, :, t0:t0 + ntok], reads=[bX1[b]], writes=[b_xs])

                    def mod_x1(dst_ap, ntok, xs, b_xs, b_dst):
                        for dt in range(8):
                            eng = ("dve", "pool")[dt % 2]
                            k.ts(eng, dst_ap[:, dt, :], xs[:, dt, 0:ntok], der[:, 3, dt, b:b + 1], der[:, 4, dt, b:b + 1],
                                 ALU.mult, ALU.add, reads=[b_xs, b_der], writes=[b_dst])

                    def load_xm2(dst_ap, t0, ntok, xs, b_xs, b_dst):
                        k.dma(xs[:, :, 0:ntok], x1v[:, :, t0:t0 + ntok], reads=[bX1[b]], writes=[b_xs])
                        for dt in range(8):
                            eng = ("dve", "pool")[dt % 2]
                            k.ts(eng, dst_ap[:, dt, :], xs[:, dt, 0:ntok], der[:, 3, dt, b:b + 1], der[:, 4, dt, b:b + 1],
                                 ALU.mult, ALU.add, reads=[b_xs, b_der], writes=[b_dst])

                    with (k.scope() if phase == "S" else contextlib.nullcontext()):
                      if phase == "S" and "S" not in SKIP:
                        Zy, b_Zy = new([128, 4, 8, 512], BF16, "Zy")
                        Zta, b_Zta = new([128, 4, 32, 8, 16], BF16, "Zta")
                        with k.scope():
                            wu, b_wu = load_w(w_in, 8, 1552, 2064)
                            xsL2 = [new([128, 8, 128], F32, "xs") for _ in range(2)]
                            xmbL = [new([128, 8, 1024], BF16, "xmb") for _ in range(2)]

                            def prepA(cb):
                                (xmb, b_xmb) = xmbL[cb % 2]
                                for q8 in range(8):
                                    (xs, b_xs) = xsL2[q8 % 2]
                                    load_xm2(xmb[:, :, q8 * 128:(q8 + 1) * 128], cb * 1024 + q8 * 128, 128, xs, b_xs, b_xmb)

                            prepA(0)
                            for cb in range(4):
                                (xmb, b_xmb) = xmbL[cb % 2]
                                if cb + 1 < 4:
                                    prepA(cb + 1)
                                for s_ in range(8):
                                    bank = pbanks[s_ % 2]; bb = bpb[s_ % 2]
                                    for kk in range(8):
                                        lv = xmb[:, kk, :].rearrange("p (c s) -> p s c", s=8)[:, s_, :]
                                        k.mm(bank[:, :], lv, wu[:, kk, :], kk == 0, kk == 7, reads=[b_xmb, b_wu], writes=[bb])
                                    k.tt("dve", Zta[:, cb, :, s_, :], bank[:, :].rearrange("p (g h) -> p g h", h=16),
                                         rowbc[:, R_BU:R_BU + 512].rearrange("p (g h) -> p g h", h=16), ALU.add,
                                         reads=[bb, b_rowbc], writes=[b_Zta])
                        NG = 8
                        for hf_ in range(32 // NG):
                          with k.scope():
                            G0 = NG * hf_
                            Ut, b_Ut = new([128, NG, 512], BF16, "Ut")
                            Ur, b_Ur = new([128, NG, 512], BF16, "Ur")
                            for cb in range(4):
                                for g4 in range(NG // 4):
                                    bank = pbanks[2 + g4 % 2]; bb = bpb[2 + g4 % 2]
                                    bank2 = pbanks[4 + g4 % 2]; bb2 = bpb[4 + g4 % 2]
                                    for gg in range(4):
                                        g = G0 + g4 * 4 + gg
                                        zl = Zta[:, cb, g, :, :].rearrange("p s h -> p (s h)")
                                        k.mm(bank[:, gg * 128:(gg + 1) * 128], zl, ident_bf[:], True, True,
                                             reads=[b_Zta, b_idbf], writes=[bb])
                                        k.mm(bank2[:, gg * 128:(gg + 1) * 128], zl, J_bf[:], True, True,
                                             reads=[b_Zta, b_idbf], writes=[bb2])
                                    k.act(Ut[:, g4 * 4:g4 * 4 + 4, cb * 128:(cb + 1) * 128],
                                          bank[:, :].rearrange("p (g c) -> p g c", g=4), AF.Copy, reads=[bb], writes=[b_Ut])
                                    k.copy("dve", Ur[:, g4 * 4:g4 * 4 + 4, (3 - cb) * 128:(4 - cb) * 128],
                                           bank2[:, :].rearrange("p (g c) -> p g c", g=4), reads=[bb2], writes=[b_Ur])
                            with k.scope():
                                W_ = lambda: new([128, 512], F32, "sw")
                                W2 = lambda: [W_(), W_()]
                                (ta_, bta), (tr_, btr), (s2t, bs2) = W_(), W_(), W_()
                                (ti_, bti) = new([128, 512], I32, "twi")
                                tcs, tsns = W2(), W2()
                                srL, siL, a1L, a2L, stRL, stIL, kRL, kIL = [W2() for _ in range(8)]
                                HpL = [[new([128, 513], BF16, "Hp") for _ in range(4)] for _ in range(2)]
                                for hl in HpL:
                                    for (h_, bh_) in hl:
                                        k.S.op("pool", lambda e, h_=h_: e.memset(h_[:], 0.0), writes=[bh_])
                                ygL = [new([128, 512], BF16, "yg") for _ in range(2)]
                                ygbL = [new([128, 512], BF16, "ygb") for _ in range(2)]
                                ygTL = [new([128, 4, 128], BF16, "ygT") for _ in range(2)]
                                pY, bY = pbanks[0], bpb[0]
                                pYb, bYb = pbanks[1], bpb[1]
                                pAr, bAr = pbanks[2], bpb[2]
                                pBr, bBr = pbanks[3], bpb[3]
                                pAi, bAi = pbanks[4], bpb[4]
                                pBi, bBi = pbanks[5], bpb[5]
                                pT, bT = pbanks[6], bpb[6]
                                pT2, bT2 = pbanks[7], bpb[7]

                                def gen_tw(gl):
                                    g = G0 + gl
                                    (tc, btc), (tsn, bts) = tcs[gl % 2], tsns[gl % 2]
                                    k.S.op("act", lambda e: e.activation(out=ta_[:], in_=iota, func=AF.Copy, scale=thr[:, g:g + 1]),
                                           reads=[b_cst, b_thr], writes=[bta])
                                    k.act(tr_[:], ta_[:], AF.Copy, reads=[bta], writes=[btr], scale=1.0 / TWO_PI)
                                    k.copy("dve", ti_[:], tr_[:], reads=[btr], writes=[bti])
                                    k.copy("dve", tr_[:], ti_[:], reads=[bti], writes=[btr])
                                    k.stt(ta_[:], tr_[:], -TWO_PI, ta_[:], ALU.mult, ALU.add, reads=[btr, bta], writes=[bta])
                                    k.act(s2t[:], ta_[:], AF.Sin, reads=[bta], writes=[bs2], scale=0.5)
                                    k.act(tr_[:], ta_[:], AF.Sin, reads=[bta], writes=[btr], scale=0.25)
                                    k.tt("pool", tc[:], s2t[:], s2t[:], ALU.mult, reads=[bs2], writes=[btc])
                                    k.ts("pool", tc[:], tc[:], -2.0, 1.0, ALU.mult, ALU.add, reads=[btc], writes=[btc])
                                    k.tt("pool", tr_[:], tr_[:], tr_[:], ALU.mult, reads=[btr], writes=[btr])
                                    k.ts("pool", tr_[:], tr_[:], -2.0, 1.0, ALU.mult, ALU.add, reads=[btr], writes=[btr])
                                    k.stt(tsn[:], s2t[:], 2.0, tr_[:], ALU.mult, ALU.mult, reads=[bs2, btr], writes=[bts])

                                def stageA(gl):
                                    g = G0 + gl
                                    (sr, bsr), (si, bsi) = srL[gl % 2], siL[gl % 2]
                                    k.mm(pAr[:, :], PTre[:, g, :], Ut[:, gl, :], True, True, reads=[b_PTre, b_Ut], writes=[bAr])
                                    k.mm(pBr[:, :], PTre[:, g, :], Ur[:, gl, :], True, True, reads=[b_PTre, b_Ur], writes=[bBr])
                                    k.mm(pAi[:, :], PTim[:, g, :], Ut[:, gl, :], True, True, reads=[b_PTim, b_Ut], writes=[bAi])
                                    k.mm(pBi[:, :], PTim[:, g, :], Ur[:, gl, :], True, True, reads=[b_PTim, b_Ur], writes=[bBi])
                                    k.act(sr[0:64, :], pAr[0:64, :], AF.Copy, reads=[bAr], writes=[bsr])
                                    k.act(sr[64:128, :], pBr[64:128, :], AF.Copy, reads=[bBr], writes=[bsr])
                                    k.act(si[0:64, :], pAi[0:64, :], AF.Copy, reads=[bAi], writes=[bsi])
                                    k.act(si[64:128, :], pBi[64:128, :], AF.Copy, reads=[bBi], writes=[bsi])

                                def stageB(gl):
                                    g = G0 + gl
                                    w = gl % 2
                                    (tc, btc), (tsn, bts) = tcs[w], tsns[w]
                                    (sr, bsr), (si, bsi), (a1, ba1), (a2, ba2) = srL[w], siL[w], a1L[w], a2L[w]
                                    (stR, bstR), (stI, bstI), (kR, bkR), (kI, bkI) = stRL[w], stIL[w], kRL[w], kIL[w]
                                    (hrf, bhrf), (hif, bhif), (hrb, bhrb), (hib, bhib) = HpL[w]
                                    (yg, byg), (ygb, bygb), (ygT, bygT) = ygL[w], ygbL[w], ygTL[w]
                                    k.tt("dve", a1[:], sr[:], tc[:], ALU.mult, reads=[bsr, btc], writes=[ba1])
                                    k.tt("pool", a2[:], si[:], tsn[:], ALU.mult, reads=[bsi, bts], writes=[ba2])
                                    k.tt("dve", stR[:], a1[:], a2[:], ALU.add, reads=[ba1, ba2], writes=[bstR])
                                    k.tt("dve", a1[:], si[:], tc[:], ALU.mult, reads=[bsi, btc], writes=[ba1])
                                    k.tt("pool", a2[:], sr[:], tsn[:], ALU.mult, reads=[bsr, bts], writes=[ba2])
                                    k.tt("dve", stI[:], a1[:], a2[:], ALU.subtract, reads=[ba1, ba2], writes=[bstI])
                                    r8b = r8[:, g:g + 1].to_broadcast([128, 512])
                                    k.S.op("dve", lambda e: e.tensor_tensor_scan(out=kR[:], data0=r8b, data1=stR[:], initial=0.0,
                                                                                 op0=ALU.mult, op1=ALU.add),
                                           reads=[bstR, b_r8], writes=[bkR])
                                    k.S.op("dve", lambda e: e.tensor_tensor_scan(out=kI[:], data0=r8b, data1=stI[:], initial=0.0,
                                                                                 op0=ALU.mult, op1=ALU.add),
                                           reads=[bstI, b_r8], writes=[bkI])
                                    if gl + 1 < NG:
                                        gen_tw(gl + 1)
                                    k.tt("dve", a1[:], kR[:], tc[:], ALU.mult, reads=[bkR, btc], writes=[ba1])
                                    k.tt("pool", a2[:], kI[:], tsn[:], ALU.mult, reads=[bkI, bts], writes=[ba2])
                                    k.tt("dve", hrf[0:64, 1:513], a1[0:64, :], a2[0:64, :], ALU.subtract, reads=[ba1, ba2], writes=[bhrf])
                                    k.tt("dve", hrb[64:128, 1:513], a1[64:128, :], a2[64:128, :], ALU.subtract, reads=[ba1, ba2], writes=[bhrb])
                                    k.tt("dve", a1[:], kI[:], tc[:], ALU.mult, reads=[bkI, btc], writes=[ba1])
                                    k.tt("pool", a2[:], kR[:], tsn[:], ALU.mult, reads=[bkR, bts], writes=[ba2])
                                    k.tt("dve", hif[0:64, 1:513], a1[0:64, :], a2[0:64, :], ALU.add, reads=[ba1, ba2], writes=[bhif])
                                    k.tt("dve", hib[64:128, 1:513], a1[64:128, :], a2[64:128, :], ALU.add, reads=[ba1, ba2], writes=[bhib])
                                    k.mm(pY[:, :], Tm[:, g, :], Ut[:, gl, :], True, False, reads=[b_Tm, b_Ut], writes=[bY])
                                    k.mm(pY[:, :], Qre[:, g, :], hrf[:, 0:512], False, False, reads=[b_Qre, bhrf], writes=[bY])
                                    k.mm(pY[:, :], Qim[:, g, :], hif[:, 0:512], False, True, reads=[b_Qim, bhif], writes=[bY])
                                    k.mm(pYb[:, :], Qre[:, g, :], hrb[:, 0:512], True, False, reads=[b_Qre, bhrb], writes=[bYb])
                                    k.mm(pYb[:, :], Qim[:, g, :], hib[:, 0:512], False, True, reads=[b_Qim, bhib], writes=[bYb])
                                    k.act(yg[:], pY[:, :], AF.Copy, reads=[bY], writes=[byg])
                                    k.act(ygb[:], pYb[:, :], AF.Copy, reads=[bYb], writes=[bygb])
                                    for jb in range(4):
                                        k.mm(pT2[:, jb * 128:(jb + 1) * 128], ygb[:, jb * 128:(jb + 1) * 128], ident_bf[:], True, True,
                                             reads=[bygb, b_idbf], writes=[bT2])
                                    k.copy("dve", ygT[:], pT2[:, :].rearrange("p (j x) -> p j x", j=4), reads=[bT2], writes=[bygT])
                                    for cb in range(4):
                                        k.mm(pT[:, cb * 128:(cb + 1) * 128], yg[:, cb * 128:(cb + 1) * 128], ident_bf[:], True, False,
                                             reads=[byg, b_idbf], writes=[bT])
                                        k.mm(pT[:, cb * 128:(cb + 1) * 128], J_bf[:], ygT[:, 3 - cb, :], False, True,
                                             reads=[bygT, b_idbf], writes=[bT])
                                    k.copy("dve", Zy[:, :, :, 16 * g:16 * g + 16],
                                           pT[:, :].rearrange("p (c t h) -> p c t h", c=4, t=8), reads=[bT], writes=[b_Zy])

                                gen_tw(0)
                                stageA(0)
                                for gl in range(NG):
                                    if gl + 1 < NG:
                                        stageA(gl + 1)
                                    stageB(gl)
                        with k.scope():
                            glw, b_glw = load_w(glu_w, 4, 0, 512)
                            yb, b_yb = new([128, 4, 1024], F32, "yb")
                            u1, b_u1 = new([128, 4, 1024], F32, "u1")
                            yab, b_yab = new([128, 4, 1024], BF16, "yab")
                            sqb, b_sqb = new([128, 4, 1024], BF16, "sqb")
                            sg_, b_sg = new([128, 512], F32, "sg")
                            rs_, b_rs = new([128, 512], F32, "rs")
                            mo, b_mo = new([128, 4, 1024], BF16, "mo")
                            GC = 2.0 * math.sqrt(2.0 / math.pi)
                            for cb in range(4):
                                for ct in range(4):
                                    for th in range(2):
                                        bank = pbanks[(ct * 2 + th) % 2]; bb = bpb[(ct * 2 + th) % 2]
                                        for t4 in range(4):
                                            t_ = th * 4 + t4
                                            k.mm(bank[:, t4 * 128:(t4 + 1) * 128], Zy[:, cb, t_, ct * 128:(ct + 1) * 128], ident_bf[:],
                                                 True, True, reads=[b_Zy, b_idbf], writes=[bb])
                                        k.act(yb[:, ct, :].rearrange("p (c s) -> p s c", s=8)[:, th * 4:th * 4 + 4, :],
                                              bank[:, :].rearrange("p (s c) -> p s c", s=4), AF.Copy, reads=[bb], writes=[b_yb])
                                k.act(u1[:], yb[:], AF.Square, reads=[b_yb], writes=[b_u1])
                                k.ts("dve", u1[:], u1[:], 0.044715, 1.0, ALU.mult, ALU.add, reads=[b_u1], writes=[b_u1])
                                k.tt("pool", u1[:], u1[:], yb[:], ALU.mult, reads=[b_u1, b_yb], writes=[b_u1])
                                k.act(u1[:], u1[:], AF.Sigmoid, reads=[b_u1], writes=[b_u1], scale=GC)
                                k.tt("dve", yb[:], yb[:], u1[:], ALU.mult, reads=[b_u1, b_yb], writes=[b_yb])
                                k.copy("pool", yab[:], yb[:], reads=[b_yb], writes=[b_yab])
                                for hf in range(2):
                                    tk = slice(hf * 512, (hf + 1) * 512)
                                    for co in range(4):
                                        bank = pbanks[2 + co % 2]; bb = bpb[2 + co % 2]
                                        for ci in range(4):
                                            k.mm(bank[:, :], glw[:, ci, co * 128:(co + 1) * 128], yab[:, ci, tk], ci == 0, ci == 3,
                                                 reads=[b_glw, b_yab], writes=[bb])
                                        k.act(sg_[:], bank[:, :], AF.Sigmoid, reads=[bb, b_cp], writes=[b_sg],
                                              bias=cp[:, C_GLUB + co:C_GLUB + co + 1])
                                        k.tt("dve", yb[:, co, tk], yb[:, co, tk], sg_[:], ALU.mult, reads=[b_yb, b_sg], writes=[b_yb])
                                k.act(sqb[:], yb[:], AF.Square, reads=[b_yb], writes=[b_sqb])
                                for hf in range(2):
                                    tk = slice(hf * 512, (hf + 1) * 512)
                                    bank = pbanks[4 + hf]; bb = bpb[4 + hf]
                                    for ci in range(4):
                                        k.mm(bank[:, :], ones_bf[:], sqb[:, ci, tk], ci == 0, ci == 3, reads=[b_sqb, b_ones], writes=[bb])
                                    k.ts("dve", rs_[:], bank[:, :], 1.0 / 512.0, LN_EPS, ALU.mult, ALU.add, reads=[bb], writes=[b_rs])
                                    k.act(rs_[:], rs_[:], AF.Sqrt, reads=[b_rs], writes=[b_rs])
                                    k.S.op("dve", lambda e: e.reciprocal(out=rs_[:], in_=rs_[:]), reads=[b_rs], writes=[b_rs])
                                    for co in range(4):
                                        k.stt(mo[:, co, tk], yb[:, co, tk], cp[:, C_SNG + co:C_SNG + co + 1], rs_[:], ALU.mult, ALU.mult,
                                              reads=[b_yb, b_rs, b_cp], writes=[b_mo])
                                k.dma(MIX[b, 512:1024, cb * 1024:(cb + 1) * 1024].rearrange("(k p) t -> p k t", p=128), mo[:],
                                      reads=[b_mo], writes=[bMIX[b]])

                    with (k.scope() if phase == "MO" else contextlib.nullcontext()):
                      if phase == "MO" and "M" not in SKIP:
                        qf, b_qf = new([128, 2, T], BF16, "qf")
                        kf, b_kf = new([128, 2, T], BF16, "kf")
                        vext, b_vext = new([128, 32, 4, 129], BF16, "vext")
                        big, b_big = new([128, 4, T + 4], F32, "big")
                        k.S.op("pool", lambda e: e.memset(vext[:, :, :, 128:129], 1.0), writes=[b_vext])
                        k.S.op("pool", lambda e: e.memset(big[:, :, 0:2], 0.0), writes=[b_big])
                        k.S.op("pool", lambda e: e.memset(big[:, :, T + 2:T + 4], 0.0), writes=[b_big])
                        with k.scope():
                            wqk, b_wqk = load_w(w_in, 8, 0, 512)
                            wv, b_wv = load_w(w_in, 8, 512, 1024)
                            wg, b_wg = load_w(w_in, 8, 1536, 1552, 16)
                            xs, b_xs = new([128, 8, 256], F32, "xs")
                            xm2L = [new([128, 8, 256], BF16, "xm2") for _ in range(2)]
                            gt, b_gt = new([16, 256], F32, "gt")
                            dma_x1(0, 256, xs, b_xs)
                            mod_x1(xm2L[0][0], 256, xs, b_xs, xm2L[0][1])
                            for ti in range(16):
                                t0_ = ti * 256
                                (xm2, b_xm2) = xm2L[ti % 2]
                                if ti + 1 < 16:
                                    dma_x1(t0_ + 256, 256, xs, b_xs)
                                for ct in range(4):
                                    bank = pbanks[ct // 2]; bb = bpb[ct // 2]
                                    half = (ct % 2) * 256
                                    for kk in range(8):
                                        k.mm(bank[:, half:half + 256], wqk[:, kk, ct * 128:(ct + 1) * 128], xm2[:, kk, :], kk == 0, kk == 7,
                                             reads=[b_wqk, b_xm2], writes=[bb])
                                    k.act(big[:, ct, 2 + t0_:2 + t0_ + 256], bank[:, half:half + 256], AF.Identity, reads=[bb, b_cp],
                                          writes=[b_big], bias=cp[:, C_BQK + ct:C_BQK + ct + 1])
                                for cc in range(2):
                                    c = ti * 2 + cc
                                    bank = pbanks[2 + cc]; bb = bpb[2 + cc]
                                    for kk in range(8):
                                        k.mm(bank[:, :], xm2[:, kk, cc * 128:(cc + 1) * 128], wv[:, kk, :], kk == 0, kk == 7,
                                             reads=[b_wv, b_xm2], writes=[bb])
                                    k.tt("dve", vext[:, c, :, 0:128], bank[:, :].rearrange("p (r v) -> p r v", r=4),
                                         rowbc[:, R_BV:R_BV + 512].rearrange("p (r v) -> p r v", r=4), ALU.add,
                                         reads=[bb, b_rowbc], writes=[b_vext])
                                bank = pbanks[4]; bb = bpb[4]
                                for kk in range(8):
                                    k.mm(bank[0:16, 0:256], wg[:, kk, 0:16], xm2[:, kk, :], kk == 0, kk == 7,
                                         reads=[b_wg, b_xm2], writes=[bb])
                                k.act(gt[0:16, :], bank[0:16, 0:256], AF.Identity, reads=[bb, b_cp], writes=[b_gt],
                                      bias=cp[0:16, C_BG + 4:C_BG + 5])
                                k.dma(GSd[b, :, :, t0_:t0_ + 256].rearrange("k r t -> (k r) t"), gt[:], reads=[b_gt], writes=[bGS[b]])
                                if ti + 1 < 16:
                                    mod_x1(xm2L[(ti + 1) % 2][0], 256, xs, b_xs, xm2L[(ti + 1) % 2][1])
                        ktm, b_ktm = new([128, 32, 256], BF16, "ktm")
                        with k.scope():
                            acc, b_acc = new([128, T], F32, "acc")
                            for ct in range(4):
                                cw = lambda j: cp[:, C_CW + 4 * j + ct:C_CW + 4 * j + ct + 1]
                                k.ts("dve", acc[:], big[:, ct, 0:T], cw(0), None, ALU.mult, reads=[b_big, b_cp], writes=[b_acc])
                                for j in range(1, 5):
                                    k.stt(acc[:], big[:, ct, j:j + T], cw(j), acc[:], ALU.mult, ALU.add, reads=[b_big, b_cp, b_acc], writes=[b_acc])
                                cb_ = cp[:, C_CB + ct:C_CB + ct + 1]
                                if ct < 2:
                                    k.act(acc[:], acc[:], AF.Silu, reads=[b_acc, b_cp], writes=[b_acc], bias=cb_)
                                    k.ts("pool", qf[:, ct, :], acc[:], 0.125, None, ALU.mult, reads=[b_acc], writes=[b_qf])
                                else:
                                    k.act(kf[:, ct - 2, :], acc[:], AF.Silu, reads=[b_acc, b_cp], writes=[b_kf], bias=cb_)
                            for c2 in range(16):
                                bank = pbanks[c2 % 2]; bb = bpb[c2 % 2]
                                for cc in range(2):
                                    c = c2 * 2 + cc
                                    for kt in range(2):
                                        o_ = cc * 256 + kt * 128
                                        k.mm(bank[:, o_:o_ + 128], kf[:, kt, c * 128:(c + 1) * 128], ident_bf[:], True, True,
                                             reads=[b_kf, b_idbf], writes=[bb])
                                k.act(ktm[:, c2 * 2:c2 * 2 + 2, :], bank[:, :].rearrange("p (c x) -> p c x", c=2), AF.Copy,
                                      reads=[bb], writes=[b_ktm])
                        WCt, b_WCt = new([128, 2, 8, 32], F32, "WCt")
                        DECt, b_DECt = new([128, 8, 32], F32, "DECt")
                        with k.scope():
                            G5 = [32, 4, 4, 128]
                            Gc, b_Gc = new(G5, F32, "Gc")
                            k.dma(Gc[:], GSd[b].rearrange("k r (c t) -> c k r t", t=128), reads=[bGS[b]], writes=[b_Gc])
                            D4 = [32, 2, 4, 128]
                            spl, b_spl = new(D4, F32); BS, b_BS = new(D4, F32); ee, b_ee = new(D4, F32)
                            k.act(spl[:], Gc[:, 2:4, :, :], AF.Exp, reads=[b_Gc], writes=[b_spl], scale=-1.0)
                            k.act(spl[:], spl[:], AF.Ln, reads=[b_spl], writes=[b_spl], bias=1.0)
                            flat = lambda t_, d: t_[:, d, :, :].rearrange("c r t -> c (r t)")
                            for d in range(2):
                                k.S.op("dve", lambda e, d=d: e.tensor_tensor_scan(out=flat(BS, d), data0=rmask[0:32, :], data1=flat(spl, d),
                                                                                  initial=0.0, op0=ALU.mult, op1=ALU.add),
                                       reads=[b_spl, b_cs2], writes=[b_BS])
                            tot = BS[:, 1, :, 127:128].to_broadcast([32, 4, 128])
                            k.tt("dve", ee[:, 1, :, :], spl[:, 1, :, :], BS[:, 1, :, :], ALU.subtract, reads=[b_spl, b_BS], writes=[b_ee])
                            k.tt("dve", BS[:, 1, :, :], ee[:, 1, :, :], tot, ALU.add, reads=[b_ee, b_BS], writes=[b_BS])
                            k.tt("dve", ee[:], Gc[:, 0:2, :, :], BS[:], ALU.add, reads=[b_Gc, b_BS], writes=[b_ee])
                            emx, b_emx = new([32, 2, 4], F32)
                            k.S.op("dve", lambda e: e.tensor_reduce(out=emx[:], in_=ee[:], axis=AX.X, op=ALU.max), reads=[b_ee], writes=[b_emx])
                            nbe, b_nbe = new([32, 2, 4], F32)
                            k.ts("dve", nbe[:, 0, :], BS[:, 0, :, 127], -1.0, None, ALU.mult, reads=[b_BS], writes=[b_nbe])
                            k.ts("dve", nbe[:, 1, :], BS[:, 1, :, 0], -1.0, None, ALU.mult, reads=[b_BS], writes=[b_nbe])
                            pbk = pbanks[0]; bbk = bpb[0]
                            for d in range(2):
                                rhs_ = ident[0:32, 0:32] if d == 0 else J32[0:32, :]
                                k.mm(pbk[0:4, d * 64:d * 64 + 32], emx[:, d, :], rhs_, True, True, reads=[b_emx, b_cst, b_cs2], writes=[bbk])
                                k.mm(pbk[0:4, d * 64 + 32:d * 64 + 64], nbe[:, d, :], rhs_, True, True, reads=[b_nbe, b_cst, b_cs2], writes=[bbk])
                            rows, b_rows = new([4, 2, 2, 32], F32)
                            k.copy("dve", rows[:], pbk[0:4, 0:128].rearrange("r (d x j) -> r d x j", d=2, x=2), reads=[bbk], writes=[b_rows])
                            mrow, b_mrow = new([4, 2, 33], F32)
                            k.S.op("pool", lambda e: e.memset(mrow[:], 0.0), writes=[b_mrow])
                            Rrow, b_Rrow = new([4, 2, 32], F32); drow, b_drow = new([4, 2, 32], F32)
                            for d in range(2):
                                k.S.op("dve", lambda e, d=d: e.tensor_tensor_scan(out=mrow[:, d, 1:33], data0=rows[:, d, 0, :], data1=rows[:, d, 1, :],
                                                                                  initial=0.0, op0=ALU.max, op1=ALU.add),
                                       reads=[b_rows, b_mrow], writes=[b_mrow])
                            k.tt("dve", Rrow[:], mrow[:, :, 0:32], rows[:, :, 0, :], ALU.max, reads=[b_mrow, b_rows], writes=[b_Rrow])
                            k.tt("dve", drow[:], mrow[:, :, 0:32], Rrow[:], ALU.subtract, reads=[b_mrow, b_Rrow], writes=[b_drow])
                            k.act(drow[:], drow[:], AF.Exp, reads=[b_drow], writes=[b_drow])
                            pbd = pbanks[1]; bbd = bpb[1]
                            for d in range(2):
                                for r in range(4):
                                    o_ = (d * 4 + r) * 32
                                    k.mm(pbd[:, o_:o_ + 32], esel[0:4, r * 128:(r + 1) * 128], drow[:, d, :], True, True,
                                         reads=[b_cs2, b_drow], writes=[bbd])
                            k.copy("dve", DECt[:], pbd[:, 0:256].rearrange("p (x j) -> p x j", x=8), reads=[bbd], writes=[b_DECt])
                            pbr = pbanks[2]; bbr = bpb[2]
                            k.mm(pbr[0:32, 0:4], Rrow[:, 0, :], ident[0:4, 0:4], True, True, reads=[b_Rrow, b_cst], writes=[bbr])
                            k.mm(pbr[0:32, 4:8], Rrow[:, 1, :], ident[0:4, 0:4], True, True, reads=[b_Rrow, b_cst], writes=[bbr])
                            RbT, b_RbT = new([32, 8], F32)
                            k.copy("dve", RbT[:], pbr[0:32, 0:8], reads=[bbr], writes=[b_RbT])
                            k.mm(pbr[0:32, 8:12], J32[0:32, :], RbT[:, 4:8], True, True, reads=[b_RbT, b_cs2], writes=[bbr])
                            Rc, b_Rc = new([32, 2, 4], F32)
                            k.copy("dve", Rc[:, 0, :], RbT[:, 0:4], reads=[b_RbT], writes=[b_Rc])
                            k.copy("dve", Rc[:, 1, :], pbr[0:32, 8:12], reads=[bbr], writes=[b_Rc])
                            Rcb = Rc[:].unsqueeze(3).to_broadcast(D4)
                            k.tt("dve", ee[:], ee[:], Rcb, ALU.subtract, reads=[b_ee, b_Rc], writes=[b_ee])
                            k.act(ee[:], ee[:], AF.Exp, reads=[b_ee], writes=[b_ee])
                            k.tt("dve", BS[:], BS[:], Rcb, ALU.subtract, reads=[b_BS, b_Rc], writes=[b_BS])
                            k.act(BS[:], BS[:], AF.Exp, reads=[b_BS], writes=[b_BS])
                            pbw = pbanks[3]; bbw = bpb[3]
                            for x_, src_, bsrc in ((0, ee, b_ee), (1, BS, b_BS)):
                                for d in range(2):
                                    for r in range(4):
                                        o_ = (x_ * 8 + d * 4 + r) * 32
                                        k.mm(pbw[:, o_:o_ + 32], src_[:, d, r, :], ident[0:32, 0:32], True, True, reads=[bsrc, b_cst], writes=[bbw])
                            k.copy("dve", WCt[:], pbw[:, :].rearrange("p (x y c) -> p x y c", x=2, y=8), reads=[bbw], writes=[b_WCt])
                        with k.scope():
                            Cst = [[new([128, 129], F32, "Cst") for r in range(4)] for d in range(2)]
                            NR = 4
                            STb = [new([128, 128], BF16, "ST") for _ in range(NR)]
                            Vpb = [new([128, 129], BF16, "Vp") for _ in range(NR)]
                            Cdb = [new([128, 129], BF16, "Cd") for _ in range(NR)]
                            ddb = [new([128, 2], F32, "dd") for _ in range(NR)]
                            insts = [(st, d, r) for st in range(32) for d in range(2) for r in range(4)]

                            def geom(it):
                                st, d, r = insts[it]
                                c = st if d == 0 else 31 - st
                                return st, d, r, c, slice(c * 128, (c + 1) * 128), r // 2, 64 * (r % 2), d * 4 + r, it % NR

                            def emitA(it):
                                st, d, r, c, tok, kt, ro, x_, w = geom(it)
                                pS = pbanks[it % 2]; bS = bpb[it % 2]
                                (ST, bST), (Vp, bVp), (Cd, bCd) = STb[w], Vpb[w], Cdb[w]
                                (Cs, bCs) = Cst[d][r]
                                k.mm(pS[:, 0:128], kf[ro:ro + 64, kt, tok], qf[ro:ro + 64, kt, tok], True, True,
                                     reads=[b_kf, b_qf], writes=[bS])
                                k.tt("dve", ST[:], pS[:, 0:128], tri[d], ALU.mult, reads=[bS, b_cst], writes=[bST])
                                zc = cs2[:, 1500:1501]
                                k.ts("pool", Vp[:], vext[:, c, r, :], WCt[:, 0, x_, c:c + 1], zc, ALU.mult, ALU.add,
                                     reads=[b_vext, b_WCt, b_cs2], writes=[bVp])
                                if st > 0:
                                    k.ts("pool", Cs[ro:ro + 64, :], Cs[ro:ro + 64, :], DECt[ro:ro + 64, x_, st:st + 1], zc[ro:ro + 64, :],
                                         ALU.mult, ALU.add, reads=[bCs, b_DECt, b_cs2], writes=[bCs])
                                    k.act(Cd[ro:ro + 64, :], Cs[ro:ro + 64, :], AF.Copy, reads=[bCs], writes=[bCd])

                            def emitB(it):
                                st, d, r, c, tok, kt, ro, x_, w = geom(it)
                                pN = pbanks[2 + it % 2]; bN = bpb[2 + it % 2]
                                pC = pbanks[4 + it % 2]; bC = bpb[4 + it % 2]
                                (ST, bST), (Vp, bVp), (Cd, bCd), (dd, bdd) = STb[w], Vpb[w], Cdb[w], ddb[w]
                                (Cs, bCs) = Cst[d][r]
                                k.mm(pN[:, 0:129], ST[:], Vp[:], True, st == 0, reads=[bST, bVp], writes=[bN])
                                if st > 0:
                                    k.mm(pN[:, 0:129], qf[ro:ro + 64, kt, tok], Cd[ro:ro + 64, :], False, True,
                                         reads=[b_qf, bCd], writes=[bN])
                                k.mm(pC[ro:ro + 64, 0:129], ktm[:, c, r * 64:(r + 1) * 64], Vp[:], True, True,
                                     reads=[b_ktm, bVp], writes=[bC])
                                if st > 0:
                                    k.tt("dve", Cs[ro:ro + 64, :], Cs[ro:ro + 64, :], pC[ro:ro + 64, 0:129], ALU.add,
                                         reads=[bCs, bC], writes=[bCs])
                                else:
                                    k.copy("dve", Cs[ro:ro + 64, :], pC[ro:ro + 64, 0:129], reads=[bC], writes=[bCs])
                                k.act(dd[:, 0:1], pN[:, 128:129], AF.Abs, reads=[bN], writes=[bdd])
                                k.ts("dve", dd[:, 0:1], dd[:, 0:1], WCt[:, 1, x_, c:c + 1], None, ALU.max,
                                     reads=[bdd, b_WCt], writes=[bdd])
                                k.S.op("dve", lambda e: e.reciprocal(out=dd[:, 1:2], in_=dd[:, 0:1]), reads=[bdd], writes=[bdd])
                                cell = big[:, :, 0:T].rearrange("p r (c v) -> p c r v", v=128)[:, c, r, :]
                                first = (d == 0 and c < 16) or (d == 1 and c >= 16)
                                if first:
                                    k.S.op("act", lambda e: e.activation(out=cell, in_=pN[:, 0:128], func=AF.Copy, scale=dd[:, 1:2]),
                                           reads=[bN, bdd], writes=[b_big])
                                else:
                                    k.stt(cell, pN[:, 0:128], dd[:, 1:2], cell, ALU.mult, ALU.add, reads=[bN, bdd, b_big], writes=[b_big])

                            for it in range(len(insts) + 1):
                                if it < len(insts):
                                    emitA(it)
                                if it >= 1:
                                    emitB(it - 1)
                        with k.scope():
                            wo, b_wo = load_w(w_in, 8, 1024, 1536)
                            xs1 = new([128, 8, 256], F32, "xs")
                            xsL = [xs1, xs1]
                            xmL = [new([128, 8, 256], BF16, "xm2") for _ in range(2)]
                            obL = [new([128, 512], F32, "ob") for _ in range(2)]
                            hgL = [new([128, 512], F32, "hg") for _ in range(2)]
                            hnL = [new([128, 512], BF16, "hnb") for _ in range(2)]
                            stL = [new([128, 4, 6], F32, "bst") for _ in range(2)]
                            mvL = [new([128, 4, 2], F32, "mv") for _ in range(2)]
                            mmL = [new([128, 4, 256], BF16, "mixm") for _ in range(2)]
                            (xs, b_xs) = xs1
                            dma_x1(0, 256, xs, b_xs)
                            mod_x1(xmL[0][0], 256, xs, b_xs, xmL[0][1])
                            for ti in range(16):
                                (xm2, b_xm2), (mm_, b_mm) = xmL[ti % 2], mmL[ti % 2]
                                if ti + 1 < 16:
                                    dma_x1((ti + 1) * 256, 256, xs, b_xs)
                                for cc in range(2):
                                    c = ti * 2 + cc
                                    (ob, b_ob), (hg, b_hg), (hnb, b_hnb), (stt_, b_stt), (mv, b_mv) = obL[cc], hgL[cc], hnL[cc], stL[cc], mvL[cc]
                                    bank = pbanks[cc]; bb = bpb[cc]
                                    for kk in range(8):
                                        k.mm(bank[:, :], xm2[:, kk, cc * 128:(cc + 1) * 128], wo[:, kk, :], kk == 0, kk == 7,
                                             reads=[b_wo, b_xm2], writes=[bb])
                                    k.tt("dve", ob[:], bank[:, :], rowbc[:, R_BO:R_BO + 512], ALU.add, reads=[bb, b_rowbc], writes=[b_ob])
                                    k.act(ob[:], ob[:], AF.Sigmoid, reads=[b_ob], writes=[b_ob])
                                    cellc = big[:, :, 0:T].rearrange("p r (c v) -> p c r v", v=128)[:, c, :, :]
                                    k.tt("dve", hg[:].rearrange("p (r v) -> p r v", r=4), ob[:].rearrange("p (r v) -> p r v", r=4), cellc,
                                         ALU.mult, reads=[b_ob, b_big], writes=[b_hg])
                                    for r in range(4):
                                        k.S.op("dve", lambda e, r=r, stt_=stt_, hg=hg: e.bn_stats(out=stt_[:, r, :], in_=hg[:, r * 128:(r + 1) * 128]),
                                               reads=[b_hg], writes=[b_stt])
                                        k.S.op("dve", lambda e, r=r, stt_=stt_, mv=mv: e.bn_aggr(out=mv[:, r, :], in_=stt_[:, r, :]),
                                               reads=[b_stt], writes=[b_mv])
                                    k.ts("dve", mv[:, :, 1], mv[:, :, 1], LN_EPS, None, ALU.add, reads=[b_mv], writes=[b_mv])
                                    k.act(mv[:, :, 1], mv[:, :, 1], AF.Sqrt, reads=[b_mv], writes=[b_mv])
                                    k.S.op("dve", lambda e, mv=mv: e.reciprocal(out=mv[:, :, 1], in_=mv[:, :, 1]), reads=[b_mv], writes=[b_mv])
                                    for r in range(4):
                                        k.ts("dve", hg[:, r * 128:(r + 1) * 128], hg[:, r * 128:(r + 1) * 128], mv[:, r, 0:1], mv[:, r, 1:2],
                                             ALU.subtract, ALU.mult, reads=[b_hg, b_mv], writes=[b_hg])
                                    k.tt("pool", hnb[:], hg[:], rowbc[:, R_MNG:R_MNG + 512], ALU.mult, reads=[b_hg, b_rowbc], writes=[b_hnb])
                                    bank2 = pbanks[2 + cc]; bb2 = bpb[2 + cc]
                                    for r in range(4):
                                        k.mm(bank2[:, r * 128:(r + 1) * 128], hnb[:, r * 128:(r + 1) * 128], ident_bf[:], True, True,
                                             reads=[b_hnb, b_idbf], writes=[bb2])
                                    k.act(mm_[:, :, cc * 128:(cc + 1) * 128], bank2[:, :].rearrange("p (r t) -> p r t", r=4), AF.Copy,
                                          reads=[bb2], writes=[b_mm])
                                k.dma(MIX[b, 0:512, ti * 256:(ti + 1) * 256].rearrange("(k p) t -> p k t", p=128), mm_[:],
                                      reads=[b_mm], writes=[bMIX[b]])
                                if ti + 1 < 16:
                                    mod_x1(xmL[(ti + 1) % 2][0], 256, xs, b_xs, xmL[(ti + 1) % 2][1])

                    with (k.scope() if phase == "MO" else contextlib.nullcontext()):
                      if phase == "MO" and "O" not in SKIP:
                        wob, b_wob = load_w(w_out, 8, 0, 1024)
                        xs = [new([128, 8, TT], F32, "xs") for _ in range(2)]
                        mx = [new([128, 8, TT], BF16, "mx") for _ in range(2)]
                        zL = [new([128, 8, TT], F32, "z") for _ in range(2)]
                        zbL = [k.sb([128, 8, TT], BF16, "zb") for _ in range(2)]
                        zqL = [k.sb([128, 8, TT], BF16, "zq") for _ in range(2)]
                        bzbqL = [k.buf() for _ in range(2)]
                        smL = [new([128, 6, TT], F32, "small") for _ in range(2)]
                        tmpL = [[k.sb([128, TT], F32, "tmp") for _ in range(2)] for _ in range(2)]
                        btmpL = [[k.buf() for _ in range(2)] for _ in range(2)]
                        def loadO(ti):
                            (xs_, bxs_), (mx_, bmx_) = xs[ti % 2], mx[ti % 2]
                            tsl = slice(ti * TT, (ti + 1) * TT)
                            k.dma(xs_[:], x1v[:, :, tsl], reads=[bX1[b]], writes=[bxs_])
                            k.dma(mx_[:], MIX[b].rearrange("(k p) t -> p k t", p=128)[:, :, tsl], reads=[bMIX[b]], writes=[bmx_])

                        loadO(0)
                        for ti in range(NT):
                            s = ti % 2
                            (xs_, bxs_), (mx_, bmx_) = xs[s], mx[s]
                            tsl = slice(ti * TT, (ti + 1) * TT)
                            if ti + 1 < NT:
                                loadO(ti + 1)
                            (z, b_z), zb, zq, b_zbq = zL[s], zbL[s], zqL[s], bzbqL[s]
                            (small, b_small), tmp, b_tmp = smL[s], tmpL[s], btmpL[s]
                            for dt in range(8):
                                bi = (dt // 2) % 2
                                half = (dt % 2) * 256
                                for kk in range(8):
                                    k.mm(pbanks[bi][:, half:half + TT], wob[:, kk, dt * 128:(dt + 1) * 128], mx_[:, kk, :], kk == 0, kk == 7,
                                         reads=[b_wob, bmx_], writes=[bpb[bi]])
                                k.stt(z[:, dt, :], pbanks[bi][:, half:half + TT], der[:, 5, dt, b:b + 1], xs_[:, dt, :], ALU.mult, ALU.add,
                                      reads=[bpb[bi], bxs_, b_der], writes=[b_z])
                            ln_tail(z, b_z, zb, zq, b_zbq, small, b_small, tmp, b_tmp, xs_, bxs_, 1, TT)
                            k.dma(X2[b].rearrange("(k p) t -> p k t", p=128)[:, :, tsl], xs_[:], reads=[bxs_], writes=[bX2[b]])

        if STAGE >= 2:
            k.stacks.pop()
            mixc.close()
            k.fence_ops = k.S.fence()
        if STAGE == 3:
            with k.scope():
                cvb, b_cvb = k.sb([128, 8, 256], BF16, "cvb"), k.buf()
                cvf, b_cvf = k.sb([128, 8, 256], F32, "cvf"), k.buf()
                for b in range(NB):
                    for ti in range(16):
                        tsl = slice(ti * 256, (ti + 1) * 256)
                        k.dma(cvb[:], MIX[b].rearrange("(k p) t -> p k t", p=128)[:, :, tsl], reads=[bMIX[b]], writes=[b_cvb])
                        k.copy("dve", cvf[:], cvb[:], reads=[b_cvb], writes=[b_cvf])
                        final_ops.append(k.dma(outT[b].rearrange("(k p) t -> p k t", p=128)[:, :, tsl], cvf[:], reads=[b_cvf], writes=[b_out[b]]))
        if STAGE >= 9:
            ffn_pass(X2, bX2, outT, b_out, f2u, f2d, 2, 2, True)

        k.S.emit(final_wait_ops=final_ops)
    return nc


def _pack_inputs(inp):
    f = lambda a: np.ascontiguousarray(np.asarray(a, dtype=np.float32))
    col = np.zeros((128, NCOLP), np.float32)

    def put8(off, v):
        col[:, off:off + 8] = f(v).reshape(8, 128).T

    for i, nm in enumerate(["ln1_g", "ln1_b", "ln2_g", "ln2_b", "ln3_g", "ln3_b"]):
        put8(C_LN + 8 * i, inp[nm][0])
    col[:, C_BADA:C_BADA + 72] = f(inp["b_ada"][0]).reshape(72, 128).T
    b_in = f(inp["b_in"][0])
    col[:, C_BQK:C_BQK + 4] = b_in[0:512].reshape(4, 128).T
    cw = f(inp["qk_conv_w"][0])
    for j in range(5):
        col[:, C_CW + 4 * j:C_CW + 4 * j + 4] = cw[j].reshape(4, 128).T
    col[:, C_CB:C_CB + 4] = f(inp["qk_conv_b"][0]).reshape(4, 128).T
    col[:, C_GLUB:C_GLUB + 4] = f(inp["ssm_glu_b"][0]).reshape(4, 128).T
    col[:, C_SNG:C_SNG + 4] = f(inp["ssm_norm_g"][0]).reshape(4, 128).T
    g0 = 1536
    for i in range(4):
        col[0:4, C_BG + i] = b_in[g0 + 4 * i:g0 + 4 * i + 4]
    col[0:16, C_BG + 4] = b_in[g0:g0 + 16]
    row = np.zeros((1, NROWP), np.float32)
    row[0, R_BV:R_BV + 512] = b_in[512:1024]
    row[0, R_BO:R_BO + 512] = b_in[1024:1536]
    row[0, R_BU:R_BU + 512] = b_in[1552:2064]
    row[0, R_MNG:R_MNG + 512] = f(inp["mlstm_norm_g"][0])
    cst = np.zeros((128, 1024), np.float32)
    cst[:, 0:128] = np.eye(128, dtype=np.float32)
    ii = np.arange(128)
    cst[:, 128:256] = (ii[:, None] <= ii[None, :]).astype(np.float32)
    cst[:, 256:384] = (ii[:, None] >= ii[None, :]).astype(np.float32)
    cst[:, 384:896] = np.arange(512, dtype=np.float32)[None, :]
    a8 = np.arange(8, dtype=np.float32)
    for half, rows in ((0, slice(0, 64)), (1, slice(64, 128))):
        cst[rows, 896:904] = (7 - a8) if half == 0 else a8
        cst[rows, 904:912] = (-a8) if half == 0 else a8
        cst[rows, 912:920] = a8 if half == 0 else (-a8)
        cst[rows, 920:928] = (a8 + 1) if half == 0 else (8 - a8)
    cs2 = np.zeros((128, 1536), np.float32)
    cs2[:, 0:512] = (np.arange(512) % 128 != 0).astype(np.float32)[None, :]
    cs2[:, 512:640] = ((ii[:, None] // 16) <= (ii[None, :] // 16)).astype(np.float32)
    cs2[:, 640:768] = ((ii[:, None] // 16) >= (ii[None, :] // 16)).astype(np.float32)
    cs2[0:32, 768:800] = np.eye(32, dtype=np.float32)[::-1]
    cs2[:, 1344:1472] = np.eye(128, dtype=np.float32)[::-1]
    for r in range(4):
        cs2[r, 800 + r * 128:800 + (r + 1) * 128] = 1.0
    dd_ = f(inp["ssm_d"][0]).reshape(32, 16)
    cs2[:, 1312:1344] = np.tile(dd_.T, (8, 1))
    s5 = np.zeros((2, 64, 32, 70), np.float32)
    s5[..., 0] = f(inp["ssm_lam_re"][0]).transpose(0, 2, 1)
    s5[..., 1] = f(inp["ssm_lam_im"][0]).transpose(0, 2, 1)
    s5[..., 2] = f(inp["ssm_log_step"][0])[:, None, :]
    s5[..., 3:19] = f(inp["ssm_b_re"][0]).transpose(1, 0, 2)[None]
    s5[..., 19:35] = f(inp["ssm_b_im"][0]).transpose(1, 0, 2)[None]
    s5[..., 35:51] = f(inp["ssm_c_re"][0]).transpose(0, 3, 1, 2)
    s5[..., 51:67] = f(inp["ssm_c_im"][0]).transpose(0, 3, 1, 2)
    s5p = np.ascontiguousarray(s5.reshape(128, 32 * 70))
    shared = {
        "w_ada": f(inp["w_ada"][0]), "colpack": col, "rowpack": row, "consts": cst,
        "ffn1_w_up": f(inp["ffn1_w_up"][0]), "ffn1_w_down": f(inp["ffn1_w_down"][0]),
        "ffn2_w_up": f(inp["ffn2_w_up"][0]), "ffn2_w_down": f(inp["ffn2_w_down"][0]),
        "w_in": f(inp["w_in"][0]), "w_out": f(inp["w_out"][0]), "glu_w": f(inp["ssm_glu_w"][0]),
        "s5p": s5p, "consts2": cs2,
    }
    x = np.asarray(inp["x"], dtype=np.float32)
    c = np.asarray(inp["c"], dtype=np.float32)
    maps = []
    for core in range(8):
        m = dict(shared)
        m["xT"] = np.ascontiguousarray(x[2 * core:2 * core + 2].transpose(0, 2, 1))
        cc = c[2 * core:2 * core + 2]
        m["cT"] = np.ascontiguousarray(cc.reshape(2, 8, 128).transpose(2, 1, 0).reshape(128, 16))
        maps.append(m)
    return maps


_NC_CACHE = {}


def kernel(**inputs):
    maps = _pack_inputs(inputs)
    if "nc" not in _NC_CACHE:
        _NC_CACHE["nc"] = build_program()
    nc = _NC_CACHE["nc"]
    res = run_bass_kernel_spmd(nc, maps, core_ids=list(range(8)))
    outs = [r["outT"] for r in res.results]
    full = np.concatenate([o.transpose(0, 2, 1) for o in outs], axis=0)
    return np.ascontiguousarray(full.astype(np.float32))
```

```python
import os
import math
import contextlib
import numpy as np
import concourse.bass as bass
import concourse.mybir as mybir
from concourse.bass_utils import run_bass_kernel_spmd

F32 = mybir.dt.float32
BF16 = mybir.dt.bfloat16
I32 = mybir.dt.int32
AF = mybir.ActivationFunctionType
ALU = mybir.AluOpType
AX = mybir.AxisListType

D = 1024
DF = 2816
T = 4096
NB = 2
TT = 256
NT = T // TT
ALPHA = 2.0 ** 0.25
LN_EPS = 1e-5
EPS_P = LN_EPS / (ALPHA * ALPHA)
IN_COLS = 2064
STAGE = int(os.environ.get("MK_STAGE", "9"))
SKIP = os.environ.get("MK_SKIP", "").split(",")

C_LN = 0
C_BADA = 48
C_BQK = 120
C_CW = 124
C_CB = 144
C_GLUB = 148
C_SNG = 152
C_BG = 156
NCOLP = 168
R_BV, R_BO, R_BU, R_MNG = 0, 512, 1024, 1536
NROWP = 2048


class Buf:
    __slots__ = ("name", "w", "r")

    def __init__(self, name="", init_readers=()):
        self.name = name
        self.w = None
        self.r = list(init_readers)


class Sched:
    ENGS = ("pe", "act", "dve", "pool", "sp")
    DMA_RING = 8
    SEM_MAX = 30000

    def __init__(self, nc):
        self.nc = nc
        self.ops = []
        self.by_eng = {e: [] for e in self.ENGS}

    def op(self, eng, fn, reads=(), writes=(), dma=False):
        oid = len(self.ops)
        deps = set()
        for b in reads:
            if b.w is not None:
                deps.add(b.w)
        for b in writes:
            if b.w is not None:
                deps.add(b.w)
            for r in b.r:
                deps.add(r)
        if eng == "pe":
            deps = {d for d in deps if self.ops[d][0] != "pe"}
        self.ops.append((eng, fn, deps, dma))
        self.by_eng[eng].append(oid)
        for b in reads:
            b.r.append(oid)
        for b in writes:
            b.w = oid
            b.r = []
        return oid

    def fence(self):
        f = []
        for e in self.ENGS:
            lst = self.by_eng[e]
            if not lst:
                continue
            if e == "sp":
                f.extend(lst[-self.DMA_RING:])
            else:
                f.append(lst[-1])
        return f

    def emit(self, final_wait_ops=()):
        nc = self.nc
        ops = self.ops
        needed = [False] * len(ops)
        for (_, _, deps, _) in ops:
            for d in deps:
                needed[d] = True
        for d in final_wait_ops:
            needed[d] = True
        sig = [None] * len(ops)
        cnt = {e: 0 for e in self.ENGS}
        dma_idx = {e: 0 for e in self.ENGS}
        for e in self.ENGS:
            for oid in self.by_eng[e]:
                isdma = ops[oid][3]
                if isdma:
                    n = dma_idx[e]
                    dma_idx[e] += 1
                    sig[oid] = (("d", e, n % self.DMA_RING), 16 * (n // self.DMA_RING + 1))
                elif needed[oid]:
                    c = cnt[e]
                    cnt[e] += 1
                    sig[oid] = (("c", e, c // self.SEM_MAX), c % self.SEM_MAX + 1)
        keys = []
        kset = set()
        for s in sig:
            if s is not None and s[0] not in kset:
                kset.add(s[0])
                keys.append(s[0])
        with contextlib.ExitStack() as st:
            sems = {}
            for k in keys:
                sems[k] = st.enter_context(nc.semaphore("s_%s_%s_%d" % k))
            block = st.enter_context(nc.Block())

            def run_engine(ename, eh):
                seen = {}
                for oid in self.by_eng[ename]:
                    _, fn, deps, isdma = ops[oid]
                    waits = {}
                    for d in deps:
                        k, v = sig[d]
                        if seen.get(k, 0) >= v:
                            continue
                        if waits.get(k, 0) < v:
                            waits[k] = v
                    if isdma:
                        k, v = sig[oid]
                        if v > 16 and seen.get(k, 0) < v - 16 and waits.get(k, 0) < v - 16:
                            waits[k] = v - 16
                    for k, v in waits.items():
                        eh.wait_ge(sems[k], v)
                        seen[k] = v
                    ins = fn(eh)
                    if sig[oid] is not None:
                        k, v = sig[oid]
                        ins.then_inc(sems[k], 16 if isdma else 1)
                if ename == "sp":
                    fw = {}
                    for d in final_wait_ops:
                        k, v = sig[d]
                        if fw.get(k, 0) < v:
                            fw[k] = v
                    for k, v in fw.items():
                        if seen.get(k, 0) < v:
                            eh.wait_ge(sems[k], v)

            @block.tensor
            def _(e):
                run_engine("pe", e)

            @block.scalar
            def _(e):
                run_engine("act", e)

            @block.vector
            def _(e):
                run_engine("dve", e)

            @block.gpsimd
            def _(e):
                run_engine("pool", e)

            @block.sync
            def _(e):
                run_engine("sp", e)


class KB:
    def __init__(self, nc):
        self.nc = nc
        self.S = Sched(nc)
        self.stacks = []
        self.fence_ops = []
        self.uid = 0

    @contextlib.contextmanager
    def scope(self):
        st = contextlib.ExitStack()
        self.stacks.append(st)
        try:
            with st:
                yield
        finally:
            self.stacks.pop()
            self.fence_ops = self.S.fence()

    def sb(self, shape, dt, name=None):
        self.uid += 1
        nm = "%s_%d" % (name or "t", self.uid)
        return self.stacks[-1].enter_context(self.nc.sbuf_tensor(nm, list(shape), dt))

    def buf(self, name=""):
        return Buf(name, self.fence_ops)

    def dma(self, out, in_, reads=(), writes=(), **kw):
        return self.S.op("sp", lambda e: e.dma_start(out=out, in_=in_, **kw), reads, writes, dma=True)

    def mm(self, out, lhsT, rhs, start, stop, reads=(), writes=()):
        return self.S.op("pe", lambda e: e.matmul(out, lhsT=lhsT, rhs=rhs, start=start, stop=stop), reads, writes)

    def act(self, out, in_, func, reads=(), writes=(), scale=1.0, bias=0.0):
        return self.S.op("act", lambda e: e.activation(out=out, in_=in_, func=func, scale=scale, bias=bias), reads, writes)

    def tt(self, eng, out, in0, in1, op, reads=(), writes=()):
        return self.S.op(eng, lambda e: e.tensor_tensor(out=out, in0=in0, in1=in1, op=op), reads, writes)

    def ts(self, eng, out, in0, s1, s2, op0, op1=None, reads=(), writes=()):
        if op1 is None:
            return self.S.op(eng, lambda e: e.tensor_scalar(out=out, in0=in0, scalar1=s1, scalar2=None, op0=op0), reads, writes)
        return self.S.op(eng, lambda e: e.tensor_scalar(out=out, in0=in0, scalar1=s1, scalar2=s2, op0=op0, op1=op1), reads, writes)

    def stt(self, out, in0, scalar, in1, op0, op1, reads=(), writes=()):
        return self.S.op("dve", lambda e: e.scalar_tensor_tensor(out=out, in0=in0, scalar=scalar, in1=in1, op0=op0, op1=op1), reads, writes)

    def copy(self, eng, out, in_, reads=(), writes=()):
        if eng == "act":
            return self.act(out, in_, AF.Copy, reads, writes)
        return self.S.op(eng, lambda e: e.tensor_copy(out=out, in_=in_), reads, writes)


def build_program():
    nc = bass.Bass("TRN2", target_bir_lowering=False)
    k = KB(nc)

    def din(name, shape, dt=F32):
        return nc.dram_tensor(name, list(shape), dt, kind="ExternalInput").ap()

    xT = din("xT", [NB, D, T])
    cT = din("cT", [128, 16])
    w_ada = din("w_ada", [D, 9 * D])
    colpack = din("colpack", [128, NCOLP])
    rowpack = din("rowpack", [1, NROWP])
    consts = din("consts", [128, 1024])
    f1u = din("ffn1_w_up", [D, 2 * DF])
    f1d = din("ffn1_w_down", [DF, D])
    f2u = din("ffn2_w_up", [D, 2 * DF])
    f2d = din("ffn2_w_down", [DF, D])
    w_in = din("w_in", [D, IN_COLS])
    w_out = din("w_out", [D, D])
    glu_w = din("glu_w", [512, 512])
    s5p = din("s5p", [128, 32 * 70])
    consts2 = din("consts2", [128, 1536])
    outT = nc.dram_tensor("outT", [NB, D, T], F32, kind="ExternalOutput").ap()
    X1 = nc.dram_tensor("X1s", [NB, D, T], F32, kind="Internal").ap()
    X2 = nc.dram_tensor("X2s", [NB, D, T], F32, kind="Internal").ap()
    MIX = nc.dram_tensor("MIXs", [NB, D, T], BF16, kind="Internal").ap()
    bX1 = [Buf("X1_%d" % b) for b in range(NB)]
    bX2 = [Buf("X2_%d" % b) for b in range(NB)]
    bMIX = [Buf("MIX_%d" % b) for b in range(NB)]

    final_ops = []
    with k.scope():
        cp = k.sb([128, NCOLP], F32, "colpack"); b_cp = k.buf()
        k.dma(cp[:], colpack[:, :], writes=[b_cp])
        cst = k.sb([128, 1024], F32, "consts"); b_cst = k.buf()
        k.dma(cst[:], consts[:, :], writes=[b_cst])
        ones_bf = k.sb([128, 128], BF16, "ones"); b_ones = k.buf()
        k.S.op("pool", lambda e: e.memset(ones_bf[:], 1.0), writes=[b_ones])
        der = k.sb([128, 9, 8, NB], F32, "der"); b_der = k.buf()
        pbanks = []
        for i in range(8):
            pbanks.append(k.stacks[-1].enter_context(nc.psum_tensor("pb%d" % i, [128, 512], F32)))
        bpb = [k.buf("pb%d" % i) for i in range(8)]

        with k.scope():
            ct = k.sb([128, 16], F32, "ct"); b_ct = k.buf()
            k.dma(ct[:], cT[:, :], writes=[b_ct])
            cact = k.sb([128, 16], F32, "cact"); b_cact = k.buf()
            k.act(cact[:], ct[:], AF.Silu, reads=[b_ct], writes=[b_cact])
            wst = [k.sb([128, 8, 512], F32, "wada") for _ in range(2)]
            b_wst = [k.buf() for _ in range(2)]
            wv = w_ada.rearrange("(k p) c -> p k c", p=128)
            for ch in range(18):
                s = ch % 2
                k.dma(wst[s][:], wv[:, :, ch * 512:(ch + 1) * 512], writes=[b_wst[s]])
                for jj in range(4):
                    j = ch * 4 + jj
                    for kk in range(8):
                        k.mm(pbanks[0][:, 2 * j:2 * j + 2], wst[s][:, kk, jj * 128:(jj + 1) * 128],
                             cact[:, 2 * kk:2 * kk + 2], kk == 0, kk == 7,
                             reads=[b_wst[s], b_cact], writes=[bpb[0]])
            modc = k.sb([128, 72, NB], F32, "modc"); b_modc = k.buf()
            k.tt("dve", modc[:], pbanks[0][:, 0:144].rearrange("p (j b) -> p j b", b=NB),
                 cp[:, C_BADA:C_BADA + 72].unsqueeze(2).to_broadcast([128, 72, NB]), ALU.add,
                 reads=[bpb[0], b_cp], writes=[b_modc])
            for sub in range(3):
                base = sub * 24
                k.copy("dve", der[:, 3 * sub + 1, :, :], modc[:, base:base + 8, :], reads=[b_modc], writes=[b_der])
                k.ts("dve", der[:, 3 * sub, :, :], modc[:, base + 8:base + 16, :], 1.0, None, ALU.add,
                     reads=[b_modc], writes=[b_der])
                gsc = (1.0 if sub == 1 else 0.5) / ALPHA
                k.ts("dve", der[:, 3 * sub + 2, :, :], modc[:, base + 16:base + 24, :], 1.0, gsc, ALU.add, ALU.mult,
                     reads=[b_modc], writes=[b_der])

        def ln_tail(z, b_z, zb, zq, b_zbq, small, b_small, tmp, b_tmp, xo, b_xo, lnidx, ntok):
            ln_tail_a(z, b_z, zb, zq, b_zbq)
            ln_tail_b(z, b_z, zb, zq, b_zbq, small, b_small, tmp, b_tmp, xo, b_xo, lnidx, ntok)

        def ln_tail_a(z, b_z, zb, zq, b_zbq):
            k.act(zb[:], z[:], AF.Copy, reads=[b_z], writes=[b_zbq])
            k.act(zq[:], z[:], AF.Square, reads=[b_z], writes=[b_zbq])

        def ln_tail_b(z, b_z, zb, zq, b_zbq, small, b_small, tmp, b_tmp, xo, b_xo, lnidx, ntok, apply_engs=("dve", "pool")):
            pbs = pbanks[6]
            for dt in range(8):
                k.mm(pbs[:, 0:ntok], ones_bf[:], zb[:, dt, :], dt == 0, dt == 7, reads=[b_zbq, b_ones], writes=[bpb[6]])
            for dt in range(8):
                k.mm(pbs[:, 256:256 + ntok], ones_bf[:], zq[:, dt, :], dt == 0, dt == 7, reads=[b_zbq], writes=[bpb[6]])
            mt, m2, vv, sd, rstd, nmr = [small[:, i, :] for i in range(6)]
            k.ts("dve", mt, pbs[:, 0:ntok], 1.0 / D, None, ALU.mult, reads=[bpb[6]], writes=[b_small])
            k.tt("dve", m2, mt, mt, ALU.mult, reads=[b_small], writes=[b_small])
            k.stt(vv, pbs[:, 256:256 + ntok], 1.0 / D, m2, ALU.mult, ALU.subtract, reads=[bpb[6], b_small], writes=[b_small])
            k.ts("dve", vv, vv, EPS_P, None, ALU.add, reads=[b_small], writes=[b_small])
            k.act(sd, vv, AF.Sqrt, reads=[b_small], writes=[b_small])
            k.S.op("dve", lambda e: e.reciprocal(out=rstd, in_=sd), reads=[b_small], writes=[b_small])
            k.stt(nmr, mt, -1.0, rstd, ALU.mult, ALU.mult, reads=[b_small], writes=[b_small])
            for dt in range(8):
                eng = apply_engs[dt % 2]
                tp = tmp[dt % 2]
                k.tt(eng, tp[:], z[:, dt, :], rstd, ALU.mult, reads=[b_z, b_small], writes=[b_tmp[dt % 2]])
                k.tt(eng, tp[:], tp[:], nmr, ALU.add, reads=[b_small, b_tmp[dt % 2]], writes=[b_tmp[dt % 2]])
                k.ts(eng, xo[:, dt, :], tp[:], cp[:, C_LN + 16 * lnidx + dt:C_LN + 16 * lnidx + dt + 1],
                     cp[:, C_LN + 16 * lnidx + 8 + dt:C_LN + 16 * lnidx + 9 + dt], ALU.mult, ALU.add,
                     reads=[b_tmp[dt % 2], b_cp], writes=[b_xo])

        def ffn_pass(src, b_src, dst, b_dst, wu_d, wd_d, sub, lnidx, is_final):
            with k.scope():
                wup = k.sb([128, 8, 2 * DF], BF16, "wup")
                wdn = k.sb([128, 22, D], BF16, "wdn")
                b_wup = [k.buf() for _ in range(22)]
                b_wdn = [k.buf() for _ in range(11)]
                xb = [k.sb([128, 8, TT], F32, "xb") for _ in range(2)]
                b_xb = [k.buf() for _ in range(2)]
                xm = k.sb([128, 8, TT], BF16, "xm"); b_xm = k.buf()
                sgt = [k.sb([128, TT], F32, "sgt") for _ in range(4)]
                b_sgt = [k.buf() for _ in range(4)]
                UB = (0, 1, 4, 5)
                hh = k.sb([128, 22, TT], BF16, "hh"); b_hh = k.buf()
                z = k.sb([128, 8, TT], F32, "z"); b_z = k.buf()
                zb = k.sb([128, 8, TT], BF16, "zb")
                zq = k.sb([128, 8, TT], BF16, "zq"); b_zbq = k.buf()
                small = k.sb([128, 6, TT], F32, "small"); b_small = k.buf()
                tmp = [k.sb([128, TT], F32, "tmp") for _ in range(2)]
                b_tmp = [k.buf() for _ in range(2)]
                wuv = wu_d.rearrange("(k p) c -> p k c", p=128)
                wdv = wd_d.rearrange("(f p) c -> p f c", p=128)
                engs = ["dve", "act", "pool"]
                n = 0
                for ch in range(22):
                    s = n % 2
                    k.dma(xb[s][:], wuv[:, :, ch * 256:(ch + 1) * 256], writes=[b_xb[s]])
                    k.copy(engs[n % 3], wup[:, :, ch * 256:(ch + 1) * 256], xb[s][:], reads=[b_xb[s]], writes=[b_wup[ch]])
                    n += 1
                for ch in range(11):
                    s = n % 2
                    sv = xb[s][:].rearrange("p a t -> p (a t)").rearrange("p (f c) -> p f c", f=2)
                    k.dma(sv, wdv[:, 2 * ch:2 * ch + 2, :], writes=[b_xb[s]])
                    k.copy(engs[n % 3], wdn[:, 2 * ch:2 * ch + 2, :], sv, reads=[b_xb[s]], writes=[b_wdn[ch]])
                    n += 1
                A = lambda dt, b: der[:, 3 * sub, dt, b:b + 1]
                Bm = lambda dt, b: der[:, 3 * sub + 1, dt, b:b + 1]
                GA = lambda dt, b: der[:, 3 * sub + 2, dt, b:b + 1]
                tiles = [(b, ti) for b in range(NB) for ti in range(NT)]

                def load_x(i):
                    b, ti = tiles[i]
                    s = i % 2
                    k.dma(xb[s][:], src[b].rearrange("(k p) t -> p k t", p=128)[:, :, ti * TT:(ti + 1) * TT],
                          reads=[b_src[b]], writes=[b_xb[s]])

                def make_xm(i):
                    b, ti = tiles[i]
                    s = i % 2
                    for dt in range(8):
                        eng = ("dve", "pool", "act", "pool")[dt % 4]
                        if eng == "act":
                            k.S.op("act", lambda e, dt=dt: e.activation(out=xm[:, dt, :], in_=xb[s][:, dt, :], func=AF.Identity,
                                                                        scale=A(dt, b), bias=Bm(dt, b)),
                                   reads=[b_xb[s], b_der], writes=[b_xm])
                        else:
                            k.ts(eng, xm[:, dt, :], xb[s][:, dt, :], A(dt, b), Bm(dt, b), ALU.mult, ALU.add,
                                 reads=[b_xb[s], b_der], writes=[b_xm])

                load_x(0)
                make_xm(0)

                def finish(i):
                    b, ti = tiles[i]
                    s = i % 2
                    ln_tail_b(z, b_z, zb, zq, b_zbq, small, b_small, tmp, b_tmp, xb[s], b_xb[s], lnidx, TT, apply_engs=("pool", "pool"))
                    o = k.dma(dst[b].rearrange("(k p) t -> p k t", p=128)[:, :, ti * TT:(ti + 1) * TT], xb[s][:],
                              reads=[b_xb[s]], writes=[b_dst[b]])
                    if is_final:
                        final_ops.append(o)

                for i, (b, ti) in enumerate(tiles):
                    s = i % 2
                    for j in range(22):
                        if j == 2:
                            if i > 0:
                                finish(i - 1)
                            if i + 1 < len(tiles):
                                load_x(i + 1)
                        bank = pbanks[UB[j % 4]]; bb = bpb[UB[j % 4]]
                        for kk in range(8):
                            k.mm(bank[:, 0:TT], wup[:, kk, j * 128:(j + 1) * 128], xm[:, kk, :], kk == 0, kk == 7,
                                 reads=[b_wup[j // 2], b_xm], writes=[bb])
                        for kk in range(8):
                            c0 = DF + j * 128
                            k.mm(bank[:, 256:256 + TT], wup[:, kk, c0:c0 + 128], xm[:, kk, :], kk == 0, kk == 7,
                                 reads=[b_wup[c0 // 256], b_xm], writes=[bb])
                        k.act(sgt[j % 4][:], bank[:, 0:TT], AF.Silu, reads=[bb], writes=[b_sgt[j % 4]])
                        k.tt("dve", hh[:, j, :], sgt[j % 4][:], bank[:, 256:256 + TT], ALU.mult,
                             reads=[bb, b_sgt[j % 4]], writes=[b_hh])
                    if i + 1 < len(tiles):
                        make_xm(i + 1)
                    for dt in range(8):
                        bi = 2 + (dt // 2) % 2
                        half = (dt % 2) * 256
                        for j in range(22):
                            k.mm(pbanks[bi][:, half:half + TT], wdn[:, j, dt * 128:(dt + 1) * 128], hh[:, j, :],
                                 j == 0, j == 21, reads=[b_wdn[j // 2], b_hh], writes=[bpb[bi]])
                        k.stt(z[:, dt, :], pbanks[bi][:, half:half + TT], GA(dt, b), xb[s][:, dt, :], ALU.mult, ALU.add,
                              reads=[bpb[bi], b_xb[s], b_der], writes=[b_z])
                    ln_tail_a(z, b_z, zb, zq, b_zbq)
                finish(len(tiles) - 1)

        b_out = [Buf("out%d" % b) for b in range(NB)]
        if STAGE == 1:
            ffn_pass(xT, [Buf(), Buf()], outT, b_out, f1u, f1d, 0, 0, True)
        if STAGE >= 2 and STAGE != 3:
            ffn_pass(xT, [Buf(), Buf()], X1, bX1, f1u, f1d, 0, 0, False)
        if STAGE == 3:
            bX1 = [Buf(), Buf()]
            X1 = xT
        if STAGE >= 2:
            TWO_PI = 2.0 * math.pi
            mixc = contextlib.ExitStack()
            k.stacks.append(mixc)
            ident = cst[:, 0:128]
            tri = [cst[:, 128:256], cst[:, 256:384]]
            iota = cst[:, 384:896]
            KTc = cst[:, 896:928]
            cs2 = k.sb([128, 1536], F32, "consts2"); b_cs2 = k.buf()
            k.dma(cs2[:], consts2[:, :], writes=[b_cs2])
            rmask = cs2[:, 0:512]
            mle2 = cs2[:, 512:640]
            mge2 = cs2[:, 640:768]
            J32 = cs2[:, 768:800]
            esel = cs2[:, 800:1312]
            dpk = cs2[:, 1312:1344]
            ident_bf = k.sb([128, 128], BF16, "identbf"); b_idbf = k.buf()
            k.copy("dve", ident_bf[:], ident, reads=[b_cst], writes=[b_idbf])
            J_bf = k.sb([128, 128], BF16, "Jbf")
            k.copy("dve", J_bf[:], cs2[:, 1344:1472], reads=[b_cs2], writes=[b_idbf])
            rowbc = k.sb([128, NROWP], F32, "rowbc"); b_rowbc = k.buf()
            k.dma(rowbc[:], rowpack[0:1, :].partition_broadcast(128), writes=[b_rowbc])
            GSd = nc.dram_tensor("GSs", [NB, 4, 4, T], F32, kind="Internal").ap()
            bGS = [Buf() for _ in range(NB)]

            def new(shape, dt, name="w"):
                return k.sb(shape, dt, name), k.buf()

            def load_w(src_ap, rows_k, c0, c1, stage_cols=128):
                wt, b_wt = new([128, rows_k, c1 - c0], BF16, "wbf")
                v = src_ap.rearrange("(k p) c -> p k c", p=128)
                with k.scope():
                    stg = [new([128, rows_k, stage_cols], F32, "wstg") for _ in range(2)]
                    i = 0
                    c = c0
                    while c < c1:
                        w_ = min(stage_cols, c1 - c)
                        st_, bs_ = stg[i % 2]
                        k.dma(st_[:, :, 0:w_], v[:, :, c:c + w_], writes=[bs_])
                        k.copy(("dve", "pool")[i % 2], wt[:, :, c - c0:c - c0 + w_], st_[:, :, 0:w_], reads=[bs_], writes=[b_wt])
                        c += w_
                        i += 1
                return wt, b_wt

            def reduce_angle(x, out, shape, bx, bo):
                tf, btf = new(shape, F32, "raf")
                ti, bti = new(shape, I32, "rai")
                k.ts("dve", tf[:], x, 1.0 / TWO_PI, None, ALU.mult, reads=[bx], writes=[btf])
                k.copy("dve", ti[:], tf[:], reads=[btf], writes=[bti])
                k.copy("dve", tf[:], ti[:], reads=[bti], writes=[btf])
                k.stt(out, tf[:], -TWO_PI, x, ALU.mult, ALU.add, reads=[btf, bx], writes=[bo])

            def sincos(x, sn, cs_, shape, bx, bsn, bcs):
                s2, bs2 = new(shape, F32, "s2")
                s4, bs4 = new(shape, F32, "s4")
                k.act(s2[:], x, AF.Sin, reads=[bx], writes=[bs2], scale=0.5)
                k.act(s4[:], x, AF.Sin, reads=[bx], writes=[bs4], scale=0.25)
                k.tt("dve", cs_, s2[:], s2[:], ALU.mult, reads=[bs2], writes=[bcs])
                k.ts("dve", cs_, cs_, -2.0, 1.0, ALU.mult, ALU.add, reads=[bcs], writes=[bcs])
                k.tt("dve", s4[:], s4[:], s4[:], ALU.mult, reads=[bs4], writes=[bs4])
                k.ts("dve", s4[:], s4[:], -2.0, 1.0, ALU.mult, ALU.add, reads=[bs4], writes=[bs4])
                k.stt(sn, s2[:], 2.0, s4[:], ALU.mult, ALU.mult, reads=[bs2, bs4], writes=[bsn])

            with k.scope():
                mst = contextlib.ExitStack()
                k.stacks.append(mst)
                Tm, b_Tm = new([128, 32, 128], BF16, "Tm")
                PTre, b_PTre = new([128, 32, 128], BF16, "PTre")
                PTim, b_PTim = new([128, 32, 128], BF16, "PTim")
                Qre, b_Qre = new([128, 32, 128], BF16, "Qre")
                Qim, b_Qim = new([128, 32, 128], BF16, "Qim")
                thr, b_thr = new([128, 32], F32, "thr")
                r8, b_r8 = new([128, 32], F32, "r8")
                with k.scope():
                    sp, b_sp = new([128, 32, 70], F32, "s5p")
                    k.dma(sp[:], s5p.rearrange("p (g f) -> p g f", f=70), writes=[b_sp])
                    lr = sp[:, :, 0]; li = sp[:, :, 1]
                    S2 = [128, 32]
                    step, b_step = new(S2, F32)
                    k.act(step[:], sp[:, :, 2], AF.Exp, reads=[b_sp], writes=[b_step])
                    lrs, b_lrs = new(S2, F32)
                    k.tt("dve", lrs[:], lr, step[:], ALU.mult, reads=[b_sp, b_step], writes=[b_lrs])
                    phi, b_phi = new(S2, F32)
                    k.tt("dve", phi[:], li, step[:], ALU.mult, reads=[b_sp, b_step], writes=[b_phi])
                    phr, b_phr = new(S2, F32)
                    reduce_angle(phi[:], phr[:], S2, b_phi, b_phr)
                    th8, b_th8 = new(S2, F32)
                    k.ts("dve", th8[:], phr[:], 8.0, None, ALU.mult, reads=[b_phr], writes=[b_th8])
                    reduce_angle(th8[:], thr[:], S2, b_th8, b_thr)
                    k.act(r8[:], lrs[:], AF.Exp, reads=[b_lrs], writes=[b_r8], scale=8.0)
                    sn1, b_sn1 = new(S2, F32); cs1, b_cs1 = new(S2, F32); mg1, b_mg1 = new(S2, F32)
                    sincos(phr[:], sn1[:], cs1[:], S2, b_phr, b_sn1, b_cs1)
                    k.act(mg1[:], lrs[:], AF.Exp, reads=[b_lrs], writes=[b_mg1])
                    nr, b_nr = new(S2, F32); ni, b_ni = new(S2, F32)
                    k.tt("dve", nr[:], mg1[:], cs1[:], ALU.mult, reads=[b_mg1, b_cs1], writes=[b_nr])
                    k.ts("dve", nr[:], nr[:], -1.0, None, ALU.add, reads=[b_nr], writes=[b_nr])
                    k.tt("dve", ni[:], mg1[:], sn1[:], ALU.mult, reads=[b_mg1, b_sn1], writes=[b_ni])
                    inv, b_inv = new(S2, F32); t0, b_t0 = new(S2, F32); t1, b_t1 = new(S2, F32)
                    k.tt("dve", inv[:], lr, lr, ALU.mult, reads=[b_sp], writes=[b_inv])
                    k.tt("dve", t0[:], li, li, ALU.mult, reads=[b_sp], writes=[b_t0])
                    k.tt("dve", inv[:], inv[:], t0[:], ALU.add, reads=[b_inv, b_t0], writes=[b_inv])
                    k.S.op("dve", lambda e: e.reciprocal(out=inv[:], in_=inv[:]), reads=[b_inv], writes=[b_inv])
                    cfr, b_cfr = new(S2, F32); cfi, b_cfi = new(S2, F32)
                    k.tt("dve", t0[:], nr[:], lr, ALU.mult, reads=[b_nr, b_sp], writes=[b_t0])
                    k.tt("dve", t1[:], ni[:], li, ALU.mult, reads=[b_ni, b_sp], writes=[b_t1])
                    k.tt("dve", t0[:], t0[:], t1[:], ALU.add, reads=[b_t0, b_t1], writes=[b_t0])
                    k.tt("dve", cfr[:], t0[:], inv[:], ALU.mult, reads=[b_t0, b_inv], writes=[b_cfr])
                    k.tt("dve", t0[:], ni[:], lr, ALU.mult, reads=[b_ni, b_sp], writes=[b_t0])
                    k.tt("dve", t1[:], nr[:], li, ALU.mult, reads=[b_nr, b_sp], writes=[b_t1])
                    k.tt("dve", t0[:], t0[:], t1[:], ALU.subtract, reads=[b_t0, b_t1], writes=[b_t0])
                    k.tt("dve", cfi[:], t0[:], inv[:], ALU.mult, reads=[b_t0, b_inv], writes=[b_cfi])
                    S3 = [128, 32, 16]
                    bre = sp[:, :, 3:19]; bim = sp[:, :, 19:35]; cre = sp[:, :, 35:51]; cim = sp[:, :, 51:67]
                    Bcr, b_Bcr = new(S3, F32); Bci, b_Bci = new(S3, F32); u0, b_u0 = new(S3, F32)
                    cfrb = cfr[:].unsqueeze(2).to_broadcast(S3); cfib = cfi[:].unsqueeze(2).to_broadcast(S3)
                    k.tt("dve", Bcr[:], bre, cfrb, ALU.mult, reads=[b_sp, b_cfr], writes=[b_Bcr])
                    k.tt("dve", u0[:], bim, cfib, ALU.mult, reads=[b_sp, b_cfi], writes=[b_u0])
                    k.tt("dve", Bcr[:], Bcr[:], u0[:], ALU.subtract, reads=[b_Bcr, b_u0], writes=[b_Bcr])
                    k.tt("dve", Bci[:], bim, cfrb, ALU.mult, reads=[b_sp, b_cfr], writes=[b_Bci])
                    k.tt("dve", u0[:], bre, cfib, ALU.mult, reads=[b_sp, b_cfi], writes=[b_u0])
                    k.tt("dve", Bci[:], Bci[:], u0[:], ALU.add, reads=[b_Bci, b_u0], writes=[b_Bci])
                    SP = [128, 32, 32]
                    pwr, b_pwr = new(SP, F32); pwi, b_pwi = new(SP, F32)
                    with k.scope():
                        ang, b_ang = new(SP, F32); angr, b_angr = new(SP, F32)
                        ktb = KTc.unsqueeze(1).to_broadcast(SP)
                        k.tt("dve", ang[:], phr[:].unsqueeze(2).to_broadcast(SP), ktb, ALU.mult, reads=[b_phr, b_cst], writes=[b_ang])
                        reduce_angle(ang[:], angr[:], SP, b_ang, b_angr)
                        psn, b_psn = new(SP, F32); pcs, b_pcs = new(SP, F32); pmg, b_pmg = new(SP, F32)
                        sincos(angr[:], psn[:], pcs[:], SP, b_angr, b_psn, b_pcs)
                        k.tt("dve", pmg[:], lrs[:].unsqueeze(2).to_broadcast(SP), ktb, ALU.mult, reads=[b_lrs, b_cst], writes=[b_pmg])
                        k.act(pmg[:], pmg[:], AF.Exp, reads=[b_pmg], writes=[b_pmg])
                        k.tt("dve", pwr[:], pmg[:], pcs[:], ALU.mult, reads=[b_pmg, b_pcs], writes=[b_pwr])
                        k.tt("dve", pwi[:], pmg[:], psn[:], ALU.mult, reads=[b_pmg, b_psn], writes=[b_pwi])
                    S4 = [128, 32, 8, 16]
                    ta, b_ta = new(S4, F32); tb_, b_tb = new(S4, F32)

                    def cprod(ar, ai, bar, bai, off, o_re, b_ore, o_im, b_oim, neg_im):
                        arb = ar.unsqueeze(2).to_broadcast(S4); aib = ai.unsqueeze(2).to_broadcast(S4)
                        pr = pwr[:, :, off:off + 8].unsqueeze(3).to_broadcast(S4)
                        pi_ = pwi[:, :, off:off + 8].unsqueeze(3).to_broadcast(S4)
                        k.tt("dve", ta[:], arb, pr, ALU.mult, reads=[bar, b_pwr], writes=[b_ta])
                        k.tt("pool", tb_[:], aib, pi_, ALU.mult, reads=[bai, b_pwi], writes=[b_tb])
                        k.tt("dve", o_re, ta[:], tb_[:], ALU.subtract, reads=[b_ta, b_tb], writes=[b_ore])
                        k.tt("dve", ta[:], arb, pi_, ALU.mult, reads=[bar, b_pwi], writes=[b_ta])
                        k.tt("pool", tb_[:], aib, pr, ALU.mult, reads=[bai, b_pwr], writes=[b_tb])
                        if neg_im:
                            k.stt(o_im, ta[:], -1.0, tb_[:], ALU.mult, ALU.subtract, reads=[b_ta, b_tb], writes=[b_oim])
                        else:
                            k.tt("dve", o_im, ta[:], tb_[:], ALU.add, reads=[b_ta, b_tb], writes=[b_oim])

                    v4 = lambda t_: t_[:].rearrange("p g (s h) -> p g s h", h=16)
                    if "noQ" not in SKIP:
                        cprod(cre, cim, b_sp, b_sp, 24, v4(Qre), b_Qre, v4(Qim), b_Qim, True)
                    with k.scope():
                      if "noP" not in SKIP:
                        Pr, b_Pr = new([128, 32, 128], F32); Pi, b_Pi = new([128, 32, 128], F32)
                        cprod(Bcr[:], Bci[:], b_Bcr, b_Bci, 0, v4(Pr), b_Pr, v4(Pi), b_Pi, False)
                        for g in range(32):
                            bank = pbanks[g % 2]; bb = bpb[g % 2]
                            k.mm(bank[:, 256:384], Pr[:, g, :], ident, True, True, reads=[b_Pr, b_cst], writes=[bb])
                            k.mm(bank[:, 384:512], Pi[:, g, :], ident, True, True, reads=[b_Pi, b_cst], writes=[bb])
                            k.act(PTre[:, g, :], bank[:, 256:384], AF.Copy, reads=[bb], writes=[b_PTre])
                            k.act(PTim[:, g, :], bank[:, 384:512], AF.Copy, reads=[bb], writes=[b_PTim])
                    with k.scope():
                      if "noT" not in SKIP:
                        Xr, b_Xr = new([128, 32, 128], F32); Xi, b_Xi = new([128, 32, 128], F32)
                        cprod(Bcr[:], Bci[:], b_Bcr, b_Bci, 8, v4(Xr), b_Xr, v4(Xi), b_Xi, False)
                        Zr, b_Zr = new([128, 32, 128], F32); Zi, b_Zi = new([128, 32, 128], F32)
                        cprod(cre, cim, b_sp, b_sp, 16, v4(Zr), b_Zr, v4(Zi), b_Zi, True)
                        Xmr = ta[:].rearrange("p g s h -> p g (s h)"); b_Xmr = b_ta
                        Xmi = tb_[:].rearrange("p g s h -> p g (s h)"); b_Xmi = b_tb
                        hm = [cst[:, 128 + 63:128 + 64], cst[:, 256 + 64:256 + 65]]
                        tA, b_tA = new([128, 128], F32); tB, b_tB = new([128, 128], F32)
                        for dd_ in range(2):
                            k.ts("dve", Xmr, Xr[:], hm[dd_], None, ALU.mult, reads=[b_Xr, b_cst], writes=[b_Xmr])
                            k.ts("pool", Xmi, Xi[:], hm[dd_], None, ALU.mult, reads=[b_Xi, b_cst], writes=[b_Xmi])
                            for g in range(32):
                                bank = pbanks[2 + g % 2]; bb = bpb[2 + g % 2]
                                k.mm(bank[:, 0:128], Xmr[:, g, :], Zr[:, g, :], True, False, reads=[b_Xmr, b_Zr], writes=[bb])
                                k.mm(bank[:, 0:128], Xmi[:, g, :], Zi[:, g, :], False, True, reads=[b_Xmi, b_Zi], writes=[bb])
                                if dd_ == 0:
                                    k.tt("dve", tA[:], bank[:, 0:128], mle2, ALU.mult, reads=[bb, b_cs2], writes=[b_tA])
                                    k.stt(Tm[:, g, :], ident, dpk[:, g:g + 1], tA[:], ALU.mult, ALU.add,
                                          reads=[b_tA, b_cs2, b_cst], writes=[b_Tm])
                                else:
                                    k.tt("dve", tB[:], bank[:, 0:128], mge2, ALU.mult, reads=[bb, b_cs2], writes=[b_tB])
                                    k.tt("dve", Tm[:, g, :], Tm[:, g, :], tB[:], ALU.add, reads=[b_tB, b_Tm], writes=[b_Tm])

                for (b, phase) in ((0, "S"), (1, "S"), (0, "MO"), (1, "MO")):
                    if phase == "MO" and mst is not None:
                        k.stacks.pop()
                        mst.close()
                        mst = None
                        k.fence_ops = k.S.fence()
                    x1v = X1[b].rearrange("(k p) t -> p k t", p=128)

                    def dma_x1(t0, ntok, xs, b_xs):
                        k.dma(xs[:, :, 0:ntok], x1v[:, :, t0:t0 + ntok], reads=[bX1[b]], writes=[b_xs])

                    def mod_x1(dst_ap, ntok, xs, b_xs, b_dst):
                        for dt in range(8):
                            eng = ("dve", "pool")[dt % 2]
                            k.ts(eng, dst_ap[:, dt, :], xs[:, dt, 0:ntok], der[:, 3, dt, b:b + 1], der[:, 4, dt, b:b + 1],
                                 ALU.mult, ALU.add, reads=[b_xs, b_der], writes=[b_dst])

                    def load_xm2(dst_ap, t0, ntok, xs, b_xs, b_dst):
                        k.dma(xs[:, :, 0:ntok], x1v[:, :, t0:t0 + ntok], reads=[bX1[b]], writes=[b_xs])
                        for dt in range(8):
                            eng = ("dve", "pool")[dt % 2]
                            k.ts(eng, dst_ap[:, dt, :], xs[:, dt, 0:ntok], der[:, 3, dt, b:b + 1], der[:, 4, dt, b:b + 1],
                                 ALU.mult, ALU.add, reads=[b_xs, b_der], writes=[b_dst])

                    with (k.scope() if phase == "S" else contextlib.nullcontext()):
                      if phase == "S" and "S" not in SKIP:
                        Zy, b_Zy = new([128, 4, 8, 512], BF16, "Zy")
                        Zta, b_Zta = new([128, 4, 32, 8, 16], BF16, "Zta")
                        with k.scope():
                            wu, b_wu = load_w(w_in, 8, 1552, 2064)
                            xsL2 = [new([128, 8, 128], F32, "xs") for _ in range(2)]
                            xmbL = [new([128, 8, 1024], BF16, "xmb") for _ in range(2)]

                            def prepA(cb):
                                (xmb, b_xmb) = xmbL[cb % 2]
                                for q8 in range(8):
                                    (xs, b_xs) = xsL2[q8 % 2]
                                    load_xm2(xmb[:, :, q8 * 128:(q8 + 1) * 128], cb * 1024 + q8 * 128, 128, xs, b_xs, b_xmb)

                            prepA(0)
                            for cb in range(4):
                                (xmb, b_xmb) = xmbL[cb % 2]
                                if cb + 1 < 4:
                                    prepA(cb + 1)
                                for s_ in range(8):
                                    bank = pbanks[s_ % 2]; bb = bpb[s_ % 2]
                                    for kk in range(8):
                                        lv = xmb[:, kk, :].rearrange("p (c s) -> p s c", s=8)[:, s_, :]
                                        k.mm(bank[:, :], lv, wu[:, kk, :], kk == 0, kk == 7, reads=[b_xmb, b_wu], writes=[bb])
                                    k.tt("dve", Zta[:, cb, :, s_, :], bank[:, :].rearrange("p (g h) -> p g h", h=16),
                                         rowbc[:, R_BU:R_BU + 512].rearrange("p (g h) -> p g h", h=16), ALU.add,
                                         reads=[bb, b_rowbc], writes=[b_Zta])
                        NG = 8
                        for hf_ in range(32 // NG):
                          with k.scope():
                            G0 = NG * hf_
                            Ut, b_Ut = new([128, NG, 512], BF16, "Ut")
                            Ur, b_Ur = new([128, NG, 512], BF16, "Ur")
                            for cb in range(4):
                                for g4 in range(NG // 4):
                                    bank = pbanks[2 + g4 % 2]; bb = bpb[2 + g4 % 2]
                                    bank2 = pbanks[4 + g4 % 2]; bb2 = bpb[4 + g4 % 2]
                                    for gg in range(4):
                                        g = G0 + g4 * 4 + gg
                                        zl = Zta[:, cb, g, :, :].rearrange("p s h -> p (s h)")
                                        k.mm(bank[:, gg * 128:(gg + 1) * 128], zl, ident_bf[:], True, True,
                                             reads=[b_Zta, b_idbf], writes=[bb])
                                        k.mm(bank2[:, gg * 128:(gg + 1) * 128], zl, J_bf[:], True, True,
                                             reads=[b_Zta, b_idbf], writes=[bb2])
                                    k.act(Ut[:, g4 * 4:g4 * 4 + 4, cb * 128:(cb + 1) * 128],
                                          bank[:, :].rearrange("p (g c) -> p g c", g=4), AF.Copy, reads=[bb], writes=[b_Ut])
                                    k.copy("dve", Ur[:, g4 * 4:g4 * 4 + 4, (3 - cb) * 128:(4 - cb) * 128],
                                           bank2[:, :].rearrange("p (g c) -> p g c", g=4), reads=[bb2], writes=[b_Ur])
                            with k.scope():
                                W_ = lambda: new([128, 512], F32, "sw")
                                W2 = lambda: [W_(), W_()]
                                (ta_, bta), (tr_, btr), (s2t, bs2) = W_(), W_(), W_()
                                (ti_, bti) = new([128, 512], I32, "twi")
                                tcs, tsns = W2(), W2()
                                srL, siL, a1L, a2L, stRL, stIL, kRL, kIL = [W2() for _ in range(8)]
                                HpL = [[new([128, 513], BF16, "Hp") for _ in range(4)] for _ in range(2)]
                                for hl in HpL:
                                    for (h_, bh_) in hl:
                                        k.S.op("pool", lambda e, h_=h_: e.memset(h_[:], 0.0), writes=[bh_])
                                ygL = [new([128, 512], BF16, "yg") for _ in range(2)]
                                ygbL = [new([128, 512], BF16, "ygb") for _ in range(2)]
                                ygTL = [new([128, 4, 128], BF16, "ygT") for _ in range(2)]
                                pY, bY = pbanks[0], bpb[0]
                                pYb, bYb = pbanks[1], bpb[1]
                                pAr, bAr = pbanks[2], bpb[2]
                                pBr, bBr = pbanks[3], bpb[3]
                                pAi, bAi = pbanks[4], bpb[4]
                                pBi, bBi = pbanks[5], bpb[5]
                                pT, bT = pbanks[6], bpb[6]
                                pT2, bT2 = pbanks[7], bpb[7]

                                def gen_tw(gl):
                                    g = G0 + gl
                                    (tc, btc), (tsn, bts) = tcs[gl % 2], tsns[gl % 2]
                                    k.S.op("act", lambda e: e.activation(out=ta_[:], in_=iota, func=AF.Copy, scale=thr[:, g:g + 1]),
                                           reads=[b_cst, b_thr], writes=[bta])
                                    k.act(tr_[:], ta_[:], AF.Copy, reads=[bta], writes=[btr], scale=1.0 / TWO_PI)
                                    k.copy("dve", ti_[:], tr_[:], reads=[btr], writes=[bti])
                                    k.copy("dve", tr_[:], ti_[:], reads=[bti], writes=[btr])
                                    k.stt(ta_[:], tr_[:], -TWO_PI, ta_[:], ALU.mult, ALU.add, reads=[btr, bta], writes=[bta])
                                    k.act(s2t[:], ta_[:], AF.Sin, reads=[bta], writes=[bs2], scale=0.5)
                                    k.act(tr_[:], ta_[:], AF.Sin, reads=[bta], writes=[btr], scale=0.25)
                                    k.tt("pool", tc[:], s2t[:], s2t[:], ALU.mult, reads=[bs2], writes=[btc])
                                    k.ts("pool", tc[:], tc[:], -2.0, 1.0, ALU.mult, ALU.add, reads=[btc], writes=[btc])
                                    k.tt("pool", tr_[:], tr_[:], tr_[:], ALU.mult, reads=[btr], writes=[btr])
                                    k.ts("pool", tr_[:], tr_[:], -2.0, 1.0, ALU.mult, ALU.add, reads=[btr], writes=[btr])
                                    k.stt(tsn[:], s2t[:], 2.0, tr_[:], ALU.mult, ALU.mult, reads=[bs2, btr], writes=[bts])

                                def stageA(gl):
                                    g = G0 + gl
                                    (sr, bsr), (si, bsi) = srL[gl % 2], siL[gl % 2]
                                    k.mm(pAr[:, :], PTre[:, g, :], Ut[:, gl, :], True, True, reads=[b_PTre, b_Ut], writes=[bAr])
                                    k.mm(pBr[:, :], PTre[:, g, :], Ur[:, gl, :], True, True, reads=[b_PTre, b_Ur], writes=[bBr])
                                    k.mm(pAi[:, :], PTim[:, g, :], Ut[:, gl, :], True, True, reads=[b_PTim, b_Ut], writes=[bAi])
                                    k.mm(pBi[:, :], PTim[:, g, :], Ur[:, gl, :], True, True, reads=[b_PTim, b_Ur], writes=[bBi])
                                    k.act(sr[0:64, :], pAr[0:64, :], AF.Copy, reads=[bAr], writes=[bsr])
                                    k.act(sr[64:128, :], pBr[64:128, :], AF.Copy, reads=[bBr], writes=[bsr])
                                    k.act(si[0:64, :], pAi[0:64, :], AF.Copy, reads=[bAi], writes=[bsi])
                                    k.act(si[64:128, :], pBi[64:128, :], AF.Copy, reads=[bBi], writes=[bsi])

                                def stageB(gl):
                                    g = G0 + gl
                                    w = gl % 2
                                    (tc, btc), (tsn, bts) = tcs[w], tsns[w]
                                    (sr, bsr), (si, bsi), (a1, ba1), (a2, ba2) = srL[w], siL[w], a1L[w], a2L[w]
                                    (stR, bstR), (stI, bstI), (kR, bkR), (kI, bkI) = stRL[w], stIL[w], kRL[w], kIL[w]
                                    (hrf, bhrf), (hif, bhif), (hrb, bhrb), (hib, bhib) = HpL[w]
                                    (yg, byg), (ygb, bygb), (ygT, bygT) = ygL[w], ygbL[w], ygTL[w]
                                    k.tt("dve", a1[:], sr[:], tc[:], ALU.mult, reads=[bsr, btc], writes=[ba1])
                                    k.tt("pool", a2[:], si[:], tsn[:], ALU.mult, reads=[bsi, bts], writes=[ba2])
                                    k.tt("dve", stR[:], a1[:], a2[:], ALU.add, reads=[ba1, ba2], writes=[bstR])
                                    k.tt("dve", a1[:], si[:], tc[:], ALU.mult, reads=[bsi, btc], writes=[ba1])
                                    k.tt("pool", a2[:], sr[:], tsn[:], ALU.mult, reads=[bsr, bts], writes=[ba2])
                                    k.tt("dve", stI[:], a1[:], a2[:], ALU.subtract, reads=[ba1, ba2], writes=[bstI])
                                    r8b = r8[:, g:g + 1].to_broadcast([128, 512])
                                    k.S.op("dve", lambda e: e.tensor_tensor_scan(out=kR[:], data0=r8b, data1=stR[:], initial=0.0,
                                                                                 op0=ALU.mult, op1=ALU.add),
                                           reads=[bstR, b_r8], writes=[bkR])
                                    k.S.op("dve", lambda e: e.tensor_tensor_scan(out=kI[:], data0=r8b, data1=stI[:], initial=0.0,
                                                                                 op0=ALU.mult, op1=ALU.add),
                                           reads=[bstI, b_r8], writes=[bkI])
                                    if gl + 1 < NG:
                                        gen_tw(gl + 1)
                                    k.tt("dve", a1[:], kR[:], tc[:], ALU.mult, reads=[bkR, btc], writes=[ba1])
                                    k.tt("pool", a2[:], kI[:], tsn[:], ALU.mult, reads=[bkI, bts], writes=[ba2])
                                    k.tt("dve", hrf[0:64, 1:513], a1[0:64, :], a2[0:64, :], ALU.subtract, reads=[ba1, ba2], writes=[bhrf])
                                    k.tt("dve", hrb[64:128, 1:513], a1[64:128, :], a2[64:128, :], ALU.subtract, reads=[ba1, ba2], writes=[bhrb])
                                    k.tt("dve", a1[:], kI[:], tc[:], ALU.mult, reads=[bkI, btc], writes=[ba1])
                                    k.tt("pool", a2[:], kR[:], tsn[:], ALU.mult, reads=[bkR, bts], writes=[ba2])
                                    k.tt("dve", hif[0:64, 1:513], a1[0:64, :], a2[0:64, :], ALU.add, reads=[ba1, ba2], writes=[bhif])
                                    k.tt("dve", hib[64:128, 1:513], a1[64:128, :], a2[64:128, :], ALU.add, reads=[ba1, ba2], writes=[bhib])
                                    k.mm(pY[:, :], Tm[:, g, :], Ut[:, gl, :], True, False, reads=[b_Tm, b_Ut], writes=[bY])
                                    k.mm(pY[:, :], Qre[:, g, :], hrf[:, 0:512], False, False, reads=[b_Qre, bhrf], writes=[bY])
                                    k.mm(pY[:, :], Qim[:, g, :], hif[:, 0:512], False, True, reads=[b_Qim, bhif], writes=[bY])
                                    k.mm(pYb[:, :], Qre[:, g, :], hrb[:, 0:512], True, False, reads=[b_Qre, bhrb], writes=[bYb])
                                    k.mm(pYb[:, :], Qim[:, g, :], hib[:, 0:512], False, True, reads=[b_Qim, bhib], writes=[bYb])
                                    k.act(yg[:], pY[:, :], AF.Copy, reads=[bY], writes=[byg])
                                    k.act(ygb[:], pYb[:, :], AF.Copy, reads=[bYb], writes=[bygb])
                                    for jb in range(4):
                                        k.mm(pT2[:, jb * 128:(jb + 1) * 128], ygb[:, jb * 128:(jb + 1) * 128], ident_bf[:], True, True,
                                             reads=[bygb, b_idbf], writes=[bT2])
                                    k.copy("dve", ygT[:], pT2[:, :].rearrange("p (j x) -> p j x", j=4), reads=[bT2], writes=[bygT])
                                    for cb in range(4):
                                        k.mm(pT[:, cb * 128:(cb + 1) * 128], yg[:, cb * 128:(cb + 1) * 128], ident_bf[:], True, False,
                                             reads=[byg, b_idbf], writes=[bT])
                                        k.mm(pT[:, cb * 128:(cb + 1) * 128], J_bf[:], ygT[:, 3 - cb, :], False, True,
                                             reads=[bygT, b_idbf], writes=[bT])
                                    k.copy("dve", Zy[:, :, :, 16 * g:16 * g + 16],
                                           pT[:, :].rearrange("p (c t h) -> p c t h", c=4, t=8), reads=[bT], writes=[b_Zy])

                                gen_tw(0)
                                stageA(0)
                                for gl in range(NG):
                                    if gl + 1 < NG:
                                        stageA(gl + 1)
                                    stageB(gl)
                        with k.scope():
                            glw, b_glw = load_w(glu_w, 4, 0, 512)
                            yb, b_yb = new([128, 4, 1024], F32, "yb")
                            u1, b_u1 = new([128, 4, 1024], F32, "u1")
                            yab, b_yab = new([128, 4, 1024], BF16, "yab")
                            sqb, b_sqb = new([128, 4, 1024], BF16, "sqb")
                            sg_, b_sg = new([128, 512], F32, "sg")
                            rs_, b_rs = new([128, 512], F32, "rs")
                            mo, b_mo = new([128, 4, 1024], BF16, "mo")
                            GC = 2.0 * math.sqrt(2.0 / math.pi)
                            for cb in range(4):
                                for ct in range(4):
                                    for th in range(2):
                                        bank = pbanks[(ct * 2 + th) % 2]; bb = bpb[(ct * 2 + th) % 2]
                                        for t4 in range(4):
                                            t_ = th * 4 + t4
                                            k.mm(bank[:, t4 * 128:(t4 + 1) * 128], Zy[:, cb, t_, ct * 128:(ct + 1) * 128], ident_bf[:],
                                                 True, True, reads=[b_Zy, b_idbf], writes=[bb])
                                        k.act(yb[:, ct, :].rearrange("p (c s) -> p s c", s=8)[:, th * 4:th * 4 + 4, :],
                                              bank[:, :].rearrange("p (s c) -> p s c", s=4), AF.Copy, reads=[bb], writes=[b_yb])
                                k.act(u1[:], yb[:], AF.Square, reads=[b_yb], writes=[b_u1])
                                k.ts("dve", u1[:], u1[:], 0.044715, 1.0, ALU.mult, ALU.add, reads=[b_u1], writes=[b_u1])
                                k.tt("pool", u1[:], u1[:], yb[:], ALU.mult, reads=[b_u1, b_yb], writes=[b_u1])
                                k.act(u1[:], u1[:], AF.Sigmoid, reads=[b_u1], writes=[b_u1], scale=GC)
                                k.tt("dve", yb[:], yb[:], u1[:], ALU.mult, reads=[b_u1, b_yb], writes=[b_yb])
                                k.copy("pool", yab[:], yb[:], reads=[b_yb], writes=[b_yab])
                                for hf in range(2):
                                    tk = slice(hf * 512, (hf + 1) * 512)
                                    for co in range(4):
                                        bank = pbanks[2 + co % 2]; bb = bpb[2 + co % 2]
                                        for ci in range(4):
                                            k.mm(bank[:, :], glw[:, ci, co * 128:(co + 1) * 128], yab[:, ci, tk], ci == 0, ci == 3,
                                                 reads=[b_glw, b_yab], writes=[bb])
                                        k.act(sg_[:], bank[:, :], AF.Sigmoid, reads=[bb, b_cp], writes=[b_sg],
                                              bias=cp[:, C_GLUB + co:C_GLUB + co + 1])
                                        k.tt("dve", yb[:, co, tk], yb[:, co, tk], sg_[:], ALU.mult, reads=[b_yb, b_sg], writes=[b_yb])
                                k.act(sqb[:], yb[:], AF.Square, reads=[b_yb], writes=[b_sqb])
                                for hf in range(2):
                                    tk = slice(hf * 512, (hf + 1) * 512)
                                    bank = pbanks[4 + hf]; bb = bpb[4 + hf]
                                    for ci in range(4):
                                        k.mm(bank[:, :], ones_bf[:], sqb[:, ci, tk], ci == 0, ci == 3, reads=[b_sqb, b_ones], writes=[bb])
                                    k.ts("dve", rs_[:], bank[:, :], 1.0 / 512.0, LN_EPS, ALU.mult, ALU.add, reads=[bb], writes=[b_rs])
                                    k.act(rs_[:], rs_[:], AF.Sqrt, reads=[b_rs], writes=[b_rs])
                                    k.S.op("dve", lambda e: e.reciprocal(out=rs_[:], in_=rs_[:]), reads=[b_rs], writes=[b_rs])
                                    for co in range(4):
                                        k.stt(mo[:, co, tk], yb[:, co, tk], cp[:, C_SNG + co:C_SNG + co + 1], rs_[:], ALU.mult, ALU.mult,
                                              reads=[b_yb, b_rs, b_cp], writes=[b_mo])
                                k.dma(MIX[b, 512:1024, cb * 1024:(cb + 1) * 1024].rearrange("(k p) t -> p k t", p=128), mo[:],
                                      reads=[b_mo], writes=[bMIX[b]])

                    with (k.scope() if phase == "MO" else contextlib.nullcontext()):
                      if phase == "MO" and "M" not in SKIP:
                        qf, b_qf = new([128, 2, T], BF16, "qf")
                        kf, b_kf = new([128, 2, T], BF16, "kf")
                        vext, b_vext = new([128, 32, 4, 129], BF16, "vext")
                        big, b_big = new([128, 4, T + 4], F32, "big")
                        k.S.op("pool", lambda e: e.memset(vext[:, :, :, 128:129], 1.0), writes=[b_vext])
                        k.S.op("pool", lambda e: e.memset(big[:, :, 0:2], 0.0), writes=[b_big])
                        k.S.op("pool", lambda e: e.memset(big[:, :, T + 2:T + 4], 0.0), writes=[b_big])
                        with k.scope():
                            wqk, b_wqk = load_w(w_in, 8, 0, 512)
                            wv, b_wv = load_w(w_in, 8, 512, 1024)
                            wg, b_wg = load_w(w_in, 8, 1536, 1552, 16)
                            xs, b_xs = new([128, 8, 256], F32, "xs")
                            xm2L = [new([128, 8, 256], BF16, "xm2") for _ in range(2)]
                            gt, b_gt = new([16, 256], F32, "gt")
                            dma_x1(0, 256, xs, b_xs)
                            mod_x1(xm2L[0][0], 256, xs, b_xs, xm2L[0][1])
                            for ti in range(16):
                                t0_ = ti * 256
                                (xm2, b_xm2) = xm2L[ti % 2]
                                if ti + 1 < 16:
                                    dma_x1(t0_ + 256, 256, xs, b_xs)
                                for ct in range(4):
                                    bank = pbanks[ct // 2]; bb = bpb[ct // 2]
                                    half = (ct % 2) * 256
                                    for kk in range(8):
                                        k.mm(bank[:, half:half + 256], wqk[:, kk, ct * 128:(ct + 1) * 128], xm2[:, kk, :], kk == 0, kk == 7,
                                             reads=[b_wqk, b_xm2], writes=[bb])
                                    k.act(big[:, ct, 2 + t0_:2 + t0_ + 256], bank[:, half:half + 256], AF.Identity, reads=[bb, b_cp],
                                          writes=[b_big], bias=cp[:, C_BQK + ct:C_BQK + ct + 1])
                                for cc in range(2):
                                    c = ti * 2 + cc
                                    bank = pbanks[2 + cc]; bb = bpb[2 + cc]
                                    for kk in range(8):
                                        k.mm(bank[:, :], xm2[:, kk, cc * 128:(cc + 1) * 128], wv[:, kk, :], kk == 0, kk == 7,
                                             reads=[b_wv, b_xm2], writes=[bb])
                                    k.tt("dve", vext[:, c, :, 0:128], bank[:, :].rearrange("p (r v) -> p r v", r=4),
                                         rowbc[:, R_BV:R_BV + 512].rearrange("p (r v) -> p r v", r=4), ALU.add,
                                         reads=[bb, b_rowbc], writes=[b_vext])
                                bank = pbanks[4]; bb = bpb[4]
                                for kk in range(8):
                                    k.mm(bank[0:16, 0:256], wg[:, kk, 0:16], xm2[:, kk, :], kk == 0, kk == 7,
                                         reads=[b_wg, b_xm2], writes=[bb])
                                k.act(gt[0:16, :], bank[0:16, 0:256], AF.Identity, reads=[bb, b_cp], writes=[b_gt],
                                      bias=cp[0:16, C_BG + 4:C_BG + 5])
                                k.dma(GSd[b, :, :, t0_:t0_ + 256].rearrange("k r t -> (k r) t"), gt[:], reads=[b_gt], writes=[bGS[b]])
                                if ti + 1 < 16:
                                    mod_x1(xm2L[(ti + 1) % 2][0], 256, xs, b_xs, xm2L[(ti + 1) % 2][1])
                        ktm, b_ktm = new([128, 32, 256], BF16, "ktm")
                        with k.scope():
                            acc, b_acc = new([128, T], F32, "acc")
                            for ct in range(4):
                                cw = lambda j: cp[:, C_CW + 4 * j + ct:C_CW + 4 * j + ct + 1]
                                k.ts("dve", acc[:], big[:, ct, 0:T], cw(0), None, ALU.mult, reads=[b_big, b_cp], writes=[b_acc])
                                for j in range(1, 5):
                                    k.stt(acc[:], big[:, ct, j:j + T], cw(j), acc[:], ALU.mult, ALU.add, reads=[b_big, b_cp, b_acc], writes=[b_acc])
                                cb_ = cp[:, C_CB + ct:C_CB + ct + 1]
                                if ct < 2:
                                    k.act(acc[:], acc[:], AF.Silu, reads=[b_acc, b_cp], writes=[b_acc], bias=cb_)
                                    k.ts("pool", qf[:, ct, :], acc[:], 0.125, None, ALU.mult, reads=[b_acc], writes=[b_qf])
                                else:
                                    k.act(kf[:, ct - 2, :], acc[:], AF.Silu, reads=[b_acc, b_cp], writes=[b_kf], bias=cb_)
                            for c2 in range(16):
                                bank = pbanks[c2 % 2]; bb = bpb[c2 % 2]
                                for cc in range(2):
                                    c = c2 * 2 + cc
                                    for kt in range(2):
                                        o_ = cc * 256 + kt * 128
                                        k.mm(bank[:, o_:o_ + 128], kf[:, kt, c * 128:(c + 1) * 128], ident_bf[:], True, True,
                                             reads=[b_kf, b_idbf], writes=[bb])
                                k.act(ktm[:, c2 * 2:c2 * 2 + 2, :], bank[:, :].rearrange("p (c x) -> p c x", c=2), AF.Copy,
                                      reads=[bb], writes=[b_ktm])
                        WCt, b_WCt = new([128, 2, 8, 32], F32, "WCt")
                        DECt, b_DECt = new([128, 8, 32], F32, "DECt")
                        with k.scope():
                            G5 = [32, 4, 4, 128]
                            Gc, b_Gc = new(G5, F32, "Gc")
                            k.dma(Gc[:], GSd[b].rearrange("k r (c t) -> c k r t", t=128), reads=[bGS[b]], writes=[b_Gc])
                            D4 = [32, 2, 4, 128]
                            spl, b_spl = new(D4, F32); BS, b_BS = new(D4, F32); ee, b_ee = new(D4, F32)
                            k.act(spl[:], Gc[:, 2:4, :, :], AF.Exp, reads=[b_Gc], writes=[b_spl], scale=-1.0)
                            k.act(spl[:], spl[:], AF.Ln, reads=[b_spl], writes=[b_spl], bias=1.0)
                            flat = lambda t_, d: t_[:, d, :, :].rearrange("c r t -> c (r t)")
                            for d in range(2):
                                k.S.op("dve", lambda e, d=d: e.tensor_tensor_scan(out=flat(BS, d), data0=rmask[0:32, :], data1=flat(spl, d),
                                                                                  initial=0.0, op0=ALU.mult, op1=ALU.add),
                                       reads=[b_spl, b_cs2], writes=[b_BS])
                            tot = BS[:, 1, :, 127:128].to_broadcast([32, 4, 128])
                            k.tt("dve", ee[:, 1, :, :], spl[:, 1, :, :], BS[:, 1, :, :], ALU.subtract, reads=[b_spl, b_BS], writes=[b_ee])
                            k.tt("dve", BS[:, 1, :, :], ee[:, 1, :, :], tot, ALU.add, reads=[b_ee, b_BS], writes=[b_BS])
                            k.tt("dve", ee[:], Gc[:, 0:2, :, :], BS[:], ALU.add, reads=[b_Gc, b_BS], writes=[b_ee])
                            emx, b_emx = new([32, 2, 4], F32)
                            k.S.op("dve", lambda e: e.tensor_reduce(out=emx[:], in_=ee[:], axis=AX.X, op=ALU.max), reads=[b_ee], writes=[b_emx])
                            nbe, b_nbe = new([32, 2, 4], F32)
                            k.ts("dve", nbe[:, 0, :], BS[:, 0, :, 127], -1.0, None, ALU.mult, reads=[b_BS], writes=[b_nbe])
                            k.ts("dve", nbe[:, 1, :], BS[:, 1, :, 0], -1.0, None, ALU.mult, reads=[b_BS], writes=[b_nbe])
                            pbk = pbanks[0]; bbk = bpb[0]
                            for d in range(2):
                                rhs_ = ident[0:32, 0:32] if d == 0 else J32[0:32, :]
                                k.mm(pbk[0:4, d * 64:d * 64 + 32], emx[:, d, :], rhs_, True, True, reads=[b_emx, b_cst, b_cs2], writes=[bbk])
                                k.mm(pbk[0:4, d * 64 + 32:d * 64 + 64], nbe[:, d, :], rhs_, True, True, reads=[b_nbe, b_cst, b_cs2], writes=[bbk])
                            rows, b_rows = new([4, 2, 2, 32], F32)
                            k.copy("dve", rows[:], pbk[0:4, 0:128].rearrange("r (d x j) -> r d x j", d=2, x=2), reads=[bbk], writes=[b_rows])
                            mrow, b_mrow = new([4, 2, 33], F32)
                            k.S.op("pool", lambda e: e.memset(mrow[:], 0.0), writes=[b_mrow])
                            Rrow, b_Rrow = new([4, 2, 32], F32); drow, b_drow = new([4, 2, 32], F32)
                            for d in range(2):
                                k.S.op("dve", lambda e, d=d: e.tensor_tensor_scan(out=mrow[:, d, 1:33], data0=rows[:, d, 0, :], data1=rows[:, d, 1, :],
                                                                                  initial=0.0, op0=ALU.max, op1=ALU.add),
                                       reads=[b_rows, b_mrow], writes=[b_mrow])
                            k.tt("dve", Rrow[:], mrow[:, :, 0:32], rows[:, :, 0, :], ALU.max, reads=[b_mrow, b_rows], writes=[b_Rrow])
                            k.tt("dve", drow[:], mrow[:, :, 0:32], Rrow[:], ALU.subtract, reads=[b_mrow, b_Rrow], writes=[b_drow])
                            k.act(drow[:], drow[:], AF.Exp, reads=[b_drow], writes=[b_drow])
                            pbd = pbanks[1]; bbd = bpb[1]
                            for d in range(2):
                                for r in range(4):
                                    o_ = (d * 4 + r) * 32
                                    k.mm(pbd[:, o_:o_ + 32], esel[0:4, r * 128:(r + 1) * 128], drow[:, d, :], True, True,
                                         reads=[b_cs2, b_drow], writes=[bbd])
                            k.copy("dve", DECt[:], pbd[:, 0:256].rearrange("p (x j) -> p x j", x=8), reads=[bbd], writes=[b_DECt])
                            pbr = pbanks[2]; bbr = bpb[2]
                            k.mm(pbr[0:32, 0:4], Rrow[:, 0, :], ident[0:4, 0:4], True, True, reads=[b_Rrow, b_cst], writes=[bbr])
                            k.mm(pbr[0:32, 4:8], Rrow[:, 1, :], ident[0:4, 0:4], True, True, reads=[b_Rrow, b_cst], writes=[bbr])
                            RbT, b_RbT = new([32, 8], F32)
                            k.copy("dve", RbT[:], pbr[0:32, 0:8], reads=[bbr], writes=[b_RbT])
                            k.mm(pbr[0:32, 8:12], J32[0:32, :], RbT[:, 4:8], True, True, reads=[b_RbT, b_cs2], writes=[bbr])
                            Rc, b_Rc = new([32, 2, 4], F32)
                            k.copy("dve", Rc[:, 0, :], RbT[:, 0:4], reads=[b_RbT], writes=[b_Rc])
                            k.copy("dve", Rc[:, 1, :], pbr[0:32, 8:12], reads=[bbr], writes=[b_Rc])
                            Rcb = Rc[:].unsqueeze(3).to_broadcast(D4)
                            k.tt("dve", ee[:], ee[:], Rcb, ALU.subtract, reads=[b_ee, b_Rc], writes=[b_ee])
                            k.act(ee[:], ee[:], AF.Exp, reads=[b_ee], writes=[b_ee])
                            k.tt("dve", BS[:], BS[:], Rcb, ALU.subtract, reads=[b_BS, b_Rc], writes=[b_BS])
                            k.act(BS[:], BS[:], AF.Exp, reads=[b_BS], writes=[b_BS])
                            pbw = pbanks[3]; bbw = bpb[3]
                            for x_, src_, bsrc in ((0, ee, b_ee), (1, BS, b_BS)):
                                for d in range(2):
                                    for r in range(4):
                                        o_ = (x_ * 8 + d * 4 + r) * 32
                                        k.mm(pbw[:, o_:o_ + 32], src_[:, d, r, :], ident[0:32, 0:32], True, True, reads=[bsrc, b_cst], writes=[bbw])
                            k.copy("dve", WCt[:], pbw[:, :].rearrange("p (x y c) -> p x y c", x=2, y=8), reads=[bbw], writes=[b_WCt])
                        with k.scope():
                            Cst = [[new([128, 129], F32, "Cst") for r in range(4)] for d in range(2)]
                            NR = 4
                            STb = [new([128, 128], BF16, "ST") for _ in range(NR)]
                            Vpb = [new([128, 129], BF16, "Vp") for _ in range(NR)]
                            Cdb = [new([128, 129], BF16, "Cd") for _ in range(NR)]
                            ddb = [new([128, 2], F32, "dd") for _ in range(NR)]
                            insts = [(st, d, r) for st in range(32) for d in range(2) for r in range(4)]

                            def geom(it):
                                st, d, r = insts[it]
                                c = st if d == 0 else 31 - st
                                return st, d, r, c, slice(c * 128, (c + 1) * 128), r // 2, 64 * (r % 2), d * 4 + r, it % NR

                            def emitA(it):
                                st, d, r, c, tok, kt, ro, x_, w = geom(it)
                                pS = pbanks[it % 2]; bS = bpb[it % 2]
                                (ST, bST), (Vp, bVp), (Cd, bCd) = STb[w], Vpb[w], Cdb[w]
                                (Cs, bCs) = Cst[d][r]
                                k.mm(pS[:, 0:128], kf[ro:ro + 64, kt, tok], qf[ro:ro + 64, kt, tok], True, True,
                                     reads=[b_kf, b_qf], writes=[bS])
                                k.tt("dve", ST[:], pS[:, 0:128], tri[d], ALU.mult, reads=[bS, b_cst], writes=[bST])
                                zc = cs2[:, 1500:1501]
                                k.ts("pool", Vp[:], vext[:, c, r, :], WCt[:, 0, x_, c:c + 1], zc, ALU.mult, ALU.add,
                                     reads=[b_vext, b_WCt, b_cs2], writes=[bVp])
                                if st > 0:
                                    k.ts("pool", Cs[ro:ro + 64, :], Cs[ro:ro + 64, :], DECt[ro:ro + 64, x_, st:st + 1], zc[ro:ro + 64, :],
                                         ALU.mult, ALU.add, reads=[bCs, b_DECt, b_cs2], writes=[bCs])
                                    k.act(Cd[ro:ro + 64, :], Cs[ro:ro + 64, :], AF.Copy, reads=[bCs], writes=[bCd])

                            def emitB(it):
                                st, d, r, c, tok, kt, ro, x_, w = geom(it)
                                pN = pbanks[2 + it % 2]; bN = bpb[2 + it % 2]
                                pC = pbanks[4 + it % 2]; bC = bpb[4 + it % 2]
                                (ST, bST), (Vp, bVp), (Cd, bCd), (dd, bdd) = STb[w], Vpb[w], Cdb[w], ddb[w]
                                (Cs, bCs) = Cst[d][r]
                                k.mm(pN[:, 0:129], ST[:], Vp[:], True, st == 0, reads=[bST, bVp], writes=[bN])
                                if st > 0:
                                    k.mm(pN[:, 0:129], qf[ro:ro + 64, kt, tok], Cd[ro:ro + 64, :], False, True,
                                         reads=[b_qf, bCd], writes=[bN])
                                k.mm(pC[ro:ro + 64, 0:129], ktm[:, c, r * 64:(r + 1) * 64], Vp[:], True, True,
                                     reads=[b_ktm, bVp], writes=[bC])
                                if st > 0:
                                    k.tt("dve", Cs[ro:ro + 64, :], Cs[ro:ro + 64, :], pC[ro:ro + 64, 0:129], ALU.add,
                                         reads=[bCs, bC], writes=[bCs])
                                else:
                                    k.copy("dve", Cs[ro:ro + 64, :], pC[ro:ro + 64, 0:129], reads=[bC], writes=[bCs])
                                k.act(dd[:, 0:1], pN[:, 128:129], AF.Abs, reads=[bN], writes=[bdd])
                                k.ts("dve", dd[:, 0:1], dd[:, 0:1], WCt[:, 1, x_, c:c + 1], None, ALU.max,
                                     reads=[bdd, b_WCt], writes=[bdd])
                                k.S.op("dve", lambda e: e.reciprocal(out=dd[:, 1:2], in_=dd[:, 0:1]), reads=[bdd], writes=[bdd])
                                cell = big[:, :, 0:T].rearrange("p r (c v) -> p c r v", v=128)[:, c, r, :]
                                first = (d == 0 and c < 16) or (d == 1 and c >= 16)
                                if first:
                                    k.S.op("act", lambda e: e.activation(out=cell, in_=pN[:, 0:128], func=AF.Copy, scale=dd[:, 1:2]),
                                           reads=[bN, bdd], writes=[b_big])
                                else:
                                    k.stt(cell, pN[:, 0:128], dd[:, 1:2], cell, ALU.mult, ALU.add, reads=[bN, bdd, b_big], writes=[b_big])

                            for it in range(len(insts) + 1):
                                if it < len(insts):
                                    emitA(it)
                                if it >= 1:
                                    emitB(it - 1)
                        with k.scope():
                            wo, b_wo = load_w(w_in, 8, 1024, 1536)
                            xs1 = new([128, 8, 256], F32, "xs")
                            xsL = [xs1, xs1]
                            xmL = [new([128, 8, 256], BF16, "xm2") for _ in range(2)]
                            obL = [new([128, 512], F32, "ob") for _ in range(2)]
                            hgL = [new([128, 512], F32, "hg") for _ in range(2)]
                            hnL = [new([128, 512], BF16, "hnb") for _ in range(2)]
                            stL = [new([128, 4, 6], F32, "bst") for _ in range(2)]
                            mvL = [new([128, 4, 2], F32, "mv") for _ in range(2)]
                            mmL = [new([128, 4, 256], BF16, "mixm") for _ in range(2)]
                            (xs, b_xs) = xs1
                            dma_x1(0, 256, xs, b_xs)
                            mod_x1(xmL[0][0], 256, xs, b_xs, xmL[0][1])
                            for ti in range(16):
                                (xm2, b_xm2), (mm_, b_mm) = xmL[ti % 2], mmL[ti % 2]
                                if ti + 1 < 16:
                                    dma_x1((ti + 1) * 256, 256, xs, b_xs)
                                for cc in range(2):
                                    c = ti * 2 + cc
                                    (ob, b_ob), (hg, b_hg), (hnb, b_hnb), (stt_, b_stt), (mv, b_mv) = obL[cc], hgL[cc], hnL[cc], stL[cc], mvL[cc]
                                    bank = pbanks[cc]; bb = bpb[cc]
                                    for kk in range(8):
                                        k.mm(bank[:, :], xm2[:, kk, cc * 128:(cc + 1) * 128], wo[:, kk, :], kk == 0, kk == 7,
                                             reads=[b_wo, b_xm2], writes=[bb])
                                    k.tt("dve", ob[:], bank[:, :], rowbc[:, R_BO:R_BO + 512], ALU.add, reads=[bb, b_rowbc], writes=[b_ob])
                                    k.act(ob[:], ob[:], AF.Sigmoid, reads=[b_ob], writes=[b_ob])
                                    cellc = big[:, :, 0:T].rearrange("p r (c v) -> p c r v", v=128)[:, c, :, :]
                                    k.tt("dve", hg[:].rearrange("p (r v) -> p r v", r=4), ob[:].rearrange("p (r v) -> p r v", r=4), cellc,
                                         ALU.mult, reads=[b_ob, b_big], writes=[b_hg])
                                    for r in range(4):
                                        k.S.op("dve", lambda e, r=r, stt_=stt_, hg=hg: e.bn_stats(out=stt_[:, r, :], in_=hg[:, r * 128:(r + 1) * 128]),
                                               reads=[b_hg], writes=[b_stt])
                                        k.S.op("dve", lambda e, r=r, stt_=stt_, mv=mv: e.bn_aggr(out=mv[:, r, :], in_=stt_[:, r, :]),
                                               reads=[b_stt], writes=[b_mv])
                                    k.ts("dve", mv[:, :, 1], mv[:, :, 1], LN_EPS, None, ALU.add, reads=[b_mv], writes=[b_mv])
                                    k.act(mv[:, :, 1], mv[:, :, 1], AF.Sqrt, reads=[b_mv], writes=[b_mv])
                                    k.S.op("dve", lambda e, mv=mv: e.reciprocal(out=mv[:, :, 1], in_=mv[:, :, 1]), reads=[b_mv], writes=[b_mv])
                                    for r in range(4):
                                        k.ts("dve", hg[:, r * 128:(r + 1) * 128], hg[:, r * 128:(r + 1) * 128], mv[:, r, 0:1], mv[:, r, 1:2],
                                             ALU.subtract, ALU.mult, reads=[b_hg, b_mv], writes=[b_hg])
                                    k.tt("pool", hnb[:], hg[:], rowbc[:, R_MNG:R_MNG + 512], ALU.mult, reads=[b_hg, b_rowbc], writes=[b_hnb])
                                    bank2 = pbanks[2 + cc]; bb2 = bpb[2 + cc]
                                    for r in range(4):
                                        k.mm(bank2[:, r * 128:(r + 1) * 128], hnb[:, r * 128:(r + 1) * 128], ident_bf[:], True, True,
                                             reads=[b_hnb, b_idbf], writes=[bb2])
                                    k.act(mm_[:, :, cc * 128:(cc + 1) * 128], bank2[:, :].rearrange("p (r t) -> p r t", r=4), AF.Copy,
                                          reads=[bb2], writes=[b_mm])
                                k.dma(MIX[b, 0:512, ti * 256:(ti + 1) * 256].rearrange("(k p) t -> p k t", p=128), mm_[:],
                                      reads=[b_mm], writes=[bMIX[b]])
                                if ti + 1 < 16:
                                    mod_x1(xmL[(ti + 1) % 2][0], 256, xs, b_xs, xmL[(ti + 1) % 2][1])

                    with (k.scope() if phase == "MO" else contextlib.nullcontext()):
                      if phase == "MO" and "O" not in SKIP:
                        wob, b_wob = load_w(w_out, 8, 0, 1024)
                        xs = [new([128, 8, TT], F32, "xs") for _ in range(2)]
                        mx = [new([128, 8, TT], BF16, "mx") for _ in range(2)]
                        zL = [new([128, 8, TT], F32, "z") for _ in range(2)]
                        zbL = [k.sb([128, 8, TT], BF16, "zb") for _ in range(2)]
                        zqL = [k.sb([128, 8, TT], BF16, "zq") for _ in range(2)]
                        bzbqL = [k.buf() for _ in range(2)]
                        smL = [new([128, 6, TT], F32, "small") for _ in range(2)]
                        tmpL = [[k.sb([128, TT], F32, "tmp") for _ in range(2)] for _ in range(2)]
                        btmpL = [[k.buf() for _ in range(2)] for _ in range(2)]
                        def loadO(ti):
                            (xs_, bxs_), (mx_, bmx_) = xs[ti % 2], mx[ti % 2]
                            tsl = slice(ti * TT, (ti + 1) * TT)
                            k.dma(xs_[:], x1v[:, :, tsl], reads=[bX1[b]], writes=[bxs_])
                            k.dma(mx_[:], MIX[b].rearrange("(k p) t -> p k t", p=128)[:, :, tsl], reads=[bMIX[b]], writes=[bmx_])

                        loadO(0)
                        for ti in range(NT):
                            s = ti % 2
                            (xs_, bxs_), (mx_, bmx_) = xs[s], mx[s]
                            tsl = slice(ti * TT, (ti + 1) * TT)
                            if ti + 1 < NT:
                                loadO(ti + 1)
                            (z, b_z), zb, zq, b_zbq = zL[s], zbL[s], zqL[s], bzbqL[s]
                            (small, b_small), tmp, b_tmp = smL[s], tmpL[s], btmpL[s]
                            for dt in range(8):
                                bi = (dt // 2) % 2
                                half = (dt % 2) * 256
                                for kk in range(8):
                                    k.mm(pbanks[bi][:, half:half + TT], wob[:, kk, dt * 128:(dt + 1) * 128], mx_[:, kk, :], kk == 0, kk == 7,
                                         reads=[b_wob, bmx_], writes=[bpb[bi]])
                                k.stt(z[:, dt, :], pbanks[bi][:, half:half + TT], der[:, 5, dt, b:b + 1], xs_[:, dt, :], ALU.mult, ALU.add,
                                      reads=[bpb[bi], bxs_, b_der], writes=[b_z])
                            ln_tail(z, b_z, zb, zq, b_zbq, small, b_small, tmp, b_tmp, xs_, bxs_, 1, TT)
                            k.dma(X2[b].rearrange("(k p) t -> p k t", p=128)[:, :, tsl], xs_[:], reads=[bxs_], writes=[bX2[b]])

        if STAGE >= 2:
            k.stacks.pop()
            mixc.close()
            k.fence_ops = k.S.fence()
        if STAGE == 3:
            with k.scope():
                cvb, b_cvb = k.sb([128, 8, 256], BF16, "cvb"), k.buf()
                cvf, b_cvf = k.sb([128, 8, 256], F32, "cvf"), k.buf()
                for b in range(NB):
                    for ti in range(16):
                        tsl = slice(ti * 256, (ti + 1) * 256)
                        k.dma(cvb[:], MIX[b].rearrange("(k p) t -> p k t", p=128)[:, :, tsl], reads=[bMIX[b]], writes=[b_cvb])
                        k.copy("dve", cvf[:], cvb[:], reads=[b_cvb], writes=[b_cvf])
                        final_ops.append(k.dma(outT[b].rearrange("(k p) t -> p k t", p=128)[:, :, tsl], cvf[:], reads=[b_cvf], writes=[b_out[b]]))
        if STAGE >= 9:
            ffn_pass(X2, bX2, outT, b_out, f2u, f2d, 2, 2, True)

        k.S.emit(final_wait_ops=final_ops)
    return nc


def _pack_inputs(inp):
    f = lambda a: np.ascontiguousarray(np.asarray(a, dtype=np.float32))
    col = np.zeros((128, NCOLP), np.float32)

    def put8(off, v):
        col[:, off:off + 8] = f(v).reshape(8, 128).T

    for i, nm in enumerate(["ln1_g", "ln1_b", "ln2_g", "ln2_b", "ln3_g", "ln3_b"]):
        put8(C_LN + 8 * i, inp[nm][0])
    col[:, C_BADA:C_BADA + 72] = f(inp["b_ada"][0]).reshape(72, 128).T
    b_in = f(inp["b_in"][0])
    col[:, C_BQK:C_BQK + 4] = b_in[0:512].reshape(4, 128).T
    cw = f(inp["qk_conv_w"][0])
    for j in range(5):
        col[:, C_CW + 4 * j:C_CW + 4 * j + 4] = cw[j].reshape(4, 128).T
    col[:, C_CB:C_CB + 4] = f(inp["qk_conv_b"][0]).reshape(4, 128).T
    col[:, C_GLUB:C_GLUB + 4] = f(inp["ssm_glu_b"][0]).reshape(4, 128).T
    col[:, C_SNG:C_SNG + 4] = f(inp["ssm_norm_g"][0]).reshape(4, 128).T
    g0 = 1536
    for i in range(4):
        col[0:4, C_BG + i] = b_in[g0 + 4 * i:g0 + 4 * i + 4]
    col[0:16, C_BG + 4] = b_in[g0:g0 + 16]
    row = np.zeros((1, NROWP), np.float32)
    row[0, R_BV:R_BV + 512] = b_in[512:1024]
    row[0, R_BO:R_BO + 512] = b_in[1024:1536]
    row[0, R_BU:R_BU + 512] = b_in[1552:2064]
    row[0, R_MNG:R_MNG + 512] = f(inp["mlstm_norm_g"][0])
    cst = np.zeros((128, 1024), np.float32)
    cst[:, 0:128] = np.eye(128, dtype=np.float32)
    ii = np.arange(128)
    cst[:, 128:256] = (ii[:, None] <= ii[None, :]).astype(np.float32)
    cst[:, 256:384] = (ii[:, None] >= ii[None, :]).astype(np.float32)
    cst[:, 384:896] = np.arange(512, dtype=np.float32)[None, :]
    a8 = np.arange(8, dtype=np.float32)
    for half, rows in ((0, slice(0, 64)), (1, slice(64, 128))):
        cst[rows, 896:904] = (7 - a8) if half == 0 else a8
        cst[rows, 904:912] = (-a8) if half == 0 else a8
        cst[rows, 912:920] = a8 if half == 0 else (-a8)
        cst[rows, 920:928] = (a8 + 1) if half == 0 else (8 - a8)
    cs2 = np.zeros((128, 1536), np.float32)
    cs2[:, 0:512] = (np.arange(512) % 128 != 0).astype(np.float32)[None, :]
    cs2[:, 512:640] = ((ii[:, None] // 16) <= (ii[None, :] // 16)).astype(np.float32)
    cs2[:, 640:768] = ((ii[:, None] // 16) >= (ii[None, :] // 16)).astype(np.float32)
    cs2[0:32, 768:800] = np.eye(32, dtype=np.float32)[::-1]
    cs2[:, 1344:1472] = np.eye(128, dtype=np.float32)[::-1]
    for r in range(4):
        cs2[r, 800 + r * 128:800 + (r + 1) * 128] = 1.0
    dd_ = f(inp["ssm_d"][0]).reshape(32, 16)
    cs2[:, 1312:1344] = np.tile(dd_.T, (8, 1))
    s5 = np.zeros((2, 64, 32, 70), np.float32)
    s5[..., 0] = f(inp["ssm_lam_re"][0]).transpose(0, 2, 1)
    s5[..., 1] = f(inp["ssm_lam_im"][0]).transpose(0, 2, 1)
    s5[..., 2] = f(inp["ssm_log_step"][0])[:, None, :]
    s5[..., 3:19] = f(inp["ssm_b_re"][0]).transpose(1, 0, 2)[None]
    s5[..., 19:35] = f(inp["ssm_b_im"][0]).transpose(1, 0, 2)[None]
    s5[..., 35:51] = f(inp["ssm_c_re"][0]).transpose(0, 3, 1, 2)
    s5[..., 51:67] = f(inp["ssm_c_im"][0]).transpose(0, 3, 1, 2)
    s5p = np.ascontiguousarray(s5.reshape(128, 32 * 70))
    shared = {
        "w_ada": f(inp["w_ada"][0]), "colpack": col, "rowpack": row, "consts": cst,
        "ffn1_w_up": f(inp["ffn1_w_up"][0]), "ffn1_w_down": f(inp["ffn1_w_down"][0]),
        "ffn2_w_up": f(inp["ffn2_w_up"][0]), "ffn2_w_down": f(inp["ffn2_w_down"][0]),
        "w_in": f(inp["w_in"][0]), "w_out": f(inp["w_out"][0]), "glu_w": f(inp["ssm_glu_w"][0]),
        "s5p": s5p, "consts2": cs2,
    }
    x = np.asarray(inp["x"], dtype=np.float32)
    c = np.asarray(inp["c"], dtype=np.float32)
    maps = []
    for core in range(8):
        m = dict(shared)
        m["xT"] = np.ascontiguousarray(x[2 * core:2 * core + 2].transpose(0, 2, 1))
        cc = c[2 * core:2 * core + 2]
        m["cT"] = np.ascontiguousarray(cc.reshape(2, 8, 128).transpose(2, 1, 0).reshape(128, 16))
        maps.append(m)
    return maps


_NC_CACHE = {}


def kernel(**inputs):
    maps = _pack_inputs(inputs)
    if "nc" not in _NC_CACHE:
        _NC_CACHE["nc"] = build_program()
    nc = _NC_CACHE["nc"]
    res = run_bass_kernel_spmd(nc, maps, core_ids=list(range(8)))
    outs = [r["outT"] for r in res.results]
    full = np.concatenate([o.transpose(0, 2, 1) for o in outs], axis=0)
    return np.ascontiguousarray(full.astype(np.float32))
```

```python
import os
import math
import contextlib
import numpy as np
import concourse.bass as bass
import concourse.mybir as mybir
from concourse.bass_utils import run_bass_kernel_spmd

F32 = mybir.dt.float32
BF16 = mybir.dt.bfloat16
I32 = mybir.dt.int32
AF = mybir.ActivationFunctionType
ALU = mybir.AluOpType
AX = mybir.AxisListType

D = 1024
DF = 2816
T = 4096
NB = 2
TT = 256
NT = T // TT
ALPHA = 2.0 ** 0.25
LN_EPS = 1e-5
EPS_P = LN_EPS / (ALPHA * ALPHA)
IN_COLS = 2064
STAGE = int(os.environ.get("MK_STAGE", "9"))
SKIP = os.environ.get("MK_SKIP", "").split(",")

C_LN = 0
C_BADA = 48
C_BQK = 120
C_CW = 124
C_CB = 144
C_GLUB = 148
C_SNG = 152
C_BG = 156
NCOLP = 168
R_BV, R_BO, R_BU, R_MNG = 0, 512, 1024, 1536
NROWP = 2048


class Buf:
    __slots__ = ("name", "w", "r")

    def __init__(self, name="", init_readers=()):
        self.name = name
        self.w = None
        self.r = list(init_readers)


class Sched:
    ENGS = ("pe", "act", "dve", "pool", "sp")
    DMA_RING = 8
    SEM_MAX = 30000

    def __init__(self, nc):
        self.nc = nc
        self.ops = []
        self.by_eng = {e: [] for e in self.ENGS}

    def op(self, eng, fn, reads=(), writes=(), dma=False):
        oid = len(self.ops)
        deps = set()
        for b in reads:
            if b.w is not None:
                deps.add(b.w)
        for b in writes:
            if b.w is not None:
                deps.add(b.w)
            for r in b.r:
                deps.add(r)
        if eng == "pe":
            deps = {d for d in deps if self.ops[d][0] != "pe"}
        self.ops.append((eng, fn, deps, dma))
        self.by_eng[eng].append(oid)
        for b in reads:
            b.r.append(oid)
        for b in writes:
            b.w = oid
            b.r = []
        return oid

    def fence(self):
        f = []
        for e in self.ENGS:
            lst = self.by_eng[e]
            if not lst:
                continue
            if e == "sp":
                f.extend(lst[-self.DMA_RING:])
            else:
                f.append(lst[-1])
        return f

    def emit(self, final_wait_ops=()):
        nc = self.nc
        ops = self.ops
        needed = [False] * len(ops)
        for (_, _, deps, _) in ops:
            for d in deps:
                needed[d] = True
        for d in final_wait_ops:
            needed[d] = True
        sig = [None] * len(ops)
        cnt = {e: 0 for e in self.ENGS}
        dma_idx = {e: 0 for e in self.ENGS}
        for e in self.ENGS:
            for oid in self.by_eng[e]:
                isdma = ops[oid][3]
                if isdma:
                    n = dma_idx[e]
                    dma_idx[e] += 1
                    sig[oid] = (("d", e, n % self.DMA_RING), 16 * (n // self.DMA_RING + 1))
                elif needed[oid]:
                    c = cnt[e]
                    cnt[e] += 1
                    sig[oid] = (("c", e, c // self.SEM_MAX), c % self.SEM_MAX + 1)
        keys = []
        kset = set()
        for s in sig:
            if s is not None and s[0] not in kset:
                kset.add(s[0])
                keys.append(s[0])
        with contextlib.ExitStack() as st:
            sems = {}
            for k in keys:
                sems[k] = st.enter_context(nc.semaphore("s_%s_%s_%d" % k))
            block = st.enter_context(nc.Block())

            def run_engine(ename, eh):
                seen = {}
                for oid in self.by_eng[ename]:
                    _, fn, deps, isdma = ops[oid]
                    waits = {}
                    for d in deps:
                        k, v = sig[d]
                        if seen.get(k, 0) >= v:
                            continue
                        if waits.get(k, 0) < v:
                            waits[k] = v
                    if isdma:
                        k, v = sig[oid]
                        if v > 16 and seen.get(k, 0) < v - 16 and waits.get(k, 0) < v - 16:
                            waits[k] = v - 16
                    for k, v in waits.items():
                        eh.wait_ge(sems[k], v)
                        seen[k] = v
                    ins = fn(eh)
                    if sig[oid] is not None:
                        k, v = sig[oid]
                        ins.then_inc(sems[k], 16 if isdma else 1)
                if ename == "sp":
                    fw = {}
                    for d in final_wait_ops:
                        k, v = sig[d]
                        if fw.get(k, 0) < v:
                            fw[k] = v
                    for k, v in fw.items():
                        if seen.get(k, 0) < v:
                            eh.wait_ge(sems[k], v)

            @block.tensor
            def _(e):
                run_engine("pe", e)

            @block.scalar
            def _(e):
                run_engine("act", e)

            @block.vector
            def _(e):
                run_engine("dve", e)

            @block.gpsimd
            def _(e):
                run_engine("pool", e)

            @block.sync
            def _(e):
                run_engine("sp", e)


class KB:
    def __init__(self, nc):
        self.nc = nc
        self.S = Sched(nc)
        self.stacks = []
        self.fence_ops = []
        self.uid = 0

    @contextlib.contextmanager
    def scope(self):
        st = contextlib.ExitStack()
        self.stacks.append(st)
        try:
            with st:
                yield
        finally:
            self.stacks.pop()
            self.fence_ops = self.S.fence()

    def sb(self, shape, dt, name=None):
        self.uid += 1
        nm = "%s_%d" % (name or "t", self.uid)
        return self.stacks[-1].enter_context(self.nc.sbuf_tensor(nm, list(shape), dt))

    def buf(self, name=""):
        return Buf(name, self.fence_ops)

    def dma(self, out, in_, reads=(), writes=(), **kw):
        return self.S.op("sp", lambda e: e.dma_start(out=out, in_=in_, **kw), reads, writes, dma=True)

    def mm(self, out, lhsT, rhs, start, stop, reads=(), writes=()):
        return self.S.op("pe", lambda e: e.matmul(out, lhsT=lhsT, rhs=rhs, start=start, stop=stop), reads, writes)

    def act(self, out, in_, func, reads=(), writes=(), scale=1.0, bias=0.0):
        return self.S.op("act", lambda e: e.activation(out=out, in_=in_, func=func, scale=scale, bias=bias), reads, writes)

    def tt(self, eng, out, in0, in1, op, reads=(), writes=()):
        return self.S.op(eng, lambda e: e.tensor_tensor(out=out, in0=in0, in1=in1, op=op), reads, writes)

    def ts(self, eng, out, in0, s1, s2, op0, op1=None, reads=(), writes=()):
        if op1 is None:
            return self.S.op(eng, lambda e: e.tensor_scalar(out=out, in0=in0, scalar1=s1, scalar2=None, op0=op0), reads, writes)
        return self.S.op(eng, lambda e: e.tensor_scalar(out=out, in0=in0, scalar1=s1, scalar2=s2, op0=op0, op1=op1), reads, writes)

    def stt(self, out, in0, scalar, in1, op0, op1, reads=(), writes=()):
        return self.S.op("dve", lambda e: e.scalar_tensor_tensor(out=out, in0=in0, scalar=scalar, in1=in1, op0=op0, op1=op1), reads, writes)

    def copy(self, eng, out, in_, reads=(), writes=()):
        if eng == "act":
            return self.act(out, in_, AF.Copy, reads, writes)
        return self.S.op(eng, lambda e: e.tensor_copy(out=out, in_=in_), reads, writes)


def build_program():
    nc = bass.Bass("TRN2", target_bir_lowering=False)
    k = KB(nc)

    def din(name, shape, dt=F32):
        return nc.dram_tensor(name, list(shape), dt, kind="ExternalInput").ap()

    xT = din("xT", [NB, D, T])
    cT = din("cT", [128, 16])
    w_ada = din("w_ada", [D, 9 * D])
    colpack = din("colpack", [128, NCOLP])
    rowpack = din("rowpack", [1, NROWP])
    consts = din("consts", [128, 1024])
    f1u = din("ffn1_w_up", [D, 2 * DF])
    f1d = din("ffn1_w_down", [DF, D])
    f2u = din("ffn2_w_up", [D, 2 * DF])
    f2d = din("ffn2_w_down", [DF, D])
    w_in = din("w_in", [D, IN_COLS])
    w_out = din("w_out", [D, D])
    glu_w = din("glu_w", [512, 512])
    s5p = din("s5p", [128, 32 * 70])
    consts2 = din("consts2", [128, 1536])
    outT = nc.dram_tensor("outT", [NB, D, T], F32, kind="ExternalOutput").ap()
    X1 = nc.dram_tensor("X1s", [NB, D, T], F32, kind="Internal").ap()
    X2 = nc.dram_tensor("X2s", [NB, D, T], F32, kind="Internal").ap()
    MIX = nc.dram_tensor("MIXs", [NB, D, T], BF16, kind="Internal").ap()
    bX1 = [Buf("X1_%d" % b) for b in range(NB)]
    bX2 = [Buf("X2_%d" % b) for b in range(NB)]
    bMIX = [Buf("MIX_%d" % b) for b in range(NB)]

    final_ops = []
    with k.scope():
        cp = k.sb([128, NCOLP], F32, "colpack"); b_cp = k.buf()
        k.dma(cp[:], colpack[:, :], writes=[b_cp])
        cst = k.sb([128, 1024], F32, "consts"); b_cst = k.buf()
        k.dma(cst[:], consts[:, :], writes=[b_cst])
        ones_bf = k.sb([128, 128], BF16, "ones"); b_ones = k.buf()
        k.S.op("pool", lambda e: e.memset(ones_bf[:], 1.0), writes=[b_ones])
        der = k.sb([128, 9, 8, NB], F32, "der"); b_der = k.buf()
        pbanks = []
        for i in range(8):
            pbanks.append(k.stacks[-1].enter_context(nc.psum_tensor("pb%d" % i, [128, 512], F32)))
        bpb = [k.buf("pb%d" % i) for i in range(8)]

        with k.scope():
            ct = k.sb([128, 16], F32, "ct"); b_ct = k.buf()
            k.dma(ct[:], cT[:, :], writes=[b_ct])
            cact = k.sb([128, 16], F32, "cact"); b_cact = k.buf()
            k.act(cact[:], ct[:], AF.Silu, reads=[b_ct], writes=[b_cact])
            wst = [k.sb([128, 8, 512], F32, "wada") for _ in range(2)]
            b_wst = [k.buf() for _ in range(2)]
            wv = w_ada.rearrange("(k p) c -> p k c", p=128)
            for ch in range(18):
                s = ch % 2
                k.dma(wst[s][:], wv[:, :, ch * 512:(ch + 1) * 512], writes=[b_wst[s]])
                for jj in range(4):
                    j = ch * 4 + jj
                    for kk in range(8):
                        k.mm(pbanks[0][:, 2 * j:2 * j + 2], wst[s][:, kk, jj * 128:(jj + 1) * 128],
                             cact[:, 2 * kk:2 * kk + 2], kk == 0, kk == 7,
                             reads=[b_wst[s], b_cact], writes=[bpb[0]])
            modc = k.sb([128, 72, NB], F32, "modc"); b_modc = k.buf()
            k.tt("dve", modc[:], pbanks[0][:, 0:144].rearrange("p (j b) -> p j b", b=NB),
                 cp[:, C_BADA:C_BADA + 72].unsqueeze(2).to_broadcast([128, 72, NB]), ALU.add,
                 reads=[bpb[0], b_cp], writes=[b_modc])
            for sub in range(3):
                base = sub * 24
                k.copy("dve", der[:, 3 * sub + 1, :, :], modc[:, base:base + 8, :], reads=[b_modc], writes=[b_der])
                k.ts("dve", der[:, 3 * sub, :, :], modc[:, base + 8:base + 16, :], 1.0, None, ALU.add,
                     reads=[b_modc], writes=[b_der])
                gsc = (1.0 if sub == 1 else 0.5) / ALPHA
                k.ts("dve", der[:, 3 * sub + 2, :, :], modc[:, base + 16:base + 24, :], 1.0, gsc, ALU.add, ALU.mult,
                     reads=[b_modc], writes=[b_der])

        def ln_tail(z, b_z, zb, zq, b_zbq, small, b_small, tmp, b_tmp, xo, b_xo, lnidx, ntok):
            ln_tail_a(z, b_z, zb, zq, b_zbq)
            ln_tail_b(z, b_z, zb, zq, b_zbq, small, b_small, tmp, b_tmp, xo, b_xo, lnidx, ntok)

        def ln_tail_a(z, b_z, zb, zq, b_zbq):
            k.act(zb[:], z[:], AF.Copy, reads=[b_z], writes=[b_zbq])
            k.act(zq[:], z[:], AF.Square, reads=[b_z], writes=[b_zbq])

        def ln_tail_b(z, b_z, zb, zq, b_zbq, small, b_small, tmp, b_tmp, xo, b_xo, lnidx, ntok, apply_engs=("dve", "pool")):
            pbs = pbanks[6]
            for dt in range(8):
                k.mm(pbs[:, 0:ntok], ones_bf[:], zb[:, dt, :], dt == 0, dt == 7, reads=[b_zbq, b_ones], writes=[bpb[6]])
            for dt in range(8):
                k.mm(pbs[:, 256:256 + ntok], ones_bf[:], zq[:, dt, :], dt == 0, dt == 7, reads=[b_zbq], writes=[bpb[6]])
            mt, m2, vv, sd, rstd, nmr = [small[:, i, :] for i in range(6)]
            if apply_engs[0] == "pool":
                k.act(mt, pbs[:, 0:ntok], AF.Copy, reads=[bpb[6]], writes=[b_small], scale=1.0 / D)
                k.act(sd, pbs[:, 256:256 + ntok], AF.Copy, reads=[bpb[6]], writes=[b_small], scale=1.0 / D)
                k.tt("pool", m2, mt, mt, ALU.mult, reads=[b_small], writes=[b_small])
                k.tt("pool", vv, sd, m2, ALU.subtract, reads=[b_small], writes=[b_small])
                k.act(sd, vv, AF.Sqrt, reads=[b_small], writes=[b_small], bias=EPS_P)
            else:
                k.ts("dve", mt, pbs[:, 0:ntok], 1.0 / D, None, ALU.mult, reads=[bpb[6]], writes=[b_small])
                k.tt("dve", m2, mt, mt, ALU.mult, reads=[b_small], writes=[b_small])
                k.stt(vv, pbs[:, 256:256 + ntok], 1.0 / D, m2, ALU.mult, ALU.subtract, reads=[bpb[6], b_small], writes=[b_small])
                k.ts("dve", vv, vv, EPS_P, None, ALU.add, reads=[b_small], writes=[b_small])
                k.act(sd, vv, AF.Sqrt, reads=[b_small], writes=[b_small])
            k.S.op("dve", lambda e: e.reciprocal(out=rstd, in_=sd), reads=[b_small], writes=[b_small])
            k.stt(nmr, mt, -1.0, rstd, ALU.mult, ALU.mult, reads=[b_small], writes=[b_small])
            for dt in range(8):
                eng = apply_engs[dt % 2]
                tp = tmp[dt % 2]
                k.tt(eng, tp[:], z[:, dt, :], rstd, ALU.mult, reads=[b_z, b_small], writes=[b_tmp[dt % 2]])
                k.tt(eng, tp[:], tp[:], nmr, ALU.add, reads=[b_small, b_tmp[dt % 2]], writes=[b_tmp[dt % 2]])
                k.ts(eng, xo[:, dt, :], tp[:], cp[:, C_LN + 16 * lnidx + dt:C_LN + 16 * lnidx + dt + 1],
                     cp[:, C_LN + 16 * lnidx + 8 + dt:C_LN + 16 * lnidx + 9 + dt], ALU.mult, ALU.add,
                     reads=[b_tmp[dt % 2], b_cp], writes=[b_xo])

        def ffn_pass(src, b_src, dst, b_dst, wu_d, wd_d, sub, lnidx, is_final):
            with k.scope():
                wup = k.sb([128, 8, 2 * DF], BF16, "wup")
                wdn = k.sb([128, 22, D], BF16, "wdn")
                b_wup = [k.buf() for _ in range(22)]
                b_wdn = [k.buf() for _ in range(11)]
                xb = [k.sb([128, 8, TT], F32, "xb") for _ in range(2)]
                b_xb = [k.buf() for _ in range(2)]
                xm = k.sb([128, 8, TT], BF16, "xm"); b_xm = k.buf()
                sgt = [k.sb([128, TT], F32, "sgt") for _ in range(4)]
                b_sgt = [k.buf() for _ in range(4)]
                UB = (0, 1, 4, 5)
                hh = k.sb([128, 22, TT], BF16, "hh"); b_hh = k.buf()
                z = k.sb([128, 8, TT], F32, "z"); b_z = k.buf()
                zb = k.sb([128, 8, TT], BF16, "zb")
                zq = k.sb([128, 8, TT], BF16, "zq"); b_zbq = k.buf()
                small = k.sb([128, 6, TT], F32, "small"); b_small = k.buf()
                tmp = [k.sb([128, TT], F32, "tmp") for _ in range(2)]
                b_tmp = [k.buf() for _ in range(2)]
                wuv = wu_d.rearrange("(k p) c -> p k c", p=128)
                wdv = wd_d.rearrange("(f p) c -> p f c", p=128)
                engs = ["dve", "act", "pool"]
                n = 0
                for ch in range(22):
                    s = n % 2
                    k.dma(xb[s][:], wuv[:, :, ch * 256:(ch + 1) * 256], writes=[b_xb[s]])
                    k.copy(engs[n % 3], wup[:, :, ch * 256:(ch + 1) * 256], xb[s][:], reads=[b_xb[s]], writes=[b_wup[ch]])
                    n += 1
                for ch in range(11):
                    s = n % 2
                    sv = xb[s][:].rearrange("p a t -> p (a t)").rearrange("p (f c) -> p f c", f=2)
                    k.dma(sv, wdv[:, 2 * ch:2 * ch + 2, :], writes=[b_xb[s]])
                    k.copy(engs[n % 3], wdn[:, 2 * ch:2 * ch + 2, :], sv, reads=[b_xb[s]], writes=[b_wdn[ch]])
                    n += 1
                A = lambda dt, b: der[:, 3 * sub, dt, b:b + 1]
                Bm = lambda dt, b: der[:, 3 * sub + 1, dt, b:b + 1]
                GA = lambda dt, b: der[:, 3 * sub + 2, dt, b:b + 1]
                tiles = [(b, ti) for b in range(NB) for ti in range(NT)]

                def load_x(i):
                    b, ti = tiles[i]
                    s = i % 2
                    k.dma(xb[s][:], src[b].rearrange("(k p) t -> p k t", p=128)[:, :, ti * TT:(ti + 1) * TT],
                          reads=[b_src[b]], writes=[b_xb[s]])

                def make_xm(i):
                    b, ti = tiles[i]
                    s = i % 2
                    for dt in range(8):
                        eng = ("act", "pool", "act", "pool")[dt % 4]
                        if eng == "act":
                            k.S.op("act", lambda e, dt=dt: e.activation(out=xm[:, dt, :], in_=xb[s][:, dt, :], func=AF.Identity,
                                                                        scale=A(dt, b), bias=Bm(dt, b)),
                                   reads=[b_xb[s], b_der], writes=[b_xm])
                        else:
                            k.ts(eng, xm[:, dt, :], xb[s][:, dt, :], A(dt, b), Bm(dt, b), ALU.mult, ALU.add,
                                 reads=[b_xb[s], b_der], writes=[b_xm])

                load_x(0)
                make_xm(0)

                def finish(i):
                    b, ti = tiles[i]
                    s = i % 2
                    ln_tail_b(z, b_z, zb, zq, b_zbq, small, b_small, tmp, b_tmp, xb[s], b_xb[s], lnidx, TT, apply_engs=("pool", "pool"))
                    o = k.dma(dst[b].rearrange("(k p) t -> p k t", p=128)[:, :, ti * TT:(ti + 1) * TT], xb[s][:],
                              reads=[b_xb[s]], writes=[b_dst[b]])
                    if is_final:
                        final_ops.append(o)

                for i, (b, ti) in enumerate(tiles):
                    s = i % 2
                    for j in range(22):
                        if j == 2:
                            if i > 0:
                                finish(i - 1)
                            if i + 1 < len(tiles):
                                load_x(i + 1)
                        bank = pbanks[UB[j % 4]]; bb = bpb[UB[j % 4]]
                        for kk in range(8):
                            k.mm(bank[:, 0:TT], wup[:, kk, j * 128:(j + 1) * 128], xm[:, kk, :], kk == 0, kk == 7,
                                 reads=[b_wup[j // 2], b_xm], writes=[bb])
                        for kk in range(8):
                            c0 = DF + j * 128
                            k.mm(bank[:, 256:256 + TT], wup[:, kk, c0:c0 + 128], xm[:, kk, :], kk == 0, kk == 7,
                                 reads=[b_wup[c0 // 256], b_xm], writes=[bb])
                        k.act(sgt[j % 4][:], bank[:, 0:TT], AF.Silu, reads=[bb], writes=[b_sgt[j % 4]])
                        k.tt("dve", hh[:, j, :], sgt[j % 4][:], bank[:, 256:256 + TT], ALU.mult,
                             reads=[bb, b_sgt[j % 4]], writes=[b_hh])
                    if i + 1 < len(tiles):
                        make_xm(i + 1)
                    for dt in range(8):
                        bi = 2 + (dt // 2) % 2
                        half = (dt % 2) * 256
                        for j in range(22):
                            k.mm(pbanks[bi][:, half:half + TT], wdn[:, j, dt * 128:(dt + 1) * 128], hh[:, j, :],
                                 j == 0, j == 21, reads=[b_wdn[j // 2], b_hh], writes=[bpb[bi]])
                        k.stt(z[:, dt, :], pbanks[bi][:, half:half + TT], GA(dt, b), xb[s][:, dt, :], ALU.mult, ALU.add,
                              reads=[bpb[bi], b_xb[s], b_der], writes=[b_z])
                    ln_tail_a(z, b_z, zb, zq, b_zbq)
                finish(len(tiles) - 1)

        b_out = [Buf("out%d" % b) for b in range(NB)]
        if STAGE == 1:
            ffn_pass(xT, [Buf(), Buf()], outT, b_out, f1u, f1d, 0, 0, True)
        if STAGE >= 2 and STAGE != 3:
            ffn_pass(xT, [Buf(), Buf()], X1, bX1, f1u, f1d, 0, 0, False)
        if STAGE == 3:
            bX1 = [Buf(), Buf()]
            X1 = xT
        if STAGE >= 2:
            TWO_PI = 2.0 * math.pi
            mixc = contextlib.ExitStack()
            k.stacks.append(mixc)
            ident = cst[:, 0:128]
            tri = [cst[:, 128:256], cst[:, 256:384]]
            iota = cst[:, 384:896]
            KTc = cst[:, 896:928]
            cs2 = k.sb([128, 1536], F32, "consts2"); b_cs2 = k.buf()
            k.dma(cs2[:], consts2[:, :], writes=[b_cs2])
            rmask = cs2[:, 0:512]
            mle2 = cs2[:, 512:640]
            mge2 = cs2[:, 640:768]
            J32 = cs2[:, 768:800]
            esel = cs2[:, 800:1312]
            dpk = cs2[:, 1312:1344]
            ident_bf = k.sb([128, 128], BF16, "identbf"); b_idbf = k.buf()
            k.copy("dve", ident_bf[:], ident, reads=[b_cst], writes=[b_idbf])
            J_bf = k.sb([128, 128], BF16, "Jbf")
            k.copy("dve", J_bf[:], cs2[:, 1344:1472], reads=[b_cs2], writes=[b_idbf])
            rowbc = k.sb([128, NROWP], F32, "rowbc"); b_rowbc = k.buf()
            k.dma(rowbc[:], rowpack[0:1, :].partition_broadcast(128), writes=[b_rowbc])
            GSd = nc.dram_tensor("GSs", [NB, 4, 4, T], F32, kind="Internal").ap()
            bGS = [Buf() for _ in range(NB)]

            def new(shape, dt, name="w"):
                return k.sb(shape, dt, name), k.buf()

            def load_w(src_ap, rows_k, c0, c1, stage_cols=128):
                wt, b_wt = new([128, rows_k, c1 - c0], BF16, "wbf")
                v = src_ap.rearrange("(k p) c -> p k c", p=128)
                with k.scope():
                    stg = [new([128, rows_k, stage_cols], F32, "wstg") for _ in range(2)]
                    i = 0
                    c = c0
                    while c < c1:
                        w_ = min(stage_cols, c1 - c)
                        st_, bs_ = stg[i % 2]
                        k.dma(st_[:, :, 0:w_], v[:, :, c:c + w_], writes=[bs_])
                        k.copy(("dve", "pool")[i % 2], wt[:, :, c - c0:c - c0 + w_], st_[:, :, 0:w_], reads=[bs_], writes=[b_wt])
                        c += w_
                        i += 1
                return wt, b_wt

            def reduce_angle(x, out, shape, bx, bo):
                tf, btf = new(shape, F32, "raf")
                ti, bti = new(shape, I32, "rai")
                k.ts("dve", tf[:], x, 1.0 / TWO_PI, None, ALU.mult, reads=[bx], writes=[btf])
                k.copy("dve", ti[:], tf[:], reads=[btf], writes=[bti])
                k.copy("dve", tf[:], ti[:], reads=[bti], writes=[btf])
                k.stt(out, tf[:], -TWO_PI, x, ALU.mult, ALU.add, reads=[btf, bx], writes=[bo])

            def sincos(x, sn, cs_, shape, bx, bsn, bcs):
                s2, bs2 = new(shape, F32, "s2")
                s4, bs4 = new(shape, F32, "s4")
                k.act(s2[:], x, AF.Sin, reads=[bx], writes=[bs2], scale=0.5)
                k.act(s4[:], x, AF.Sin, reads=[bx], writes=[bs4], scale=0.25)
                k.tt("dve", cs_, s2[:], s2[:], ALU.mult, reads=[bs2], writes=[bcs])
                k.ts("dve", cs_, cs_, -2.0, 1.0, ALU.mult, ALU.add, reads=[bcs], writes=[bcs])
                k.tt("dve", s4[:], s4[:], s4[:], ALU.mult, reads=[bs4], writes=[bs4])
                k.ts("dve", s4[:], s4[:], -2.0, 1.0, ALU.mult, ALU.add, reads=[bs4], writes=[bs4])
                k.stt(sn, s2[:], 2.0, s4[:], ALU.mult, ALU.mult, reads=[bs2, bs4], writes=[bsn])

            with k.scope():
                mst = contextlib.ExitStack()
                k.stacks.append(mst)
                Tm, b_Tm = new([128, 32, 128], BF16, "Tm")
                PTre, b_PTre = new([128, 32, 128], BF16, "PTre")
                PTim, b_PTim = new([128, 32, 128], BF16, "PTim")
                Qre, b_Qre = new([128, 32, 128], BF16, "Qre")
                Qim, b_Qim = new([128, 32, 128], BF16, "Qim")
                thr, b_thr = new([128, 32], F32, "thr")
                r8, b_r8 = new([128, 32], F32, "r8")
                with k.scope():
                    sp, b_sp = new([128, 32, 70], F32, "s5p")
                    k.dma(sp[:], s5p.rearrange("p (g f) -> p g f", f=70), writes=[b_sp])
                    lr = sp[:, :, 0]; li = sp[:, :, 1]
                    S2 = [128, 32]
                    step, b_step = new(S2, F32)
                    k.act(step[:], sp[:, :, 2], AF.Exp, reads=[b_sp], writes=[b_step])
                    lrs, b_lrs = new(S2, F32)
                    k.tt("dve", lrs[:], lr, step[:], ALU.mult, reads=[b_sp, b_step], writes=[b_lrs])
                    phi, b_phi = new(S2, F32)
                    k.tt("dve", phi[:], li, step[:], ALU.mult, reads=[b_sp, b_step], writes=[b_phi])
                    phr, b_phr = new(S2, F32)
                    reduce_angle(phi[:], phr[:], S2, b_phi, b_phr)
                    th8, b_th8 = new(S2, F32)
                    k.ts("dve", th8[:], phr[:], 8.0, None, ALU.mult, reads=[b_phr], writes=[b_th8])
                    reduce_angle(th8[:], thr[:], S2, b_th8, b_thr)
                    k.act(r8[:], lrs[:], AF.Exp, reads=[b_lrs], writes=[b_r8], scale=8.0)
                    sn1, b_sn1 = new(S2, F32); cs1, b_cs1 = new(S2, F32); mg1, b_mg1 = new(S2, F32)
                    sincos(phr[:], sn1[:], cs1[:], S2, b_phr, b_sn1, b_cs1)
                    k.act(mg1[:], lrs[:], AF.Exp, reads=[b_lrs], writes=[b_mg1])
                    nr, b_nr = new(S2, F32); ni, b_ni = new(S2, F32)
                    k.tt("dve", nr[:], mg1[:], cs1[:], ALU.mult, reads=[b_mg1, b_cs1], writes=[b_nr])
                    k.ts("dve", nr[:], nr[:], -1.0, None, ALU.add, reads=[b_nr], writes=[b_nr])
                    k.tt("dve", ni[:], mg1[:], sn1[:], ALU.mult, reads=[b_mg1, b_sn1], writes=[b_ni])
                    inv, b_inv = new(S2, F32); t0, b_t0 = new(S2, F32); t1, b_t1 = new(S2, F32)
                    k.tt("dve", inv[:], lr, lr, ALU.mult, reads=[b_sp], writes=[b_inv])
                    k.tt("dve", t0[:], li, li, ALU.mult, reads=[b_sp], writes=[b_t0])
                    k.tt("dve", inv[:], inv[:], t0[:], ALU.add, reads=[b_inv, b_t0], writes=[b_inv])
                    k.S.op("dve", lambda e: e.reciprocal(out=inv[:], in_=inv[:]), reads=[b_inv], writes=[b_inv])
                    cfr, b_cfr = new(S2, F32); cfi, b_cfi = new(S2, F32)
                    k.tt("dve", t0[:], nr[:], lr, ALU.mult, reads=[b_nr, b_sp], writes=[b_t0])
                    k.tt("dve", t1[:], ni[:], li, ALU.mult, reads=[b_ni, b_sp], writes=[b_t1])
                    k.tt("dve", t0[:], t0[:], t1[:], ALU.add, reads=[b_t0, b_t1], writes=[b_t0])
                    k.tt("dve", cfr[:], t0[:], inv[:], ALU.mult, reads=[b_t0, b_inv], writes=[b_cfr])
                    k.tt("dve", t0[:], ni[:], lr, ALU.mult, reads=[b_ni, b_sp], writes=[b_t0])
                    k.tt("dve", t1[:], nr[:], li, ALU.mult, reads=[b_nr, b_sp], writes=[b_t1])
                    k.tt("dve", t0[:], t0[:], t1[:], ALU.subtract, reads=[b_t0, b_t1], writes=[b_t0])
                    k.tt("dve", cfi[:], t0[:], inv[:], ALU.mult, reads=[b_t0, b_inv], writes=[b_cfi])
                    S3 = [128, 32, 16]
                    bre = sp[:, :, 3:19]; bim = sp[:, :, 19:35]; cre = sp[:, :, 35:51]; cim = sp[:, :, 51:67]
                    Bcr, b_Bcr = new(S3, F32); Bci, b_Bci = new(S3, F32); u0, b_u0 = new(S3, F32)
                    cfrb = cfr[:].unsqueeze(2).to_broadcast(S3); cfib = cfi[:].unsqueeze(2).to_broadcast(S3)
                    k.tt("dve", Bcr[:], bre, cfrb, ALU.mult, reads=[b_sp, b_cfr], writes=[b_Bcr])
                    k.tt("dve", u0[:], bim, cfib, ALU.mult, reads=[b_sp, b_cfi], writes=[b_u0])
                    k.tt("dve", Bcr[:], Bcr[:], u0[:], ALU.subtract, reads=[b_Bcr, b_u0], writes=[b_Bcr])
                    k.tt("dve", Bci[:], bim, cfrb, ALU.mult, reads=[b_sp, b_cfr], writes=[b_Bci])
                    k.tt("dve", u0[:], bre, cfib, ALU.mult, reads=[b_sp, b_cfi], writes=[b_u0])
                    k.tt("dve", Bci[:], Bci[:], u0[:], ALU.add, reads=[b_Bci, b_u0], writes=[b_Bci])
                    SP = [128, 32, 32]
                    pwr, b_pwr = new(SP, F32); pwi, b_pwi = new(SP, F32)
                    with k.scope():
                        ang, b_ang = new(SP, F32); angr, b_angr = new(SP, F32)
                        ktb = KTc.unsqueeze(1).to_broadcast(SP)
                        k.tt("dve", ang[:], phr[:].unsqueeze(2).to_broadcast(SP), ktb, ALU.mult, reads=[b_phr, b_cst], writes=[b_ang])
                        reduce_angle(ang[:], angr[:], SP, b_ang, b_angr)
                        psn, b_psn = new(SP, F32); pcs, b_pcs = new(SP, F32); pmg, b_pmg = new(SP, F32)
                        sincos(angr[:], psn[:], pcs[:], SP, b_angr, b_psn, b_pcs)
                        k.tt("dve", pmg[:], lrs[:].unsqueeze(2).to_broadcast(SP), ktb, ALU.mult, reads=[b_lrs, b_cst], writes=[b_pmg])
                        k.act(pmg[:], pmg[:], AF.Exp, reads=[b_pmg], writes=[b_pmg])
                        k.tt("dve", pwr[:], pmg[:], pcs[:], ALU.mult, reads=[b_pmg, b_pcs], writes=[b_pwr])
                        k.tt("dve", pwi[:], pmg[:], psn[:], ALU.mult, reads=[b_pmg, b_psn], writes=[b_pwi])
                    S4 = [128, 32, 8, 16]
                    ta, b_ta = new(S4, F32); tb_, b_tb = new(S4, F32)

                    def cprod(ar, ai, bar, bai, off, o_re, b_ore, o_im, b_oim, neg_im):
                        arb = ar.unsqueeze(2).to_broadcast(S4); aib = ai.unsqueeze(2).to_broadcast(S4)
                        pr = pwr[:, :, off:off + 8].unsqueeze(3).to_broadcast(S4)
                        pi_ = pwi[:, :, off:off + 8].unsqueeze(3).to_broadcast(S4)
                        k.tt("dve", ta[:], arb, pr, ALU.mult, reads=[bar, b_pwr], writes=[b_ta])
                        k.tt("pool", tb_[:], aib, pi_, ALU.mult, reads=[bai, b_pwi], writes=[b_tb])
                        k.tt("dve", o_re, ta[:], tb_[:], ALU.subtract, reads=[b_ta, b_tb], writes=[b_ore])
                        k.tt("dve", ta[:], arb, pi_, ALU.mult, reads=[bar, b_pwi], writes=[b_ta])
                        k.tt("pool", tb_[:], aib, pr, ALU.mult, reads=[bai, b_pwr], writes=[b_tb])
                        if neg_im:
                            k.stt(o_im, ta[:], -1.0, tb_[:], ALU.mult, ALU.subtract, reads=[b_ta, b_tb], writes=[b_oim])
                        else:
                            k.tt("dve", o_im, ta[:], tb_[:], ALU.add, reads=[b_ta, b_tb], writes=[b_oim])

                    v4 = lambda t_: t_[:].rearrange("p g (s h) -> p g s h", h=16)
                    if "noQ" not in SKIP:
                        cprod(cre, cim, b_sp, b_sp, 24, v4(Qre), b_Qre, v4(Qim), b_Qim, True)
                    with k.scope():
                      if "noP" not in SKIP:
                        Pr, b_Pr = new([128, 32, 128], F32); Pi, b_Pi = new([128, 32, 128], F32)
                        cprod(Bcr[:], Bci[:], b_Bcr, b_Bci, 0, v4(Pr), b_Pr, v4(Pi), b_Pi, False)
                        for g in range(32):
                            bank = pbanks[g % 2]; bb = bpb[g % 2]
                            k.mm(bank[:, 256:384], Pr[:, g, :], ident, True, True, reads=[b_Pr, b_cst], writes=[bb])
                            k.mm(bank[:, 384:512], Pi[:, g, :], ident, True, True, reads=[b_Pi, b_cst], writes=[bb])
                            k.act(PTre[:, g, :], bank[:, 256:384], AF.Copy, reads=[bb], writes=[b_PTre])
                            k.act(PTim[:, g, :], bank[:, 384:512], AF.Copy, reads=[bb], writes=[b_PTim])
                    with k.scope():
                      if "noT" not in SKIP:
                        Xr, b_Xr = new([128, 32, 128], F32); Xi, b_Xi = new([128, 32, 128], F32)
                        cprod(Bcr[:], Bci[:], b_Bcr, b_Bci, 8, v4(Xr), b_Xr, v4(Xi), b_Xi, False)
                        Zr, b_Zr = new([128, 32, 128], F32); Zi, b_Zi = new([128, 32, 128], F32)
                        cprod(cre, cim, b_sp, b_sp, 16, v4(Zr), b_Zr, v4(Zi), b_Zi, True)
                        Xmr = ta[:].rearrange("p g s h -> p g (s h)"); b_Xmr = b_ta
                        Xmi = tb_[:].rearrange("p g s h -> p g (s h)"); b_Xmi = b_tb
                        hm = [cst[:, 128 + 63:128 + 64], cst[:, 256 + 64:256 + 65]]
                        tA, b_tA = new([128, 128], F32); tB, b_tB = new([128, 128], F32)
                        for dd_ in range(2):
                            k.ts("dve", Xmr, Xr[:], hm[dd_], None, ALU.mult, reads=[b_Xr, b_cst], writes=[b_Xmr])
                            k.ts("pool", Xmi, Xi[:], hm[dd_], None, ALU.mult, reads=[b_Xi, b_cst], writes=[b_Xmi])
                            for g in range(32):
                                bank = pbanks[2 + g % 2]; bb = bpb[2 + g % 2]
                                k.mm(bank[:, 0:128], Xmr[:, g, :], Zr[:, g, :], True, False, reads=[b_Xmr, b_Zr], writes=[bb])
                                k.mm(bank[:, 0:128], Xmi[:, g, :], Zi[:, g, :], False, True, reads=[b_Xmi, b_Zi], writes=[bb])
                                if dd_ == 0:
                                    k.tt("dve", tA[:], bank[:, 0:128], mle2, ALU.mult, reads=[bb, b_cs2], writes=[b_tA])
                                    k.stt(Tm[:, g, :], ident, dpk[:, g:g + 1], tA[:], ALU.mult, ALU.add,
                                          reads=[b_tA, b_cs2, b_cst], writes=[b_Tm])
                                else:
                                    k.tt("dve", tB[:], bank[:, 0:128], mge2, ALU.mult, reads=[bb, b_cs2], writes=[b_tB])
                                    k.tt("dve", Tm[:, g, :], Tm[:, g, :], tB[:], ALU.add, reads=[b_tB, b_Tm], writes=[b_Tm])

                for (b, phase) in ((0, "S"), (1, "S"), (0, "MO"), (1, "MO")):
                    if phase == "MO" and mst is not None:
                        k.stacks.pop()
                        mst.close()
                        mst = None
                        k.fence_ops = k.S.fence()
                    x1v = X1[b].rearrange("(k p) t -> p k t", p=128)

                    def dma_x1(t0, ntok, xs, b_xs):
                        k.dma(xs[:, :, 0:ntok], x1v[:, :, t0:t0 + ntok], reads=[bX1[b]], writes=[b_xs])

                    def mod_x1(dst_ap, ntok, xs, b_xs, b_dst):
                        for dt in range(8):
                            eng = ("dve", "pool")[dt % 2]
                            k.ts(eng, dst_ap[:, dt, :], xs[:, dt, 0:ntok], der[:, 3, dt, b:b + 1], der[:, 4, dt, b:b + 1],
                                 ALU.mult, ALU.add, reads=[b_xs, b_der], writes=[b_dst])

                    def load_xm2(dst_ap, t0, ntok, xs, b_xs, b_dst):
                        k.dma(xs[:, :, 0:ntok], x1v[:, :, t0:t0 + ntok], reads=[bX1[b]], writes=[b_xs])
                        for dt in range(8):
                            eng = ("dve", "pool")[dt % 2]
                            k.ts(eng, dst_ap[:, dt, :], xs[:, dt, 0:ntok], der[:, 3, dt, b:b + 1], der[:, 4, dt, b:b + 1],
                                 ALU.mult, ALU.add, reads=[b_xs, b_der], writes=[b_dst])

                    with (k.scope() if phase == "S" else contextlib.nullcontext()):
                      if phase == "S" and "S" not in SKIP:
                        Zy, b_Zy = new([128, 4, 8, 512], BF16, "Zy")
                        Zta, b_Zta = new([128, 4, 32, 8, 16], BF16, "Zta")
                        with k.scope():
                            wu, b_wu = load_w(w_in, 8, 1552, 2064)
                            xsL2 = [new([128, 8, 128], F32, "xs") for _ in range(2)]
                            xmbL = [new([128, 8, 1024], BF16, "xmb") for _ in range(2)]

                            def prepA(cb):
                                (xmb, b_xmb) = xmbL[cb % 2]
                                for q8 in range(8):
                                    (xs, b_xs) = xsL2[q8 % 2]
                                    load_xm2(xmb[:, :, q8 * 128:(q8 + 1) * 128], cb * 1024 + q8 * 128, 128, xs, b_xs, b_xmb)

                            prepA(0)
                            for cb in range(4):
                                (xmb, b_xmb) = xmbL[cb % 2]
                                if cb + 1 < 4:
                                    prepA(cb + 1)
                                for s_ in range(8):
                                    bank = pbanks[s_ % 2]; bb = bpb[s_ % 2]
                                    for kk in range(8):
                                        lv = xmb[:, kk, :].rearrange("p (c s) -> p s c", s=8)[:, s_, :]
                                        k.mm(bank[:, :], lv, wu[:, kk, :], kk == 0, kk == 7, reads=[b_xmb, b_wu], writes=[bb])
                                    k.tt("dve", Zta[:, cb, :, s_, :], bank[:, :].rearrange("p (g h) -> p g h", h=16),
                                         rowbc[:, R_BU:R_BU + 512].rearrange("p (g h) -> p g h", h=16), ALU.add,
                                         reads=[bb, b_rowbc], writes=[b_Zta])
                        NG = 8
                        for hf_ in range(32 // NG):
                          with k.scope():
                            G0 = NG * hf_
                            Ut, b_Ut = new([128, NG, 512], BF16, "Ut")
                            Ur, b_Ur = new([128, NG, 512], BF16, "Ur")
                            for cb in range(4):
                                for g4 in range(NG // 4):
                                    bank = pbanks[2 + g4 % 2]; bb = bpb[2 + g4 % 2]
                                    bank2 = pbanks[4 + g4 % 2]; bb2 = bpb[4 + g4 % 2]
                                    for gg in range(4):
                                        g = G0 + g4 * 4 + gg
                                        zl = Zta[:, cb, g, :, :].rearrange("p s h -> p (s h)")
                                        k.mm(bank[:, gg * 128:(gg + 1) * 128], zl, ident_bf[:], True, True,
                                             reads=[b_Zta, b_idbf], writes=[bb])
                                        k.mm(bank2[:, gg * 128:(gg + 1) * 128], zl, J_bf[:], True, True,
                                             reads=[b_Zta, b_idbf], writes=[bb2])
                                    k.act(Ut[:, g4 * 4:g4 * 4 + 4, cb * 128:(cb + 1) * 128],
                                          bank[:, :].rearrange("p (g c) -> p g c", g=4), AF.Copy, reads=[bb], writes=[b_Ut])
                                    k.copy("dve", Ur[:, g4 * 4:g4 * 4 + 4, (3 - cb) * 128:(4 - cb) * 128],
                                           bank2[:, :].rearrange("p (g c) -> p g c", g=4), reads=[bb2], writes=[b_Ur])
                            with k.scope():
                                W_ = lambda: new([128, 512], F32, "sw")
                                W2 = lambda: [W_(), W_()]
                                (ta_, bta), (tr_, btr), (s2t, bs2) = W_(), W_(), W_()
                                (ti_, bti) = new([128, 512], I32, "twi")
                                tcs, tsns = W2(), W2()
                                srL, siL, a1L, a2L, stRL, stIL, kRL, kIL = [W2() for _ in range(8)]
                                HpL = [[new([128, 513], BF16, "Hp") for _ in range(4)] for _ in range(2)]
                                for hl in HpL:
                                    for (h_, bh_) in hl:
                                        k.S.op("pool", lambda e, h_=h_: e.memset(h_[:], 0.0), writes=[bh_])
                                ygL = [new([128, 512], BF16, "yg") for _ in range(2)]
                                ygbL = [new([128, 512], BF16, "ygb") for _ in range(2)]
                                ygTL = [new([128, 4, 128], BF16, "ygT") for _ in range(2)]
                                pY, bY = pbanks[0], bpb[0]
                                pYb, bYb = pbanks[1], bpb[1]
                                pAr, bAr = pbanks[2], bpb[2]
                                pBr, bBr = pbanks[3], bpb[3]
                                pAi, bAi = pbanks[4], bpb[4]
                                pBi, bBi = pbanks[5], bpb[5]
                                pT, bT = pbanks[6], bpb[6]
                                pT2, bT2 = pbanks[7], bpb[7]

                                def gen_tw(gl):
                                    g = G0 + gl
                                    (tc, btc), (tsn, bts) = tcs[gl % 2], tsns[gl % 2]
                                    k.S.op("act", lambda e: e.activation(out=ta_[:], in_=iota, func=AF.Copy, scale=thr[:, g:g + 1]),
                                           reads=[b_cst, b_thr], writes=[bta])
                                    k.act(tr_[:], ta_[:], AF.Copy, reads=[bta], writes=[btr], scale=1.0 / TWO_PI)
                                    k.copy("dve", ti_[:], tr_[:], reads=[btr], writes=[bti])
                                    k.copy("dve", tr_[:], ti_[:], reads=[bti], writes=[btr])
                                    k.stt(ta_[:], tr_[:], -TWO_PI, ta_[:], ALU.mult, ALU.add, reads=[btr, bta], writes=[bta])
                                    k.act(s2t[:], ta_[:], AF.Sin, reads=[bta], writes=[bs2], scale=0.5)
                                    k.act(tr_[:], ta_[:], AF.Sin, reads=[bta], writes=[btr], scale=0.25)
                                    k.tt("pool", tc[:], s2t[:], s2t[:], ALU.mult, reads=[bs2], writes=[btc])
                                    k.ts("pool", tc[:], tc[:], -2.0, 1.0, ALU.mult, ALU.add, reads=[btc], writes=[btc])
                                    k.tt("pool", tr_[:], tr_[:], tr_[:], ALU.mult, reads=[btr], writes=[btr])
                                    k.ts("pool", tr_[:], tr_[:], -2.0, 1.0, ALU.mult, ALU.add, reads=[btr], writes=[btr])
                                    k.stt(tsn[:], s2t[:], 2.0, tr_[:], ALU.mult, ALU.mult, reads=[bs2, btr], writes=[bts])

                                def stageA(gl):
                                    g = G0 + gl
                                    (sr, bsr), (si, bsi) = srL[gl % 2], siL[gl % 2]
                                    k.mm(pAr[:, :], PTre[:, g, :], Ut[:, gl, :], True, True, reads=[b_PTre, b_Ut], writes=[bAr])
                                    k.mm(pBr[:, :], PTre[:, g, :], Ur[:, gl, :], True, True, reads=[b_PTre, b_Ur], writes=[bBr])
                                    k.mm(pAi[:, :], PTim[:, g, :], Ut[:, gl, :], True, True, reads=[b_PTim, b_Ut], writes=[bAi])
                                    k.mm(pBi[:, :], PTim[:, g, :], Ur[:, gl, :], True, True, reads=[b_PTim, b_Ur], writes=[bBi])
                                    k.act(sr[0:64, :], pAr[0:64, :], AF.Copy, reads=[bAr], writes=[bsr])
                                    k.act(sr[64:128, :], pBr[64:128, :], AF.Copy, reads=[bBr], writes=[bsr])
                                    k.act(si[0:64, :], pAi[0:64, :], AF.Copy, reads=[bAi], writes=[bsi])
                                    k.act(si[64:128, :], pBi[64:128, :], AF.Copy, reads=[bBi], writes=[bsi])

                                def stageB(gl):
                                    g = G0 + gl
                                    w = gl % 2
                                    (tc, btc), (tsn, bts) = tcs[w], tsns[w]
                                    (sr, bsr), (si, bsi), (a1, ba1), (a2, ba2) = srL[w], siL[w], a1L[w], a2L[w]
                                    (stR, bstR), (stI, bstI), (kR, bkR), (kI, bkI) = stRL[w], stIL[w], kRL[w], kIL[w]
                                    (hrf, bhrf), (hif, bhif), (hrb, bhrb), (hib, bhib) = HpL[w]
                                    (yg, byg), (ygb, bygb), (ygT, bygT) = ygL[w], ygbL[w], ygTL[w]
                                    k.tt("dve", a1[:], sr[:], tc[:], ALU.mult, reads=[bsr, btc], writes=[ba1])
                                    k.tt("pool", a2[:], si[:], tsn[:], ALU.mult, reads=[bsi, bts], writes=[ba2])
                                    k.tt("dve", stR[:], a1[:], a2[:], ALU.add, reads=[ba1, ba2], writes=[bstR])
                                    k.tt("dve", a1[:], si[:], tc[:], ALU.mult, reads=[bsi, btc], writes=[ba1])
                                    k.tt("pool", a2[:], sr[:], tsn[:], ALU.mult, reads=[bsr, bts], writes=[ba2])
                                    k.tt("dve", stI[:], a1[:], a2[:], ALU.subtract, reads=[ba1, ba2], writes=[bstI])
                                    r8b = r8[:, g:g + 1].to_broadcast([128, 512])
                                    k.S.op("dve", lambda e: e.tensor_tensor_scan(out=kR[:], data0=r8b, data1=stR[:], initial=0.0,
                                                                                 op0=ALU.mult, op1=ALU.add),
                                           reads=[bstR, b_r8], writes=[bkR])
                                    k.S.op("dve", lambda e: e.tensor_tensor_scan(out=kI[:], data0=r8b, data1=stI[:], initial=0.0,
                                                                                 op0=ALU.mult, op1=ALU.add),
                                           reads=[bstI, b_r8], writes=[bkI])
                                    if gl + 1 < NG:
                                        gen_tw(gl + 1)
                                    k.tt("dve", a1[:], kR[:], tc[:], ALU.mult, reads=[bkR, btc], writes=[ba1])
                                    k.tt("pool", a2[:], kI[:], tsn[:], ALU.mult, reads=[bkI, bts], writes=[ba2])
                                    k.tt("dve", hrf[0:64, 1:513], a1[0:64, :], a2[0:64, :], ALU.subtract, reads=[ba1, ba2], writes=[bhrf])
                                    k.tt("dve", hrb[64:128, 1:513], a1[64:128, :], a2[64:128, :], ALU.subtract, reads=[ba1, ba2], writes=[bhrb])
                                    k.tt("dve", a1[:], kI[:], tc[:], ALU.mult, reads=[bkI, btc], writes=[ba1])
                                    k.tt("pool", a2[:], kR[:], tsn[:], ALU.mult, reads=[bkR, bts], writes=[ba2])
                                    k.tt("dve", hif[0:64, 1:513], a1[0:64, :], a2[0:64, :], ALU.add, reads=[ba1, ba2], writes=[bhif])
                                    k.tt("dve", hib[64:128, 1:513], a1[64:128, :], a2[64:128, :], ALU.add, reads=[ba1, ba2], writes=[bhib])
                                    k.mm(pY[:, :], Tm[:, g, :], Ut[:, gl, :], True, False, reads=[b_Tm, b_Ut], writes=[bY])
                                    k.mm(pY[:, :], Qre[:, g, :], hrf[:, 0:512], False, False, reads=[b_Qre, bhrf], writes=[bY])
                                    k.mm(pY[:, :], Qim[:, g, :], hif[:, 0:512], False, True, reads=[b_Qim, bhif], writes=[bY])
                                    k.mm(pYb[:, :], Qre[:, g, :], hrb[:, 0:512], True, False, reads=[b_Qre, bhrb], writes=[bYb])
                                    k.mm(pYb[:, :], Qim[:, g, :], hib[:, 0:512], False, True, reads=[b_Qim, bhib], writes=[bYb])
                                    k.act(yg[:], pY[:, :], AF.Copy, reads=[bY], writes=[byg])
                                    k.act(ygb[:], pYb[:, :], AF.Copy, reads=[bYb], writes=[bygb])
                                    for jb in range(4):
                                        k.mm(pT2[:, jb * 128:(jb + 1) * 128], ygb[:, jb * 128:(jb + 1) * 128], ident_bf[:], True, True,
                                             reads=[bygb, b_idbf], writes=[bT2])
                                    k.copy("dve", ygT[:], pT2[:, :].rearrange("p (j x) -> p j x", j=4), reads=[bT2], writes=[bygT])
                                    for cb in range(4):
                                        k.mm(pT[:, cb * 128:(cb + 1) * 128], yg[:, cb * 128:(cb + 1) * 128], ident_bf[:], True, False,
                                             reads=[byg, b_idbf], writes=[bT])
                                        k.mm(pT[:, cb * 128:(cb + 1) * 128], J_bf[:], ygT[:, 3 - cb, :], False, True,
                                             reads=[bygT, b_idbf], writes=[bT])
                                    k.copy("dve", Zy[:, :, :, 16 * g:16 * g + 16],
                                           pT[:, :].rearrange("p (c t h) -> p c t h", c=4, t=8), reads=[bT], writes=[b_Zy])

                                gen_tw(0)
                                stageA(0)
                                for gl in range(NG):
                                    if gl + 1 < NG:
                                        stageA(gl + 1)
                                    stageB(gl)
                        with k.scope():
                            glw, b_glw = load_w(glu_w, 4, 0, 512)
                            yb, b_yb = new([128, 4, 1024], F32, "yb")
                            u1, b_u1 = new([128, 4, 1024], F32, "u1")
                            yab, b_yab = new([128, 4, 1024], BF16, "yab")
                            sqb, b_sqb = new([128, 4, 1024], BF16, "sqb")
                            sg_, b_sg = new([128, 512], F32, "sg")
                            rs_, b_rs = new([128, 512], F32, "rs")
                            mo, b_mo = new([128, 4, 1024], BF16, "mo")
                            GC = 2.0 * math.sqrt(2.0 / math.pi)
                            for cb in range(4):
                                for ct in range(4):
                                    for th in range(2):
                                        bank = pbanks[(ct * 2 + th) % 2]; bb = bpb[(ct * 2 + th) % 2]
                                        for t4 in range(4):
                                            t_ = th * 4 + t4
                                            k.mm(bank[:, t4 * 128:(t4 + 1) * 128], Zy[:, cb, t_, ct * 128:(ct + 1) * 128], ident_bf[:],
                                                 True, True, reads=[b_Zy, b_idbf], writes=[bb])
                                        k.act(yb[:, ct, :].rearrange("p (c s) -> p s c", s=8)[:, th * 4:th * 4 + 4, :],
                                              bank[:, :].rearrange("p (s c) -> p s c", s=4), AF.Copy, reads=[bb], writes=[b_yb])
                                k.act(u1[:], yb[:], AF.Square, reads=[b_yb], writes=[b_u1])
                                k.ts("dve", u1[:], u1[:], 0.044715, 1.0, ALU.mult, ALU.add, reads=[b_u1], writes=[b_u1])
                                k.tt("pool", u1[:], u1[:], yb[:], ALU.mult, reads=[b_u1, b_yb], writes=[b_u1])
                                k.act(u1[:], u1[:], AF.Sigmoid, reads=[b_u1], writes=[b_u1], scale=GC)
                                k.tt("dve", yb[:], yb[:], u1[:], ALU.mult, reads=[b_u1, b_yb], writes=[b_yb])
                                k.copy("pool", yab[:], yb[:], reads=[b_yb], writes=[b_yab])
                                for hf in range(2):
                                    tk = slice(hf * 512, (hf + 1) * 512)
                                    for co in range(4):
                                        bank = pbanks[2 + co % 2]; bb = bpb[2 + co % 2]
                                        for ci in range(4):
                                            k.mm(bank[:, :], glw[:, ci, co * 128:(co + 1) * 128], yab[:, ci, tk], ci == 0, ci == 3,
                                                 reads=[b_glw, b_yab], writes=[bb])
                                        k.act(sg_[:], bank[:, :], AF.Sigmoid, reads=[bb, b_cp], writes=[b_sg],
                                              bias=cp[:, C_GLUB + co:C_GLUB + co + 1])
                                        k.tt("dve", yb[:, co, tk], yb[:, co, tk], sg_[:], ALU.mult, reads=[b_yb, b_sg], writes=[b_yb])
                                k.act(sqb[:], yb[:], AF.Square, reads=[b_yb], writes=[b_sqb])
                                for hf in range(2):
                                    tk = slice(hf * 512, (hf + 1) * 512)
                                    bank = pbanks[4 + hf]; bb = bpb[4 + hf]
                                    for ci in range(4):
                                        k.mm(bank[:, :], ones_bf[:], sqb[:, ci, tk], ci == 0, ci == 3, reads=[b_sqb, b_ones], writes=[bb])
                                    k.ts("dve", rs_[:], bank[:, :], 1.0 / 512.0, LN_EPS, ALU.mult, ALU.add, reads=[bb], writes=[b_rs])
                                    k.act(rs_[:], rs_[:], AF.Sqrt, reads=[b_rs], writes=[b_rs])
                                    k.S.op("dve", lambda e: e.reciprocal(out=rs_[:], in_=rs_[:]), reads=[b_rs], writes=[b_rs])
                                    for co in range(4):
                                        k.stt(mo[:, co, tk], yb[:, co, tk], cp[:, C_SNG + co:C_SNG + co + 1], rs_[:], ALU.mult, ALU.mult,
                                              reads=[b_yb, b_rs, b_cp], writes=[b_mo])
                                k.dma(MIX[b, 512:1024, cb * 1024:(cb + 1) * 1024].rearrange("(k p) t -> p k t", p=128), mo[:],
                                      reads=[b_mo], writes=[bMIX[b]])

                    with (k.scope() if phase == "MO" else contextlib.nullcontext()):
                      if phase == "MO" and "M" not in SKIP:
                        qf, b_qf = new([128, 2, T], BF16, "qf")
                        kf, b_kf = new([128, 2, T], BF16, "kf")
                        vext, b_vext = new([128, 32, 4, 129], BF16, "vext")
                        big, b_big = new([128, 4, T + 4], F32, "big")
                        k.S.op("pool", lambda e: e.memset(vext[:, :, :, 128:129], 1.0), writes=[b_vext])
                        k.S.op("pool", lambda e: e.memset(big[:, :, 0:2], 0.0), writes=[b_big])
                        k.S.op("pool", lambda e: e.memset(big[:, :, T + 2:T + 4], 0.0), writes=[b_big])
                        with k.scope():
                            wqk, b_wqk = load_w(w_in, 8, 0, 512)
                            wv, b_wv = load_w(w_in, 8, 512, 1024)
                            wg, b_wg = load_w(w_in, 8, 1536, 1552, 16)
                            xs, b_xs = new([128, 8, 256], F32, "xs")
                            xm2L = [new([128, 8, 256], BF16, "xm2") for _ in range(2)]
                            gt, b_gt = new([16, 256], F32, "gt")
                            dma_x1(0, 256, xs, b_xs)
                            mod_x1(xm2L[0][0], 256, xs, b_xs, xm2L[0][1])
                            for ti in range(16):
                                t0_ = ti * 256
                                (xm2, b_xm2) = xm2L[ti % 2]
                                if ti + 1 < 16:
                                    dma_x1(t0_ + 256, 256, xs, b_xs)
                                for ct in range(4):
                                    bank = pbanks[ct // 2]; bb = bpb[ct // 2]
                                    half = (ct % 2) * 256
                                    for kk in range(8):
                                        k.mm(bank[:, half:half + 256], wqk[:, kk, ct * 128:(ct + 1) * 128], xm2[:, kk, :], kk == 0, kk == 7,
                                             reads=[b_wqk, b_xm2], writes=[bb])
                                    k.act(big[:, ct, 2 + t0_:2 + t0_ + 256], bank[:, half:half + 256], AF.Identity, reads=[bb, b_cp],
                                          writes=[b_big], bias=cp[:, C_BQK + ct:C_BQK + ct + 1])
                                for cc in range(2):
                                    c = ti * 2 + cc
                                    bank = pbanks[2 + cc]; bb = bpb[2 + cc]
                                    for kk in range(8):
                                        k.mm(bank[:, :], xm2[:, kk, cc * 128:(cc + 1) * 128], wv[:, kk, :], kk == 0, kk == 7,
                                             reads=[b_wv, b_xm2], writes=[bb])
                                    k.tt("dve", vext[:, c, :, 0:128], bank[:, :].rearrange("p (r v) -> p r v", r=4),
                                         rowbc[:, R_BV:R_BV + 512].rearrange("p (r v) -> p r v", r=4), ALU.add,
                                         reads=[bb, b_rowbc], writes=[b_vext])
                                bank = pbanks[4]; bb = bpb[4]
                                for kk in range(8):
                                    k.mm(bank[0:16, 0:256], wg[:, kk, 0:16], xm2[:, kk, :], kk == 0, kk == 7,
                                         reads=[b_wg, b_xm2], writes=[bb])
                                k.act(gt[0:16, :], bank[0:16, 0:256], AF.Identity, reads=[bb, b_cp], writes=[b_gt],
                                      bias=cp[0:16, C_BG + 4:C_BG + 5])
                                k.dma(GSd[b, :, :, t0_:t0_ + 256].rearrange("k r t -> (k r) t"), gt[:], reads=[b_gt], writes=[bGS[b]])
                                if ti + 1 < 16:
                                    mod_x1(xm2L[(ti + 1) % 2][0], 256, xs, b_xs, xm2L[(ti + 1) % 2][1])
                        ktm, b_ktm = new([128, 32, 256], BF16, "ktm")
                        with k.scope():
                            acc, b_acc = new([128, T], F32, "acc")
                            for ct in range(4):
                                cw = lambda j: cp[:, C_CW + 4 * j + ct:C_CW + 4 * j + ct + 1]
                                k.ts("dve", acc[:], big[:, ct, 0:T], cw(0), None, ALU.mult, reads=[b_big, b_cp], writes=[b_acc])
                                for j in range(1, 5):
                                    k.stt(acc[:], big[:, ct, j:j + T], cw(j), acc[:], ALU.mult, ALU.add, reads=[b_big, b_cp, b_acc], writes=[b_acc])
                                cb_ = cp[:, C_CB + ct:C_CB + ct + 1]
                                if ct < 2:
                                    k.act(acc[:], acc[:], AF.Silu, reads=[b_acc, b_cp], writes=[b_acc], bias=cb_)
                                    k.ts("pool", qf[:, ct, :], acc[:], 0.125, None, ALU.mult, reads=[b_acc], writes=[b_qf])
                                else:
                                    k.act(kf[:, ct - 2, :], acc[:], AF.Silu, reads=[b_acc, b_cp], writes=[b_kf], bias=cb_)
                            for c2 in range(16):
                                bank = pbanks[c2 % 2]; bb = bpb[c2 % 2]
                                for cc in range(2):
                                    c = c2 * 2 + cc
                                    for kt in range(2):
                                        o_ = cc * 256 + kt * 128
                                        k.mm(bank[:, o_:o_ + 128], kf[:, kt, c * 128:(c + 1) * 128], ident_bf[:], True, True,
                                             reads=[b_kf, b_idbf], writes=[bb])
                                k.act(ktm[:, c2 * 2:c2 * 2 + 2, :], bank[:, :].rearrange("p (c x) -> p c x", c=2), AF.Copy,
                                      reads=[bb], writes=[b_ktm])
                        WCt, b_WCt = new([128, 2, 8, 32], F32, "WCt")
                        DECt, b_DECt = new([128, 8, 32], F32, "DECt")
                        with k.scope():
                            G5 = [32, 4, 4, 128]
                            Gc, b_Gc = new(G5, F32, "Gc")
                            k.dma(Gc[:], GSd[b].rearrange("k r (c t) -> c k r t", t=128), reads=[bGS[b]], writes=[b_Gc])
                            D4 = [32, 2, 4, 128]
                            spl, b_spl = new(D4, F32); BS, b_BS = new(D4, F32); ee, b_ee = new(D4, F32)
                            k.act(spl[:], Gc[:, 2:4, :, :], AF.Exp, reads=[b_Gc], writes=[b_spl], scale=-1.0)
                            k.act(spl[:], spl[:], AF.Ln, reads=[b_spl], writes=[b_spl], bias=1.0)
                            flat = lambda t_, d: t_[:, d, :, :].rearrange("c r t -> c (r t)")
                            for d in range(2):
                                k.S.op("dve", lambda e, d=d: e.tensor_tensor_scan(out=flat(BS, d), data0=rmask[0:32, :], data1=flat(spl, d),
                                                                                  initial=0.0, op0=ALU.mult, op1=ALU.add),
                                       reads=[b_spl, b_cs2], writes=[b_BS])
                            tot = BS[:, 1, :, 127:128].to_broadcast([32, 4, 128])
                            k.tt("dve", ee[:, 1, :, :], spl[:, 1, :, :], BS[:, 1, :, :], ALU.subtract, reads=[b_spl, b_BS], writes=[b_ee])
                            k.tt("dve", BS[:, 1, :, :], ee[:, 1, :, :], tot, ALU.add, reads=[b_ee, b_BS], writes=[b_BS])
                            k.tt("dve", ee[:], Gc[:, 0:2, :, :], BS[:], ALU.add, reads=[b_Gc, b_BS], writes=[b_ee])
                            emx, b_emx = new([32, 2, 4], F32)
                            k.S.op("dve", lambda e: e.tensor_reduce(out=emx[:], in_=ee[:], axis=AX.X, op=ALU.max), reads=[b_ee], writes=[b_emx])
                            nbe, b_nbe = new([32, 2, 4], F32)
                            k.ts("dve", nbe[:, 0, :], BS[:, 0, :, 127], -1.0, None, ALU.mult, reads=[b_BS], writes=[b_nbe])
                            k.ts("dve", nbe[:, 1, :], BS[:, 1, :, 0], -1.0, None, ALU.mult, reads=[b_BS], writes=[b_nbe])
                            pbk = pbanks[0]; bbk = bpb[0]
                            for d in range(2):
                                rhs_ = ident[0:32, 0:32] if d == 0 else J32[0:32, :]
                                k.mm(pbk[0:4, d * 64:d * 64 + 32], emx[:, d, :], rhs_, True, True, reads=[b_emx, b_cst, b_cs2], writes=[bbk])
                                k.mm(pbk[0:4, d * 64 + 32:d * 64 + 64], nbe[:, d, :], rhs_, True, True, reads=[b_nbe, b_cst, b_cs2], writes=[bbk])
                            rows, b_rows = new([4, 2, 2, 32], F32)
                            k.copy("dve", rows[:], pbk[0:4, 0:128].rearrange("r (d x j) -> r d x j", d=2, x=2), reads=[bbk], writes=[b_rows])
                            mrow, b_mrow = new([4, 2, 33], F32)
                            k.S.op("pool", lambda e: e.memset(mrow[:], 0.0), writes=[b_mrow])
                            Rrow, b_Rrow = new([4, 2, 32], F32); drow, b_drow = new([4, 2, 32], F32)
                            for d in range(2):
                                k.S.op("dve", lambda e, d=d: e.tensor_tensor_scan(out=mrow[:, d, 1:33], data0=rows[:, d, 0, :], data1=rows[:, d, 1, :],
                                                                                  initial=0.0, op0=ALU.max, op1=ALU.add),
                                       reads=[b_rows, b_mrow], writes=[b_mrow])
                            k.tt("dve", Rrow[:], mrow[:, :, 0:32], rows[:, :, 0, :], ALU.max, reads=[b_mrow, b_rows], writes=[b_Rrow])
                            k.tt("dve", drow[:], mrow[:, :, 0:32], Rrow[:], ALU.subtract, reads=[b_mrow, b_Rrow], writes=[b_drow])
                            k.act(drow[:], drow[:], AF.Exp, reads=[b_drow], writes=[b_drow])
                            pbd = pbanks[1]; bbd = bpb[1]
                            for d in range(2):
                                for r in range(4):
                                    o_ = (d * 4 + r) * 32
                                    k.mm(pbd[:, o_:o_ + 32], esel[0:4, r * 128:(r + 1) * 128], drow[:, d, :], True, True,
                                         reads=[b_cs2, b_drow], writes=[bbd])
                            k.copy("dve", DECt[:], pbd[:, 0:256].rearrange("p (x j) -> p x j", x=8), reads=[bbd], writes=[b_DECt])
                            pbr = pbanks[2]; bbr = bpb[2]
                            k.mm(pbr[0:32, 0:4], Rrow[:, 0, :], ident[0:4, 0:4], True, True, reads=[b_Rrow, b_cst], writes=[bbr])
                            k.mm(pbr[0:32, 4:8], Rrow[:, 1, :], ident[0:4, 0:4], True, True, reads=[b_Rrow, b_cst], writes=[bbr])
                            RbT, b_RbT = new([32, 8], F32)
                            k.copy("dve", RbT[:], pbr[0:32, 0:8], reads=[bbr], writes=[b_RbT])
                            k.mm(pbr[0:32, 8:12], J32[0:32, :], RbT[:, 4:8], True, True, reads=[b_RbT, b_cs2], writes=[bbr])
                            Rc, b_Rc = new([32, 2, 4], F32)
                            k.copy("dve", Rc[:, 0, :], RbT[:, 0:4], reads=[b_RbT], writes=[b_Rc])
                            k.copy("dve", Rc[:, 1, :], pbr[0:32, 8:12], reads=[bbr], writes=[b_Rc])
                            Rcb = Rc[:].unsqueeze(3).to_broadcast(D4)
                            k.tt("dve", ee[:], ee[:], Rcb, ALU.subtract, reads=[b_ee, b_Rc], writes=[b_ee])
                            k.act(ee[:], ee[:], AF.Exp, reads=[b_ee], writes=[b_ee])
                            k.tt("dve", BS[:], BS[:], Rcb, ALU.subtract, reads=[b_BS, b_Rc], writes=[b_BS])
                            k.act(BS[:], BS[:], AF.Exp, reads=[b_BS], writes=[b_BS])
                            pbw = pbanks[3]; bbw = bpb[3]
                            for x_, src_, bsrc in ((0, ee, b_ee), (1, BS, b_BS)):
                                for d in range(2):
                                    for r in range(4):
                                        o_ = (x_ * 8 + d * 4 + r) * 32
                                        k.mm(pbw[:, o_:o_ + 32], src_[:, d, r, :], ident[0:32, 0:32], True, True, reads=[bsrc, b_cst], writes=[bbw])
                            k.copy("dve", WCt[:], pbw[:, :].rearrange("p (x y c) -> p x y c", x=2, y=8), reads=[bbw], writes=[b_WCt])
                        with k.scope():
                            Cst = [[new([128, 129], F32, "Cst") for r in range(4)] for d in range(2)]
                            NR = 4
                            STb = [new([128, 128], BF16, "ST") for _ in range(NR)]
                            Vpb = [new([128, 129], BF16, "Vp") for _ in range(NR)]
                            Cdb = [new([128, 129], BF16, "Cd") for _ in range(NR)]
                            ddb = [new([128, 2], F32, "dd") for _ in range(NR)]
                            insts = [(st, d, r) for st in range(32) for d in range(2) for r in range(4)]

                            def geom(it):
                                st, d, r = insts[it]
                                c = st if d == 0 else 31 - st
                                return st, d, r, c, slice(c * 128, (c + 1) * 128), r // 2, 64 * (r % 2), d * 4 + r, it % NR

                            def emitA(it):
                                st, d, r, c, tok, kt, ro, x_, w = geom(it)
                                pS = pbanks[it % 2]; bS = bpb[it % 2]
                                (ST, bST), (Vp, bVp), (Cd, bCd) = STb[w], Vpb[w], Cdb[w]
                                (Cs, bCs) = Cst[d][r]
                                k.mm(pS[:, 0:128], kf[ro:ro + 64, kt, tok], qf[ro:ro + 64, kt, tok], True, True,
                                     reads=[b_kf, b_qf], writes=[bS])
                                k.tt("dve", ST[:], pS[:, 0:128], tri[d], ALU.mult, reads=[bS, b_cst], writes=[bST])
                                zc = cs2[:, 1500:1501]
                                k.ts("pool", Vp[:], vext[:, c, r, :], WCt[:, 0, x_, c:c + 1], zc, ALU.mult, ALU.add,
                                     reads=[b_vext, b_WCt, b_cs2], writes=[bVp])
                                if st > 0:
                                    k.ts("pool", Cs[ro:ro + 64, :], Cs[ro:ro + 64, :], DECt[ro:ro + 64, x_, st:st + 1], zc[ro:ro + 64, :],
                                         ALU.mult, ALU.add, reads=[bCs, b_DECt, b_cs2], writes=[bCs])
                                    k.act(Cd[ro:ro + 64, :], Cs[ro:ro + 64, :], AF.Copy, reads=[bCs], writes=[bCd])

                            def emitB(it):
                                st, d, r, c, tok, kt, ro, x_, w = geom(it)
                                pN = pbanks[2 + it % 2]; bN = bpb[2 + it % 2]
                                pC = pbanks[4 + it % 2]; bC = bpb[4 + it % 2]
                                (ST, bST), (Vp, bVp), (Cd, bCd), (dd, bdd) = STb[w], Vpb[w], Cdb[w], ddb[w]
                                (Cs, bCs) = Cst[d][r]
                                k.mm(pN[:, 0:129], ST[:], Vp[:], True, st == 0, reads=[bST, bVp], writes=[bN])
                                if st > 0:
                                    k.mm(pN[:, 0:129], qf[ro:ro + 64, kt, tok], Cd[ro:ro + 64, :], False, True,
                                         reads=[b_qf, bCd], writes=[bN])
                                k.mm(pC[ro:ro + 64, 0:129], ktm[:, c, r * 64:(r + 1) * 64], Vp[:], True, True,
                                     reads=[b_ktm, bVp], writes=[bC])
                                if st > 0:
                                    k.tt("dve", Cs[ro:ro + 64, :], Cs[ro:ro + 64, :], pC[ro:ro + 64, 0:129], ALU.add,
                                         reads=[bCs, bC], writes=[bCs])
                                else:
                                    k.copy("dve", Cs[ro:ro + 64, :], pC[ro:ro + 64, 0:129], reads=[bC], writes=[bCs])
                                k.act(dd[:, 0:1], pN[:, 128:129], AF.Abs, reads=[bN], writes=[bdd])
                                k.ts("dve", dd[:, 0:1], dd[:, 0:1], WCt[:, 1, x_, c:c + 1], None, ALU.max,
                                     reads=[bdd, b_WCt], writes=[bdd])
                                k.S.op("dve", lambda e: e.reciprocal(out=dd[:, 1:2], in_=dd[:, 0:1]), reads=[bdd], writes=[bdd])
                                cell = big[:, :, 0:T].rearrange("p r (c v) -> p c r v", v=128)[:, c, r, :]
                                first = (d == 0 and c < 16) or (d == 1 and c >= 16)
                                if first:
                                    k.S.op("act", lambda e: e.activation(out=cell, in_=pN[:, 0:128], func=AF.Copy, scale=dd[:, 1:2]),
                                           reads=[bN, bdd], writes=[b_big])
                                else:
                                    k.stt(cell, pN[:, 0:128], dd[:, 1:2], cell, ALU.mult, ALU.add, reads=[bN, bdd, b_big], writes=[b_big])

                            for it in range(len(insts) + 1):
                                if it < len(insts):
                                    emitA(it)
                                if it >= 1:
                                    emitB(it - 1)
                        with k.scope():
                            wo, b_wo = load_w(w_in, 8, 1024, 1536)
                            xs1 = new([128, 8, 256], F32, "xs")
                            xsL = [xs1, xs1]
                            xmL = [new([128, 8, 256], BF16, "xm2") for _ in range(2)]
                            obL = [new([128, 512], F32, "ob") for _ in range(2)]
                            hgL = [new([128, 512], F32, "hg") for _ in range(2)]
                            hnL = [new([128, 512], BF16, "hnb") for _ in range(2)]
                            stL = [new([128, 4, 6], F32, "bst") for _ in range(2)]
                            mvL = [new([128, 4, 2], F32, "mv") for _ in range(2)]
                            mmL = [new([128, 4, 256], BF16, "mixm") for _ in range(2)]
                            (xs, b_xs) = xs1
                            dma_x1(0, 256, xs, b_xs)
                            mod_x1(xmL[0][0], 256, xs, b_xs, xmL[0][1])
                            for ti in range(16):
                                (xm2, b_xm2), (mm_, b_mm) = xmL[ti % 2], mmL[ti % 2]
                                if ti + 1 < 16:
                                    dma_x1((ti + 1) * 256, 256, xs, b_xs)
                                for cc in range(2):
                                    c = ti * 2 + cc
                                    (ob, b_ob), (hg, b_hg), (hnb, b_hnb), (stt_, b_stt), (mv, b_mv) = obL[cc], hgL[cc], hnL[cc], stL[cc], mvL[cc]
                                    bank = pbanks[cc]; bb = bpb[cc]
                                    for kk in range(8):
                                        k.mm(bank[:, :], xm2[:, kk, cc * 128:(cc + 1) * 128], wo[:, kk, :], kk == 0, kk == 7,
                                             reads=[b_wo, b_xm2], writes=[bb])
                                    k.tt("dve", ob[:], bank[:, :], rowbc[:, R_BO:R_BO + 512], ALU.add, reads=[bb, b_rowbc], writes=[b_ob])
                                    k.act(ob[:], ob[:], AF.Sigmoid, reads=[b_ob], writes=[b_ob])
                                    cellc = big[:, :, 0:T].rearrange("p r (c v) -> p c r v", v=128)[:, c, :, :]
                                    k.tt("dve", hg[:].rearrange("p (r v) -> p r v", r=4), ob[:].rearrange("p (r v) -> p r v", r=4), cellc,
                                         ALU.mult, reads=[b_ob, b_big], writes=[b_hg])
                                    for r in range(4):
                                        k.S.op("dve", lambda e, r=r, stt_=stt_, hg=hg: e.bn_stats(out=stt_[:, r, :], in_=hg[:, r * 128:(r + 1) * 128]),
                                               reads=[b_hg], writes=[b_stt])
                                        k.S.op("dve", lambda e, r=r, stt_=stt_, mv=mv: e.bn_aggr(out=mv[:, r, :], in_=stt_[:, r, :]),
                                               reads=[b_stt], writes=[b_mv])
                                    k.ts("dve", mv[:, :, 1], mv[:, :, 1], LN_EPS, None, ALU.add, reads=[b_mv], writes=[b_mv])
                                    k.act(mv[:, :, 1], mv[:, :, 1], AF.Sqrt, reads=[b_mv], writes=[b_mv])
                                    k.S.op("dve", lambda e, mv=mv: e.reciprocal(out=mv[:, :, 1], in_=mv[:, :, 1]), reads=[b_mv], writes=[b_mv])
                                    for r in range(4):
                                        k.ts("dve", hg[:, r * 128:(r + 1) * 128], hg[:, r * 128:(r + 1) * 128], mv[:, r, 0:1], mv[:, r, 1:2],
                                             ALU.subtract, ALU.mult, reads=[b_hg, b_mv], writes=[b_hg])
                                    k.tt("pool", hnb[:], hg[:], rowbc[:, R_MNG:R_MNG + 512], ALU.mult, reads=[b_hg, b_rowbc], writes=[b_hnb])
                                    bank2 = pbanks[2 + cc]; bb2 = bpb[2 + cc]
                                    for r in range(4):
                                        k.mm(bank2[:, r * 128:(r + 1) * 128], hnb[:, r * 128:(r + 1) * 128], ident_bf[:], True, True,
                                             reads=[b_hnb, b_idbf], writes=[bb2])
                                    k.act(mm_[:, :, cc * 128:(cc + 1) * 128], bank2[:, :].rearrange("p (r t) -> p r t", r=4), AF.Copy,
                                          reads=[bb2], writes=[b_mm])
                                k.dma(MIX[b, 0:512, ti * 256:(ti + 1) * 256].rearrange("(k p) t -> p k t", p=128), mm_[:],
                                      reads=[b_mm], writes=[bMIX[b]])
                                if ti + 1 < 16:
                                    mod_x1(xmL[(ti + 1) % 2][0], 256, xs, b_xs, xmL[(ti + 1) % 2][1])

                    with (k.scope() if phase == "MO" else contextlib.nullcontext()):
                      if phase == "MO" and "O" not in SKIP:
                        wob, b_wob = load_w(w_out, 8, 0, 1024)
                        xs = [new([128, 8, TT], F32, "xs") for _ in range(2)]
                        mx = [new([128, 8, TT], BF16, "mx") for _ in range(2)]
                        zL = [new([128, 8, TT], F32, "z") for _ in range(2)]
                        zbL = [k.sb([128, 8, TT], BF16, "zb") for _ in range(2)]
                        zqL = [k.sb([128, 8, TT], BF16, "zq") for _ in range(2)]
                        bzbqL = [k.buf() for _ in range(2)]
                        smL = [new([128, 6, TT], F32, "small") for _ in range(2)]
                        tmpL = [[k.sb([128, TT], F32, "tmp") for _ in range(2)] for _ in range(2)]
                        btmpL = [[k.buf() for _ in range(2)] for _ in range(2)]
                        def loadO(ti):
                            (xs_, bxs_), (mx_, bmx_) = xs[ti % 2], mx[ti % 2]
                            tsl = slice(ti * TT, (ti + 1) * TT)
                            k.dma(xs_[:], x1v[:, :, tsl], reads=[bX1[b]], writes=[bxs_])
                            k.dma(mx_[:], MIX[b].rearrange("(k p) t -> p k t", p=128)[:, :, tsl], reads=[bMIX[b]], writes=[bmx_])

                        loadO(0)
                        for ti in range(NT):
                            s = ti % 2
                            (xs_, bxs_), (mx_, bmx_) = xs[s], mx[s]
                            tsl = slice(ti * TT, (ti + 1) * TT)
                            if ti + 1 < NT:
                                loadO(ti + 1)
                            (z, b_z), zb, zq, b_zbq = zL[s], zbL[s], zqL[s], bzbqL[s]
                            (small, b_small), tmp, b_tmp = smL[s], tmpL[s], btmpL[s]
                            for dt in range(8):
                                bi = (dt // 2) % 2
                                half = (dt % 2) * 256
                                for kk in range(8):
                                    k.mm(pbanks[bi][:, half:half + TT], wob[:, kk, dt * 128:(dt + 1) * 128], mx_[:, kk, :], kk == 0, kk == 7,
                                         reads=[b_wob, bmx_], writes=[bpb[bi]])
                                k.stt(z[:, dt, :], pbanks[bi][:, half:half + TT], der[:, 5, dt, b:b + 1], xs_[:, dt, :], ALU.mult, ALU.add,
                                      reads=[bpb[bi], bxs_, b_der], writes=[b_z])
                            ln_tail(z, b_z, zb, zq, b_zbq, small, b_small, tmp, b_tmp, xs_, bxs_, 1, TT)
                            k.dma(X2[b].rearrange("(k p) t -> p k t", p=128)[:, :, tsl], xs_[:], reads=[bxs_], writes=[bX2[b]])

        if STAGE >= 2:
            k.stacks.pop()
            mixc.close()
            k.fence_ops = k.S.fence()
        if STAGE == 3:
            with k.scope():
                cvb, b_cvb = k.sb([128, 8, 256], BF16, "cvb"), k.buf()
                cvf, b_cvf = k.sb([128, 8, 256], F32, "cvf"), k.buf()
                for b in range(NB):
                    for ti in range(16):
                        tsl = slice(ti * 256, (ti + 1) * 256)
                        k.dma(cvb[:], MIX[b].rearrange("(k p) t -> p k t", p=128)[:, :, tsl], reads=[bMIX[b]], writes=[b_cvb])
                        k.copy("dve", cvf[:], cvb[:], reads=[b_cvb], writes=[b_cvf])
                        final_ops.append(k.dma(outT[b].rearrange("(k p) t -> p k t", p=128)[:, :, tsl], cvf[:], reads=[b_cvf], writes=[b_out[b]]))
        if STAGE >= 9:
            ffn_pass(X2, bX2, outT, b_out, f2u, f2d, 2, 2, True)

        k.S.emit(final_wait_ops=final_ops)
    return nc


def _pack_inputs(inp):
    f = lambda a: np.ascontiguousarray(np.asarray(a, dtype=np.float32))
    col = np.zeros((128, NCOLP), np.float32)

    def put8(off, v):
        col[:, off:off + 8] = f(v).reshape(8, 128).T

    for i, nm in enumerate(["ln1_g", "ln1_b", "ln2_g", "ln2_b", "ln3_g", "ln3_b"]):
        put8(C_LN + 8 * i, inp[nm][0])
    col[:, C_BADA:C_BADA + 72] = f(inp["b_ada"][0]).reshape(72, 128).T
    b_in = f(inp["b_in"][0])
    col[:, C_BQK:C_BQK + 4] = b_in[0:512].reshape(4, 128).T
    cw = f(inp["qk_conv_w"][0])
    for j in range(5):
        col[:, C_CW + 4 * j:C_CW + 4 * j + 4] = cw[j].reshape(4, 128).T
    col[:, C_CB:C_CB + 4] = f(inp["qk_conv_b"][0]).reshape(4, 128).T
    col[:, C_GLUB:C_GLUB + 4] = f(inp["ssm_glu_b"][0]).reshape(4, 128).T
    col[:, C_SNG:C_SNG + 4] = f(inp["ssm_norm_g"][0]).reshape(4, 128).T
    g0 = 1536
    for i in range(4):
        col[0:4, C_BG + i] = b_in[g0 + 4 * i:g0 + 4 * i + 4]
    col[0:16, C_BG + 4] = b_in[g0:g0 + 16]
    row = np.zeros((1, NROWP), np.float32)
    row[0, R_BV:R_BV + 512] = b_in[512:1024]
    row[0, R_BO:R_BO + 512] = b_in[1024:1536]
    row[0, R_BU:R_BU + 512] = b_in[1552:2064]
    row[0, R_MNG:R_MNG + 512] = f(inp["mlstm_norm_g"][0])
    cst = np.zeros((128, 1024), np.float32)
    cst[:, 0:128] = np.eye(128, dtype=np.float32)
    ii = np.arange(128)
    cst[:, 128:256] = (ii[:, None] <= ii[None, :]).astype(np.float32)
    cst[:, 256:384] = (ii[:, None] >= ii[None, :]).astype(np.float32)
    cst[:, 384:896] = np.arange(512, dtype=np.float32)[None, :]
    a8 = np.arange(8, dtype=np.float32)
    for half, rows in ((0, slice(0, 64)), (1, slice(64, 128))):
        cst[rows, 896:904] = (7 - a8) if half == 0 else a8
        cst[rows, 904:912] = (-a8) if half == 0 else a8
        cst[rows, 912:920] = a8 if half == 0 else (-a8)
        cst[rows, 920:928] = (a8 + 1) if half == 0 else (8 - a8)
    cs2 = np.zeros((128, 1536), np.float32)
    cs2[:, 0:512] = (np.arange(512) % 128 != 0).astype(np.float32)[None, :]
    cs2[:, 512:640] = ((ii[:, None] // 16) <= (ii[None, :] // 16)).astype(np.float32)
    cs2[:, 640:768] = ((ii[:, None] // 16) >= (ii[None, :] // 16)).astype(np.float32)
    cs2[0:32, 768:800] = np.eye(32, dtype=np.float32)[::-1]
    cs2[:, 1344:1472] = np.eye(128, dtype=np.float32)[::-1]
    for r in range(4):
        cs2[r, 800 + r * 128:800 + (r + 1) * 128] = 1.0
    dd_ = f(inp["ssm_d"][0]).reshape(32, 16)
    cs2[:, 1312:1344] = np.tile(dd_.T, (8, 1))
    s5 = np.zeros((2, 64, 32, 70), np.float32)
    s5[..., 0] = f(inp["ssm_lam_re"][0]).transpose(0, 2, 1)
    s5[..., 1] = f(inp["ssm_lam_im"][0]).transpose(0, 2, 1)
    s5[..., 2] = f(inp["ssm_log_step"][0])[:, None, :]
    s5[..., 3:19] = f(inp["ssm_b_re"][0]).transpose(1, 0, 2)[None]
    s5[..., 19:35] = f(inp["ssm_b_im"][0]).transpose(1, 0, 2)[None]
    s5[..., 35:51] = f(inp["ssm_c_re"][0]).transpose(0, 3, 1, 2)
    s5[..., 51:67] = f(inp["ssm_c_im"][0]).transpose(0, 3, 1, 2)
    s5p = np.ascontiguousarray(s5.reshape(128, 32 * 70))
    shared = {
        "w_ada": f(inp["w_ada"][0]), "colpack": col, "rowpack": row, "consts": cst,
        "ffn1_w_up": f(inp["ffn1_w_up"][0]), "ffn1_w_down": f(inp["ffn1_w_down"][0]),
        "ffn2_w_up": f(inp["ffn2_w_up"][0]), "ffn2_w_down": f(inp["ffn2_w_down"][0]),
        "w_in": f(inp["w_in"][0]), "w_out": f(inp["w_out"][0]), "glu_w": f(inp["ssm_glu_w"][0]),
        "s5p": s5p, "consts2": cs2,
    }
    x = np.asarray(inp["x"], dtype=np.float32)
    c = np.asarray(inp["c"], dtype=np.float32)
    maps = []
    for core in range(8):
        m = dict(shared)
        m["xT"] = np.ascontiguousarray(x[2 * core:2 * core + 2].transpose(0, 2, 1))
        cc = c[2 * core:2 * core + 2]
        m["cT"] = np.ascontiguousarray(cc.reshape(2, 8, 128).transpose(2, 1, 0).reshape(128, 16))
        maps.append(m)
    return maps


_NC_CACHE = {}


def kernel(**inputs):
    maps = _pack_inputs(inputs)
    if "nc" not in _NC_CACHE:
        _NC_CACHE["nc"] = build_program()
    nc = _NC_CACHE["nc"]
    res = run_bass_kernel_spmd(nc, maps, core_ids=list(range(8)))
    outs = [r["outT"] for r in res.results]
    full = np.concatenate([o.transpose(0, 2, 1) for o in outs], axis=0)
    return np.ascontiguousarray(full.astype(np.float32))
```
